# Optimizing a Trainium2 kernel written in Bass

```python
import math
import jax
import jax.numpy as jnp
from jax import lax
import numpy as np

D_MODEL = 1024
BATCH = 2
SEQ = 8192
DEPTH = 4

HEAD_DIM = 64
ATTN_SCALE = HEAD_DIM ** -0.5
ROPE_THETA = 10000.0
Q_BLOCK = 128
NEG_INF = -1e30
NORM_EPS = 1e-6

A_HEADS = D_MODEL // (4 * HEAD_DIM)
A_KV_HEADS = max(1, A_HEADS // 2)
A_WINDOW = 128

B_HEADS = D_MODEL // (4 * HEAD_DIM)
B_KV_HEADS = 1
CMP_STRIDE = 16
CMP_LEN = 2 * CMP_STRIDE
CMP_HIDDEN = 4 * HEAD_DIM
SLC_LEN = 64
SLC_TOPK = 16
NSA_WINDOW = 512
SLC_FORCED_SCORE = 1e9

C_HEAD_DIM = 128
C_HEADS = D_MODEL // (2 * C_HEAD_DIM)
CONV_WIDTH = 4
GDN_CHUNK = 64

D_FF_RAW = -(-8 * D_MODEL // 3)
D_FF = -(-D_FF_RAW // 256) * 256

A_Q = A_HEADS * HEAD_DIM
A_KV = A_KV_HEADS * HEAD_DIM
B_Q = B_HEADS * HEAD_DIM
B_KV = B_KV_HEADS * HEAD_DIM
B_GATES = 3 * B_HEADS
C_QK = C_HEADS * C_HEAD_DIM
MIX_WIDTH = A_Q + B_Q + C_QK
IN_SPLITS = (A_Q, A_KV, A_KV, B_Q, B_KV, B_KV, B_KV, B_KV, B_KV, B_KV, B_GATES, 3 * C_QK, C_QK, C_HEADS, C_HEADS)
IN_WIDTH = sum(IN_SPLITS)

kernel_name = 'hymba_swa_nsa_gdn_adaln_trunk'


def rms_norm(x, gain):
    xf = x.astype(jnp.float32)
    y = xf * lax.rsqrt(jnp.mean(xf * xf, axis=-1, keepdims=True) + NORM_EPS)
    return (y * gain.astype(jnp.float32)).astype(x.dtype)


def l2norm(x):
    return x * lax.rsqrt(jnp.sum(x * x, axis=-1, keepdims=True) + NORM_EPS)


def rope_tables(seq, dim):
    inv = 1.0 / (ROPE_THETA ** (jnp.arange(0, dim, 2, dtype=jnp.float32) / dim))
    ang = jnp.arange(seq, dtype=jnp.float32)[:, None] * inv[None, :]
    return jnp.cos(ang), jnp.sin(ang)


def apply_rope(x, cos, sin):
    x1, x2 = jnp.split(x.astype(jnp.float32), 2, axis=-1)
    c = cos[None, :, None, :]
    s = sin[None, :, None, :]
    return jnp.concatenate([x1 * c - x2 * s, x2 * c + x1 * s], axis=-1).astype(x.dtype)


def banded_attention(q, k, v, window, sink=None):
    B, S, H, d = q.shape
    Hkv = k.shape[2]
    G = H // Hkv
    nb = S // Q_BLOCK
    n_prev = window // Q_BLOCK
    span = (n_prev + 1) * Q_BLOCK
    pad = ((0, 0), (window, 0), (0, 0), (0, 0))
    kb = jnp.pad(k, pad).reshape(B, nb + n_prev, Q_BLOCK, Hkv, d)
    vb = jnp.pad(v, pad).reshape(B, nb + n_prev, Q_BLOCK, Hkv, d)
    k_ctx = jnp.concatenate([kb[:, j:j + nb] for j in range(n_prev + 1)], axis=2)
    v_ctx = jnp.concatenate([vb[:, j:j + nb] for j in range(n_prev + 1)], axis=2)
    qb = q.reshape(B, nb, Q_BLOCK, Hkv, G, d)
    s = jnp.einsum('bnqhgd,bnkhd->bnhgqk', qb, k_ctx, preferred_element_type=jnp.float32) * ATTN_SCALE
    q_rel = jnp.arange(Q_BLOCK)[:, None]
    k_rel = jnp.arange(span)[None, :] - window
    band = (k_rel <= q_rel) & (k_rel > q_rel - window)
    in_seq = (jnp.arange(nb)[:, None] * Q_BLOCK + k_rel) >= 0
    mask = band[None, :, :] & in_seq[:, None, :]
    s = jnp.where(mask[None, :, None, None], s, NEG_INF)
    if sink is None:
        p = jax.nn.softmax(s, axis=-1)
    else:
        sink_col = jnp.broadcast_to(sink.astype(jnp.float32).reshape(1, 1, Hkv, G, 1, 1), s.shape[:-1] + (1,))
        p = jax.nn.softmax(jnp.concatenate([s, sink_col], axis=-1), axis=-1)[..., :-1]
    o = jnp.einsum('bnhgqk,bnkhd->bnqhgd', p.astype(v.dtype), v_ctx)
    return o.reshape(B, S, H, d)


def compress_tokens(x, w1, w2, pe):
    B, S, Hkv, d = x.shape
    strides = x.reshape(B, S // CMP_STRIDE, CMP_STRIDE, Hkv, d)
    win = jnp.concatenate([strides[:, :-1], strides[:, 1:]], axis=2) + pe[:, None, :]
    flat = win.transpose(0, 1, 3, 2, 4).reshape(B, S // CMP_STRIDE - 1, Hkv, CMP_LEN * d)
    return jax.nn.silu(flat @ w1) @ w2


def nsa_mixer(q, k_cmp_in, v_cmp_in, k_slc, v_slc, k_win, v_win, gate_logits, cos, sin,
              cmp_k_w1, cmp_k_w2, cmp_v_w1, cmp_v_w2, cmp_pe_k, cmp_pe_v):
    B, S, H, d = q.shape
    Hkv = k_slc.shape[2]
    G = H // Hkv
    n_cmp = S // CMP_STRIDE - 1
    n_slc = S // SLC_LEN
    nb = S // Q_BLOCK
    ratio = SLC_LEN // CMP_STRIDE
    top_k = min(SLC_TOPK, n_slc)

    k_cmp = compress_tokens(k_cmp_in, cmp_k_w1, cmp_k_w2, cmp_pe_k)
    v_cmp = compress_tokens(v_cmp_in, cmp_v_w1, cmp_v_w2, cmp_pe_v)
    q_rot = apply_rope(q, cos, sin)
    ks_blocks = apply_rope(k_slc, cos, sin).reshape(B, n_slc, SLC_LEN, Hkv, d).transpose(0, 3, 1, 2, 4)
    vs_blocks = v_slc.reshape(B, n_slc, SLC_LEN, Hkv, d).transpose(0, 3, 1, 2, 4)
    cmp_end = jnp.arange(n_cmp) * CMP_STRIDE + (CMP_LEN - 1)
    blk_ids = jnp.arange(n_slc)
    gather = jax.vmap(jax.vmap(lambda blocks, ids: blocks[ids]))

    def block_fn(args):
        i, qb, qrb = args
        t = i * Q_BLOCK + jnp.arange(Q_BLOCK)
        s = jnp.einsum('bqhgd,bnhd->bhgqn', qb, k_cmp, preferred_element_type=jnp.float32) * ATTN_SCALE
        valid = cmp_end[None, :] <= t[:, None]
        p_cmp = jnp.where(valid, jax.nn.softmax(jnp.where(valid, s, NEG_INF), axis=-1), 0.0)
        o_cmp = jnp.einsum('bhgqn,bnhd->bqhgd', p_cmp.astype(v_cmp.dtype), v_cmp)
        imp = jnp.pad(p_cmp.sum(axis=2), ((0, 0), (0, 0), (0, 0), (0, ratio * n_slc - n_cmp)))
        imp = imp.reshape(B, Hkv, Q_BLOCK, n_slc, ratio)
        imp = imp.sum(-1) + jnp.pad(imp[..., :-1, -1], ((0, 0), (0, 0), (0, 0), (1, 0)))
        cur = (t // SLC_LEN)[:, None]
        forced = (blk_ids == 0) | (blk_ids == cur) | (blk_ids == cur - 1)
        causal = blk_ids * SLC_LEN <= t[:, None]
        score = jnp.where(forced, SLC_FORCED_SCORE, jnp.where(causal, imp, NEG_INF))
        _, idx = lax.top_k(score, top_k)
        kg = gather(ks_blocks, idx).reshape(B, Hkv, Q_BLOCK, top_k * SLC_LEN, d)
        vg = gather(vs_blocks, idx).reshape(B, Hkv, Q_BLOCK, top_k * SLC_LEN, d)
        kpos = (idx[..., None] * SLC_LEN + jnp.arange(SLC_LEN)).reshape(B, Hkv, Q_BLOCK, top_k * SLC_LEN)
        sel_ok = kpos <= t[None, None, :, None]
        s = jnp.einsum('bqhgd,bhqkd->bhgqk', qrb, kg, preferred_element_type=jnp.float32) * ATTN_SCALE
        p = jax.nn.softmax(jnp.where(sel_ok[:, :, None], s, NEG_INF), axis=-1)
        o_slc = jnp.einsum('bhgqk,bhqkd->bqhgd', p.astype(vg.dtype), vg)
        return o_cmp, o_slc

    def to_blocks(a):
        return a.reshape(B, nb, Q_BLOCK, Hkv, G, d).transpose(1, 0, 2, 3, 4, 5)

    def from_blocks(a):
        return a.transpose(1, 0, 2, 3, 4, 5).reshape(B, S, H, d)

    o_cmp, o_slc = lax.map(block_fn, (jnp.arange(nb), to_blocks(q), to_blocks(q_rot)))
    o_win = banded_attention(q_rot, apply_rope(k_win, cos, sin), v_win, NSA_WINDOW)
    gates = jax.nn.sigmoid(gate_logits.astype(jnp.float32)).reshape(B, S, 3, H, 1).astype(q.dtype)
    o = gates[:, :, 0] * from_blocks(o_cmp) + gates[:, :, 1] * from_blocks(o_slc) + gates[:, :, 2] * o_win
    return o.reshape(B, S, H * d)


def short_causal_conv(x, w):
    y = lax.conv_general_dilated(x, w[:, None, :].astype(x.dtype), window_strides=(1,),
                                 padding=[(CONV_WIDTH - 1, 0)], dimension_numbers=('NWC', 'WIO', 'NWC'),
                                 feature_group_count=x.shape[-1])
    return jax.nn.silu(y)


def gated_delta_net(qkv, a, b, z, conv_w, A_log, dt_bias, norm_w):
    B, S, _ = qkv.shape
    H, dk = C_HEADS, C_HEAD_DIM
    C = GDN_CHUNK
    nc = S // C
    f32 = jnp.float32
    q, k, v = jnp.split(short_causal_conv(qkv, conv_w).astype(f32), 3, axis=-1)
    q = l2norm(q.reshape(B, S, H, dk)) * (dk ** -0.5)
    k = l2norm(k.reshape(B, S, H, dk))
    v = v.reshape(B, S, H, dk)
    beta = jax.nn.sigmoid(b.astype(f32))
    g = -jnp.exp(A_log.astype(f32)) * jax.nn.softplus(a.astype(f32) + dt_bias.astype(f32))

    def chunk(t):
        return t.reshape(B, nc, C, H, dk).transpose(0, 3, 1, 2, 4)

    q, k, v = chunk(q), chunk(k), chunk(v)
    beta = beta.reshape(B, nc, C, H).transpose(0, 3, 1, 2)
    gc = jnp.cumsum(g.reshape(B, nc, C, H).transpose(0, 3, 1, 2), axis=-1)
    lower = jnp.tril(jnp.ones((C, C), dtype=bool))
    strict = jnp.tril(jnp.ones((C, C), dtype=bool), -1)
    decay = jnp.where(lower, jnp.exp(jnp.where(lower, gc[..., :, None] - gc[..., None, :], 0.0)), 0.0)
    k_beta = k * beta[..., None]
    a_mat = jnp.where(strict, jnp.einsum('bhncd,bhnsd->bhncs', k_beta, k) * decay, 0.0)
    rhs = jnp.concatenate([v * beta[..., None], k_beta * jnp.exp(gc)[..., None]], axis=-1)
    sol = lax.linalg.triangular_solve(a_mat + jnp.eye(C, dtype=f32), rhs, left_side=True, lower=True,
                                      unit_diagonal=True)
    u, w = sol[..., :dk], sol[..., dk:]
    qk = jnp.where(lower, jnp.einsum('bhncd,bhnsd->bhncs', q, k) * decay, 0.0)
    q_dec = q * jnp.exp(gc)[..., None]
    k_dec = k * jnp.exp(gc[..., -1:] - gc)[..., None]
    g_last = jnp.exp(gc[..., -1])

    def step(state, inp):
        u_c, w_c, qk_c, qd_c, kd_c, gl_c = inp
        v_new = u_c - jnp.einsum('bhcd,bhde->bhce', w_c, state)
        o_c = jnp.einsum('bhcd,bhde->bhce', qd_c, state) + jnp.einsum('bhcs,bhse->bhce', qk_c, v_new)
        state = state * gl_c[..., None, None] + jnp.einsum('bhcd,bhce->bhde', kd_c, v_new)
        return state, o_c

    xs = tuple(jnp.moveaxis(t, 2, 0) for t in (u, w, qk, q_dec, k_dec, g_last))
    _, o = lax.scan(step, jnp.zeros((B, H, dk, dk), f32), xs)
    o = o.transpose(1, 0, 3, 2, 4).reshape(B, S, H, dk)
    o = rms_norm(o, norm_w) * jax.nn.silu(z.reshape(B, S, H, dk).astype(f32))
    return o.reshape(B, S, H * dk).astype(z.dtype)


def hybrid_layer(x, c, cos, sin, norm_mix, norm_ffn, ada_w, ada_b, w_in, attn_sinks,
                 cmp_k_w1, cmp_k_w2, cmp_v_w1, cmp_v_w2, cmp_pe_k, cmp_pe_v,
                 conv_w, A_log, dt_bias, gdn_norm, w_out, w_gate_up, w_down):
    B, S, _ = x.shape
    mod = jax.nn.silu(c) @ ada_w + ada_b
    sh_m, sc_m, gt_m, sh_f, sc_f, gt_f = jnp.split(mod[:, None, :], 6, axis=-1)

    h = rms_norm(x, norm_mix) * (1 + sc_m) + sh_m
    points = np.cumsum(IN_SPLITS)[:-1].tolist()
    (aq, ak, av, bq, bkc, bvc, bks, bvs, bkw, bvw, bg, cqkv, cz, ca, cb) = jnp.split(h @ w_in, points, axis=-1)

    def heads(t, n):
        return t.reshape(B, S, n, -1)

    o_a = banded_attention(apply_rope(heads(aq, A_HEADS), cos, sin), apply_rope(heads(ak, A_KV_HEADS), cos, sin),
                           heads(av, A_KV_HEADS), A_WINDOW, attn_sinks).reshape(B, S, A_Q)
    o_b = nsa_mixer(heads(bq, B_HEADS), heads(bkc, B_KV_HEADS), heads(bvc, B_KV_HEADS),
                    heads(bks, B_KV_HEADS), heads(bvs, B_KV_HEADS), heads(bkw, B_KV_HEADS), heads(bvw, B_KV_HEADS),
                    bg, cos, sin, cmp_k_w1, cmp_k_w2, cmp_v_w1, cmp_v_w2, cmp_pe_k, cmp_pe_v)
    o_c = gated_delta_net(cqkv, ca, cb, cz, conv_w, A_log, dt_bias, gdn_norm)
    x = x + gt_m * (jnp.concatenate([o_a, o_b, o_c], axis=-1) @ w_out)

    h = rms_norm(x, norm_ffn) * (1 + sc_f) + sh_f
    gate, up = jnp.split(h @ w_gate_up, 2, axis=-1)
    return x + gt_f * ((jax.nn.silu(gate) * up) @ w_down)


def setup_inputs(seed: int = 0) -> dict:
    key = jax.random.key(seed)
    ks = jax.random.split(key, 24)
    f32 = jnp.float32
    L, D = DEPTH, D_MODEL
    cmp_in = CMP_LEN * HEAD_DIM

    def nrm(k, shape, scale):
        return jax.random.normal(k, shape, f32) * scale

    dt = jnp.exp(jax.random.uniform(ks[16], (L, C_HEADS), f32, math.log(1e-3), math.log(1e-1)))
    return {
        'x': nrm(ks[0], (BATCH, SEQ, D), 1.0),
        'c': nrm(ks[1], (BATCH, D), 1.0),
        'norm_mix': 1.0 + nrm(ks[2], (L, D), 0.1),
        'norm_ffn': 1.0 + nrm(ks[3], (L, D), 0.1),
        'ada_w': nrm(ks[4], (L, D, 6 * D), 0.5 * D ** -0.5),
        'ada_b': nrm(ks[5], (L, 6 * D), 0.01),
        'w_in': nrm(ks[6], (L, D, IN_WIDTH), D ** -0.5),
        'attn_sinks': nrm(ks[7], (L, A_HEADS), 1.0),
        'cmp_k_w1': nrm(ks[8], (L, cmp_in, CMP_HIDDEN), cmp_in ** -0.5),
        'cmp_k_w2': nrm(ks[9], (L, CMP_HIDDEN, HEAD_DIM), CMP_HIDDEN ** -0.5),
        'cmp_v_w1': nrm(ks[10], (L, cmp_in, CMP_HIDDEN), cmp_in ** -0.5),
        'cmp_v_w2': nrm(ks[11], (L, CMP_HIDDEN, HEAD_DIM), CMP_HIDDEN ** -0.5),
        'cmp_pe_k': nrm(ks[12], (L, CMP_LEN, HEAD_DIM), 0.1),
        'cmp_pe_v': nrm(ks[13], (L, CMP_LEN, HEAD_DIM), 0.1),
        'gdn_conv_w': nrm(ks[14], (L, CONV_WIDTH, 3 * C_QK), CONV_WIDTH ** -0.5),
        'gdn_A_log': jnp.log(jax.random.uniform(ks[15], (L, C_HEADS), f32, 1.0, 16.0)),
        'gdn_dt_bias': dt + jnp.log(-jnp.expm1(-dt)),
        'gdn_norm': 1.0 + nrm(ks[17], (L, C_HEAD_DIM), 0.1),
        'w_out': nrm(ks[18], (L, MIX_WIDTH, D), MIX_WIDTH ** -0.5),
        'w_gate_up': nrm(ks[19], (L, D, 2 * D_FF), D ** -0.5),
        'w_down': nrm(ks[20], (L, D_FF, D), D_FF ** -0.5),
        'final_norm': 1.0 + nrm(ks[21], (D,), 0.1),
    }


def reference(x, c, norm_mix, norm_ffn, ada_w, ada_b, w_in, attn_sinks, cmp_k_w1, cmp_k_w2, cmp_v_w1,
              cmp_v_w2, cmp_pe_k, cmp_pe_v, gdn_conv_w, gdn_A_log, gdn_dt_bias, gdn_norm, w_out,
              w_gate_up, w_down, final_norm):
    cos, sin = rope_tables(x.shape[1], HEAD_DIM)
    for l in range(DEPTH):
        x = hybrid_layer(x, c, cos, sin, norm_mix[l], norm_ffn[l], ada_w[l], ada_b[l], w_in[l], attn_sinks[l],
                         cmp_k_w1[l], cmp_k_w2[l], cmp_v_w1[l], cmp_v_w2[l], cmp_pe_k[l], cmp_pe_v[l],
                         gdn_conv_w[l], gdn_A_log[l], gdn_dt_bias[l], gdn_norm[l], w_out[l],
                         w_gate_up[l], w_down[l])
    return rms_norm(x, final_norm)
```

```python
import bisect
import math
from contextlib import ExitStack

import numpy as np
import ml_dtypes
import concourse.bass as bass
import concourse.mybir as mybir
from concourse.bass_utils import run_bass_kernel_spmd

F32 = mybir.dt.float32
BF16 = mybir.dt.bfloat16
I32 = mybir.dt.int32
AF = mybir.ActivationFunctionType
ALU = mybir.AluOpType
AX = mybir.AxisListType

D = 1024
SEQ = 8192
BATCH = 2
DEPTH = 4
T = 2048
NT = 512
DFF = 2816
EPS = 1e-6
ENGS = ("pe", "act", "dve", "pool", "sp")


class _Op:
    __slots__ = ("eng", "fn", "reads", "writes", "dma", "deps", "signal", "ordinal", "gidx")


def _is_psum(r):
    if isinstance(r, tuple):
        r = r[0]
    return isinstance(r, str) and (r.startswith("ps") or r.startswith("pb"))


class Prog:
    def __init__(self, nc):
        self.nc = nc
        self.ops = []
        self.per_eng = {e: [] for e in ENGS}
        self.last_w = {}
        self.readers = {}
        self.dma_groups = {}

    def add(self, eng, fn, reads=(), writes=(), dma=None):
        op = _Op()
        op.eng, op.fn, op.reads, op.writes, op.dma = eng, fn, tuple(reads), tuple(writes), dma
        op.deps = set()
        op.signal = False
        op.ordinal = None
        op.gidx = len(self.ops)
        for r in op.reads:
            w = self.last_w.get(r)
            if w is not None:
                op.deps.add(w)
            if _is_psum(r):
                for rd in self.readers.get(r, ()):
                    if rd.eng != eng:
                        op.deps.add(rd)
            self.readers.setdefault(r, []).append(op)
        for r in op.writes:
            w = self.last_w.get(r)
            if w is not None:
                if dma is not None and w.dma == dma:
                    op.deps |= w.deps
                else:
                    op.deps.add(w)
            for rd in self.readers.get(r, ()):
                if rd is not op:
                    op.deps.add(rd)
            self.readers[r] = []
            self.last_w[r] = op
        op.deps.discard(op)
        self.ops.append(op)
        self.per_eng[eng].append(op)
        if dma is not None:
            self.dma_groups.setdefault(dma, []).append(op)
        return op

    def pe(self, fn, reads=(), writes=()):
        return self.add("pe", fn, reads, writes)

    def act(self, fn, reads=(), writes=()):
        return self.add("act", fn, reads, writes)

    def dve(self, fn, reads=(), writes=()):
        return self.add("dve", fn, reads, writes)

    def pool(self, fn, reads=(), writes=()):
        return self.add("pool", fn, reads, writes)

    def dma(self, q, sem, fn, reads=(), writes=()):
        return self.add(q, fn, reads, writes, dma=sem)

    def emit(self, final_sems=()):
        nc = self.nc
        for op in self.ops:
            for d in op.deps:
                if d.dma is None:
                    if d.eng == "pe" and op.eng == "pe":
                        continue
                    d.signal = True
        for e in ENGS:
            n = 0
            for op in self.per_eng[e]:
                if op.dma is None and op.signal:
                    n += 1
                    op.ordinal = n
        gidx_of = {k: [o.gidx for o in ops] for k, ops in self.dma_groups.items()}
        with ExitStack() as es:
            sems = {e: es.enter_context(nc.semaphore("s_" + e)) for e in ENGS}
            dsems = {k: es.enter_context(nc.semaphore("d_" + str(k))) for k in self.dma_groups}
            block = es.enter_context(nc.Block())
            deco = {"pe": block.tensor, "act": block.scalar, "dve": block.vector, "pool": block.gpsimd,
                    "sp": block.sync}

            def make(e):
                def body(eng):
                    known = {}
                    for op in self.per_eng[e]:
                        need = {}
                        for d in op.deps:
                            if d.dma is not None:
                                key = ("d", d.dma)
                                v = 16 * bisect.bisect_left(gidx_of[d.dma], op.gidx)
                            else:
                                if d.eng == "pe" and e == "pe":
                                    continue
                                key = ("e", d.eng)
                                v = d.ordinal
                            if v > need.get(key, 0):
                                need[key] = v
                        for key, v in need.items():
                            if known.get(key, 0) >= v:
                                continue
                            known[key] = v
                            s = dsems[key[1]] if key[0] == "d" else sems[key[1]]
                            eng.wait_ge(s, v)
                        ins = op.fn(eng)
                        if op.dma is not None:
                            ins.then_inc(dsems[op.dma], 16)
                        elif op.signal:
                            ins.then_inc(sems[e], 1)
                    if e == "sp":
                        for k in final_sems:
                            if k not in dsems:
                                continue
                            eng.wait_ge(dsems[k], 16 * len(self.dma_groups[k]))
                return body

            for e in ENGS:
                if self.per_eng[e] or e == "sp":
                    deco[e](make(e))


class Ctx:
    def __init__(self, nc, es):
        self.nc, self.es = nc, es
        self.p = Prog(nc)
        self.n = 0

    def sb(self, shape, dt, name=None):
        self.n += 1
        return self.es.enter_context(self.nc.sbuf_tensor("s_" + (name or f"sb{self.n}"), list(shape), dt))

    def ps(self, shape, dt=F32, name=None):
        self.n += 1
        return self.es.enter_context(self.nc.psum_tensor(name or f"ps{self.n}", list(shape), dt))

    def din(self, name, shape, dt=F32):
        return self.nc.dram_tensor(name, list(shape), dt, kind="ExternalInput").ap()

    def dout(self, name, shape, dt=F32):
        return self.nc.dram_tensor(name, list(shape), dt, kind="ExternalOutput").ap()


def _swap(c0):
    return [(c0 + 32, c0 + 64), (c0, c0 + 32)]


def _plain(c0, n=64):
    return [(c0, c0 + n)]


FM_BLOCKS = [
    ("aq01", _plain(0, 128)), ("aq01s", _swap(0) + _swap(64)),
    ("aq23", _plain(128, 128)), ("aq23s", _swap(128) + _swap(192)),
    ("ak0", _plain(256) + _plain(256)), ("ak0s", _swap(256) + _swap(256)),
    ("ak1", _plain(320) + _plain(320)), ("ak1s", _swap(320) + _swap(320)),
    ("bq01", _plain(512, 128)), ("bq01s", _swap(512) + _swap(576)),
    ("bq23", _plain(640, 128)), ("bq23s", _swap(640) + _swap(704)),
    ("bks", _plain(896) + _plain(896)), ("bkss", _swap(896) + _swap(896)),
    ("bkw", _plain(1024) + _plain(1024)), ("bkws", _swap(1024) + _swap(1024)),
    ("cmp", _plain(768, 128)),
] + [("c%d" % i, _plain(1164 + 128 * i, 128)) for i in range(12)]
FM_OFF = {n: 128 * i for i, (n, _) in enumerate(FM_BLOCKS)}
GATE_OFF = 128 * len(FM_BLOCKS)
TM_OFF = GATE_OFF + 12
TM_COLS = [(384, 512), (960, 1024), (1088, 1152), (2700, 3212), (3212, 3220)]
TM_N = 128 + 64 + 64 + 512 + 8
WIN_COLS = TM_OFF + TM_N


def build_A(dbg=9):
    nc = bass.Bass("TRN2", target_bir_lowering=False)
    with ExitStack() as es:
        c = Ctx(nc, es)
        p = c.p
        xT = c.din("xT", [128, 8, T])
        tabs = c.din("tabs", [128, 24])
        pos0 = c.din("pos0", [128, 1])
        w = c.din("w", [D, 3220])
        wr = w.rearrange("(k p) c -> p k c", p=128)
        o_rope = c.dout("o_rope", [8, 128, T], BF16)
        o_plain = c.dout("o_plain", [3, 128, T], BF16)
        o_c = c.dout("o_c", [12, 128, T], F32)
        o_g = c.dout("o_g", [12, T], F32)
        o_v = c.dout("o_v", [T, 256], BF16)
        o_z = c.dout("o_z", [T, 520], F32)

        W = c.sb([128, 8, WIN_COLS], BF16, "W")
        tb = c.sb([128, 24], F32, "tb")
        s1 = c.sb([128, 8], F32, "s1")
        p0 = c.sb([128, 1], F32, "p0")
        S2 = c.sb([128, T], F32, "S2")
        ang = c.sb([128, T], F32, "ang")
        C2 = ang
        invrow = c.sb([1, 128], F32, "invrow")
        one1 = c.sb([1, 1], F32, "one1")
        inv = c.sb([128, 1], F32, "inv")
        ones_bf = c.sb([128, 128], BF16, "ones_bf")
        xt = [c.sb([128, 8, NT], F32, f"xt{i}") for i in range(1)]
        sq = c.sb([128, 8, NT], BF16, "sq")
        rstd = c.sb([128, NT], F32, "rstd")
        tmp = [c.sb([128, NT], F32, f"tmp{i}") for i in range(2)]
        hT = c.sb([128, 8, NT], BF16, "hT")
        outR = [c.sb([128, T], BF16, f"outR{i}") for i in range(8)]
        outP = [c.sb([128, T], BF16, f"outP{i}") for i in range(3)]
        t1 = [c.sb([128, NT], F32, f"t1_{i}") for i in range(2)]
        t2 = [c.sb([128, NT], F32, f"t2_{i}") for i in range(2)]
        stc = [c.sb([128, NT], F32, f"stc{i}") for i in range(3)]
        stg = c.sb([12, T], F32, "stg")
        stv = [c.sb([128, 256], BF16, f"stv{i}") for i in range(2)]
        stz = [c.sb([128, 520], F32, f"stz{i}") for i in range(2)]
        psb = [c.ps([128, 512], F32, f"psb{i}") for i in range(8)]

        p.dma("sp", "tb", lambda e: e.dma_start(out=tb[:], in_=tabs[:, :]), writes=["tb"])
        p.dma("sp", "p0", lambda e: e.dma_start(out=p0[:], in_=pos0[:, :]), writes=["p0"])
        col = 0
        wi = 0
        for name, rngs in FM_BLOCKS:
            for (a, b) in rngs:
                p.dma("pool", "W", lambda e, a=a, b=b, col=col: e.dma_start(
                    out=W[:, :, col:col + b - a], in_=wr[:, :, a:b]), writes=["W"])
                col += b - a
        p.dma("pool", "W", lambda e: e.dma_start(out=W[:, :, GATE_OFF:GATE_OFF + 12], in_=wr[:, :, 1152:1164]),
              writes=["W"])
        col = TM_OFF
        for (a, b) in TM_COLS:
            p.dma("pool", "W", lambda e, a=a, b=b, col=col: e.dma_start(
                out=W[:, :, col:col + b - a], in_=wr[:, :, a:b]), writes=["W"])
            col += b - a

        p.dve(lambda e: e.tensor_scalar(s1[:], tb[:, 8:16], 1.0, 32.0, op0=ALU.add, op1=ALU.mult),
              reads=["tb"], writes=["s1"])
        p.dve(lambda e: e.tensor_tensor(s1[:], s1[:], tb[:, 0:8], op=ALU.mult), reads=["tb", "s1"], writes=["s1"])
        p.dve(lambda e: e.memset(ones_bf[:], 1.0), writes=["ones_bf"])
        p.dve(lambda e: e.memset(one1[:], 1.0), writes=["one1"])
        for i in range(32):
            v = float(np.float32(1.0) / np.float32(np.float32(10000.0) ** np.float32(2 * i / 64.0)))
            p.dve(lambda e, i=i, v=v: e.memset(invrow[0:1, i:128:32], v), writes=["invrow"])
        p.pe(lambda e: e.matmul(psb[0][:, 0:1], invrow[0:1, :], one1[0:1, 0:1], start=True, stop=True),
             reads=["invrow", "one1"], writes=["psb0"])
        p.act(lambda e: e.copy(inv[:], psb[0][:, 0:1]), reads=["psb0"], writes=["inv"])
        p.pool(lambda e: e.iota(ang[:], [[1, T]], base=0, channel_multiplier=0,
                                allow_small_or_imprecise_dtypes=True), writes=["ang"])
        p.dve(lambda e: e.tensor_scalar(ang[:], ang[:], p0[:, 0:1], inv[:, 0:1], op0=ALU.add, op1=ALU.mult),
              reads=["ang", "p0", "inv"], writes=["ang"])
        TWO_PI = 2.0 * math.pi
        CW1 = 6.28125
        CW2 = TWO_PI - CW1
        nI = c.sb([128, NT], I32, "nI")
        yy = c.sb([128, NT], F32, "yy")
        mm_ = c.sb([128, NT], F32, "mm_")

        def sin_tile(dst, ts, shift):
            p.dve(lambda e: e.tensor_scalar(yy[:], ang[:, ts], float(shift), None, op0=ALU.add),
                  reads=["ang"], writes=["yy"])
            p.dve(lambda e: e.tensor_scalar(nI[:], yy[:], 1.0 / TWO_PI, None, op0=ALU.mult),
                  reads=["yy"], writes=["nI"])
            p.dve(lambda e: e.scalar_tensor_tensor(yy[:], nI[:], -CW1, yy[:], op0=ALU.mult, op1=ALU.add),
                  reads=["yy", "nI"], writes=["yy"])
            p.dve(lambda e: e.scalar_tensor_tensor(yy[:], nI[:], -CW2, yy[:], op0=ALU.mult, op1=ALU.add),
                  reads=["yy", "nI"], writes=["yy"])
            p.dve(lambda e: e.tensor_scalar(mm_[:], yy[:], math.pi, -TWO_PI, op0=ALU.is_gt, op1=ALU.mult),
                  reads=["yy"], writes=["mm_"])
            p.dve(lambda e: e.tensor_tensor(yy[:], yy[:], mm_[:], op=ALU.add), reads=["yy", "mm_"], writes=["yy"])
            p.dve(lambda e: e.tensor_scalar(yy[:], yy[:], -math.pi, math.pi, op0=ALU.max, op1=ALU.min),
                  reads=["yy"], writes=["yy"])
            p.act(lambda e: e.activation(dst[:, ts], yy[:], AF.Sin), reads=["yy"], writes=["S2", "ang", "C2"])

        for ti in range(T // NT):
            ts_ = slice(ti * NT, (ti + 1) * NT)
            sin_tile(S2, ts_, 0.0)
            sin_tile(C2, ts_, 0.5 * math.pi)
        for base in (0, 64):
            p.act(lambda e, base=base: e.mul(S2[base:base + 32, :], S2[base:base + 32, :], -1.0),
                  reads=["S2"], writes=["S2"])

        rope_pairs = [("aq01", 0), ("aq23", 1), ("ak0", 2), ("ak1", 3), ("bq01", 4), ("bq23", 5), ("bks", 6),
                      ("bkw", 7)]
        bank = [0]
        deps = c.sb([128, 1], F32, "deps")
        p.dve(lambda e: e.memset(deps[:], float(D * EPS)), writes=["deps"])

        def nextbank():
            b = bank[0]
            bank[0] = (b + 1) % 8
            return b

        def mm_block(coff, ncols_m, bk, rd):
            for k in range(8):
                p.pe(lambda e, k=k: e.matmul(psb[bk][0:ncols_m, :], W[:, k, coff:coff + ncols_m], hT[:, k, :],
                                             start=(k == 0), stop=(k == 7)),
                     reads=["W", ("hT", k)], writes=[f"psb{bk}"])

        for ti in range(T // NT if dbg >= 2 else 0):
            x_ = xt[0]
            xr = "xt0"
            ts = slice(ti * NT, (ti + 1) * NT)
            p.dma("sp", xr, lambda e, x_=x_, ts=ts: e.dma_start(out=x_[:], in_=xT[:, :, ts]), writes=[xr])
            p.act(lambda e, x_=x_: e.activation(sq[:], x_[:], AF.Square), reads=[xr], writes=["sq"])
            b0 = nextbank()
            for k in range(8):
                p.pe(lambda e, k=k, b0=b0: e.matmul(psb[b0][:], ones_bf[:], sq[:, k, :], start=(k == 0), stop=(k == 7)),
                     reads=["ones_bf", "sq"], writes=[f"psb{b0}"])
            p.act(lambda e, b0=b0: e.activation(rstd[:], psb[b0][:], AF.Sqrt, bias=deps[:, 0:1]),
                  reads=[f"psb{b0}", "deps"], writes=["rstd"])
            p.dve(lambda e: e.reciprocal(rstd[:], rstd[:]), reads=["rstd"], writes=["rstd"])
            for k in range(8):
                tm = tmp[k % 2]
                tr = f"tmp{k % 2}"
                p.dve(lambda e, k=k, tm=tm, x_=x_: e.tensor_tensor(tm[:], x_[:, k, :], rstd[:], op=ALU.mult),
                      reads=[xr, "rstd"], writes=[tr])
                p.act(lambda e, k=k, tm=tm: e.activation(hT[:, k, :], tm[:], AF.Identity, bias=tb[:, 16 + k:17 + k],
                                                         scale=s1[:, k:k + 1]),
                      reads=[tr, "s1", "tb"], writes=[("hT", k)])
            for name, oi in (rope_pairs if dbg >= 3 else []):
                b1 = nextbank()
                mm_block(FM_OFF[name], 128, b1, None)
                b2 = nextbank()
                mm_block(FM_OFF[name + "s"], 128, b2, None)
                a1 = t1[oi % 2]
                a2 = t2[oi % 2]
                p.dve(lambda e, b1=b1, a1=a1, ts=ts: e.tensor_tensor(a1[:], psb[b1][:], C2[:, ts], op=ALU.mult),
                      reads=[f"psb{b1}", "C2"], writes=[f"t1_{oi % 2}"])
                if name.startswith("bq"):
                    po = outP[oi - 4]
                    p.act(lambda e, b1=b1, po=po, ts=ts: e.copy(po[:, ts], psb[b1][:]),
                          reads=[f"psb{b1}"], writes=[f"outP{oi - 4}"])
                p.dve(lambda e, b2=b2, a2=a2, ts=ts: e.tensor_tensor(a2[:], psb[b2][:], S2[:, ts], op=ALU.mult),
                      reads=[f"psb{b2}", "S2"], writes=[f"t2_{oi % 2}"])
                ro = outR[oi]
                p.dve(lambda e, a1=a1, a2=a2, ro=ro, ts=ts: e.tensor_tensor(ro[:, ts], a1[:], a2[:], op=ALU.add),
                       reads=[f"t1_{oi % 2}", f"t2_{oi % 2}"], writes=[f"outR{oi}"])
            if dbg < 4:
                continue
            b1 = nextbank()
            mm_block(FM_OFF["cmp"], 128, b1, None)
            p.act(lambda e, b1=b1, ts=ts: e.copy(outP[2][:, ts], psb[b1][:]), reads=[f"psb{b1}"], writes=["outP2"])
            for i in range(12):
                b1 = nextbank()
                mm_block(FM_OFF["c%d" % i], 128, b1, None)
                sidx = i % 3
                st = stc[sidx]
                p.act(lambda e, b1=b1, st=st: e.copy(st[:], psb[b1][:]), reads=[f"psb{b1}"], writes=[f"stc{sidx}"])
                p.dma("sp", f"stc{sidx}", lambda e, st=st, i=i, ts=ts: e.dma_start(out=o_c[i, :, ts], in_=st[:]),
                      reads=[f"stc{sidx}"], writes=[("o_c", i, ti)])
            if dbg < 5:
                continue
            b1 = nextbank()
            mm_block(GATE_OFF, 12, b1, None)
            p.act(lambda e, b1=b1, ts=ts: e.copy(stg[:, ts], psb[b1][0:12, :]), reads=[f"psb{b1}"], writes=["stg"])
            if dbg < 6:
                continue
            for s in range(NT // 128):
                ss = slice(s * 128, (s + 1) * 128)
                tok = slice(ti * NT + s * 128, ti * NT + (s + 1) * 128)
                b1 = nextbank()
                for k in range(8):
                    p.pe(lambda e, k=k, b1=b1, ss=ss: e.matmul(psb[b1][:, 0:256], hT[:, k, ss],
                                                             W[:, k, TM_OFF:TM_OFF + 256], start=(k == 0), stop=(k == 7)),
                         reads=["W", ("hT", k)], writes=[f"psb{b1}"])
                b2 = nextbank()
                for k in range(8):
                    p.pe(lambda e, k=k, b2=b2, ss=ss: e.matmul(psb[b2][:, 0:512], hT[:, k, ss],
                                                             W[:, k, TM_OFF + 256:TM_OFF + 768], start=(k == 0),
                                                             stop=(k == 7)),
                         reads=["W", ("hT", k)], writes=[f"psb{b2}"])
                b3 = nextbank()
                for k in range(8):
                    p.pe(lambda e, k=k, b3=b3, ss=ss: e.matmul(psb[b3][:, 0:8], hT[:, k, ss],
                                                             W[:, k, TM_OFF + 768:TM_OFF + 776], start=(k == 0),
                                                             stop=(k == 7)),
                         reads=["W", ("hT", k)], writes=[f"psb{b3}"])
                sv = stv[s % 2]
                sz = stz[s % 2]
                p.act(lambda e, b1=b1, sv=sv: e.copy(sv[:], psb[b1][:, 0:256]), reads=[f"psb{b1}"], writes=[f"stv{s % 2}"])
                p.dve(lambda e, b2=b2, sz=sz: e.tensor_copy(sz[:, 0:512], psb[b2][:, 0:512]), reads=[f"psb{b2}"],
                      writes=[f"stz{s % 2}"])
                p.dve(lambda e, b3=b3, sz=sz: e.tensor_copy(sz[:, 512:520], psb[b3][:, 0:8]), reads=[f"psb{b3}"],
                      writes=[f"stz{s % 2}"])
                p.dma("sp", f"stv{s % 2}", lambda e, sv=sv, tok=tok: e.dma_start(out=o_v[tok, :], in_=sv[:]),
                      reads=[f"stv{s % 2}"], writes=[("o_v", ti, s)])
                p.dma("sp", f"stz{s % 2}", lambda e, sz=sz, tok=tok: e.dma_start(out=o_z[tok, :], in_=sz[:]),
                      reads=[f"stz{s % 2}"], writes=[("o_z", ti, s)])
        fs = ["stc0", "stc1", "stc2", "stv0", "stv1", "stz0", "stz1"]
        for i in range(8):
            p.dma("sp", f"oR{i}", lambda e, i=i: e.dma_start(out=o_rope[i, :, :], in_=outR[i][:]),
                  reads=[f"outR{i}"], writes=[("o_rope", i)])
            fs.append(f"oR{i}")
        for i in range(3):
            p.dma("sp", f"oP{i}", lambda e, i=i: e.dma_start(out=o_plain[i, :, :], in_=outP[i][:]),
                  reads=[f"outP{i}"], writes=[("o_plain", i)])
            fs.append(f"oP{i}")
        p.dma("sp", "og", lambda e: e.dma_start(out=o_g[:, :], in_=stg[:]), reads=["stg"], writes=["o_g"])
        fs.append("og")
        p.emit(final_sems=fs)
    return nc


NTC = 256


def build_C(final=False):
    nc = bass.Bass("TRN2", target_bir_lowering=False)
    with ExitStack() as es:
        c = Ctx(nc, es)
        p = c.p
        xT = c.din("xT", [128, 8, T])
        oc = c.din("ocat", [8, 128, T], BF16)
        tabs = c.din("tabs", [128, 48])
        wo = c.din("w_out", [D, D])
        wgu = c.din("w_gu", [D, 2 * DFF])
        wd = c.din("w_down", [DFF, D])
        xo = c.dout("xo", [128, 8, T])
        wor = wo.rearrange("(k p) c -> p k c", p=128)
        wgur = wgu.rearrange("(k p) c -> p k c", p=128)
        wdr = wd.rearrange("(k p) c -> p k c", p=128)
        NF = DFF // 128

        Wo = c.sb([128, 8, D], BF16, "Wo")
        Wg = c.sb([128, 8, 2 * DFF], BF16, "Wg")
        Wd = c.sb([128, NF, D], BF16, "Wd")
        tb = c.sb([128, 48], F32, "tb")
        s2 = c.sb([128, 8], F32, "s2")
        sfin = c.sb([128, 8], F32, "sfin")
        ones_bf = c.sb([128, 128], BF16, "ones_bf")
        deps = c.sb([128, 1], F32, "deps")
        xt = c.sb([128, 8, NTC], F32, "xt")
        ot = c.sb([128, 8, NTC], BF16, "ot")
        sq = c.sb([128, 8, NTC], BF16, "sq")
        rstd = c.sb([128, NTC], F32, "rstd")
        tmp = [c.sb([128, NTC], F32, f"tmp{i}") for i in range(2)]
        hT = c.sb([128, 8, NTC], BF16, "hT")
        gs = [c.sb([128, NTC], F32, f"gs{i}") for i in range(2)]
        aT = c.sb([128, NF, NTC], BF16, "aT")
        xout = c.sb([128, 8, NTC], F32, "xout")
        psb = [c.ps([128, 2, NTC], F32, f"psb{i}") for i in range(8)]
        bank = [0]

        def nextbank():
            b = bank[0]
            bank[0] = (b + 1) % 8
            return b

        p.dma("sp", "tb", lambda e: e.dma_start(out=tb[:], in_=tabs[:, :]), writes=["tb"])
        for j in range(2):
            p.dma("pool", "Wo", lambda e, j=j: e.dma_start(out=Wo[:, :, j * 512:(j + 1) * 512],
                                                         in_=wor[:, :, j * 512:(j + 1) * 512]), writes=["Wo"])
        for j in range(11):
            p.dma("pool", "Wg", lambda e, j=j: e.dma_start(out=Wg[:, :, j * 512:(j + 1) * 512],
                                                         in_=wgur[:, :, j * 512:(j + 1) * 512]), writes=["Wg"])
        for j in range(2):
            p.dma("pool", "Wd", lambda e, j=j: e.dma_start(out=Wd[:, :, j * 512:(j + 1) * 512],
                                                         in_=wdr[:, :, j * 512:(j + 1) * 512]), writes=["Wd"])
        p.dve(lambda e: e.memset(ones_bf[:], 1.0), writes=["ones_bf"])
        p.dve(lambda e: e.memset(deps[:], float(D * EPS)), writes=["deps"])
        p.dve(lambda e: e.tensor_scalar(s2[:], tb[:, 16:24], 1.0, 32.0, op0=ALU.add, op1=ALU.mult),
              reads=["tb"], writes=["s2"])
        p.dve(lambda e: e.tensor_tensor(s2[:], s2[:], tb[:, 8:16], op=ALU.mult), reads=["tb", "s2"], writes=["s2"])
        p.dve(lambda e: e.tensor_scalar(sfin[:], tb[:, 40:48], 32.0, None, op0=ALU.mult), reads=["tb"], writes=["sfin"])

        def rms_stats(src, srcres):
            p.act(lambda e: e.activation(sq[:], src[:], AF.Square), reads=srcres, writes=["sq"])
            b0 = nextbank()
            for k in range(8):
                p.pe(lambda e, k=k, b0=b0: e.matmul(psb[b0][:, 0, :], ones_bf[:], sq[:, k, :], start=(k == 0),
                                                  stop=(k == 7)), reads=["ones_bf", "sq"], writes=[f"psb{b0}"])
            p.act(lambda e, b0=b0: e.activation(rstd[:], psb[b0][:, 0, :], AF.Sqrt, bias=deps[:, 0:1]),
                  reads=[f"psb{b0}", "deps"], writes=["rstd"])
            p.dve(lambda e: e.reciprocal(rstd[:], rstd[:]), reads=["rstd"], writes=["rstd"])

        for ti in range(T // NTC):
            ts = slice(ti * NTC, (ti + 1) * NTC)
            p.dma("sp", "xt", lambda e, ts=ts: e.dma_start(out=xt[:], in_=xT[:, :, ts]), writes=[("xt", k) for k in range(8)])
            p.dma("sp", "ot", lambda e, ts=ts: e.dma_start(out=ot[:], in_=oc[:, :, ts].rearrange("k p t -> p k t")),
                  writes=["ot"])
            xres = [("xt", k) for k in range(8)]
            for fo2 in range(4):
                b1 = nextbank()
                for j in range(2):
                    fo = fo2 * 2 + j
                    for kc in range(8):
                        p.pe(lambda e, b1=b1, j=j, fo=fo, kc=kc: e.matmul(
                            psb[b1][:, j, :], Wo[:, kc, fo * 128:(fo + 1) * 128], ot[:, kc, :], start=(kc == 0),
                            stop=(kc == 7)), reads=["Wo", "ot"], writes=[f"psb{b1}"])
                for j in range(2):
                    fo = fo2 * 2 + j
                    p.dve(lambda e, b1=b1, j=j, fo=fo: e.scalar_tensor_tensor(
                        xt[:, fo, :], psb[b1][:, j, :], tb[:, fo:fo + 1], xt[:, fo, :], op0=ALU.mult, op1=ALU.add),
                        reads=[f"psb{b1}", "tb", ("xt", fo)], writes=[("xt", fo)])
            rms_stats(xt, xres)
            for k in range(8):
                tm = tmp[k % 2]
                tr = f"tmp{k % 2}"
                p.dve(lambda e, k=k, tm=tm: e.tensor_tensor(tm[:], xt[:, k, :], rstd[:], op=ALU.mult),
                      reads=[("xt", k), "rstd"], writes=[tr])
                p.act(lambda e, k=k, tm=tm: e.activation(hT[:, k, :], tm[:], AF.Identity, bias=tb[:, 24 + k:25 + k],
                                                         scale=s2[:, k:k + 1]),
                      reads=[tr, "s2", "tb"], writes=[("hT", k)])
            for f in range(NF):
                b1 = nextbank()
                for j in range(2):
                    co = j * DFF + f * 128
                    for k in range(8):
                        p.pe(lambda e, b1=b1, j=j, co=co, k=k: e.matmul(
                            psb[b1][:, j, :], Wg[:, k, co:co + 128], hT[:, k, :], start=(k == 0), stop=(k == 7)),
                            reads=["Wg", ("hT", k)], writes=[f"psb{b1}"])
                g_ = gs[f % 2]
                p.act(lambda e, b1=b1, g_=g_: e.activation(g_[:], psb[b1][:, 0, :], AF.Silu),
                      reads=[f"psb{b1}"], writes=[f"gs{f % 2}"])
                p.dve(lambda e, b1=b1, g_=g_, f=f: e.tensor_tensor(aT[:, f, :], g_[:], psb[b1][:, 1, :], op=ALU.mult),
                      reads=[f"psb{b1}", f"gs{f % 2}"], writes=[("aT", f)])
            for fo2 in range(4):
                b1 = nextbank()
                for j in range(2):
                    fo = fo2 * 2 + j
                    for f in range(NF):
                        p.pe(lambda e, b1=b1, j=j, fo=fo, f=f: e.matmul(
                            psb[b1][:, j, :], Wd[:, f, fo * 128:(fo + 1) * 128], aT[:, f, :], start=(f == 0),
                            stop=(f == NF - 1)), reads=["Wd", ("aT", f)], writes=[f"psb{b1}"])
                for j in range(2):
                    fo = fo2 * 2 + j
                    dst = xt if final else xout
                    dres = ("xt", fo) if final else ("xout", fo)
                    p.dve(lambda e, b1=b1, j=j, fo=fo, dst=dst: e.scalar_tensor_tensor(
                        dst[:, fo, :], psb[b1][:, j, :], tb[:, 32 + fo:33 + fo], xt[:, fo, :], op0=ALU.mult,
                        op1=ALU.add), reads=[f"psb{b1}", "tb", ("xt", fo)], writes=[dres])
            if final:
                rms_stats(xt, xres)
                for k in range(8):
                    tm = tmp[k % 2]
                    tr = f"tmp{k % 2}"
                    p.dve(lambda e, k=k, tm=tm: e.tensor_tensor(tm[:], xt[:, k, :], rstd[:], op=ALU.mult),
                          reads=[("xt", k), "rstd"], writes=[tr])
                    p.act(lambda e, k=k, tm=tm: e.activation(xout[:, k, :], tm[:], AF.Copy, scale=sfin[:, k:k + 1]),
                          reads=[tr, "sfin"], writes=[("xout", k)])
            p.dma("sp", "xout", lambda e, ts=ts: e.dma_start(out=xo[:, :, ts], in_=xout[:]),
                  reads=[("xout", k) for k in range(8)], writes=[("xo", ti)])
        p.emit(final_sems=["xout"])
    return nc


def build_M():
    nc = bass.Bass("TRN2", target_bir_lowering=False)
    HC = 3072
    with ExitStack() as es:
        c = Ctx(nc, es)
        p = c.p
        cT = c.din("cT", [128, 8, 2])
        aw = c.din("aw", [D, HC])
        ab = c.din("ab", [128, HC // 128])
        mo = c.dout("mo", [128, HC // 128, 2])
        awr = aw.rearrange("(k p) c -> p k c", p=128)
        NM = HC // 128
        Wt = [c.sb([128, 8, 768], F32, f"Wt{i}") for i in range(2)]
        ct = c.sb([128, 8, 2], F32, "ct")
        sg = c.sb([128, 8, 2], F32, "sg")
        sc = c.sb([128, 8, 2], F32, "sc")
        abt = c.sb([128, NM], F32, "abt")
        res = c.sb([128, NM, 2], F32, "res")
        ps = c.ps([128, NM, 2], F32, "ps_m")
        p.dma("sp", "ct", lambda e: e.dma_start(out=ct[:], in_=cT[:, :, :]), writes=["ct"])
        p.dma("sp", "abt", lambda e: e.dma_start(out=abt[:], in_=ab[:, :]), writes=["abt"])
        p.act(lambda e: e.activation(sg[:], ct[:], AF.Sigmoid), reads=["ct"], writes=["sg"])
        p.dve(lambda e: e.tensor_tensor(sc[:], sg[:], ct[:], op=ALU.mult), reads=["sg", "ct"], writes=["sc"])
        for j in range(HC // 768):
            wt = Wt[j % 2]
            wr_ = f"Wt{j % 2}"
            p.dma("sp", wr_, lambda e, wt=wt, j=j: e.dma_start(out=wt[:], in_=awr[:, :, j * 768:(j + 1) * 768]),
                  writes=[wr_])
            for mm in range(6):
                m = j * 6 + mm
                for k in range(8):
                    p.pe(lambda e, wt=wt, mm=mm, m=m, k=k: e.matmul(ps[:, m, :], wt[:, k, mm * 128:(mm + 1) * 128],
                                                                   sc[:, k, :], start=(k == 0), stop=(k == 7)),
                         reads=[wr_, "sc"], writes=["ps_m"])
        p.dve(lambda e: e.tensor_tensor(res[:], ps[:], abt[:, :, None].to_broadcast([128, NM, 2]), op=ALU.add),
              reads=["ps_m", "abt"], writes=["res"])
        p.dma("sp", "res", lambda e: e.dma_start(out=mo[:, :, :], in_=res[:]), reads=["res"], writes=["mo"])
        p.emit(final_sems=["res"])
    return nc


GT = 512


def build_G(ntiles=SEQ // GT, dbg=99):
    nc = bass.Bass("TRN2", target_bir_lowering=False)
    S_ = ntiles * GT
    NB = S_ // 128
    with ExitStack() as es:
        c = Ctx(nc, es)
        p = c.p
        qkv = c.din("qkv", [3, 128, S_])
        cw = c.din("cw", [128, 12])
        gab = c.din("gab", [128, NB, 2])
        hp = c.din("hp", [128, 2])
        zin = c.din("z", [128, NB, 128])
        nwb = c.din("nwb", [128, 128])
        yo = c.dout("yo", [128, NB, 128], BF16)

        val = c.sb([128, 128], F32, "val")
        MU = c.sb([128, 128], F32, "MU")
        ML = c.sb([128, 128], F32, "ML")
        ID = c.sb([128, 128], F32, "ID")
        ones_bf = c.sb([128, 128], BF16, "ones_bf")
        ones_f = c.sb([128, 128], F32, "ones_f")
        epsc = c.sb([128, 1], F32, "epsc")
        cwt = c.sb([128, 12], F32, "cwt")
        hpt = c.sb([128, 2], F32, "hpt")
        nw = c.sb([128, 128], F32, "nw")
        p.dma("sp", "cwt", lambda e: e.dma_start(out=cwt[:], in_=cw[:, :]), writes=["cwt"])
        p.dma("sp", "hpt", lambda e: e.dma_start(out=hpt[:], in_=hp[:, :]), writes=["hpt"])
        p.dma("sp", "nw", lambda e: e.dma_start(out=nw[:], in_=nwb[:, :]), writes=["nw"])
        p.pool(lambda e: e.iota(val[:], [[1, 128]], base=0, channel_multiplier=-1,
                                allow_small_or_imprecise_dtypes=True), writes=["val"])
        p.dve(lambda e: e.tensor_scalar(MU[:], val[:], 0.0, None, op0=ALU.is_ge), reads=["val"], writes=["MU"])
        p.dve(lambda e: e.tensor_scalar(ML[:], val[:], 0.0, None, op0=ALU.is_lt), reads=["val"], writes=["ML"])
        p.dve(lambda e: e.tensor_scalar(ID[:], val[:], 0.0, None, op0=ALU.is_equal), reads=["val"], writes=["ID"])
        p.dve(lambda e: e.memset(MU[0:64, 64:128], 0.0), reads=["MU"], writes=["MU"])
        p.dve(lambda e: e.memset(ML[64:128, 0:64], 0.0), reads=["ML"], writes=["ML"])
        p.dve(lambda e: e.memset(ones_bf[:], 1.0), writes=["ones_bf"])
        p.dve(lambda e: e.memset(ones_f[:], 1.0), writes=["ones_f"])
        p.dve(lambda e: e.memset(epsc[:], EPS), writes=["epsc"])

        pb = [c.ps([128, 4, 128], F32, f"pb{i}") for i in range(8)]

        gt_ = c.sb([128, NB, 2], F32, "gt_")
        g = c.sb([128, NB], F32, "g")
        beta = c.sb([128, NB], F32, "beta")
        nbeta = c.sb([128, NB], F32, "nbeta")
        tA = c.sb([128, NB], F32, "tA")
        eA = c.sb([128, 1], F32, "eA")
        gh = [c.sb([128, NB], F32, f"gh{a}") for a in range(2)]
        egc = c.sb([128, NB], F32, "egc")
        erem = c.sb([128, NB], F32, "erem")
        egl = [c.sb([128, NB], F32, f"egl{a}") for a in range(2)]
        sckb = c.sb([128, NB], F32, "sckb")
        p.dma("sp", "gt_", lambda e: e.dma_start(out=gt_[:], in_=gab[:, :, :]), writes=["gt_"])
        p.act(lambda e: e.activation(tA[:], gt_[:, :, 0], AF.Exp, bias=hpt[:, 1:2]), reads=["gt_", "hpt"], writes=["tA"])
        p.act(lambda e: e.activation(tA[:], tA[:], AF.Ln, bias=1.0), reads=["tA"], writes=["tA"])
        p.act(lambda e: e.activation(eA[:], hpt[:, 0:1], AF.Exp), reads=["hpt"], writes=["eA"])
        p.dve(lambda e: e.tensor_scalar(g[:], tA[:], eA[:, 0:1], -1.0, op0=ALU.mult, op1=ALU.mult),
              reads=["tA", "eA"], writes=["g"])
        p.act(lambda e: e.activation(beta[:], gt_[:, :, 1], AF.Exp, scale=-1.0), reads=["gt_"], writes=["beta"])
        p.dve(lambda e: e.tensor_scalar(beta[:], beta[:], 1.0, None, op0=ALU.add), reads=["beta"], writes=["beta"])
        p.dve(lambda e: e.reciprocal(beta[:], beta[:]), reads=["beta"], writes=["beta"])
        p.dve(lambda e: e.tensor_scalar(nbeta[:], beta[:], -1.0, None, op0=ALU.mult), reads=["beta"], writes=["nbeta"])
        for a in range(2):
            p.dve(lambda e, a=a: e.memset(gh[a][:], 0.0), writes=[f"gh{a}"])
            p.dve(lambda e, a=a: e.tensor_copy(gh[a][64 * a:64 * a + 64, :], g[64 * a:64 * a + 64, :]),
                  reads=["g", f"gh{a}"], writes=[f"gh{a}"])
        NBC = min(NB, 64)
        p.pe(lambda e: e.matmul(pb[0][:, 0, 0:NB], MU[:], g[:], start=True, stop=True), reads=["MU", "g"], writes=["pb0"])
        p.act(lambda e: e.activation(egc[:], pb[0][:, 0, 0:NB], AF.Exp), reads=["pb0"], writes=["egc"])
        p.pe(lambda e: e.matmul(pb[1][:, 0, 0:NB], ML[:], g[:], start=True, stop=True), reads=["ML", "g"], writes=["pb1"])
        p.act(lambda e: e.activation(erem[:], pb[1][:, 0, 0:NB], AF.Exp), reads=["pb1"], writes=["erem"])
        for a in range(2):
            p.pe(lambda e, a=a: e.matmul(pb[2 + a][:, 0, 0:NB], ones_f[:], gh[a][:], start=True, stop=True),
                 reads=["ones_f", f"gh{a}"], writes=[f"pb{2 + a}"])
            p.act(lambda e, a=a: e.activation(egl[a][:], pb[2 + a][:, 0, 0:NB], AF.Exp), reads=[f"pb{2 + a}"],
                  writes=[f"egl{a}"])
        p.dve(lambda e: e.tensor_tensor(sckb[:], beta[:], egc[:], op=ALU.mult), reads=["beta", "egc"], writes=["sckb"])

        xin = [c.sb([128, GT + 3], F32, f"xin{i}") for i in range(3)]
        acc = [c.sb([128, GT], F32, f"acc{i}") for i in range(3)]
        sil = [c.sb([128, GT], F32, f"sil{i}") for i in range(3)]
        sqb = [c.sb([128, GT], BF16, f"sqb{i}") for i in range(2)]
        rn = [c.sb([128, GT], F32, f"rn{i}") for i in range(2)]
        qn = c.sb([128, 4, 128], F32, "qn")
        kn = c.sb([128, 4, 128], F32, "kn")
        kbg = c.sb([128, 4, 128], BF16, "kbg")
        kdec = c.sb([128, 4, 128], BF16, "kdec")
        vb = c.sb([128, 4, 128], BF16, "vb")
        gU2 = c.sb([128, 4, 128], F32, "gU2")
        gU1 = c.sb([128, 4, 128], F32, "gU1")
        DecL = c.sb([128, 4, 128], F32, "DecL")
        DecT = c.sb([128, 4, 128], F32, "DecT")
        Xs = [c.sb([128, 4, 128], F32, f"Xs{i}") for i in range(2)]
        Ys = [c.sb([128, 4, 128], F32, f"Ys{i}") for i in range(2)]
        Q = c.sb([128, 4, 128], F32, "Q")
        TTb = c.sb([128, 4, 128], BF16, "TTb")
        qkT = c.sb([128, 4, 128], BF16, "qkT")
        u_sb = c.sb([128, 4, 128], F32, "u_sb")
        wT_sb = c.sb([128, 4, 128], F32, "wT_sb")
        o_sb = c.sb([128, 4, 128], F32, "o_sb")
        otmp = c.sb([128, 128], F32, "otmp")
        vnb = c.sb([128, 128], BF16, "vnb")
        S = c.sb([128, 128], F32, "S")
        zt = c.sb([128, 4, 128], F32, "zt")
        zs = c.sb([128, 4, 128], F32, "zs")
        junk = c.sb([128, 128], F32, "junk")
        ss = c.sb([128, 4], F32, "ss")
        yt = c.sb([128, 4, 128], F32, "yt")
        ytb = c.sb([128, 4, 128], BF16, "ytb")
        p.dve(lambda e: e.memset(S[:], 0.0), writes=["S"])

        def bc4(ap2):
            return ap2.unsqueeze(2).to_broadcast([128, 4, 128])

        def bcm(ap2):
            return ap2.unsqueeze(1).to_broadcast([128, 4, 128])

        for ti in range(ntiles):
            t0 = ti * GT
            b0 = ti * 4
            bs = slice(b0, b0 + 4)
            for i in range(3):
                if ti == 0:
                    p.dve(lambda e, i=i: e.memset(xin[i][:, 0:3], 0.0), writes=[f"xin{i}"])
                    p.dma("sp", f"xin{i}", lambda e, i=i: e.dma_start(out=xin[i][:, 3:GT + 3], in_=qkv[i, :, 0:GT]),
                          writes=[f"xin{i}"])
                else:
                    p.dma("sp", f"xin{i}", lambda e, i=i, t0=t0: e.dma_start(out=xin[i][:], in_=qkv[i, :, t0 - 3:t0 + GT]),
                          writes=[f"xin{i}"])
            p.dma("sp", "zt", lambda e, bs=bs: e.dma_start(out=zt[:], in_=zin[:, bs, :]), writes=["zt"])
            if dbg < 2:
                p.dma("sp", "yt", lambda e, bs=bs: e.dma_start(out=yo[:, bs, :], in_=ytb[:]), reads=["ytb"], writes=[("yo", ti)])
                continue
            for i in range(3):
                p.dve(lambda e, i=i: e.tensor_scalar(acc[i][:], xin[i][:, 3:GT + 3], cwt[:, 4 * i + 3:4 * i + 4], None,
                                                     op0=ALU.mult), reads=[f"xin{i}", "cwt"], writes=[f"acc{i}"])
                for j in range(3):
                    p.dve(lambda e, i=i, j=j: e.scalar_tensor_tensor(
                        acc[i][:], xin[i][:, j:GT + j], cwt[:, 4 * i + j:4 * i + j + 1], acc[i][:], op0=ALU.mult,
                        op1=ALU.add), reads=[f"xin{i}", "cwt", f"acc{i}"], writes=[f"acc{i}"])
                p.act(lambda e, i=i: e.activation(sil[i][:], acc[i][:], AF.Silu), reads=[f"acc{i}"], writes=[f"sil{i}"])
            for i in range(2):
                p.act(lambda e, i=i: e.activation(sqb[i][:], sil[i][:], AF.Square), reads=[f"sil{i}"], writes=[f"sqb{i}"])
                p.pe(lambda e, i=i: e.matmul(pb[i][:].rearrange("p a b -> p (a b)"), ones_bf[:], sqb[i][:], start=True,
                                             stop=True), reads=["ones_bf", f"sqb{i}"], writes=[f"pb{i}"])
                p.act(lambda e, i=i: e.activation(rn[i][:], pb[i][:].rearrange("p a b -> p (a b)"), AF.Sqrt,
                                                  bias=epsc[:, 0:1]), reads=[f"pb{i}", "epsc"], writes=[f"rn{i}"])
                p.dve(lambda e, i=i: e.reciprocal(rn[i][:], rn[i][:]), reads=[f"rn{i}"], writes=[f"rn{i}"])
            p.dve(lambda e: e.scalar_tensor_tensor(qn[:].rearrange("p a b -> p (a b)"), sil[0][:], float(128 ** -0.5),
                                                   rn[0][:], op0=ALU.mult, op1=ALU.mult),
                  reads=["sil0", "rn0"], writes=["qn"])
            p.dve(lambda e: e.tensor_tensor(kn[:].rearrange("p a b -> p (a b)"), sil[1][:], rn[1][:], op=ALU.mult),
                  reads=["sil1", "rn1"], writes=["kn"])
            if dbg < 3:
                p.dma("sp", "yt", lambda e, bs=bs: e.dma_start(out=yo[:, bs, :], in_=ytb[:]), reads=["ytb"], writes=[("yo", ti)])
                continue
            for pr in range(4):
                p.pe(lambda e, pr=pr: e.transpose(pb[2][:, pr, :], kn[:, pr, :], ID[:]), reads=["kn", "ID"], writes=["pb2"])
            for pr in range(4):
                p.pe(lambda e, pr=pr: e.transpose(pb[3][:, pr, :], sil[2][:, pr * 128:(pr + 1) * 128], ID[:]),
                     reads=["sil2", "ID"], writes=["pb3"])
            p.dve(lambda e, bs=bs: e.tensor_tensor(kbg[:], pb[2][:], bc4(sckb[:, bs]), op=ALU.mult),
                  reads=["pb2", "sckb"], writes=["kbg"])
            p.dve(lambda e, bs=bs: e.tensor_tensor(kdec[:], pb[2][:], bc4(erem[:, bs]), op=ALU.mult),
                  reads=["pb2", "erem"], writes=["kdec"])
            p.dve(lambda e, bs=bs: e.tensor_tensor(vb[:], pb[3][:], bc4(beta[:, bs]), op=ALU.mult),
                  reads=["pb3", "beta"], writes=["vb"])
            if dbg < 4:
                p.dma("sp", "yt", lambda e, bs=bs: e.dma_start(out=yo[:, bs, :], in_=ytb[:]), reads=["ytb"], writes=[("yo", ti)])
                continue
            p.dve(lambda e, bs=bs: e.tensor_tensor(gU2[:], bcm(ML[:]), bc4(g[:, bs]), op=ALU.mult),
                  reads=["ML", "g"], writes=["gU2"])
            p.dve(lambda e, bs=bs: e.tensor_tensor(gU1[:], bcm(MU[:]), bc4(g[:, bs]), op=ALU.mult),
                  reads=["MU", "g"], writes=["gU1"])
            p.pe(lambda e: e.matmul(pb[4][:].rearrange("p a b -> p (a b)"), MU[:], gU2[:].rearrange("p a b -> p (a b)"),
                                    start=True, stop=True), reads=["MU", "gU2"], writes=["pb4"])
            p.pe(lambda e: e.matmul(pb[5][:].rearrange("p a b -> p (a b)"), ML[:], gU1[:].rearrange("p a b -> p (a b)"),
                                    start=True, stop=True), reads=["ML", "gU1"], writes=["pb5"])
            p.act(lambda e: e.activation(DecL[:], pb[4][:], AF.Exp), reads=["pb4"], writes=["DecL"])
            p.act(lambda e: e.activation(DecT[:], pb[5][:], AF.Exp), reads=["pb5"], writes=["DecT"])
            p.dve(lambda e: e.tensor_tensor(DecL[:], DecL[:], bcm(ML[:]), op=ALU.mult), reads=["DecL", "ML"], writes=["DecL"])
            p.dve(lambda e: e.tensor_tensor(DecT[:], DecT[:], bcm(MU[:]), op=ALU.mult), reads=["DecT", "MU"], writes=["DecT"])
            if dbg < 5:
                p.dma("sp", "yt", lambda e, bs=bs: e.dma_start(out=yo[:, bs, :], in_=ytb[:]), reads=["ytb"], writes=[("yo", ti)])
                continue
            for pr in range(4):
                p.pe(lambda e, pr=pr: e.matmul(pb[0][:, pr, :], kn[:, pr, :], kn[:, pr, :], start=True, stop=True),
                     reads=["kn"], writes=["pb0"])
            for pr in range(4):
                p.pe(lambda e, pr=pr: e.matmul(pb[1][:, pr, :], kn[:, pr, :], qn[:, pr, :], start=True, stop=True),
                     reads=["kn", "qn"], writes=["pb1"])
            X, Y = Xs[0], Ys[0]
            p.dve(lambda e: e.tensor_tensor(X[:], pb[0][:], DecL[:], op=ALU.mult), reads=["pb0", "DecL"], writes=["Xs0"])
            p.dve(lambda e, bs=bs: e.tensor_tensor(X[:], X[:], bc4(nbeta[:, bs]), op=ALU.mult),
                  reads=["Xs0", "nbeta"], writes=["Xs0"])
            p.dve(lambda e: e.tensor_tensor(qkT[:], pb[1][:], DecT[:], op=ALU.mult), reads=["pb1", "DecT"], writes=["qkT"])
            for pr in range(4):
                p.pe(lambda e, pr=pr: e.transpose(pb[2][:, pr, :], Xs[0][:, pr, :], ID[:]), reads=["Xs0", "ID"],
                     writes=["pb2"])
            p.act(lambda e: e.copy(Ys[0][:], pb[2][:]), reads=["pb2"], writes=["Ys0"])
            p.dve(lambda e: e.tensor_tensor(Q[:], pb[2][:], bcm(ID[:]), op=ALU.add), reads=["pb2", "ID"], writes=["Q"])
            if dbg < 6:
                p.dma("sp", "yt", lambda e, bs=bs: e.dma_start(out=yo[:, bs, :], in_=ytb[:]), reads=["ytb"], writes=[("yo", ti)])
                continue
            for lvl in range(1, 6):
                cur, nxt = (lvl - 1) % 2, lvl % 2
                for pr in range(4):
                    p.pe(lambda e, pr=pr, cur=cur: e.matmul(pb[3][:, pr, :], Ys[cur][:, pr, :], Xs[cur][:, pr, :],
                                                          start=True, stop=True),
                         reads=[f"Ys{cur}", f"Xs{cur}"], writes=["pb3"])
                if lvl < 5:
                    for pr in range(4):
                        p.pe(lambda e, pr=pr, cur=cur: e.matmul(pb[4][:, pr, :], Xs[cur][:, pr, :], Ys[cur][:, pr, :],
                                                              start=True, stop=True),
                             reads=[f"Ys{cur}", f"Xs{cur}"], writes=["pb4"])
                p.act(lambda e, nxt=nxt: e.copy(Xs[nxt][:], pb[3][:]), reads=["pb3"], writes=[f"Xs{nxt}"])
                if lvl < 5:
                    p.dve(lambda e, nxt=nxt: e.tensor_copy(Ys[nxt][:], pb[4][:]), reads=["pb4"], writes=[f"Ys{nxt}"])
                for pr in range(4):
                    p.pe(lambda e, pr=pr, nxt=nxt: e.matmul(pb[5][:, pr, :], Xs[nxt][:, pr, :], Q[:, pr, :], start=True,
                                                          stop=True), reads=[f"Xs{nxt}", "Q"], writes=["pb5"])
                p.dve(lambda e: e.tensor_tensor(Q[:], Q[:], pb[5][:], op=ALU.add), reads=["Q", "pb5"], writes=["Q"])
            p.act(lambda e: e.copy(TTb[:], Q[:]), reads=["Q"], writes=["TTb"])
            if dbg < 7:
                p.dma("sp", "yt", lambda e, bs=bs: e.dma_start(out=yo[:, bs, :], in_=ytb[:]), reads=["ytb"], writes=[("yo", ti)])
                continue
            for pr in range(4):
                p.pe(lambda e, pr=pr: e.matmul(pb[6][:, pr, :], TTb[:, pr, :], vb[:, pr, :], start=True, stop=True),
                     reads=["TTb", "vb"], writes=["pb6"])
            for pr in range(4):
                p.pe(lambda e, pr=pr: e.matmul(pb[7][:, pr, :], kbg[:, pr, :], TTb[:, pr, :], start=True, stop=True),
                     reads=["TTb", "kbg"], writes=["pb7"])
            p.act(lambda e: e.copy(u_sb[:], pb[6][:]), reads=["pb6"], writes=["u_sb"])
            p.dve(lambda e: e.tensor_copy(wT_sb[:], pb[7][:]), reads=["pb7"], writes=["wT_sb"])
            if dbg < 8:
                p.dma("sp", "yt", lambda e, bs=bs: e.dma_start(out=yo[:, bs, :], in_=ytb[:]), reads=["ytb"], writes=[("yo", ti)])
                continue
            for pr in range(4):
                blk = b0 + pr
                for a in range(2):
                    rs = slice(64 * a, 64 * a + 64)
                    cs = slice(64 * a, 64 * a + 64)
                    p.pe(lambda e, pr=pr, rs=rs, cs=cs: e.matmul(pb[0][rs, 0, :], wT_sb[:, pr, cs], S[:], start=True,
                                                               stop=True), reads=["wT_sb", "S"], writes=["pb0"])
                    p.pe(lambda e, pr=pr, rs=rs, cs=cs: e.matmul(pb[1][rs, 0, :], qn[:, pr, cs], S[:], start=True,
                                                               stop=True), reads=["qn", "S"], writes=["pb1"])
                    p.dve(lambda e, pr=pr, rs=rs: e.tensor_tensor(vnb[rs, :], u_sb[rs, pr, :], pb[0][rs, 0, :],
                                                                 op=ALU.subtract), reads=["u_sb", "pb0"], writes=["vnb"])
                    p.act(lambda e, rs=rs, blk=blk: e.activation(otmp[rs, :], pb[1][rs, 0, :], AF.Copy,
                                                                 scale=egc[rs, blk:blk + 1]),
                          reads=["pb1", "egc"], writes=["otmp"])
                    p.pe(lambda e, pr=pr, rs=rs, cs=cs: e.matmul(pb[2][rs, 0, :], qkT[rs, pr, cs], vnb[rs, :], start=True,
                                                               stop=True), reads=["qkT", "vnb"], writes=["pb2"])
                    p.pe(lambda e, pr=pr, rs=rs: e.matmul(pb[3][:, 0, :], kdec[rs, pr, :], vnb[rs, :], start=True,
                                                        stop=True), reads=["kdec", "vnb"], writes=["pb3"])
                    p.dve(lambda e, pr=pr, rs=rs: e.tensor_tensor(o_sb[rs, pr, :], otmp[rs, :], pb[2][rs, 0, :],
                                                                 op=ALU.add), reads=["otmp", "pb2"], writes=["o_sb"])
                    p.dve(lambda e, a=a, blk=blk: e.scalar_tensor_tensor(S[:], S[:], egl[a][:, blk:blk + 1],
                                                                        pb[3][:, 0, :], op0=ALU.mult, op1=ALU.add),
                          reads=["S", f"egl{a}", "pb3"], writes=["S"])
            if dbg < 9:
                p.dma("sp", "yt", lambda e, bs=bs: e.dma_start(out=yo[:, bs, :], in_=ytb[:]), reads=["ytb"], writes=[("yo", ti)])
                continue
            for pr in range(4):
                p.act(lambda e, pr=pr: e.activation(junk[:], o_sb[:, pr, :], AF.Square, accum_out=ss[:, pr:pr + 1]),
                      reads=["o_sb"], writes=["junk", ("ss", pr)])
            p.act(lambda e: e.activation(ss[:], ss[:], AF.Sqrt, bias=epsc[:, 0:1], scale=1.0 / 128.0),
                  reads=[("ss", i) for i in range(4)] + ["epsc"], writes=[("ss", i) for i in range(4)])
            p.dve(lambda e: e.reciprocal(ss[:], ss[:]), reads=[("ss", i) for i in range(4)],
                  writes=[("ss", i) for i in range(4)])
            p.act(lambda e: e.activation(zs[:], zt[:], AF.Silu), reads=["zt"], writes=["zs"])
            p.dve(lambda e: e.tensor_tensor(yt[:], o_sb[:], bc4(ss[:, :]), op=ALU.mult),
                  reads=["o_sb"] + [("ss", i) for i in range(4)], writes=["yt"])
            p.dve(lambda e: e.tensor_tensor(yt[:], yt[:], bcm(nw[:]), op=ALU.mult), reads=["yt", "nw"], writes=["yt"])
            p.dve(lambda e: e.tensor_tensor(ytb[:], yt[:], zs[:], op=ALU.mult), reads=["yt", "zs"], writes=["ytb"])
            p.dma("sp", "yt", lambda e, bs=bs: e.dma_start(out=yo[:, bs, :], in_=ytb[:]), reads=["ytb"], writes=[("yo", ti)])
        p.emit(final_sems=["yt"])
    return nc


SCALE = 0.125
NEG = -30000.0


def build_B(qblocks=tuple(range(16)), dbg=99):
    nc = bass.Bass("TRN2", target_bir_lowering=False)
    with ExitStack() as es:
        c = Ctx(nc, es)
        p = c.p
        aq_d = c.din("aq", [2, 128, T], BF16)
        ak_d = c.din("ak", [2, 128, T + 128], BF16)
        av_d = c.din("av", [128, 17, 128], BF16)
        sk_d = c.din("sinks", [128, 4])
        offs_d = c.din("offs", [128, 4])
        bqr_d = c.din("bqr", [2, 128, T], BF16)
        bq_d = c.din("bq", [2, 128, T], BF16)
        bkw_d = c.din("bkw", [128, T + 512], BF16)
        bvw_d = c.din("bvw", [128, 20, 64], BF16)
        bks_d = c.din("bks", [128, SEQ], BF16)
        bvs_d = c.din("bvs", [128, 64, 64], BF16)
        cmp_d = c.din("cmpT", [128, SEQ], BF16)
        bg_d = c.din("bg", [12, T])
        w1k_d = c.din("w1k", [2048, 256])
        w1v_d = c.din("w1v", [2048, 256])
        w2k_d = c.din("w2k", [256, 64])
        w2v_d = c.din("w2v", [256, 64])
        pek_d = c.din("pek", [128, 16])
        pev_d = c.din("pev", [128, 16])
        oc_d = c.dout("oc", [4, 128, T], BF16)

        pb = [c.ps([128, 4, 128], F32, f"pb{i}") for i in range(8)]
        PB2 = [("pb2", k) for k in range(4)]

        aq = c.sb([128, 2, T], BF16, "aq")
        ak = c.sb([128, 2, T + 128], BF16, "ak")
        avg = c.sb([128, 17, 2, 128], BF16, "avg")
        bqr = c.sb([128, 2, T], BF16, "bqr")
        bq = c.sb([128, 2, T], BF16, "bq")
        bkw = c.sb([128, T + 512], BF16, "bkw")
        bvwg = c.sb([128, 20, 128], BF16, "bvwg")
        bks = c.sb([128, SEQ], BF16, "bks")
        bvsg = c.sb([128, 64, 128], BF16, "bvsg")
        KKk = c.sb([128, SEQ], BF16, "KKk")
        KKv = c.sb([128, SEQ], BF16, "KKv")
        sk = c.sb([128, 4], F32, "sk")
        offs = c.sb([128, 4], F32, "offs")
        bg = c.sb([12, T], F32, "bg")
        w1k = c.sb([128, 16, 256], BF16, "w1k")
        w1v = c.sb([128, 16, 256], BF16, "w1v")
        w2k = c.sb([128, 2, 128], BF16, "w2k")
        w2v = c.sb([128, 2, 64], BF16, "w2v")
        pek = c.sb([128, 16], BF16, "pek")
        pev = c.sb([128, 16], BF16, "pev")
        for j in range(2):
            p.dma("sp", "aq", lambda e, j=j: e.dma_start(out=aq[:, j, :], in_=aq_d[j, :, :]), writes=["aq"])
            p.dma("sp", "ak", lambda e, j=j: e.dma_start(out=ak[:, j, :], in_=ak_d[j, :, :]), writes=["ak"])
            p.dma("sp", "bqr", lambda e, j=j: e.dma_start(out=bqr[:, j, :], in_=bqr_d[j, :, :]), writes=["bqr"])
            p.dma("sp", "bq", lambda e, j=j: e.dma_start(out=bq[:, j, :], in_=bq_d[j, :, :]), writes=["bq"])
        p.dve(lambda e: e.memset(avg[:], 1.0), writes=["avg"])
        p.dve(lambda e: e.memset(bvwg[:], 1.0), writes=["bvwg"])
        p.dve(lambda e: e.memset(bvsg[:], 1.0), writes=["bvsg"])
        for j in range(2):
            p.dma("sp", "avg", lambda e, j=j: e.dma_start(out=avg[:, :, j, 0:64], in_=av_d[:, :, 64 * j:64 * j + 64]),
                  reads=["avg"], writes=["avg"])
        p.dma("sp", "bvwg", lambda e: e.dma_start(out=bvwg[:, :, 0:64], in_=bvw_d[:, :, :]), reads=["bvwg"], writes=["bvwg"])
        p.dma("sp", "bvsg", lambda e: e.dma_start(out=bvsg[:, :, 0:64], in_=bvs_d[:, :, :]), reads=["bvsg"], writes=["bvsg"])
        p.dma("sp", "bkw", lambda e: e.dma_start(out=bkw[:], in_=bkw_d[:, :]), writes=["bkw"])
        p.dma("sp", "bks", lambda e: e.dma_start(out=bks[:], in_=bks_d[:, :]), writes=["bks"])
        p.dve(lambda e: e.memset(KKk[:, SEQ - 8:SEQ], 0.0), writes=["KKk"])
        p.dve(lambda e: e.memset(KKv[:, SEQ - 8:SEQ], 0.0), writes=["KKv"])
        p.dma("sp", "KKk", lambda e: e.dma_start(out=KKk[0:64, :], in_=cmp_d[0:64, :]), reads=["KKk"], writes=["KKk"])
        p.dma("sp", "KKk", lambda e: e.dma_start(out=KKk[64:128, 0:SEQ - 1], in_=cmp_d[0:64, 1:SEQ]), reads=["KKk"],
              writes=["KKk"])
        p.dma("sp", "KKv", lambda e: e.dma_start(out=KKv[0:64, :], in_=cmp_d[64:128, :]), reads=["KKv"], writes=["KKv"])
        p.dma("sp", "KKv", lambda e: e.dma_start(out=KKv[64:128, 0:SEQ - 1], in_=cmp_d[64:128, 1:SEQ]), reads=["KKv"],
              writes=["KKv"])
        p.dma("sp", "sk", lambda e: e.dma_start(out=sk[:], in_=sk_d[:, :]), writes=["sk"])
        p.dma("sp", "offs", lambda e: e.dma_start(out=offs[:], in_=offs_d[:, :]), writes=["offs"])
        p.dma("sp", "bg", lambda e: e.dma_start(out=bg[:], in_=bg_d[:, :]), writes=["bg"])
        p.dma("pool", "w1k", lambda e: e.dma_start(out=w1k[:], in_=w1k_d.rearrange("(m p) c -> p m c", p=128)), writes=["w1k"])
        p.dma("pool", "w1v", lambda e: e.dma_start(out=w1v[:], in_=w1v_d.rearrange("(m p) c -> p m c", p=128)), writes=["w1v"])
        for dup in range(2):
            p.dma("pool", "w2k", lambda e, dup=dup: e.dma_start(out=w2k[:, :, 64 * dup:64 * dup + 64],
                                                              in_=w2k_d.rearrange("(m p) c -> p m c", p=128)), writes=["w2k"])
        p.dma("pool", "w2v", lambda e: e.dma_start(out=w2v[:], in_=w2v_d.rearrange("(m p) c -> p m c", p=128)), writes=["w2v"])
        p.dma("pool", "pek", lambda e: e.dma_start(out=pek[:], in_=pek_d[:, :]), writes=["pek"])
        p.dma("pool", "pev", lambda e: e.dma_start(out=pev[:], in_=pev_d[:, :]), writes=["pev"])

        val = c.sb([128, 128], F32, "val")
        MUq = c.sb([128, 128], BF16, "MUq")
        MLq = c.sb([128, 128], BF16, "MLq")
        M2 = c.sb([128, 2, 128], BF16, "M2")
        IDb = c.sb([128, 128], BF16, "IDb")
        ones_b = c.sb([128, 128], BF16, "ones_b")
        p.pool(lambda e: e.iota(val[:], [[1, 128]], base=0, channel_multiplier=-1, allow_small_or_imprecise_dtypes=True),
               writes=["val"])
        p.dve(lambda e: e.tensor_scalar(MUq[:], val[:], 0.0, None, op0=ALU.is_ge), reads=["val"], writes=["MUq"])
        p.dve(lambda e: e.tensor_scalar(MLq[:], val[:], 0.0, None, op0=ALU.is_lt), reads=["val"], writes=["MLq"])
        p.dve(lambda e: e.tensor_scalar(IDb[:], val[:], 0.0, None, op0=ALU.is_equal), reads=["val"], writes=["IDb"])
        p.dve(lambda e: e.tensor_copy(M2[:, 0, :], MLq[:]), reads=["MLq"], writes=["M2"])
        p.dve(lambda e: e.tensor_copy(M2[:, 1, :], MUq[:]), reads=["MUq", "M2"], writes=["M2"])
        p.dve(lambda e: e.memset(ones_b[:], 1.0), writes=["ones_b"])
        Et = c.sb([128, 16, 2, 64], F32, "Et")
        E = c.sb([128, 64, 128], BF16, "E")
        for qtr in range(4):
            p.pool(lambda e, qtr=qtr: e.iota(Et[:], [[-2, 16], [-1, 2], [0, 64]], base=-32 * qtr, channel_multiplier=1,
                                             allow_small_or_imprecise_dtypes=True), reads=["Et"], writes=["Et"])
            p.dve(lambda e, qtr=qtr: e.tensor_scalar(E[:, 16 * qtr:16 * qtr + 16, :].rearrange("p a (b c) -> p a b c", b=2),
                                                     Et[:], 0.0, None, op0=ALU.is_equal), reads=["Et"], writes=["E", "Et"])
        Gt = c.sb([128, 4, 128], F32, "Gt")
        Gm = c.sb([128, 4, 128], F32, "Gm")
        Gm2 = Gt
        p.pool(lambda e: e.iota(Gt[:], [[128, 4], [-4, 128]], base=0, channel_multiplier=1,
                                allow_small_or_imprecise_dtypes=True), writes=["Gt"])
        p.dve(lambda e: e.tensor_scalar(Gm[:], Gt[:], -1.0, None, op0=ALU.is_ge), reads=["Gt"], writes=["Gm"])
        p.dve(lambda e: e.tensor_scalar(Gm2[:], Gt[:], 3.0, None, op0=ALU.is_le), reads=["Gt", "Gm"], writes=["Gm2", "Gt"])
        p.dve(lambda e: e.tensor_tensor(Gm[:], Gm[:], Gm2[:], op=ALU.mult), reads=["Gm", "Gm2"], writes=["Gm"])
        valc = c.sb([128, 2, 128], F32, "valc")
        p.pool(lambda e: e.iota(valc[:], [[2048, 2], [-1, 128]], base=4096, channel_multiplier=16,
                                allow_small_or_imprecise_dtypes=True), writes=["valc"])
        nidx = c.sb([128, 4], F32, "nidx")
        biasn = c.sb([128, 4], F32, "biasn")
        bt2 = c.sb([128, 4], F32, "bt2")
        p.pool(lambda e: e.iota(nidx[:], [[128, 4]], base=0, channel_multiplier=1, allow_small_or_imprecise_dtypes=True),
               writes=["nidx"])
        p.dve(lambda e: e.tensor_scalar(biasn[:], nidx[:], offs[:, 1:2], NEG, op0=ALU.is_lt, op1=ALU.mult),
              reads=["nidx", "offs"], writes=["biasn"])
        p.dve(lambda e: e.tensor_scalar(bt2[:], nidx[:], 510.5, NEG, op0=ALU.is_gt, op1=ALU.mult),
              reads=["nidx"], writes=["bt2"])
        p.dve(lambda e: e.tensor_tensor(biasn[:], biasn[:], bt2[:], op=ALU.add), reads=["biasn", "bt2"], writes=["biasn"])
        bbi = c.sb([128, 128], F32, "bbi")
        validblk = c.sb([128, 128], F32, "validblk")
        f0m = c.sb([128, 128], F32, "f0m")
        p.pool(lambda e: e.iota(bbi[:], [[1, 128]], base=0, channel_multiplier=0, allow_small_or_imprecise_dtypes=True),
               writes=["bbi"])
        p.dve(lambda e: e.tensor_scalar(validblk[:], bbi[:], offs[:, 2:3], None, op0=ALU.is_ge), reads=["bbi", "offs"],
              writes=["validblk"])
        p.dve(lambda e: e.tensor_scalar(f0m[:], bbi[:], offs[:, 2:3], 2e9, op0=ALU.is_equal, op1=ALU.mult),
              reads=["bbi", "offs"], writes=["f0m"])
        p.dve(lambda e: e.tensor_scalar(f0m[:], f0m[:], -1e9, None, op0=ALU.add), reads=["f0m"], writes=["f0m"])
        esk = c.sb([128, 4], F32, "esk")
        p.act(lambda e: e.activation(esk[:], sk[:], AF.Exp), reads=["sk"], writes=["esk"])
        sg = bg
        p.act(lambda e: e.activation(sg[:], bg[:], AF.Exp, scale=-1.0), reads=["bg"], writes=["sg", "bg"])
        p.dve(lambda e: e.tensor_scalar(sg[:], sg[:], 1.0, None, op0=ALU.add), reads=["sg"], writes=["sg"])
        p.dve(lambda e: e.reciprocal(sg[:], sg[:]), reads=["sg"], writes=["sg"])
        selg = c.sb([12, 12, 64], F32, "selg")
        p.pool(lambda e: e.iota(selg[:], [[-1, 12], [0, 64]], base=0, channel_multiplier=1,
                                allow_small_or_imprecise_dtypes=True), writes=["selg"])
        p.dve(lambda e: e.tensor_scalar(selg[:], selg[:], 0.0, None, op0=ALU.is_equal), reads=["selg"], writes=["selg"])

        hid = [c.sb([128, 2, 512], BF16, f"hid{i}") for i in range(2)]
        bh = c.sb([128, 4], F32, "bh")
        kcT = c.sb([128, 512], BF16, "kcT")
        vcg = c.sb([128, 4, 128], BF16, "vcg")
        p.dve(lambda e: e.memset(vcg[:], 1.0), writes=["vcg"])
        for wi, (w1, KK, pe_) in enumerate([(w1k, KKk, pek), (w1v, KKv, pev)]):
            w1n, kkn, pen = ["w1k", "w1v"][wi], ["KKk", "KKv"][wi], ["pek", "pev"][wi]
            p.dve(lambda e, wi=wi: e.memset(hid[wi][:], 0.0), writes=[f"hid{wi}"])
            for hh in range(2):
                bk = 2 * wi + hh
                for m in range(16):
                    p.pe(lambda e, bk=bk, m=m, hh=hh, w1=w1, pe_=pe_: e.matmul(
                        pb[4][:, bk, 0:1], w1[:, m, hh * 128:(hh + 1) * 128], pe_[:, m:m + 1], start=(m == 0),
                        stop=(m == 15)), reads=[w1n, pen], writes=["pb4"])
            p.dve(lambda e, wi=wi: e.tensor_copy(bh[:, 2 * wi:2 * wi + 2], pb[4][:, 2 * wi:2 * wi + 2, 0]),
                  reads=["pb4"], writes=["bh"])
            for hh in range(2):
                bk = hh
                for m in range(16):
                    p.pe(lambda e, bk=bk, m=m, hh=hh, w1=w1, KK=KK: e.matmul(
                        pb[bk][:].rearrange("p a b -> p (a b)")[:, 0:511], w1[:, m, hh * 128:(hh + 1) * 128],
                        KK[:, 2 * m:2 * m + 16 * 510 + 1:16], start=(m == 0), stop=(m == 15)),
                        reads=[w1n, kkn], writes=[f"pb{bk}"])
                p.act(lambda e, bk=bk, hh=hh, wi=wi: e.activation(
                    hid[wi][:, hh, 0:511], pb[bk][:].rearrange("p a b -> p (a b)")[:, 0:511], AF.Silu,
                    bias=bh[:, 2 * wi + hh:2 * wi + hh + 1]), reads=[f"pb{bk}", "bh"], writes=[f"hid{wi}"])
        for hh in range(2):
            p.pe(lambda e, hh=hh: e.matmul(pb[2][:].rearrange("p a b -> p (a b)"), w2k[:, hh, :], hid[0][:, hh, :],
                                           start=(hh == 0), stop=(hh == 1)), reads=["w2k", "hid0"], writes=PB2)
        p.act(lambda e: e.copy(kcT[:], pb[2][:].rearrange("p a b -> p (a b)")), reads=PB2, writes=["kcT"])
        for cc in range(4):
            for hh in range(2):
                p.pe(lambda e, cc=cc, hh=hh: e.matmul(pb[3][:, cc, 0:64], hid[1][:, hh, cc * 128:(cc + 1) * 128],
                                                      w2v[:, hh, :], start=(hh == 0), stop=(hh == 1)),
                     reads=["w2v", "hid1"], writes=["pb3"])
        p.act(lambda e: e.copy(vcg[:, :, 0:64], pb[3][:, :, 0:64]), reads=["pb3", "vcg"], writes=["vcg"])

        PT = [c.sb([128, 4, 128], BF16, f"PT{i}") for i in range(3)]
        QBD = {}
        for nm in ("QA", "QR", "QU"):
            for par in range(2):
                for j in range(2):
                    t_ = c.sb([128, 256], BF16, f"{nm}{par}{j}")
                    QBD[(nm, par, j)] = t_
                    p.pool(lambda e, t_=t_: e.memset(t_[:], 0.0), writes=[f"{nm}{par}{j}"])
        PTc = c.sb([128, 4, 4, 128], BF16, "PTc")
        mk = c.sb([128, 2, 128], BF16, "mk")
        ocA = KKk[:, 0:2 * T].rearrange("p (j t) -> p j t", j=2)
        ocB = KKk[:, 2 * T:4 * T].rearrange("p (j t) -> p j t", j=2)
        Etf = Et[:].rearrange("p a b c -> p (a b c)")
        rd = Etf[:, 0:512].rearrange("p (a b) -> p a b", a=4)
        rdA = c.sb([64, 4, 128], F32, "rdA")
        accB = c.sb([64, 4, 128], F32, "accB")
        tmpB = c.sb([64, 4, 128], F32, "tmpB")
        fB = c.sb([64, 4, 128], F32, "fB")
        pn = Etf[:, 512:1024].rearrange("p (a b) -> p a b", a=4)
        psh = Etf[:, 1024:1536].rearrange("p (a b) -> p a b", a=4)
        score = c.sb([128, 128], F32, "score")
        work = c.sb([128, 128], F32, "work")
        m8a = c.sb([128, 8], F32, "m8a")
        m8b = c.sb([128, 8], F32, "m8b")
        sel = c.sb([128, 128], BF16, "sel")
        selT = c.sb([128, 128], BF16, "selT")
        pbT = c.ps([128, 128], BF16, "pbT") if False else None
        ptn = [0]

        def next_pt():
            k = ptn[0]
            ptn[0] = (k + 1) % 3
            return k
        sbank = [0]

        def next_sb():
            k = sbank[0]
            sbank[0] = 1 - k
            return k

        def gate_finish(br, i, acc_first):
            qs = slice(128 * i, 128 * i + 128)
            for h in range(4):
                p.pe(lambda e, h=h: e.matmul(pb[6][0:64, h, :], selg[:, br * 4 + h, :], sg[:, qs], start=True, stop=True),
                     reads=["selg", "sg"], writes=["pb6"])
            src = 7 if br == 1 else 3
            p.dve(lambda e: e.tensor_scalar(fB[:], pb[src][64:128, :, :], 1e-30, None, op0=ALU.add),
                  reads=[f"pb{src}"], writes=["fB"])
            p.dve(lambda e: e.reciprocal(fB[:], fB[:]), reads=["fB"], writes=["fB"])
            p.dve(lambda e: e.tensor_tensor(fB[:], fB[:], pb[6][0:64, :, :], op=ALU.mult), reads=["fB", "pb6"], writes=["fB"])
            if acc_first:
                p.dve(lambda e: e.tensor_tensor(accB[:], pb[src][0:64, :, :], fB[:], op=ALU.mult),
                      reads=[f"pb{src}", "fB"], writes=["accB"])
            else:
                p.dve(lambda e: e.tensor_tensor(tmpB[:], pb[src][0:64, :, :], fB[:], op=ALU.mult),
                      reads=[f"pb{src}", "fB"], writes=["tmpB"])
                p.dve(lambda e: e.tensor_tensor(accB[:], accB[:], tmpB[:], op=ALU.add), reads=["accB", "tmpB"],
                      writes=["accB"])

        def do_qblock(i):
            ibb = 48 + i
            qs = slice(128 * i, 128 * i + 128)
            par = i % 2
            for nm, src, srcn in (("QA", aq, "aq"), ("QR", bqr, "bqr"), ("QU", bq, "bq")):
                for j in range(2):
                    t_ = QBD[(nm, par, j)]
                    for g_ in range(2):
                        p.pool(lambda e, t_=t_, src=src, j=j, g_=g_: e.tensor_copy(
                            t_[64 * g_:64 * g_ + 64, 128 * g_:128 * g_ + 128], src[64 * g_:64 * g_ + 64, j, qs]),
                            reads=[srcn, f"{nm}{par}{j}"], writes=[f"{nm}{par}{j}"])
            if dbg < 2:
                return
            v = 9
            for j in range(2):
                sbk = next_sb()
                for r in range(2):
                    ks = slice(128 * (i + r), 128 * (i + r) + 128)
                    p.pe(lambda e, j=j, r=r, ks=ks, sbk=sbk: e.matmul(
                        pb[sbk][:, 2 * r:2 * r + 2, :].rearrange("p a b -> p (a b)"), ak[:, j, ks], QBD[("QA", par, j)][:],
                        start=True, stop=True), reads=["ak", f"QA{par}{j}"], writes=[f"pb{sbk}"])
                k_ = next_pt()
                p.act(lambda e, k_=k_, sbk=sbk: e.activation(PT[k_][:], pb[sbk][:], AF.Exp, scale=SCALE),
                      reads=[f"pb{sbk}"], writes=[f"PT{k_}"])
                if v < 2:
                    continue
                p.dve(lambda e, k_=k_: e.tensor_tensor(
                    PT[k_][:].rearrange("p (r g) q -> p r g q", r=2), PT[k_][:].rearrange("p (r g) q -> p r g q", r=2),
                    M2[:].unsqueeze(2).to_broadcast([128, 2, 2, 128]), op=ALU.mult),
                    reads=[f"PT{k_}", "M2"], writes=[f"PT{k_}"])
                if v < 3:
                    continue
                if i == 0:
                    p.dve(lambda e, k_=k_: e.tensor_scalar(PT[k_][:, 0:2, :], PT[k_][:, 0:2, :], offs[:, 3:4], None,
                                                          op0=ALU.mult), reads=[f"PT{k_}", "offs"], writes=[f"PT{k_}"])
                if v < 4:
                    continue
                for r in range(2):
                    p.pe(lambda e, j=j, r=r, k_=k_: e.matmul(
                        pb[2][:, 2 * j:2 * j + 2, :].rearrange("p a b -> p (a b)"), avg[:, i + r, j, :],
                        PT[k_][:, 2 * r:2 * r + 2, :].rearrange("p a b -> p (a b)"), start=(r == 0), stop=(r == 1)),
                        reads=["avg", f"PT{k_}"], writes=PB2)
            if v < 5:
                return
            p.dve(lambda e: e.tensor_tensor(rdA[:], pb[2][64:128, :, :], esk[64:128, :].unsqueeze(2).to_broadcast([64, 4, 128]),
                                            op=ALU.add), reads=PB2 + ["esk"], writes=["rdA"])
            p.dve(lambda e: e.reciprocal(rdA[:], rdA[:]), reads=["rdA"], writes=["rdA"])
            if v < 6:
                return
            for g_ in range(2):
                p.dve(lambda e, g_=g_: e.tensor_tensor(
                    ocA[64 * g_:64 * g_ + 64, :, qs], pb[2][0:64, :, :].rearrange("p (j g) q -> p j g q", g=2)[:, :, g_, :],
                    rdA[:].rearrange("p (j g) q -> p j g q", g=2)[:, :, g_, :], op=ALU.mult),
                    reads=PB2 + ["rdA"], writes=["ocA", "KKk"])
            if dbg < 3:
                return
            for r in range(5):
                sbk = next_sb()
                ks = slice(128 * (i + r), 128 * (i + r) + 128)
                for j in range(2):
                    p.pe(lambda e, j=j, ks=ks, sbk=sbk: e.matmul(
                        pb[sbk][:, 2 * j:2 * j + 2, :].rearrange("p a b -> p (a b)"), bkw[:, ks], QBD[("QR", par, j)][:],
                        start=True, stop=True), reads=["bkw", f"QR{par}{j}"], writes=[f"pb{sbk}"])
                k_ = next_pt()
                p.act(lambda e, k_=k_, sbk=sbk: e.activation(PT[k_][:], pb[sbk][:], AF.Exp, scale=SCALE),
                      reads=[f"pb{sbk}"], writes=[f"PT{k_}"])
                if r == 0 or r == 4:
                    msk = MLq if r == 0 else MUq
                    mn = "MLq" if r == 0 else "MUq"
                    p.dve(lambda e, k_=k_, msk=msk: e.tensor_tensor(PT[k_][:], PT[k_][:],
                                                                   msk[:].unsqueeze(1).to_broadcast([128, 4, 128]),
                                                                   op=ALU.mult), reads=[f"PT{k_}", mn], writes=[f"PT{k_}"])
                if i + r < 4:
                    p.dve(lambda e, k_=k_: e.tensor_scalar(PT[k_][:], PT[k_][:], offs[:, 3:4], None, op0=ALU.mult),
                          reads=[f"PT{k_}", "offs"], writes=[f"PT{k_}"])
                p.pe(lambda e, r=r, k_=k_: e.matmul(pb[3][:].rearrange("p a b -> p (a b)"), bvwg[:, i + r, :],
                                                    PT[k_][:].rearrange("p a b -> p (a b)"), start=(r == 0), stop=(r == 4)),
                     reads=["bvwg", f"PT{k_}"], writes=["pb3"])
            gate_finish(2, i, True)
            if dbg < 4:
                return
            p.dve(lambda e: e.tensor_scalar(mk[:], valc[:], float(128 * ibb - 31), None, op0=ALU.is_le),
                  reads=["valc"], writes=["mk"])
            for cc in range(4):
                sbk = next_sb()
                for j in range(2):
                    p.pe(lambda e, j=j, cc=cc, sbk=sbk: e.matmul(
                        pb[sbk][:, 2 * j:2 * j + 2, :].rearrange("p a b -> p (a b)"), kcT[:, cc * 128:(cc + 1) * 128],
                        QBD[("QU", par, j)][:], start=True, stop=True), reads=["kcT", f"QU{par}{j}"], writes=[f"pb{sbk}"])
                p.act(lambda e, cc=cc, sbk=sbk: e.activation(PTc[:, cc, :, :], pb[sbk][:], AF.Exp, scale=SCALE,
                                                             bias=biasn[:, cc:cc + 1]),
                      reads=[f"pb{sbk}", "biasn"], writes=[("PTc", cc)])
                if cc >= 2:
                    p.dve(lambda e, cc=cc: e.tensor_tensor(PTc[:, cc, :, :], PTc[:, cc, :, :],
                                                          mk[:, cc - 2, :].unsqueeze(1).to_broadcast([128, 4, 128]),
                                                          op=ALU.mult), reads=[("PTc", cc), "mk"], writes=[("PTc", cc)])
                p.pe(lambda e, cc=cc: e.matmul(pb[3][:].rearrange("p a b -> p (a b)"), vcg[:, cc, :],
                                               PTc[:, cc, :, :].rearrange("p a b -> p (a b)"), start=(cc == 0),
                                               stop=(cc == 3)), reads=["vcg", ("PTc", cc)], writes=["pb3"])
                p.pe(lambda e, cc=cc: e.matmul(pb[4][:].rearrange("p a b -> p (a b)"), ones_b[:],
                                               PTc[:, cc, :, :].rearrange("p a b -> p (a b)"), start=(cc == 0),
                                               stop=(cc == 3)), reads=["ones_b", ("PTc", cc)], writes=["pb4"])
            p.dve(lambda e: e.tensor_scalar(rd[:], pb[4][:], 1e-30, None, op0=ALU.add), reads=["pb4", "Et"], writes=["rd"])
            p.dve(lambda e: e.reciprocal(rd[:], rd[:]), reads=["rd"], writes=["rd"])
            for cc in range(4):
                p.dve(lambda e, cc=cc: e.tensor_tensor(pn[:], PTc[:, cc, :, :], rd[:], op=ALU.mult),
                      reads=[("PTc", cc), "rd"], writes=["pn"])
                p.dve(lambda e, cc=cc: e.tensor_reduce(psh[:, cc, :], pn[:].rearrange("p h q -> p q h"), axis=AX.X,
                                                       op=ALU.add), reads=["pn"], writes=[("psh", cc)])
                p.pe(lambda e, cc=cc: e.matmul(pb[5][:, 0, :], psh[:, cc, :], Gm[:, cc, :], start=(cc == 0), stop=(cc == 3)),
                     reads=[("psh", cc), "Gm"], writes=["pb5"])
            gate_finish(0, i, False)
            p.dve(lambda e: e.tensor_tensor(score[:], pb[5][:, 0, :], f0m[:], op=ALU.max), reads=["pb5", "f0m"],
                  writes=["score"])
            c0 = 2 * ibb
            if c0 + 2 < 128:
                p.dve(lambda e: e.memset(score[:, c0 + 2:128], -1e30), reads=["score"], writes=["score"])
            p.dve(lambda e: e.memset(score[0:64, c0 + 1:c0 + 2], -1e30), reads=["score"], writes=["score"])
            p.dve(lambda e: e.memset(score[0:64, c0 - 1:c0 + 1], 1e9), reads=["score"], writes=["score"])
            p.dve(lambda e: e.memset(score[64:128, c0:c0 + 2], 1e9), reads=["score"], writes=["score"])
            p.dve(lambda e: e.max(m8a[:], score[:]), reads=["score"], writes=["m8a"])
            p.dve(lambda e: e.match_replace(work[:], m8a[:], score[:], -3e38), reads=["score", "m8a"], writes=["work"])
            p.dve(lambda e: e.max(m8b[:], work[:]), reads=["work"], writes=["m8b"])
            p.dve(lambda e: e.tensor_scalar(work[:], score[:], m8b[:, 7:8], None, op0=ALU.is_ge), reads=["score", "m8b"],
                  writes=["work"])
            p.dve(lambda e: e.tensor_tensor(sel[:], work[:], validblk[:], op=ALU.mult), reads=["work", "validblk"],
                  writes=["sel"])
            if c0 + 2 < 128:
                p.dve(lambda e: e.memset(sel[:, c0 + 2:128], 0.0), reads=["sel"], writes=["sel"])
            p.dve(lambda e: e.memset(sel[0:64, c0 + 1:c0 + 2], 0.0), reads=["sel"], writes=["sel"])
            p.pe(lambda e: e.transpose(pb[5][:].rearrange("p a b -> p (a b)").bitcast(BF16)[:, 0:128], sel[:], IDb[:]),
                 reads=["sel", "IDb"], writes=["pb5"])
            p.act(lambda e: e.copy(selT[:], pb[5][:].rearrange("p a b -> p (a b)").bitcast(BF16)[:, 0:128]),
                  reads=["pb5"], writes=["selT"])
            if dbg < 5:
                return
            for cb in range(ibb + 1):
                sbk = next_sb()
                ks = slice(128 * cb, 128 * cb + 128)
                for j in range(2):
                    p.pe(lambda e, j=j, ks=ks, sbk=sbk: e.matmul(
                        pb[sbk][:, 2 * j:2 * j + 2, :].rearrange("p a b -> p (a b)"), bks[:, ks], QBD[("QR", par, j)][:],
                        start=True, stop=True), reads=["bks", f"QR{par}{j}"], writes=[f"pb{sbk}"])
                p.pe(lambda e, cb=cb: e.matmul(pb[2][:, cb % 4, :], E[:, cb, :], selT[:], start=True, stop=True),
                     reads=["E", "selT"], writes=[("pb2", cb % 4)])
                k_ = next_pt()
                p.act(lambda e, k_=k_, sbk=sbk: e.activation(PT[k_][:], pb[sbk][:], AF.Exp, scale=SCALE),
                      reads=[f"pb{sbk}"], writes=[f"PT{k_}"])
                p.dve(lambda e, k_=k_, cb=cb: e.tensor_tensor(PT[k_][:], PT[k_][:],
                                                             pb[2][:, cb % 4, :].unsqueeze(1).to_broadcast([128, 4, 128]),
                                                             op=ALU.mult), reads=[f"PT{k_}", ("pb2", cb % 4)],
                      writes=[f"PT{k_}"])
                if cb == ibb:
                    p.dve(lambda e, k_=k_: e.tensor_tensor(PT[k_][:], PT[k_][:],
                                                          MUq[:].unsqueeze(1).to_broadcast([128, 4, 128]), op=ALU.mult),
                          reads=[f"PT{k_}", "MUq"], writes=[f"PT{k_}"])
                p.pe(lambda e, cb=cb, k_=k_: e.matmul(pb[7][:].rearrange("p a b -> p (a b)"), bvsg[:, cb, :],
                                                      PT[k_][:].rearrange("p a b -> p (a b)"), start=(cb == 0),
                                                      stop=(cb == ibb)), reads=["bvsg", f"PT{k_}"], writes=["pb7"])
            gate_finish(1, i, False)
            for g_ in range(2):
                p.act(lambda e, g_=g_: e.copy(ocB[64 * g_:64 * g_ + 64, :, qs],
                                              accB[:].rearrange("p (j g) q -> p j g q", g=2)[:, :, g_, :]),
                      reads=["accB"], writes=["ocB", "KKk"])
        for i_ in qblocks:
            do_qblock(i_)
        for j in range(2):
            p.dma("sp", "ocA", lambda e, j=j: e.dma_start(out=oc_d[j, :, :], in_=ocA[:, j, :]), reads=["ocA"],
                  writes=[("oc", j)])
            p.dma("sp", "ocB", lambda e, j=j: e.dma_start(out=oc_d[2 + j, :, :], in_=ocB[:, j, :]), reads=["ocB"],
                  writes=[("oc", 2 + j)])
        p.emit(final_sems=["ocA", "ocB"])
    return nc


def assemble_B(Ab, qtr, wl):
    bf = ml_dtypes.bfloat16
    t0 = qtr * T
    off = 3 * T - t0
    own = Ab[qtr]

    def hist_cols(idx, src, n):
        if qtr == 0:
            return np.zeros((128, n), bf)
        return Ab[qtr - 1][src][idx][:, T - n:]

    def hist_rows(c0, c1, n):
        if qtr == 0:
            return np.zeros((n, c1 - c0), bf)
        return Ab[qtr - 1]["o_v"][T - n:, c0:c1]

    def tokmaj(rows, nblk):
        return np.ascontiguousarray(rows.reshape(nblk, 128, rows.shape[1]).transpose(1, 0, 2))

    d = {}
    d["aq"] = np.ascontiguousarray(own["o_rope"][0:2])
    d["ak"] = np.ascontiguousarray(np.stack([np.concatenate([hist_cols(2 + j, "o_rope", 128), own["o_rope"][2 + j]], 1)
                                             for j in range(2)]))
    d["av"] = tokmaj(np.concatenate([hist_rows(0, 128, 128), own["o_v"][:, 0:128]], 0), 17)
    d["sinks"] = np.ascontiguousarray(np.tile(wl["attn_sinks"][None, :], (128, 1)).astype(np.float32))
    d["offs"] = np.tile(np.array([[off, off // 16, off // 64, 0.0 if qtr == 0 else 1.0]], np.float32), (128, 1))
    d["bqr"] = np.ascontiguousarray(own["o_rope"][4:6])
    d["bq"] = np.ascontiguousarray(own["o_plain"][0:2])
    d["bkw"] = np.ascontiguousarray(np.concatenate([hist_cols(7, "o_rope", 512), own["o_rope"][7]], 1))
    d["bvw"] = tokmaj(np.concatenate([hist_rows(192, 256, 512), own["o_v"][:, 192:256]], 0), 20)
    pad = np.zeros((128, off), bf)
    d["bks"] = np.ascontiguousarray(np.concatenate([pad] + [Ab[k]["o_rope"][6] for k in range(qtr + 1)], 1))
    d["cmpT"] = np.ascontiguousarray(np.concatenate([pad] + [Ab[k]["o_plain"][2] for k in range(qtr + 1)], 1))
    d["bvs"] = tokmaj(np.concatenate([np.zeros((off, 64), bf)] + [Ab[k]["o_v"][:, 128:192] for k in range(qtr + 1)], 0), 64)
    d["bg"] = np.ascontiguousarray(own["o_g"])
    d["w1k"], d["w1v"], d["w2k"], d["w2v"] = wl["cmp_k_w1"], wl["cmp_v_w1"], wl["cmp_k_w2"], wl["cmp_v_w2"]
    for nm, key in (("pek", "cmp_pe_k"), ("pev", "cmp_pe_v")):
        d[nm] = np.ascontiguousarray(wl[key].reshape(16, 2, 64).transpose(1, 2, 0).reshape(128, 16))
    return d


def _fm(v):
    return np.ascontiguousarray(np.asarray(v, np.float32).reshape(8, 128).T)


def _toT(x):
    return np.ascontiguousarray(x.reshape(T, 8, 128).transpose(2, 1, 0))


def assemble_G(Ab, h, l, inp):
    NBK = SEQ // 128
    qkv = np.stack([np.concatenate([Ab[q]["o_c"][w * 4 + h] for q in range(4)], 1) for w in range(3)])
    cwl = inp["gdn_conv_w"][l]
    cw = np.stack([cwl[:, w * 512 + h * 128:w * 512 + (h + 1) * 128].T for w in range(3)], 1).reshape(128, 12)
    oz = np.concatenate([Ab[q]["o_z"] for q in range(4)], 0)
    gab = np.stack([oz[:, 512 + h].reshape(NBK, 128).T, oz[:, 516 + h].reshape(NBK, 128).T], -1)
    hp = np.tile(np.array([[inp["gdn_A_log"][l, h], inp["gdn_dt_bias"][l, h]]], np.float32), (128, 1))
    z = oz[:, h * 128:(h + 1) * 128].reshape(NBK, 128, 128).transpose(1, 0, 2)
    nwb = np.tile(inp["gdn_norm"][l][None, :], (128, 1))
    f = lambda a: np.ascontiguousarray(a, dtype=np.float32)
    return dict(qkv=f(qkv), cw=f(cw), gab=f(gab), hp=f(hp), z=f(z), nwb=f(nwb))


_PROGS = {}


def _prog(name):
    if name not in _PROGS:
        _PROGS[name] = {"M": build_M, "A": build_A, "B": build_B, "G": build_G, "C": lambda: build_C(False),
                        "CF": lambda: build_C(True)}[name]()
    return _PROGS[name]


def _run(name, in_maps):
    res = run_bass_kernel_spmd(_prog(name), in_maps, core_ids=list(range(8)))
    return res.results


def kernel(**inp):
    inp = {k: np.asarray(v) for k, v in inp.items()}
    cores = list(range(8))
    cT = np.ascontiguousarray(inp["c"].astype(np.float32).reshape(2, 8, 128).transpose(2, 1, 0))
    ims = []
    for core in cores:
        l, hf = core // 2, core % 2
        ims.append(dict(cT=cT, aw=np.ascontiguousarray(inp["ada_w"][l][:, hf * 3072:(hf + 1) * 3072]),
                        ab=np.ascontiguousarray(inp["ada_b"][l][hf * 3072:(hf + 1) * 3072].reshape(24, 128).T)))
    rm = _run("M", ims)
    mods = np.zeros((DEPTH, BATCH, 6 * D), np.float32)
    for core in cores:
        l, hf = core // 2, core % 2
        mods[l, :, hf * 3072:(hf + 1) * 3072] = rm[core]["mo"].transpose(2, 1, 0).reshape(2, 3072)
    xT = [_toT(inp["x"][core // 4, (core % 4) * T:(core % 4 + 1) * T].astype(np.float32)) for core in cores]
    for l in range(DEPTH):
        wl = {k: inp[k][l] for k in ("attn_sinks", "cmp_k_w1", "cmp_v_w1", "cmp_k_w2", "cmp_v_w2", "cmp_pe_k", "cmp_pe_v")}
        ims = []
        for core in cores:
            b, q = core // 4, core % 4
            m = mods[l, b]
            tabs = np.concatenate([_fm(inp["norm_mix"][l]), _fm(m[1024:2048]), _fm(m[0:1024])], 1)
            ims.append(dict(xT=xT[core], tabs=tabs, pos0=np.full((128, 1), q * T, np.float32), w=inp["w_in"][l]))
        ra = _run("A", ims)
        rb = _run("B", [assemble_B([ra[(core // 4) * 4 + q] for q in range(4)], core % 4, wl) for core in cores])
        rg = _run("G", [assemble_G([ra[(core // 4) * 4 + q] for q in range(4)], core % 4, l, inp) for core in cores])
        ims = []
        for core in cores:
            b, q = core // 4, core % 4
            m = mods[l, b]
            gd = [np.ascontiguousarray(rg[b * 4 + h]["yo"].transpose(1, 0, 2).reshape(SEQ, 128)[q * T:(q + 1) * T].T)
                  for h in range(4)]
            ocat = np.ascontiguousarray(np.concatenate([rb[core]["oc"], np.stack(gd)], 0))
            tabs = np.concatenate([_fm(m[2048:3072]), _fm(inp["norm_ffn"][l]), _fm(m[4096:5120]), _fm(m[3072:4096]),
                                   _fm(m[5120:6144]), _fm(inp["final_norm"])], 1)
            ims.append(dict(xT=xT[core], ocat=ocat, tabs=tabs, w_out=inp["w_out"][l], w_gu=inp["w_gate_up"][l],
                            w_down=inp["w_down"][l]))
        rc = _run("CF" if l == DEPTH - 1 else "C", ims)
        xT = [rc[core]["xo"] for core in cores]
    out = np.zeros((BATCH, SEQ, D), np.float32)
    for core in cores:
        b, q = core // 4, core % 4
        out[b, q * T:(q + 1) * T] = xT[core].transpose(2, 1, 0).reshape(T, D)
    return out
```

```python
import bisect
import math
from contextlib import ExitStack

import numpy as np
import ml_dtypes
import concourse.bass as bass
import concourse.mybir as mybir
from concourse.bass_utils import run_bass_kernel_spmd

F32 = mybir.dt.float32
BF16 = mybir.dt.bfloat16
I32 = mybir.dt.int32
AF = mybir.ActivationFunctionType
ALU = mybir.AluOpType
AX = mybir.AxisListType

D = 1024
SEQ = 8192
BATCH = 2
DEPTH = 4
T = 2048
NT = 512
DFF = 2816
EPS = 1e-6
ENGS = ("pe", "act", "dve", "pool", "sp")


class _Op:
    __slots__ = ("eng", "fn", "reads", "writes", "dma", "deps", "signal", "ordinal", "gidx")


def _is_psum(r):
    if isinstance(r, tuple):
        r = r[0]
    return isinstance(r, str) and (r.startswith("ps") or r.startswith("pb"))


class Prog:
    def __init__(self, nc):
        self.nc = nc
        self.ops = []
        self.per_eng = {e: [] for e in ENGS}
        self.last_w = {}
        self.readers = {}
        self.dma_groups = {}

    def add(self, eng, fn, reads=(), writes=(), dma=None):
        op = _Op()
        op.eng, op.fn, op.reads, op.writes, op.dma = eng, fn, tuple(reads), tuple(writes), dma
        op.deps = set()
        op.signal = False
        op.ordinal = None
        op.gidx = len(self.ops)
        for r in op.reads:
            w = self.last_w.get(r)
            if w is not None:
                op.deps.add(w)
            if _is_psum(r):
                for rd in self.readers.get(r, ()):
                    if rd.eng != eng:
                        op.deps.add(rd)
            self.readers.setdefault(r, []).append(op)
        for r in op.writes:
            w = self.last_w.get(r)
            if w is not None:
                if dma is not None and w.dma == dma:
                    op.deps |= w.deps
                else:
                    op.deps.add(w)
            for rd in self.readers.get(r, ()):
                if rd is not op:
                    op.deps.add(rd)
            self.readers[r] = []
            self.last_w[r] = op
        op.deps.discard(op)
        self.ops.append(op)
        self.per_eng[eng].append(op)
        if dma is not None:
            self.dma_groups.setdefault(dma, []).append(op)
        return op

    def pe(self, fn, reads=(), writes=()):
        return self.add("pe", fn, reads, writes)

    def act(self, fn, reads=(), writes=()):
        return self.add("act", fn, reads, writes)

    def dve(self, fn, reads=(), writes=()):
        return self.add("dve", fn, reads, writes)

    def pool(self, fn, reads=(), writes=()):
        return self.add("pool", fn, reads, writes)

    def dma(self, q, sem, fn, reads=(), writes=()):
        return self.add(q, fn, reads, writes, dma=sem)

    def emit(self, final_sems=()):
        nc = self.nc
        for op in self.ops:
            for d in op.deps:
                if d.dma is None:
                    if d.eng == "pe" and op.eng == "pe":
                        continue
                    d.signal = True
        for e in ENGS:
            n = 0
            for op in self.per_eng[e]:
                if op.dma is None and op.signal:
                    n += 1
                    op.ordinal = n
        gidx_of = {k: [o.gidx for o in ops] for k, ops in self.dma_groups.items()}
        with ExitStack() as es:
            sems = {e: es.enter_context(nc.semaphore("s_" + e)) for e in ENGS}
            dsems = {k: es.enter_context(nc.semaphore("d_" + str(k))) for k in self.dma_groups}
            block = es.enter_context(nc.Block())
            deco = {"pe": block.tensor, "act": block.scalar, "dve": block.vector, "pool": block.gpsimd,
                    "sp": block.sync}

            def make(e):
                def body(eng):
                    known = {}
                    for op in self.per_eng[e]:
                        need = {}
                        for d in op.deps:
                            if d.dma is not None:
                                key = ("d", d.dma)
                                v = 16 * bisect.bisect_left(gidx_of[d.dma], op.gidx)
                            else:
                                if d.eng == "pe" and e == "pe":
                                    continue
                                key = ("e", d.eng)
                                v = d.ordinal
                            if v > need.get(key, 0):
                                need[key] = v
                        for key, v in need.items():
                            if known.get(key, 0) >= v:
                                continue
                            known[key] = v
                            s = dsems[key[1]] if key[0] == "d" else sems[key[1]]
                            eng.wait_ge(s, v)
                        ins = op.fn(eng)
                        if op.dma is not None:
                            ins.then_inc(dsems[op.dma], 16)
                        elif op.signal:
                            ins.then_inc(sems[e], 1)
                    if e == "sp":
                        for k in final_sems:
                            if k not in dsems:
                                continue
                            eng.wait_ge(dsems[k], 16 * len(self.dma_groups[k]))
                return body

            for e in ENGS:
                if self.per_eng[e] or e == "sp":
                    deco[e](make(e))


class Ctx:
    def __init__(self, nc, es):
        self.nc, self.es = nc, es
        self.p = Prog(nc)
        self.n = 0

    def sb(self, shape, dt, name=None):
        self.n += 1
        return self.es.enter_context(self.nc.sbuf_tensor("s_" + (name or f"sb{self.n}"), list(shape), dt))

    def ps(self, shape, dt=F32, name=None):
        self.n += 1
        return self.es.enter_context(self.nc.psum_tensor(name or f"ps{self.n}", list(shape), dt))

    def din(self, name, shape, dt=F32):
        return self.nc.dram_tensor(name, list(shape), dt, kind="ExternalInput").ap()

    def dout(self, name, shape, dt=F32):
        return self.nc.dram_tensor(name, list(shape), dt, kind="ExternalOutput").ap()


def _swap(c0):
    return [(c0 + 32, c0 + 64), (c0, c0 + 32)]


def _plain(c0, n=64):
    return [(c0, c0 + n)]


FM_BLOCKS = [
    ("aq01", _plain(0, 128)), ("aq01s", _swap(0) + _swap(64)),
    ("aq23", _plain(128, 128)), ("aq23s", _swap(128) + _swap(192)),
    ("ak0", _plain(256) + _plain(256)), ("ak0s", _swap(256) + _swap(256)),
    ("ak1", _plain(320) + _plain(320)), ("ak1s", _swap(320) + _swap(320)),
    ("bq01", _plain(512, 128)), ("bq01s", _swap(512) + _swap(576)),
    ("bq23", _plain(640, 128)), ("bq23s", _swap(640) + _swap(704)),
    ("bks", _plain(896) + _plain(896)), ("bkss", _swap(896) + _swap(896)),
    ("bkw", _plain(1024) + _plain(1024)), ("bkws", _swap(1024) + _swap(1024)),
    ("cmp", _plain(768, 128)),
] + [("c%d" % i, _plain(1164 + 128 * i, 128)) for i in range(12)]
FM_OFF = {n: 128 * i for i, (n, _) in enumerate(FM_BLOCKS)}
GATE_OFF = 128 * len(FM_BLOCKS)
TM_OFF = GATE_OFF + 12
TM_COLS = [(384, 512), (960, 1024), (1088, 1152), (2700, 3212), (3212, 3220)]
TM_N = 128 + 64 + 64 + 512 + 8
WIN_COLS = TM_OFF + TM_N


def build_A(dbg=9):
    nc = bass.Bass("TRN2", target_bir_lowering=False)
    with ExitStack() as es:
        c = Ctx(nc, es)
        p = c.p
        xT = c.din("xT", [128, 8, T])
        tabs = c.din("tabs", [128, 24])
        pos0 = c.din("pos0", [128, 1])
        w = c.din("w", [D, 3220])
        wr = w.rearrange("(k p) c -> p k c", p=128)
        o_rope = c.dout("o_rope", [8, 128, T], BF16)
        o_plain = c.dout("o_plain", [3, 128, T], BF16)
        o_c = c.dout("o_c", [12, 128, T], F32)
        o_g = c.dout("o_g", [12, T], F32)
        o_v = c.dout("o_v", [T, 256], BF16)
        o_z = c.dout("o_z", [T, 520], F32)

        W = c.sb([128, 8, WIN_COLS], BF16, "W")
        tb = c.sb([128, 24], F32, "tb")
        s1 = c.sb([128, 8], F32, "s1")
        p0 = c.sb([128, 1], F32, "p0")
        S2 = c.sb([128, T], F32, "S2")
        ang = c.sb([128, T], F32, "ang")
        C2 = ang
        invrow = c.sb([1, 128], F32, "invrow")
        one1 = c.sb([1, 1], F32, "one1")
        inv = c.sb([128, 1], F32, "inv")
        ones_bf = c.sb([128, 128], BF16, "ones_bf")
        xt = [c.sb([128, 8, NT], F32, f"xt{i}") for i in range(1)]
        sq = c.sb([128, 8, NT], BF16, "sq")
        rstd = c.sb([128, NT], F32, "rstd")
        tmp = [c.sb([128, NT], F32, f"tmp{i}") for i in range(2)]
        hT = c.sb([128, 8, NT], BF16, "hT")
        outR = [c.sb([128, T], BF16, f"outR{i}") for i in range(8)]
        outP = [c.sb([128, T], BF16, f"outP{i}") for i in range(3)]
        t1 = [c.sb([128, NT], F32, f"t1_{i}") for i in range(2)]
        t2 = [c.sb([128, NT], F32, f"t2_{i}") for i in range(2)]
        stc = [c.sb([128, NT], F32, f"stc{i}") for i in range(3)]
        stg = c.sb([12, T], F32, "stg")
        stv = [c.sb([128, 256], BF16, f"stv{i}") for i in range(2)]
        stz = [c.sb([128, 520], F32, f"stz{i}") for i in range(2)]
        psb = [c.ps([128, 512], F32, f"psb{i}") for i in range(8)]

        p.dma("sp", "tb", lambda e: e.dma_start(out=tb[:], in_=tabs[:, :]), writes=["tb"])
        p.dma("sp", "p0", lambda e: e.dma_start(out=p0[:], in_=pos0[:, :]), writes=["p0"])
        col = 0
        wi = 0
        for name, rngs in FM_BLOCKS:
            for (a, b) in rngs:
                p.dma("pool", "W", lambda e, a=a, b=b, col=col: e.dma_start(
                    out=W[:, :, col:col + b - a], in_=wr[:, :, a:b]), writes=["W"])
                col += b - a
        p.dma("pool", "W", lambda e: e.dma_start(out=W[:, :, GATE_OFF:GATE_OFF + 12], in_=wr[:, :, 1152:1164]),
              writes=["W"])
        col = TM_OFF
        for (a, b) in TM_COLS:
            p.dma("pool", "W", lambda e, a=a, b=b, col=col: e.dma_start(
                out=W[:, :, col:col + b - a], in_=wr[:, :, a:b]), writes=["W"])
            col += b - a

        p.dve(lambda e: e.tensor_scalar(s1[:], tb[:, 8:16], 1.0, 32.0, op0=ALU.add, op1=ALU.mult),
              reads=["tb"], writes=["s1"])
        p.dve(lambda e: e.tensor_tensor(s1[:], s1[:], tb[:, 0:8], op=ALU.mult), reads=["tb", "s1"], writes=["s1"])
        p.dve(lambda e: e.memset(ones_bf[:], 1.0), writes=["ones_bf"])
        p.dve(lambda e: e.memset(one1[:], 1.0), writes=["one1"])
        for i in range(32):
            v = float(np.float32(1.0) / np.float32(np.float32(10000.0) ** np.float32(2 * i / 64.0)))
            p.dve(lambda e, i=i, v=v: e.memset(invrow[0:1, i:128:32], v), writes=["invrow"])
        p.pe(lambda e: e.matmul(psb[0][:, 0:1], invrow[0:1, :], one1[0:1, 0:1], start=True, stop=True),
             reads=["invrow", "one1"], writes=["psb0"])
        p.act(lambda e: e.copy(inv[:], psb[0][:, 0:1]), reads=["psb0"], writes=["inv"])
        p.pool(lambda e: e.iota(ang[:], [[1, T]], base=0, channel_multiplier=0,
                                allow_small_or_imprecise_dtypes=True), writes=["ang"])
        p.dve(lambda e: e.tensor_scalar(ang[:], ang[:], p0[:, 0:1], inv[:, 0:1], op0=ALU.add, op1=ALU.mult),
              reads=["ang", "p0", "inv"], writes=["ang"])
        TWO_PI = 2.0 * math.pi
        CW1 = 6.28125
        CW2 = TWO_PI - CW1
        nI = c.sb([128, NT], I32, "nI")
        yy = c.sb([128, NT], F32, "yy")
        mm_ = c.sb([128, NT], F32, "mm_")

        def sin_tile(dst, ts, shift):
            p.dve(lambda e: e.tensor_scalar(yy[:], ang[:, ts], float(shift), None, op0=ALU.add),
                  reads=["ang"], writes=["yy"])
            p.dve(lambda e: e.tensor_scalar(nI[:], yy[:], 1.0 / TWO_PI, None, op0=ALU.mult),
                  reads=["yy"], writes=["nI"])
            p.dve(lambda e: e.scalar_tensor_tensor(yy[:], nI[:], -CW1, yy[:], op0=ALU.mult, op1=ALU.add),
                  reads=["yy", "nI"], writes=["yy"])
            p.dve(lambda e: e.scalar_tensor_tensor(yy[:], nI[:], -CW2, yy[:], op0=ALU.mult, op1=ALU.add),
                  reads=["yy", "nI"], writes=["yy"])
            p.dve(lambda e: e.tensor_scalar(mm_[:], yy[:], math.pi, -TWO_PI, op0=ALU.is_gt, op1=ALU.mult),
                  reads=["yy"], writes=["mm_"])
            p.dve(lambda e: e.tensor_tensor(yy[:], yy[:], mm_[:], op=ALU.add), reads=["yy", "mm_"], writes=["yy"])
            p.dve(lambda e: e.tensor_scalar(yy[:], yy[:], -math.pi, math.pi, op0=ALU.max, op1=ALU.min),
                  reads=["yy"], writes=["yy"])
            p.act(lambda e: e.activation(dst[:, ts], yy[:], AF.Sin), reads=["yy"], writes=["S2", "ang", "C2"])

        for ti in range(T // NT):
            ts_ = slice(ti * NT, (ti + 1) * NT)
            sin_tile(S2, ts_, 0.0)
            sin_tile(C2, ts_, 0.5 * math.pi)
        for base in (0, 64):
            p.act(lambda e, base=base: e.mul(S2[base:base + 32, :], S2[base:base + 32, :], -1.0),
                  reads=["S2"], writes=["S2"])

        rope_pairs = [("aq01", 0), ("aq23", 1), ("ak0", 2), ("ak1", 3), ("bq01", 4), ("bq23", 5), ("bks", 6),
                      ("bkw", 7)]
        bank = [0]
        deps = c.sb([128, 1], F32, "deps")
        p.dve(lambda e: e.memset(deps[:], float(D * EPS)), writes=["deps"])

        def nextbank():
            b = bank[0]
            bank[0] = (b + 1) % 8
            return b

        def mm_block(coff, ncols_m, bk, rd):
            for k in range(8):
                p.pe(lambda e, k=k: e.matmul(psb[bk][0:ncols_m, :], W[:, k, coff:coff + ncols_m], hT[:, k, :],
                                             start=(k == 0), stop=(k == 7)),
                     reads=["W", ("hT", k)], writes=[f"psb{bk}"])

        for ti in range(T // NT if dbg >= 2 else 0):
            x_ = xt[0]
            xr = "xt0"
            ts = slice(ti * NT, (ti + 1) * NT)
            p.dma("sp", xr, lambda e, x_=x_, ts=ts: e.dma_start(out=x_[:], in_=xT[:, :, ts]), writes=[xr])
            p.act(lambda e, x_=x_: e.activation(sq[:], x_[:], AF.Square), reads=[xr], writes=["sq"])
            b0 = nextbank()
            for k in range(8):
                p.pe(lambda e, k=k, b0=b0: e.matmul(psb[b0][:], ones_bf[:], sq[:, k, :], start=(k == 0), stop=(k == 7)),
                     reads=["ones_bf", "sq"], writes=[f"psb{b0}"])
            p.act(lambda e, b0=b0: e.activation(rstd[:], psb[b0][:], AF.Sqrt, bias=deps[:, 0:1]),
                  reads=[f"psb{b0}", "deps"], writes=["rstd"])
            p.dve(lambda e: e.reciprocal(rstd[:], rstd[:]), reads=["rstd"], writes=["rstd"])
            for k in range(8):
                tm = tmp[k % 2]
                tr = f"tmp{k % 2}"
                p.dve(lambda e, k=k, tm=tm, x_=x_: e.tensor_tensor(tm[:], x_[:, k, :], rstd[:], op=ALU.mult),
                      reads=[xr, "rstd"], writes=[tr])
                p.act(lambda e, k=k, tm=tm: e.activation(hT[:, k, :], tm[:], AF.Identity, bias=tb[:, 16 + k:17 + k],
                                                         scale=s1[:, k:k + 1]),
                      reads=[tr, "s1", "tb"], writes=[("hT", k)])
            for name, oi in (rope_pairs if dbg >= 3 else []):
                b1 = nextbank()
                mm_block(FM_OFF[name], 128, b1, None)
                b2 = nextbank()
                mm_block(FM_OFF[name + "s"], 128, b2, None)
                a1 = t1[oi % 2]
                a2 = t2[oi % 2]
                p.dve(lambda e, b1=b1, a1=a1, ts=ts: e.tensor_tensor(a1[:], psb[b1][:], C2[:, ts], op=ALU.mult),
                      reads=[f"psb{b1}", "C2"], writes=[f"t1_{oi % 2}"])
                if name.startswith("bq"):
                    po = outP[oi - 4]
                    p.act(lambda e, b1=b1, po=po, ts=ts: e.copy(po[:, ts], psb[b1][:]),
                          reads=[f"psb{b1}"], writes=[f"outP{oi - 4}"])
                p.dve(lambda e, b2=b2, a2=a2, ts=ts: e.tensor_tensor(a2[:], psb[b2][:], S2[:, ts], op=ALU.mult),
                      reads=[f"psb{b2}", "S2"], writes=[f"t2_{oi % 2}"])
                ro = outR[oi]
                p.dve(lambda e, a1=a1, a2=a2, ro=ro, ts=ts: e.tensor_tensor(ro[:, ts], a1[:], a2[:], op=ALU.add),
                       reads=[f"t1_{oi % 2}", f"t2_{oi % 2}"], writes=[f"outR{oi}"])
            if dbg < 4:
                continue
            b1 = nextbank()
            mm_block(FM_OFF["cmp"], 128, b1, None)
            p.act(lambda e, b1=b1, ts=ts: e.copy(outP[2][:, ts], psb[b1][:]), reads=[f"psb{b1}"], writes=["outP2"])
            for i in range(12):
                b1 = nextbank()
                mm_block(FM_OFF["c%d" % i], 128, b1, None)
                sidx = i % 3
                st = stc[sidx]
                p.act(lambda e, b1=b1, st=st: e.copy(st[:], psb[b1][:]), reads=[f"psb{b1}"], writes=[f"stc{sidx}"])
                p.dma("sp", f"stc{sidx}", lambda e, st=st, i=i, ts=ts: e.dma_start(out=o_c[i, :, ts], in_=st[:]),
                      reads=[f"stc{sidx}"], writes=[("o_c", i, ti)])
            if dbg < 5:
                continue
            b1 = nextbank()
            mm_block(GATE_OFF, 12, b1, None)
            p.act(lambda e, b1=b1, ts=ts: e.copy(stg[:, ts], psb[b1][0:12, :]), reads=[f"psb{b1}"], writes=["stg"])
            if dbg < 6:
                continue
            for s in range(NT // 128):
                ss = slice(s * 128, (s + 1) * 128)
                tok = slice(ti * NT + s * 128, ti * NT + (s + 1) * 128)
                b1 = nextbank()
                for k in range(8):
                    p.pe(lambda e, k=k, b1=b1, ss=ss: e.matmul(psb[b1][:, 0:256], hT[:, k, ss],
                                                             W[:, k, TM_OFF:TM_OFF + 256], start=(k == 0), stop=(k == 7)),
                         reads=["W", ("hT", k)], writes=[f"psb{b1}"])
                b2 = nextbank()
                for k in range(8):
                    p.pe(lambda e, k=k, b2=b2, ss=ss: e.matmul(psb[b2][:, 0:512], hT[:, k, ss],
                                                             W[:, k, TM_OFF + 256:TM_OFF + 768], start=(k == 0),
                                                             stop=(k == 7)),
                         reads=["W", ("hT", k)], writes=[f"psb{b2}"])
                b3 = nextbank()
                for k in range(8):
                    p.pe(lambda e, k=k, b3=b3, ss=ss: e.matmul(psb[b3][:, 0:8], hT[:, k, ss],
                                                             W[:, k, TM_OFF + 768:TM_OFF + 776], start=(k == 0),
                                                             stop=(k == 7)),
                         reads=["W", ("hT", k)], writes=[f"psb{b3}"])
                sv = stv[s % 2]
                sz = stz[s % 2]
                p.act(lambda e, b1=b1, sv=sv: e.copy(sv[:], psb[b1][:, 0:256]), reads=[f"psb{b1}"], writes=[f"stv{s % 2}"])
                p.dve(lambda e, b2=b2, sz=sz: e.tensor_copy(sz[:, 0:512], psb[b2][:, 0:512]), reads=[f"psb{b2}"],
                      writes=[f"stz{s % 2}"])
                p.dve(lambda e, b3=b3, sz=sz: e.tensor_copy(sz[:, 512:520], psb[b3][:, 0:8]), reads=[f"psb{b3}"],
                      writes=[f"stz{s % 2}"])
                p.dma("sp", f"stv{s % 2}", lambda e, sv=sv, tok=tok: e.dma_start(out=o_v[tok, :], in_=sv[:]),
                      reads=[f"stv{s % 2}"], writes=[("o_v", ti, s)])
                p.dma("sp", f"stz{s % 2}", lambda e, sz=sz, tok=tok: e.dma_start(out=o_z[tok, :], in_=sz[:]),
                      reads=[f"stz{s % 2}"], writes=[("o_z", ti, s)])
        fs = ["stc0", "stc1", "stc2", "stv0", "stv1", "stz0", "stz1"]
        for i in range(8):
            p.dma("sp", f"oR{i}", lambda e, i=i: e.dma_start(out=o_rope[i, :, :], in_=outR[i][:]),
                  reads=[f"outR{i}"], writes=[("o_rope", i)])
            fs.append(f"oR{i}")
        for i in range(3):
            p.dma("sp", f"oP{i}", lambda e, i=i: e.dma_start(out=o_plain[i, :, :], in_=outP[i][:]),
                  reads=[f"outP{i}"], writes=[("o_plain", i)])
            fs.append(f"oP{i}")
        p.dma("sp", "og", lambda e: e.dma_start(out=o_g[:, :], in_=stg[:]), reads=["stg"], writes=["o_g"])
        fs.append("og")
        p.emit(final_sems=fs)
    return nc


NTC = 256


def build_C(final=False):
    nc = bass.Bass("TRN2", target_bir_lowering=False)
    with ExitStack() as es:
        c = Ctx(nc, es)
        p = c.p
        xT = c.din("xT", [128, 8, T])
        oc = c.din("ocat", [8, 128, T], BF16)
        tabs = c.din("tabs", [128, 48])
        wo = c.din("w_out", [D, D])
        wgu = c.din("w_gu", [D, 2 * DFF])
        wd = c.din("w_down", [DFF, D])
        xo = c.dout("xo", [128, 8, T])
        wor = wo.rearrange("(k p) c -> p k c", p=128)
        wgur = wgu.rearrange("(k p) c -> p k c", p=128)
        wdr = wd.rearrange("(k p) c -> p k c", p=128)
        NF = DFF // 128

        Wo = c.sb([128, 8, D], BF16, "Wo")
        Wg = c.sb([128, 8, 2 * DFF], BF16, "Wg")
        Wd = c.sb([128, NF, D], BF16, "Wd")
        tb = c.sb([128, 48], F32, "tb")
        s2 = c.sb([128, 8], F32, "s2")
        sfin = c.sb([128, 8], F32, "sfin")
        ones_bf = c.sb([128, 128], BF16, "ones_bf")
        deps = c.sb([128, 1], F32, "deps")
        xt = c.sb([128, 8, NTC], F32, "xt")
        ot = c.sb([128, 8, NTC], BF16, "ot")
        sq = c.sb([128, 8, NTC], BF16, "sq")
        rstd = c.sb([128, NTC], F32, "rstd")
        tmp = [c.sb([128, NTC], F32, f"tmp{i}") for i in range(2)]
        hT = c.sb([128, 8, NTC], BF16, "hT")
        gs = [c.sb([128, NTC], F32, f"gs{i}") for i in range(2)]
        aT = c.sb([128, NF, NTC], BF16, "aT")
        xout = c.sb([128, 8, NTC], F32, "xout")
        psb = [c.ps([128, 2, NTC], F32, f"psb{i}") for i in range(8)]
        bank = [0]

        def nextbank():
            b = bank[0]
            bank[0] = (b + 1) % 8
            return b

        p.dma("sp", "tb", lambda e: e.dma_start(out=tb[:], in_=tabs[:, :]), writes=["tb"])
        for j in range(2):
            p.dma("pool", "Wo", lambda e, j=j: e.dma_start(out=Wo[:, :, j * 512:(j + 1) * 512],
                                                         in_=wor[:, :, j * 512:(j + 1) * 512]), writes=["Wo"])
        for j in range(11):
            p.dma("pool", "Wg", lambda e, j=j: e.dma_start(out=Wg[:, :, j * 512:(j + 1) * 512],
                                                         in_=wgur[:, :, j * 512:(j + 1) * 512]), writes=["Wg"])
        for j in range(2):
            p.dma("pool", "Wd", lambda e, j=j: e.dma_start(out=Wd[:, :, j * 512:(j + 1) * 512],
                                                         in_=wdr[:, :, j * 512:(j + 1) * 512]), writes=["Wd"])
        p.dve(lambda e: e.memset(ones_bf[:], 1.0), writes=["ones_bf"])
        p.dve(lambda e: e.memset(deps[:], float(D * EPS)), writes=["deps"])
        p.dve(lambda e: e.tensor_scalar(s2[:], tb[:, 16:24], 1.0, 32.0, op0=ALU.add, op1=ALU.mult),
              reads=["tb"], writes=["s2"])
        p.dve(lambda e: e.tensor_tensor(s2[:], s2[:], tb[:, 8:16], op=ALU.mult), reads=["tb", "s2"], writes=["s2"])
        p.dve(lambda e: e.tensor_scalar(sfin[:], tb[:, 40:48], 32.0, None, op0=ALU.mult), reads=["tb"], writes=["sfin"])

        def rms_stats(src, srcres):
            p.act(lambda e: e.activation(sq[:], src[:], AF.Square), reads=srcres, writes=["sq"])
            b0 = nextbank()
            for k in range(8):
                p.pe(lambda e, k=k, b0=b0: e.matmul(psb[b0][:, 0, :], ones_bf[:], sq[:, k, :], start=(k == 0),
                                                  stop=(k == 7)), reads=["ones_bf", "sq"], writes=[f"psb{b0}"])
            p.act(lambda e, b0=b0: e.activation(rstd[:], psb[b0][:, 0, :], AF.Sqrt, bias=deps[:, 0:1]),
                  reads=[f"psb{b0}", "deps"], writes=["rstd"])
            p.dve(lambda e: e.reciprocal(rstd[:], rstd[:]), reads=["rstd"], writes=["rstd"])

        for ti in range(T // NTC):
            ts = slice(ti * NTC, (ti + 1) * NTC)
            p.dma("sp", "xt", lambda e, ts=ts: e.dma_start(out=xt[:], in_=xT[:, :, ts]), writes=[("xt", k) for k in range(8)])
            p.dma("sp", "ot", lambda e, ts=ts: e.dma_start(out=ot[:], in_=oc[:, :, ts].rearrange("k p t -> p k t")),
                  writes=["ot"])
            xres = [("xt", k) for k in range(8)]
            for fo2 in range(4):
                b1 = nextbank()
                for j in range(2):
                    fo = fo2 * 2 + j
                    for kc in range(8):
                        p.pe(lambda e, b1=b1, j=j, fo=fo, kc=kc: e.matmul(
                            psb[b1][:, j, :], Wo[:, kc, fo * 128:(fo + 1) * 128], ot[:, kc, :], start=(kc == 0),
                            stop=(kc == 7)), reads=["Wo", "ot"], writes=[f"psb{b1}"])
                for j in range(2):
                    fo = fo2 * 2 + j
                    p.dve(lambda e, b1=b1, j=j, fo=fo: e.scalar_tensor_tensor(
                        xt[:, fo, :], psb[b1][:, j, :], tb[:, fo:fo + 1], xt[:, fo, :], op0=ALU.mult, op1=ALU.add),
                        reads=[f"psb{b1}", "tb", ("xt", fo)], writes=[("xt", fo)])
            rms_stats(xt, xres)
            for k in range(8):
                tm = tmp[k % 2]
                tr = f"tmp{k % 2}"
                p.dve(lambda e, k=k, tm=tm: e.tensor_tensor(tm[:], xt[:, k, :], rstd[:], op=ALU.mult),
                      reads=[("xt", k), "rstd"], writes=[tr])
                p.act(lambda e, k=k, tm=tm: e.activation(hT[:, k, :], tm[:], AF.Identity, bias=tb[:, 24 + k:25 + k],
                                                         scale=s2[:, k:k + 1]),
                      reads=[tr, "s2", "tb"], writes=[("hT", k)])
            for f in range(NF):
                b1 = nextbank()
                for j in range(2):
                    co = j * DFF + f * 128
                    for k in range(8):
                        p.pe(lambda e, b1=b1, j=j, co=co, k=k: e.matmul(
                            psb[b1][:, j, :], Wg[:, k, co:co + 128], hT[:, k, :], start=(k == 0), stop=(k == 7)),
                            reads=["Wg", ("hT", k)], writes=[f"psb{b1}"])
                g_ = gs[f % 2]
                p.act(lambda e, b1=b1, g_=g_: e.activation(g_[:], psb[b1][:, 0, :], AF.Silu),
                      reads=[f"psb{b1}"], writes=[f"gs{f % 2}"])
                p.dve(lambda e, b1=b1, g_=g_, f=f: e.tensor_tensor(aT[:, f, :], g_[:], psb[b1][:, 1, :], op=ALU.mult),
                      reads=[f"psb{b1}", f"gs{f % 2}"], writes=[("aT", f)])
            for fo2 in range(4):
                b1 = nextbank()
                for j in range(2):
                    fo = fo2 * 2 + j
                    for f in range(NF):
                        p.pe(lambda e, b1=b1, j=j, fo=fo, f=f: e.matmul(
                            psb[b1][:, j, :], Wd[:, f, fo * 128:(fo + 1) * 128], aT[:, f, :], start=(f == 0),
                            stop=(f == NF - 1)), reads=["Wd", ("aT", f)], writes=[f"psb{b1}"])
                for j in range(2):
                    fo = fo2 * 2 + j
                    dst = xt if final else xout
                    dres = ("xt", fo) if final else ("xout", fo)
                    p.dve(lambda e, b1=b1, j=j, fo=fo, dst=dst: e.scalar_tensor_tensor(
                        dst[:, fo, :], psb[b1][:, j, :], tb[:, 32 + fo:33 + fo], xt[:, fo, :], op0=ALU.mult,
                        op1=ALU.add), reads=[f"psb{b1}", "tb", ("xt", fo)], writes=[dres])
            if final:
                rms_stats(xt, xres)
                for k in range(8):
                    tm = tmp[k % 2]
                    tr = f"tmp{k % 2}"
                    p.dve(lambda e, k=k, tm=tm: e.tensor_tensor(tm[:], xt[:, k, :], rstd[:], op=ALU.mult),
                          reads=[("xt", k), "rstd"], writes=[tr])
                    p.act(lambda e, k=k, tm=tm: e.activation(xout[:, k, :], tm[:], AF.Copy, scale=sfin[:, k:k + 1]),
                          reads=[tr, "sfin"], writes=[("xout", k)])
            p.dma("sp", "xout", lambda e, ts=ts: e.dma_start(out=xo[:, :, ts], in_=xout[:]),
                  reads=[("xout", k) for k in range(8)], writes=[("xo", ti)])
        p.emit(final_sems=["xout"])
    return nc


def build_M():
    nc = bass.Bass("TRN2", target_bir_lowering=False)
    HC = 3072
    with ExitStack() as es:
        c = Ctx(nc, es)
        p = c.p
        cT = c.din("cT", [128, 8, 2])
        aw = c.din("aw", [D, HC])
        ab = c.din("ab", [128, HC // 128])
        mo = c.dout("mo", [128, HC // 128, 2])
        awr = aw.rearrange("(k p) c -> p k c", p=128)
        NM = HC // 128
        Wt = [c.sb([128, 8, 768], F32, f"Wt{i}") for i in range(2)]
        ct = c.sb([128, 8, 2], F32, "ct")
        sg = c.sb([128, 8, 2], F32, "sg")
        sc = c.sb([128, 8, 2], F32, "sc")
        abt = c.sb([128, NM], F32, "abt")
        res = c.sb([128, NM, 2], F32, "res")
        ps = c.ps([128, NM, 2], F32, "ps_m")
        p.dma("sp", "ct", lambda e: e.dma_start(out=ct[:], in_=cT[:, :, :]), writes=["ct"])
        p.dma("sp", "abt", lambda e: e.dma_start(out=abt[:], in_=ab[:, :]), writes=["abt"])
        p.act(lambda e: e.activation(sg[:], ct[:], AF.Sigmoid), reads=["ct"], writes=["sg"])
        p.dve(lambda e: e.tensor_tensor(sc[:], sg[:], ct[:], op=ALU.mult), reads=["sg", "ct"], writes=["sc"])
        for j in range(HC // 768):
            wt = Wt[j % 2]
            wr_ = f"Wt{j % 2}"
            p.dma("sp", wr_, lambda e, wt=wt, j=j: e.dma_start(out=wt[:], in_=awr[:, :, j * 768:(j + 1) * 768]),
                  writes=[wr_])
            for mm in range(6):
                m = j * 6 + mm
                for k in range(8):
                    p.pe(lambda e, wt=wt, mm=mm, m=m, k=k: e.matmul(ps[:, m, :], wt[:, k, mm * 128:(mm + 1) * 128],
                                                                   sc[:, k, :], start=(k == 0), stop=(k == 7)),
                         reads=[wr_, "sc"], writes=["ps_m"])
        p.dve(lambda e: e.tensor_tensor(res[:], ps[:], abt[:, :, None].to_broadcast([128, NM, 2]), op=ALU.add),
              reads=["ps_m", "abt"], writes=["res"])
        p.dma("sp", "res", lambda e: e.dma_start(out=mo[:, :, :], in_=res[:]), reads=["res"], writes=["mo"])
        p.emit(final_sems=["res"])
    return nc


GT = 512


def build_G(ntiles=SEQ // GT, dbg=99):
    nc = bass.Bass("TRN2", target_bir_lowering=False)
    S_ = ntiles * GT
    NB = S_ // 128
    with ExitStack() as es:
        c = Ctx(nc, es)
        p = c.p
        qkv = c.din("qkv", [3, 128, S_])
        cw = c.din("cw", [128, 12])
        gab = c.din("gab", [128, NB, 2])
        hp = c.din("hp", [128, 2])
        zin = c.din("z", [128, NB, 128])
        nwb = c.din("nwb", [128, 128])
        yo = c.dout("yo", [128, NB, 128], BF16)

        val = c.sb([128, 128], F32, "val")
        MU = c.sb([128, 128], F32, "MU")
        ML = c.sb([128, 128], F32, "ML")
        ID = c.sb([128, 128], F32, "ID")
        ones_bf = c.sb([128, 128], BF16, "ones_bf")
        ones_f = c.sb([128, 128], F32, "ones_f")
        epsc = c.sb([128, 1], F32, "epsc")
        cwt = c.sb([128, 12], F32, "cwt")
        hpt = c.sb([128, 2], F32, "hpt")
        nw = c.sb([128, 128], F32, "nw")
        p.dma("sp", "cwt", lambda e: e.dma_start(out=cwt[:], in_=cw[:, :]), writes=["cwt"])
        p.dma("sp", "hpt", lambda e: e.dma_start(out=hpt[:], in_=hp[:, :]), writes=["hpt"])
        p.dma("sp", "nw", lambda e: e.dma_start(out=nw[:], in_=nwb[:, :]), writes=["nw"])
        p.pool(lambda e: e.iota(val[:], [[1, 128]], base=0, channel_multiplier=-1,
                                allow_small_or_imprecise_dtypes=True), writes=["val"])
        p.dve(lambda e: e.tensor_scalar(MU[:], val[:], 0.0, None, op0=ALU.is_ge), reads=["val"], writes=["MU"])
        p.dve(lambda e: e.tensor_scalar(ML[:], val[:], 0.0, None, op0=ALU.is_lt), reads=["val"], writes=["ML"])
        p.dve(lambda e: e.tensor_scalar(ID[:], val[:], 0.0, None, op0=ALU.is_equal), reads=["val"], writes=["ID"])
        p.dve(lambda e: e.memset(MU[0:64, 64:128], 0.0), reads=["MU"], writes=["MU"])
        p.dve(lambda e: e.memset(ML[64:128, 0:64], 0.0), reads=["ML"], writes=["ML"])
        p.dve(lambda e: e.memset(ones_bf[:], 1.0), writes=["ones_bf"])
        p.dve(lambda e: e.memset(ones_f[:], 1.0), writes=["ones_f"])
        p.dve(lambda e: e.memset(epsc[:], EPS), writes=["epsc"])

        pb = [c.ps([128, 4, 128], F32, f"pb{i}") for i in range(8)]

        gt_ = c.sb([128, NB, 2], F32, "gt_")
        g = c.sb([128, NB], F32, "g")
        beta = c.sb([128, NB], F32, "beta")
        nbeta = c.sb([128, NB], F32, "nbeta")
        tA = c.sb([128, NB], F32, "tA")
        eA = c.sb([128, 1], F32, "eA")
        gh = [c.sb([128, NB], F32, f"gh{a}") for a in range(2)]
        egc = c.sb([128, NB], F32, "egc")
        erem = c.sb([128, NB], F32, "erem")
        egl = [c.sb([128, NB], F32, f"egl{a}") for a in range(2)]
        sckb = c.sb([128, NB], F32, "sckb")
        p.dma("sp", "gt_", lambda e: e.dma_start(out=gt_[:], in_=gab[:, :, :]), writes=["gt_"])
        p.act(lambda e: e.activation(tA[:], gt_[:, :, 0], AF.Exp, bias=hpt[:, 1:2]), reads=["gt_", "hpt"], writes=["tA"])
        p.act(lambda e: e.activation(tA[:], tA[:], AF.Ln, bias=1.0), reads=["tA"], writes=["tA"])
        p.act(lambda e: e.activation(eA[:], hpt[:, 0:1], AF.Exp), reads=["hpt"], writes=["eA"])
        p.dve(lambda e: e.tensor_scalar(g[:], tA[:], eA[:, 0:1], -1.0, op0=ALU.mult, op1=ALU.mult),
              reads=["tA", "eA"], writes=["g"])
        p.act(lambda e: e.activation(beta[:], gt_[:, :, 1], AF.Exp, scale=-1.0), reads=["gt_"], writes=["beta"])
        p.dve(lambda e: e.tensor_scalar(beta[:], beta[:], 1.0, None, op0=ALU.add), reads=["beta"], writes=["beta"])
        p.dve(lambda e: e.reciprocal(beta[:], beta[:]), reads=["beta"], writes=["beta"])
        p.dve(lambda e: e.tensor_scalar(nbeta[:], beta[:], -1.0, None, op0=ALU.mult), reads=["beta"], writes=["nbeta"])
        for a in range(2):
            p.dve(lambda e, a=a: e.memset(gh[a][:], 0.0), writes=[f"gh{a}"])
            p.dve(lambda e, a=a: e.tensor_copy(gh[a][64 * a:64 * a + 64, :], g[64 * a:64 * a + 64, :]),
                  reads=["g", f"gh{a}"], writes=[f"gh{a}"])
        NBC = min(NB, 64)
        p.pe(lambda e: e.matmul(pb[0][:, 0, 0:NB], MU[:], g[:], start=True, stop=True), reads=["MU", "g"], writes=["pb0"])
        p.act(lambda e: e.activation(egc[:], pb[0][:, 0, 0:NB], AF.Exp), reads=["pb0"], writes=["egc"])
        p.pe(lambda e: e.matmul(pb[1][:, 0, 0:NB], ML[:], g[:], start=True, stop=True), reads=["ML", "g"], writes=["pb1"])
        p.act(lambda e: e.activation(erem[:], pb[1][:, 0, 0:NB], AF.Exp), reads=["pb1"], writes=["erem"])
        for a in range(2):
            p.pe(lambda e, a=a: e.matmul(pb[2 + a][:, 0, 0:NB], ones_f[:], gh[a][:], start=True, stop=True),
                 reads=["ones_f", f"gh{a}"], writes=[f"pb{2 + a}"])
            p.act(lambda e, a=a: e.activation(egl[a][:], pb[2 + a][:, 0, 0:NB], AF.Exp), reads=[f"pb{2 + a}"],
                  writes=[f"egl{a}"])
        p.dve(lambda e: e.tensor_tensor(sckb[:], beta[:], egc[:], op=ALU.mult), reads=["beta", "egc"], writes=["sckb"])

        xin = [c.sb([128, GT + 3], F32, f"xin{i}") for i in range(3)]
        acc = [c.sb([128, GT], F32, f"acc{i}") for i in range(3)]
        sil = [c.sb([128, GT], F32, f"sil{i}") for i in range(3)]
        sqb = [c.sb([128, GT], BF16, f"sqb{i}") for i in range(2)]
        rn = [c.sb([128, GT], F32, f"rn{i}") for i in range(2)]
        qn = c.sb([128, 4, 128], F32, "qn")
        kn = c.sb([128, 4, 128], F32, "kn")
        kbg = c.sb([128, 4, 128], BF16, "kbg")
        kdec = c.sb([128, 4, 128], BF16, "kdec")
        vb = c.sb([128, 4, 128], BF16, "vb")
        gU2 = c.sb([128, 4, 128], F32, "gU2")
        gU1 = c.sb([128, 4, 128], F32, "gU1")
        DecL = c.sb([128, 4, 128], F32, "DecL")
        DecT = c.sb([128, 4, 128], F32, "DecT")
        Xs = [c.sb([128, 4, 128], F32, f"Xs{i}") for i in range(2)]
        Ys = [c.sb([128, 4, 128], F32, f"Ys{i}") for i in range(2)]
        Q = c.sb([128, 4, 128], F32, "Q")
        TTb = c.sb([128, 4, 128], BF16, "TTb")
        qkT = c.sb([128, 4, 128], BF16, "qkT")
        u_sb = c.sb([128, 4, 128], F32, "u_sb")
        wT_sb = c.sb([128, 4, 128], F32, "wT_sb")
        o_sb = c.sb([128, 4, 128], F32, "o_sb")
        otmp = c.sb([128, 128], F32, "otmp")
        vnb = c.sb([128, 128], BF16, "vnb")
        S = c.sb([128, 128], F32, "S")
        zt = c.sb([128, 4, 128], F32, "zt")
        zs = c.sb([128, 4, 128], F32, "zs")
        junk = c.sb([128, 128], F32, "junk")
        ss = c.sb([128, 4], F32, "ss")
        yt = c.sb([128, 4, 128], F32, "yt")
        ytb = c.sb([128, 4, 128], BF16, "ytb")
        p.dve(lambda e: e.memset(S[:], 0.0), writes=["S"])

        def bc4(ap2):
            return ap2.unsqueeze(2).to_broadcast([128, 4, 128])

        def bcm(ap2):
            return ap2.unsqueeze(1).to_broadcast([128, 4, 128])

        for ti in range(ntiles):
            t0 = ti * GT
            b0 = ti * 4
            bs = slice(b0, b0 + 4)
            for i in range(3):
                if ti == 0:
                    p.dve(lambda e, i=i: e.memset(xin[i][:, 0:3], 0.0), writes=[f"xin{i}"])
                    p.dma("sp", f"xin{i}", lambda e, i=i: e.dma_start(out=xin[i][:, 3:GT + 3], in_=qkv[i, :, 0:GT]),
                          writes=[f"xin{i}"])
                else:
                    p.dma("sp", f"xin{i}", lambda e, i=i, t0=t0: e.dma_start(out=xin[i][:], in_=qkv[i, :, t0 - 3:t0 + GT]),
                          writes=[f"xin{i}"])
            p.dma("sp", "zt", lambda e, bs=bs: e.dma_start(out=zt[:], in_=zin[:, bs, :]), writes=["zt"])
            if dbg < 2:
                p.dma("sp", "yt", lambda e, bs=bs: e.dma_start(out=yo[:, bs, :], in_=ytb[:]), reads=["ytb"], writes=[("yo", ti)])
                continue
            for i in range(3):
                p.dve(lambda e, i=i: e.tensor_scalar(acc[i][:], xin[i][:, 3:GT + 3], cwt[:, 4 * i + 3:4 * i + 4], None,
                                                     op0=ALU.mult), reads=[f"xin{i}", "cwt"], writes=[f"acc{i}"])
                for j in range(3):
                    p.dve(lambda e, i=i, j=j: e.scalar_tensor_tensor(
                        acc[i][:], xin[i][:, j:GT + j], cwt[:, 4 * i + j:4 * i + j + 1], acc[i][:], op0=ALU.mult,
                        op1=ALU.add), reads=[f"xin{i}", "cwt", f"acc{i}"], writes=[f"acc{i}"])
                p.act(lambda e, i=i: e.activation(sil[i][:], acc[i][:], AF.Silu), reads=[f"acc{i}"], writes=[f"sil{i}"])
            for i in range(2):
                p.act(lambda e, i=i: e.activation(sqb[i][:], sil[i][:], AF.Square), reads=[f"sil{i}"], writes=[f"sqb{i}"])
                p.pe(lambda e, i=i: e.matmul(pb[i][:].rearrange("p a b -> p (a b)"), ones_bf[:], sqb[i][:], start=True,
                                             stop=True), reads=["ones_bf", f"sqb{i}"], writes=[f"pb{i}"])
                p.act(lambda e, i=i: e.activation(rn[i][:], pb[i][:].rearrange("p a b -> p (a b)"), AF.Sqrt,
                                                  bias=epsc[:, 0:1]), reads=[f"pb{i}", "epsc"], writes=[f"rn{i}"])
                p.dve(lambda e, i=i: e.reciprocal(rn[i][:], rn[i][:]), reads=[f"rn{i}"], writes=[f"rn{i}"])
            p.dve(lambda e: e.scalar_tensor_tensor(qn[:].rearrange("p a b -> p (a b)"), sil[0][:], float(128 ** -0.5),
                                                   rn[0][:], op0=ALU.mult, op1=ALU.mult),
                  reads=["sil0", "rn0"], writes=["qn"])
            p.dve(lambda e: e.tensor_tensor(kn[:].rearrange("p a b -> p (a b)"), sil[1][:], rn[1][:], op=ALU.mult),
                  reads=["sil1", "rn1"], writes=["kn"])
            if dbg < 3:
                p.dma("sp", "yt", lambda e, bs=bs: e.dma_start(out=yo[:, bs, :], in_=ytb[:]), reads=["ytb"], writes=[("yo", ti)])
                continue
            for pr in range(4):
                p.pe(lambda e, pr=pr: e.transpose(pb[2][:, pr, :], kn[:, pr, :], ID[:]), reads=["kn", "ID"], writes=["pb2"])
            for pr in range(4):
                p.pe(lambda e, pr=pr: e.transpose(pb[3][:, pr, :], sil[2][:, pr * 128:(pr + 1) * 128], ID[:]),
                     reads=["sil2", "ID"], writes=["pb3"])
            p.dve(lambda e, bs=bs: e.tensor_tensor(kbg[:], pb[2][:], bc4(sckb[:, bs]), op=ALU.mult),
                  reads=["pb2", "sckb"], writes=["kbg"])
            p.dve(lambda e, bs=bs: e.tensor_tensor(kdec[:], pb[2][:], bc4(erem[:, bs]), op=ALU.mult),
                  reads=["pb2", "erem"], writes=["kdec"])
            p.dve(lambda e, bs=bs: e.tensor_tensor(vb[:], pb[3][:], bc4(beta[:, bs]), op=ALU.mult),
                  reads=["pb3", "beta"], writes=["vb"])
            if dbg < 4:
                p.dma("sp", "yt", lambda e, bs=bs: e.dma_start(out=yo[:, bs, :], in_=ytb[:]), reads=["ytb"], writes=[("yo", ti)])
                continue
            p.dve(lambda e, bs=bs: e.tensor_tensor(gU2[:], bcm(ML[:]), bc4(g[:, bs]), op=ALU.mult),
                  reads=["ML", "g"], writes=["gU2"])
            p.dve(lambda e, bs=bs: e.tensor_tensor(gU1[:], bcm(MU[:]), bc4(g[:, bs]), op=ALU.mult),
                  reads=["MU", "g"], writes=["gU1"])
            p.pe(lambda e: e.matmul(pb[4][:].rearrange("p a b -> p (a b)"), MU[:], gU2[:].rearrange("p a b -> p (a b)"),
                                    start=True, stop=True), reads=["MU", "gU2"], writes=["pb4"])
            p.pe(lambda e: e.matmul(pb[5][:].rearrange("p a b -> p (a b)"), ML[:], gU1[:].rearrange("p a b -> p (a b)"),
                                    start=True, stop=True), reads=["ML", "gU1"], writes=["pb5"])
            p.act(lambda e: e.activation(DecL[:], pb[4][:], AF.Exp), reads=["pb4"], writes=["DecL"])
            p.act(lambda e: e.activation(DecT[:], pb[5][:], AF.Exp), reads=["pb5"], writes=["DecT"])
            p.dve(lambda e: e.tensor_tensor(DecL[:], DecL[:], bcm(ML[:]), op=ALU.mult), reads=["DecL", "ML"], writes=["DecL"])
            p.dve(lambda e: e.tensor_tensor(DecT[:], DecT[:], bcm(MU[:]), op=ALU.mult), reads=["DecT", "MU"], writes=["DecT"])
            if dbg < 5:
                p.dma("sp", "yt", lambda e, bs=bs: e.dma_start(out=yo[:, bs, :], in_=ytb[:]), reads=["ytb"], writes=[("yo", ti)])
                continue
            for pr in range(4):
                p.pe(lambda e, pr=pr: e.matmul(pb[0][:, pr, :], kn[:, pr, :], kn[:, pr, :], start=True, stop=True),
                     reads=["kn"], writes=["pb0"])
            for pr in range(4):
                p.pe(lambda e, pr=pr: e.matmul(pb[1][:, pr, :], kn[:, pr, :], qn[:, pr, :], start=True, stop=True),
                     reads=["kn", "qn"], writes=["pb1"])
            X, Y = Xs[0], Ys[0]
            p.dve(lambda e: e.tensor_tensor(X[:], pb[0][:], DecL[:], op=ALU.mult), reads=["pb0", "DecL"], writes=["Xs0"])
            p.dve(lambda e, bs=bs: e.tensor_tensor(X[:], X[:], bc4(nbeta[:, bs]), op=ALU.mult),
                  reads=["Xs0", "nbeta"], writes=["Xs0"])
            p.dve(lambda e: e.tensor_tensor(qkT[:], pb[1][:], DecT[:], op=ALU.mult), reads=["pb1", "DecT"], writes=["qkT"])
            for pr in range(4):
                p.pe(lambda e, pr=pr: e.transpose(pb[2][:, pr, :], Xs[0][:, pr, :], ID[:]), reads=["Xs0", "ID"],
                     writes=["pb2"])
            p.act(lambda e: e.copy(Ys[0][:], pb[2][:]), reads=["pb2"], writes=["Ys0"])
            p.dve(lambda e: e.tensor_tensor(Q[:], pb[2][:], bcm(ID[:]), op=ALU.add), reads=["pb2", "ID"], writes=["Q"])
            if dbg < 6:
                p.dma("sp", "yt", lambda e, bs=bs: e.dma_start(out=yo[:, bs, :], in_=ytb[:]), reads=["ytb"], writes=[("yo", ti)])
                continue
            for lvl in range(1, 6):
                cur, nxt = (lvl - 1) % 2, lvl % 2
                for pr in range(4):
                    p.pe(lambda e, pr=pr, cur=cur: e.matmul(pb[3][:, pr, :], Ys[cur][:, pr, :], Xs[cur][:, pr, :],
                                                          start=True, stop=True),
                         reads=[f"Ys{cur}", f"Xs{cur}"], writes=["pb3"])
                if lvl < 5:
                    for pr in range(4):
                        p.pe(lambda e, pr=pr, cur=cur: e.matmul(pb[4][:, pr, :], Xs[cur][:, pr, :], Ys[cur][:, pr, :],
                                                              start=True, stop=True),
                             reads=[f"Ys{cur}", f"Xs{cur}"], writes=["pb4"])
                p.act(lambda e, nxt=nxt: e.copy(Xs[nxt][:], pb[3][:]), reads=["pb3"], writes=[f"Xs{nxt}"])
                if lvl < 5:
                    p.dve(lambda e, nxt=nxt: e.tensor_copy(Ys[nxt][:], pb[4][:]), reads=["pb4"], writes=[f"Ys{nxt}"])
                for pr in range(4):
                    p.pe(lambda e, pr=pr, nxt=nxt: e.matmul(pb[5][:, pr, :], Xs[nxt][:, pr, :], Q[:, pr, :], start=True,
                                                          stop=True), reads=[f"Xs{nxt}", "Q"], writes=["pb5"])
                p.dve(lambda e: e.tensor_tensor(Q[:], Q[:], pb[5][:], op=ALU.add), reads=["Q", "pb5"], writes=["Q"])
            p.act(lambda e: e.copy(TTb[:], Q[:]), reads=["Q"], writes=["TTb"])
            if dbg < 7:
                p.dma("sp", "yt", lambda e, bs=bs: e.dma_start(out=yo[:, bs, :], in_=ytb[:]), reads=["ytb"], writes=[("yo", ti)])
                continue
            for pr in range(4):
                p.pe(lambda e, pr=pr: e.matmul(pb[6][:, pr, :], TTb[:, pr, :], vb[:, pr, :], start=True, stop=True),
                     reads=["TTb", "vb"], writes=["pb6"])
            for pr in range(4):
                p.pe(lambda e, pr=pr: e.matmul(pb[7][:, pr, :], kbg[:, pr, :], TTb[:, pr, :], start=True, stop=True),
                     reads=["TTb", "kbg"], writes=["pb7"])
            p.act(lambda e: e.copy(u_sb[:], pb[6][:]), reads=["pb6"], writes=["u_sb"])
            p.dve(lambda e: e.tensor_copy(wT_sb[:], pb[7][:]), reads=["pb7"], writes=["wT_sb"])
            if dbg < 8:
                p.dma("sp", "yt", lambda e, bs=bs: e.dma_start(out=yo[:, bs, :], in_=ytb[:]), reads=["ytb"], writes=[("yo", ti)])
                continue
            for pr in range(4):
                blk = b0 + pr
                for a in range(2):
                    rs = slice(64 * a, 64 * a + 64)
                    cs = slice(64 * a, 64 * a + 64)
                    p.pe(lambda e, pr=pr, rs=rs, cs=cs: e.matmul(pb[0][rs, 0, :], wT_sb[:, pr, cs], S[:], start=True,
                                                               stop=True), reads=["wT_sb", "S"], writes=["pb0"])
                    p.pe(lambda e, pr=pr, rs=rs, cs=cs: e.matmul(pb[1][rs, 0, :], qn[:, pr, cs], S[:], start=True,
                                                               stop=True), reads=["qn", "S"], writes=["pb1"])
                    p.dve(lambda e, pr=pr, rs=rs: e.tensor_tensor(vnb[rs, :], u_sb[rs, pr, :], pb[0][rs, 0, :],
                                                                 op=ALU.subtract), reads=["u_sb", "pb0"], writes=["vnb"])
                    p.act(lambda e, rs=rs, blk=blk: e.activation(otmp[rs, :], pb[1][rs, 0, :], AF.Copy,
                                                                 scale=egc[rs, blk:blk + 1]),
                          reads=["pb1", "egc"], writes=["otmp"])
                    p.pe(lambda e, pr=pr, rs=rs, cs=cs: e.matmul(pb[2][rs, 0, :], qkT[rs, pr, cs], vnb[rs, :], start=True,
                                                               stop=True), reads=["qkT", "vnb"], writes=["pb2"])
                    p.pe(lambda e, pr=pr, rs=rs: e.matmul(pb[3][:, 0, :], kdec[rs, pr, :], vnb[rs, :], start=True,
                                                        stop=True), reads=["kdec", "vnb"], writes=["pb3"])
                    p.dve(lambda e, pr=pr, rs=rs: e.tensor_tensor(o_sb[rs, pr, :], otmp[rs, :], pb[2][rs, 0, :],
                                                                 op=ALU.add), reads=["otmp", "pb2"], writes=["o_sb"])
                    p.dve(lambda e, a=a, blk=blk: e.scalar_tensor_tensor(S[:], S[:], egl[a][:, blk:blk + 1],
                                                                        pb[3][:, 0, :], op0=ALU.mult, op1=ALU.add),
                          reads=["S", f"egl{a}", "pb3"], writes=["S"])
            if dbg < 9:
                p.dma("sp", "yt", lambda e, bs=bs: e.dma_start(out=yo[:, bs, :], in_=ytb[:]), reads=["ytb"], writes=[("yo", ti)])
                continue
            for pr in range(4):
                p.act(lambda e, pr=pr: e.activation(junk[:], o_sb[:, pr, :], AF.Square, accum_out=ss[:, pr:pr + 1]),
                      reads=["o_sb"], writes=["junk", ("ss", pr)])
            p.act(lambda e: e.activation(ss[:], ss[:], AF.Sqrt, bias=epsc[:, 0:1], scale=1.0 / 128.0),
                  reads=[("ss", i) for i in range(4)] + ["epsc"], writes=[("ss", i) for i in range(4)])
            p.dve(lambda e: e.reciprocal(ss[:], ss[:]), reads=[("ss", i) for i in range(4)],
                  writes=[("ss", i) for i in range(4)])
            p.act(lambda e: e.activation(zs[:], zt[:], AF.Silu), reads=["zt"], writes=["zs"])
            p.dve(lambda e: e.tensor_tensor(yt[:], o_sb[:], bc4(ss[:, :]), op=ALU.mult),
                  reads=["o_sb"] + [("ss", i) for i in range(4)], writes=["yt"])
            p.dve(lambda e: e.tensor_tensor(yt[:], yt[:], bcm(nw[:]), op=ALU.mult), reads=["yt", "nw"], writes=["yt"])
            p.dve(lambda e: e.tensor_tensor(ytb[:], yt[:], zs[:], op=ALU.mult), reads=["yt", "zs"], writes=["ytb"])
            p.dma("sp", "yt", lambda e, bs=bs: e.dma_start(out=yo[:, bs, :], in_=ytb[:]), reads=["ytb"], writes=[("yo", ti)])
        p.emit(final_sems=["yt"])
    return nc


SCALE = 0.125
NEG = -30000.0


def build_B(qblocks=tuple(range(16)), dbg=99):
    nc = bass.Bass("TRN2", target_bir_lowering=False)
    with ExitStack() as es:
        c = Ctx(nc, es)
        p = c.p
        aq_d = c.din("aq", [2, 128, T], BF16)
        ak_d = c.din("ak", [2, 128, T + 128], BF16)
        av_d = c.din("av", [128, 17, 128], BF16)
        sk_d = c.din("sinks", [128, 4])
        offs_d = c.din("offs", [128, 4])
        bqr_d = c.din("bqr", [2, 128, T], BF16)
        bq_d = c.din("bq", [2, 128, T], BF16)
        bkw_d = c.din("bkw", [128, T + 512], BF16)
        bvw_d = c.din("bvw", [128, 20, 64], BF16)
        bks_d = c.din("bks", [128, SEQ], BF16)
        bvs_d = c.din("bvs", [128, 64, 64], BF16)
        cmp_d = c.din("cmpT", [128, SEQ], BF16)
        bg_d = c.din("bg", [12, T])
        w1k_d = c.din("w1k", [2048, 256])
        w1v_d = c.din("w1v", [2048, 256])
        w2k_d = c.din("w2k", [256, 64])
        w2v_d = c.din("w2v", [256, 64])
        pek_d = c.din("pek", [128, 16])
        pev_d = c.din("pev", [128, 16])
        oc_d = c.dout("oc", [4, 128, T], BF16)

        pb = [c.ps([128, 4, 128], F32, f"pb{i}") for i in range(8)]
        PB2 = ["pb2"]

        aq = c.sb([128, 2, T], BF16, "aq")
        ak = c.sb([128, 2, T + 128], BF16, "ak")
        avg = c.sb([128, 17, 2, 128], BF16, "avg")
        bqr = c.sb([128, 2, T], BF16, "bqr")
        bq = c.sb([128, 2, T], BF16, "bq")
        bkw = c.sb([128, T + 512], BF16, "bkw")
        bvwg = c.sb([128, 20, 128], BF16, "bvwg")
        bks = c.sb([128, SEQ], BF16, "bks")
        bvsg = c.sb([128, 64, 128], BF16, "bvsg")
        KKk = c.sb([128, SEQ], BF16, "KKk")
        KKv = c.sb([128, SEQ], BF16, "KKv")
        sk = c.sb([128, 4], F32, "sk")
        offs = c.sb([128, 4], F32, "offs")
        bg = c.sb([12, T], F32, "bg")
        w1k = c.sb([128, 16, 256], BF16, "w1k")
        w1v = c.sb([128, 16, 256], BF16, "w1v")
        w2k = c.sb([128, 2, 128], BF16, "w2k")
        w2v = c.sb([128, 2, 64], BF16, "w2v")
        pek = c.sb([128, 16], BF16, "pek")
        pev = c.sb([128, 16], BF16, "pev")
        for j in range(2):
            p.dma("sp", "aq", lambda e, j=j: e.dma_start(out=aq[:, j, :], in_=aq_d[j, :, :]), writes=["aq"])
            p.dma("sp", "ak", lambda e, j=j: e.dma_start(out=ak[:, j, :], in_=ak_d[j, :, :]), writes=["ak"])
            p.dma("sp", "bqr", lambda e, j=j: e.dma_start(out=bqr[:, j, :], in_=bqr_d[j, :, :]), writes=["bqr"])
            p.dma("sp", "bq", lambda e, j=j: e.dma_start(out=bq[:, j, :], in_=bq_d[j, :, :]), writes=["bq"])
        p.dve(lambda e: e.memset(avg[:], 1.0), writes=["avg"])
        p.dve(lambda e: e.memset(bvwg[:], 1.0), writes=["bvwg"])
        p.dve(lambda e: e.memset(bvsg[:], 1.0), writes=["bvsg"])
        for j in range(2):
            p.dma("sp", "avg", lambda e, j=j: e.dma_start(out=avg[:, :, j, 0:64], in_=av_d[:, :, 64 * j:64 * j + 64]),
                  reads=["avg"], writes=["avg"])
        p.dma("sp", "bvwg", lambda e: e.dma_start(out=bvwg[:, :, 0:64], in_=bvw_d[:, :, :]), reads=["bvwg"], writes=["bvwg"])
        p.dma("sp", "bvsg", lambda e: e.dma_start(out=bvsg[:, :, 0:64], in_=bvs_d[:, :, :]), reads=["bvsg"], writes=["bvsg"])
        p.dma("sp", "bkw", lambda e: e.dma_start(out=bkw[:], in_=bkw_d[:, :]), writes=["bkw"])
        p.dma("sp", "bks", lambda e: e.dma_start(out=bks[:], in_=bks_d[:, :]), writes=["bks"])
        p.dve(lambda e: e.memset(KKk[:, SEQ - 8:SEQ], 0.0), writes=["KKk"])
        p.dve(lambda e: e.memset(KKv[:, SEQ - 8:SEQ], 0.0), writes=["KKv"])
        p.dma("sp", "KKk", lambda e: e.dma_start(out=KKk[0:64, :], in_=cmp_d[0:64, :]), reads=["KKk"], writes=["KKk"])
        p.dma("sp", "KKk", lambda e: e.dma_start(out=KKk[64:128, 0:SEQ - 1], in_=cmp_d[0:64, 1:SEQ]), reads=["KKk"],
              writes=["KKk"])
        p.dma("sp", "KKv", lambda e: e.dma_start(out=KKv[0:64, :], in_=cmp_d[64:128, :]), reads=["KKv"], writes=["KKv"])
        p.dma("sp", "KKv", lambda e: e.dma_start(out=KKv[64:128, 0:SEQ - 1], in_=cmp_d[64:128, 1:SEQ]), reads=["KKv"],
              writes=["KKv"])
        p.dma("sp", "sk", lambda e: e.dma_start(out=sk[:], in_=sk_d[:, :]), writes=["sk"])
        p.dma("sp", "offs", lambda e: e.dma_start(out=offs[:], in_=offs_d[:, :]), writes=["offs"])
        p.dma("sp", "bg", lambda e: e.dma_start(out=bg[:], in_=bg_d[:, :]), writes=["bg"])
        p.dma("pool", "w1k", lambda e: e.dma_start(out=w1k[:], in_=w1k_d.rearrange("(m p) c -> p m c", p=128)), writes=["w1k"])
        p.dma("pool", "w1v", lambda e: e.dma_start(out=w1v[:], in_=w1v_d.rearrange("(m p) c -> p m c", p=128)), writes=["w1v"])
        for dup in range(2):
            p.dma("pool", "w2k", lambda e, dup=dup: e.dma_start(out=w2k[:, :, 64 * dup:64 * dup + 64],
                                                              in_=w2k_d.rearrange("(m p) c -> p m c", p=128)), writes=["w2k"])
        p.dma("pool", "w2v", lambda e: e.dma_start(out=w2v[:], in_=w2v_d.rearrange("(m p) c -> p m c", p=128)), writes=["w2v"])
        p.dma("pool", "pek", lambda e: e.dma_start(out=pek[:], in_=pek_d[:, :]), writes=["pek"])
        p.dma("pool", "pev", lambda e: e.dma_start(out=pev[:], in_=pev_d[:, :]), writes=["pev"])

        val = c.sb([128, 128], F32, "val")
        MUq = c.sb([128, 128], BF16, "MUq")
        MLq = c.sb([128, 128], BF16, "MLq")
        M2 = c.sb([128, 2, 128], BF16, "M2")
        IDb = c.sb([128, 128], BF16, "IDb")
        ones_b = c.sb([128, 128], BF16, "ones_b")
        p.pool(lambda e: e.iota(val[:], [[1, 128]], base=0, channel_multiplier=-1, allow_small_or_imprecise_dtypes=True),
               writes=["val"])
        p.dve(lambda e: e.tensor_scalar(MUq[:], val[:], 0.0, None, op0=ALU.is_ge), reads=["val"], writes=["MUq"])
        p.dve(lambda e: e.tensor_scalar(MLq[:], val[:], 0.0, None, op0=ALU.is_lt), reads=["val"], writes=["MLq"])
        p.dve(lambda e: e.tensor_scalar(IDb[:], val[:], 0.0, None, op0=ALU.is_equal), reads=["val"], writes=["IDb"])
        p.dve(lambda e: e.tensor_copy(M2[:, 0, :], MLq[:]), reads=["MLq"], writes=["M2"])
        p.dve(lambda e: e.tensor_copy(M2[:, 1, :], MUq[:]), reads=["MUq", "M2"], writes=["M2"])
        p.dve(lambda e: e.memset(ones_b[:], 1.0), writes=["ones_b"])
        Et = c.sb([128, 16, 2, 64], F32, "Et")
        E = c.sb([128, 64, 128], BF16, "E")
        for qtr in range(4):
            p.pool(lambda e, qtr=qtr: e.iota(Et[:], [[-2, 16], [-1, 2], [0, 64]], base=-32 * qtr, channel_multiplier=1,
                                             allow_small_or_imprecise_dtypes=True), reads=["Et"], writes=["Et"])
            p.dve(lambda e, qtr=qtr: e.tensor_scalar(E[:, 16 * qtr:16 * qtr + 16, :].rearrange("p a (b c) -> p a b c", b=2),
                                                     Et[:], 0.0, None, op0=ALU.is_equal), reads=["Et"], writes=["E", "Et"])
        Gt = c.sb([128, 4, 128], F32, "Gt")
        Gm = c.sb([128, 4, 128], F32, "Gm")
        Gm2 = Gt
        p.pool(lambda e: e.iota(Gt[:], [[128, 4], [-4, 128]], base=0, channel_multiplier=1,
                                allow_small_or_imprecise_dtypes=True), writes=["Gt"])
        p.dve(lambda e: e.tensor_scalar(Gm[:], Gt[:], -1.0, None, op0=ALU.is_ge), reads=["Gt"], writes=["Gm"])
        p.dve(lambda e: e.tensor_scalar(Gm2[:], Gt[:], 3.0, None, op0=ALU.is_le), reads=["Gt", "Gm"], writes=["Gm2", "Gt"])
        p.dve(lambda e: e.tensor_tensor(Gm[:], Gm[:], Gm2[:], op=ALU.mult), reads=["Gm", "Gm2"], writes=["Gm"])
        valc = c.sb([128, 2, 128], F32, "valc")
        p.pool(lambda e: e.iota(valc[:], [[2048, 2], [-1, 128]], base=4096, channel_multiplier=16,
                                allow_small_or_imprecise_dtypes=True), writes=["valc"])
        nidx = c.sb([128, 4], F32, "nidx")
        biasn = c.sb([128, 4], F32, "biasn")
        bt2 = c.sb([128, 4], F32, "bt2")
        p.pool(lambda e: e.iota(nidx[:], [[128, 4]], base=0, channel_multiplier=1, allow_small_or_imprecise_dtypes=True),
               writes=["nidx"])
        p.dve(lambda e: e.tensor_scalar(biasn[:], nidx[:], offs[:, 1:2], NEG, op0=ALU.is_lt, op1=ALU.mult),
              reads=["nidx", "offs"], writes=["biasn"])
        p.dve(lambda e: e.tensor_scalar(bt2[:], nidx[:], 510.5, NEG, op0=ALU.is_gt, op1=ALU.mult),
              reads=["nidx"], writes=["bt2"])
        p.dve(lambda e: e.tensor_tensor(biasn[:], biasn[:], bt2[:], op=ALU.add), reads=["biasn", "bt2"], writes=["biasn"])
        bbi = c.sb([128, 128], F32, "bbi")
        validblk = c.sb([128, 128], F32, "validblk")
        f0m = c.sb([128, 128], F32, "f0m")
        p.pool(lambda e: e.iota(bbi[:], [[1, 128]], base=0, channel_multiplier=0, allow_small_or_imprecise_dtypes=True),
               writes=["bbi"])
        p.dve(lambda e: e.tensor_scalar(validblk[:], bbi[:], offs[:, 2:3], None, op0=ALU.is_ge), reads=["bbi", "offs"],
              writes=["validblk"])
        p.dve(lambda e: e.tensor_scalar(f0m[:], bbi[:], offs[:, 2:3], 2e9, op0=ALU.is_equal, op1=ALU.mult),
              reads=["bbi", "offs"], writes=["f0m"])
        p.dve(lambda e: e.tensor_scalar(f0m[:], f0m[:], -1e9, None, op0=ALU.add), reads=["f0m"], writes=["f0m"])
        esk = c.sb([128, 4], F32, "esk")
        p.act(lambda e: e.activation(esk[:], sk[:], AF.Exp), reads=["sk"], writes=["esk"])
        sg = bg
        p.act(lambda e: e.activation(sg[:], bg[:], AF.Exp, scale=-1.0), reads=["bg"], writes=["sg", "bg"])
        p.dve(lambda e: e.tensor_scalar(sg[:], sg[:], 1.0, None, op0=ALU.add), reads=["sg"], writes=["sg"])
        p.dve(lambda e: e.reciprocal(sg[:], sg[:]), reads=["sg"], writes=["sg"])
        selg = c.sb([12, 12, 64], F32, "selg")
        p.pool(lambda e: e.iota(selg[:], [[-1, 12], [0, 64]], base=0, channel_multiplier=1,
                                allow_small_or_imprecise_dtypes=True), writes=["selg"])
        p.dve(lambda e: e.tensor_scalar(selg[:], selg[:], 0.0, None, op0=ALU.is_equal), reads=["selg"], writes=["selg"])

        hid = [c.sb([128, 2, 512], BF16, f"hid{i}") for i in range(2)]
        bh = c.sb([128, 4], F32, "bh")
        kcT = c.sb([128, 512], BF16, "kcT")
        vcg = c.sb([128, 4, 128], BF16, "vcg")
        p.dve(lambda e: e.memset(vcg[:], 1.0), writes=["vcg"])
        for wi, (w1, KK, pe_) in enumerate([(w1k, KKk, pek), (w1v, KKv, pev)]):
            w1n, kkn, pen = ["w1k", "w1v"][wi], ["KKk", "KKv"][wi], ["pek", "pev"][wi]
            p.dve(lambda e, wi=wi: e.memset(hid[wi][:], 0.0), writes=[f"hid{wi}"])
            for hh in range(2):
                bk = 2 * wi + hh
                for m in range(16):
                    p.pe(lambda e, bk=bk, m=m, hh=hh, w1=w1, pe_=pe_: e.matmul(
                        pb[4][:, bk, 0:1], w1[:, m, hh * 128:(hh + 1) * 128], pe_[:, m:m + 1], start=(m == 0),
                        stop=(m == 15)), reads=[w1n, pen], writes=["pb4"])
            p.dve(lambda e, wi=wi: e.tensor_copy(bh[:, 2 * wi:2 * wi + 2], pb[4][:, 2 * wi:2 * wi + 2, 0]),
                  reads=["pb4"], writes=["bh"])
            for hh in range(2):
                bk = hh
                for m in range(16):
                    p.pe(lambda e, bk=bk, m=m, hh=hh, w1=w1, KK=KK: e.matmul(
                        pb[bk][:].rearrange("p a b -> p (a b)")[:, 0:511], w1[:, m, hh * 128:(hh + 1) * 128],
                        KK[:, 2 * m:2 * m + 16 * 510 + 1:16], start=(m == 0), stop=(m == 15)),
                        reads=[w1n, kkn], writes=[f"pb{bk}"])
                p.act(lambda e, bk=bk, hh=hh, wi=wi: e.activation(
                    hid[wi][:, hh, 0:511], pb[bk][:].rearrange("p a b -> p (a b)")[:, 0:511], AF.Silu,
                    bias=bh[:, 2 * wi + hh:2 * wi + hh + 1]), reads=[f"pb{bk}", "bh"], writes=[f"hid{wi}"])
        for hh in range(2):
            p.pe(lambda e, hh=hh: e.matmul(pb[2][:].rearrange("p a b -> p (a b)"), w2k[:, hh, :], hid[0][:, hh, :],
                                           start=(hh == 0), stop=(hh == 1)), reads=["w2k", "hid0"], writes=PB2)
        p.act(lambda e: e.copy(kcT[:], pb[2][:].rearrange("p a b -> p (a b)")), reads=PB2, writes=["kcT"])
        for cc in range(4):
            for hh in range(2):
                p.pe(lambda e, cc=cc, hh=hh: e.matmul(pb[3][:, cc, 0:64], hid[1][:, hh, cc * 128:(cc + 1) * 128],
                                                      w2v[:, hh, :], start=(hh == 0), stop=(hh == 1)),
                     reads=["w2v", "hid1"], writes=["pb3"])
        p.act(lambda e: e.copy(vcg[:, :, 0:64], pb[3][:, :, 0:64]), reads=["pb3", "vcg"], writes=["vcg"])

        PT = [c.sb([128, 4, 128], BF16, f"PT{i}") for i in range(5)]
        QBD = {}
        for nm in ("QA", "QR", "QU"):
            for par in range(2):
                for j in range(2):
                    t_ = c.sb([128, 256], BF16, f"{nm}{par}{j}")
                    QBD[(nm, par, j)] = t_
                    p.pool(lambda e, t_=t_: e.memset(t_[:], 0.0), writes=[f"{nm}{par}{j}"])
        PTc = c.sb([128, 4, 4, 128], BF16, "PTc")
        mk = c.sb([128, 2, 128], BF16, "mk")
        ocA = KKk[:, 0:2 * T].rearrange("p (j t) -> p j t", j=2)
        ocB = KKk[:, 2 * T:4 * T].rearrange("p (j t) -> p j t", j=2)
        Etf = Et[:].rearrange("p a b c -> p (a b c)")
        rd = Etf[:, 0:512].rearrange("p (a b) -> p a b", a=4)
        rdA = c.sb([64, 4, 128], F32, "rdA")
        accB = c.sb([64, 4, 128], F32, "accB")
        tmpB = c.sb([64, 4, 128], F32, "tmpB")
        fB = c.sb([64, 4, 128], F32, "fB")
        pn = Etf[:, 512:1024].rearrange("p (a b) -> p a b", a=4)
        psh = Etf[:, 1024:1536].rearrange("p (a b) -> p a b", a=4)
        score = c.sb([128, 128], F32, "score")
        work = c.sb([128, 128], F32, "work")
        m8a = c.sb([128, 8], F32, "m8a")
        m8b = c.sb([128, 8], F32, "m8b")
        sel = c.sb([128, 128], BF16, "sel")
        selT = c.sb([128, 128], BF16, "selT")
        pbT = c.ps([128, 128], BF16, "pbT") if False else None
        ptn = [0]

        def next_pt():
            k = ptn[0]
            ptn[0] = (k + 1) % 5
            return k
        sbank = [0]

        def next_sb():
            k = sbank[0]
            sbank[0] = 1 - k
            return k

        def gate_finish(br, i, acc_first):
            qs = slice(128 * i, 128 * i + 128)
            for h in range(4):
                p.pe(lambda e, h=h: e.matmul(pb[6][0:64, h, :], selg[:, br * 4 + h, :], sg[:, qs], start=True, stop=True),
                     reads=["selg", "sg"], writes=["pb6"])
            src = 7 if br == 1 else 3
            p.dve(lambda e: e.tensor_scalar(fB[:], pb[src][64:128, :, :], 1e-30, None, op0=ALU.add),
                  reads=[f"pb{src}"], writes=["fB"])
            p.dve(lambda e: e.reciprocal(fB[:], fB[:]), reads=["fB"], writes=["fB"])
            p.dve(lambda e: e.tensor_tensor(fB[:], fB[:], pb[6][0:64, :, :], op=ALU.mult), reads=["fB", "pb6"], writes=["fB"])
            if acc_first:
                p.dve(lambda e: e.tensor_tensor(accB[:], pb[src][0:64, :, :], fB[:], op=ALU.mult),
                      reads=[f"pb{src}", "fB"], writes=["accB"])
            else:
                p.dve(lambda e: e.tensor_tensor(tmpB[:], pb[src][0:64, :, :], fB[:], op=ALU.mult),
                      reads=[f"pb{src}", "fB"], writes=["tmpB"])
                p.dve(lambda e: e.tensor_tensor(accB[:], accB[:], tmpB[:], op=ALU.add), reads=["accB", "tmpB"],
                      writes=["accB"])

        def do_qblock(i):
            ibb = 48 + i
            qs = slice(128 * i, 128 * i + 128)
            par = i % 2
            for nm, src, srcn in (("QA", aq, "aq"), ("QR", bqr, "bqr"), ("QU", bq, "bq")):
                for j in range(2):
                    t_ = QBD[(nm, par, j)]
                    for g_ in range(2):
                        p.pool(lambda e, t_=t_, src=src, j=j, g_=g_: e.tensor_copy(
                            t_[64 * g_:64 * g_ + 64, 128 * g_:128 * g_ + 128], src[64 * g_:64 * g_ + 64, j, qs]),
                            reads=[srcn, f"{nm}{par}{j}"], writes=[f"{nm}{par}{j}"])
            if dbg < 2:
                return
            v = 9
            for j in range(2):
                sbk = next_sb()
                for r in range(2):
                    ks = slice(128 * (i + r), 128 * (i + r) + 128)
                    p.pe(lambda e, j=j, r=r, ks=ks, sbk=sbk: e.matmul(
                        pb[sbk][:, 2 * r:2 * r + 2, :].rearrange("p a b -> p (a b)"), ak[:, j, ks], QBD[("QA", par, j)][:],
                        start=True, stop=True), reads=["ak", f"QA{par}{j}"], writes=[f"pb{sbk}"])
                k_ = next_pt()
                p.act(lambda e, k_=k_, sbk=sbk: e.activation(PT[k_][:], pb[sbk][:], AF.Exp, scale=SCALE),
                      reads=[f"pb{sbk}"], writes=[f"PT{k_}"])
                if v < 2:
                    continue
                p.dve(lambda e, k_=k_: e.tensor_tensor(
                    PT[k_][:].rearrange("p (r g) q -> p r g q", r=2), PT[k_][:].rearrange("p (r g) q -> p r g q", r=2),
                    M2[:].unsqueeze(2).to_broadcast([128, 2, 2, 128]), op=ALU.mult),
                    reads=[f"PT{k_}", "M2"], writes=[f"PT{k_}"])
                if v < 3:
                    continue
                if i == 0:
                    p.dve(lambda e, k_=k_: e.tensor_scalar(PT[k_][:, 0:2, :], PT[k_][:, 0:2, :], offs[:, 3:4], None,
                                                          op0=ALU.mult), reads=[f"PT{k_}", "offs"], writes=[f"PT{k_}"])
                if v < 4:
                    continue
                for r in range(2):
                    p.pe(lambda e, j=j, r=r, k_=k_: e.matmul(
                        pb[2][:, 2 * j:2 * j + 2, :].rearrange("p a b -> p (a b)"), avg[:, i + r, j, :],
                        PT[k_][:, 2 * r:2 * r + 2, :].rearrange("p a b -> p (a b)"), start=(r == 0), stop=(r == 1)),
                        reads=["avg", f"PT{k_}"], writes=PB2)
            if v < 5:
                return
            p.dve(lambda e: e.tensor_tensor(rdA[:], pb[2][64:128, :, :], esk[64:128, :].unsqueeze(2).to_broadcast([64, 4, 128]),
                                            op=ALU.add), reads=PB2 + ["esk"], writes=["rdA"])
            p.dve(lambda e: e.reciprocal(rdA[:], rdA[:]), reads=["rdA"], writes=["rdA"])
            if v < 6:
                return
            for g_ in range(2):
                p.dve(lambda e, g_=g_: e.tensor_tensor(
                    ocA[64 * g_:64 * g_ + 64, :, qs], pb[2][0:64, :, :].rearrange("p (j g) q -> p j g q", g=2)[:, :, g_, :],
                    rdA[:].rearrange("p (j g) q -> p j g q", g=2)[:, :, g_, :], op=ALU.mult),
                    reads=PB2 + ["rdA"], writes=["ocA", "KKk"])
            if dbg < 3:
                return
            for r in range(5):
                sbk = next_sb()
                ks = slice(128 * (i + r), 128 * (i + r) + 128)
                for j in range(2):
                    p.pe(lambda e, j=j, ks=ks, sbk=sbk: e.matmul(
                        pb[sbk][:, 2 * j:2 * j + 2, :].rearrange("p a b -> p (a b)"), bkw[:, ks], QBD[("QR", par, j)][:],
                        start=True, stop=True), reads=["bkw", f"QR{par}{j}"], writes=[f"pb{sbk}"])
                k_ = next_pt()
                p.act(lambda e, k_=k_, sbk=sbk: e.activation(PT[k_][:], pb[sbk][:], AF.Exp, scale=SCALE),
                      reads=[f"pb{sbk}"], writes=[f"PT{k_}"])
                if r == 0 or r == 4:
                    msk = MLq if r == 0 else MUq
                    mn = "MLq" if r == 0 else "MUq"
                    p.dve(lambda e, k_=k_, msk=msk: e.tensor_tensor(PT[k_][:], PT[k_][:],
                                                                   msk[:].unsqueeze(1).to_broadcast([128, 4, 128]),
                                                                   op=ALU.mult), reads=[f"PT{k_}", mn], writes=[f"PT{k_}"])
                if i + r < 4:
                    p.dve(lambda e, k_=k_: e.tensor_scalar(PT[k_][:], PT[k_][:], offs[:, 3:4], None, op0=ALU.mult),
                          reads=[f"PT{k_}", "offs"], writes=[f"PT{k_}"])
                p.pe(lambda e, r=r, k_=k_: e.matmul(pb[3][:].rearrange("p a b -> p (a b)"), bvwg[:, i + r, :],
                                                    PT[k_][:].rearrange("p a b -> p (a b)"), start=(r == 0), stop=(r == 4)),
                     reads=["bvwg", f"PT{k_}"], writes=["pb3"])
            gate_finish(2, i, True)
            if dbg < 4:
                return
            p.dve(lambda e: e.tensor_scalar(mk[:], valc[:], float(128 * ibb - 31), None, op0=ALU.is_le),
                  reads=["valc"], writes=["mk"])
            for cc in range(4):
                sbk = next_sb()
                for j in range(2):
                    p.pe(lambda e, j=j, cc=cc, sbk=sbk: e.matmul(
                        pb[sbk][:, 2 * j:2 * j + 2, :].rearrange("p a b -> p (a b)"), kcT[:, cc * 128:(cc + 1) * 128],
                        QBD[("QU", par, j)][:], start=True, stop=True), reads=["kcT", f"QU{par}{j}"], writes=[f"pb{sbk}"])
                p.act(lambda e, cc=cc, sbk=sbk: e.activation(PTc[:, cc, :, :], pb[sbk][:], AF.Exp, scale=SCALE,
                                                             bias=biasn[:, cc:cc + 1]),
                      reads=[f"pb{sbk}", "biasn"], writes=[("PTc", cc)])
                if cc >= 2:
                    p.dve(lambda e, cc=cc: e.tensor_tensor(PTc[:, cc, :, :], PTc[:, cc, :, :],
                                                          mk[:, cc - 2, :].unsqueeze(1).to_broadcast([128, 4, 128]),
                                                          op=ALU.mult), reads=[("PTc", cc), "mk"], writes=[("PTc", cc)])
                p.pe(lambda e, cc=cc: e.matmul(pb[3][:].rearrange("p a b -> p (a b)"), vcg[:, cc, :],
                                               PTc[:, cc, :, :].rearrange("p a b -> p (a b)"), start=(cc == 0),
                                               stop=(cc == 3)), reads=["vcg", ("PTc", cc)], writes=["pb3"])
                p.pe(lambda e, cc=cc: e.matmul(pb[4][:].rearrange("p a b -> p (a b)"), ones_b[:],
                                               PTc[:, cc, :, :].rearrange("p a b -> p (a b)"), start=(cc == 0),
                                               stop=(cc == 3)), reads=["ones_b", ("PTc", cc)], writes=["pb4"])
            p.dve(lambda e: e.tensor_scalar(rd[:], pb[4][:], 1e-30, None, op0=ALU.add), reads=["pb4", "Et"], writes=["rd"])
            p.dve(lambda e: e.reciprocal(rd[:], rd[:]), reads=["rd"], writes=["rd"])
            for cc in range(4):
                p.dve(lambda e, cc=cc: e.tensor_tensor(pn[:], PTc[:, cc, :, :], rd[:], op=ALU.mult),
                      reads=[("PTc", cc), "rd"], writes=["pn"])
                p.dve(lambda e, cc=cc: e.tensor_reduce(psh[:, cc, :], pn[:].rearrange("p h q -> p q h"), axis=AX.X,
                                                       op=ALU.add), reads=["pn"], writes=[("psh", cc)])
                p.pe(lambda e, cc=cc: e.matmul(pb[5][:, 0, :], psh[:, cc, :], Gm[:, cc, :], start=(cc == 0), stop=(cc == 3)),
                     reads=[("psh", cc), "Gm"], writes=["pb5"])
            gate_finish(0, i, False)
            p.dve(lambda e: e.tensor_tensor(score[:], pb[5][:, 0, :], f0m[:], op=ALU.max), reads=["pb5", "f0m"],
                  writes=["score"])
            c0 = 2 * ibb
            if c0 + 2 < 128:
                p.dve(lambda e: e.memset(score[:, c0 + 2:128], -1e30), reads=["score"], writes=["score"])
            p.dve(lambda e: e.memset(score[0:64, c0 + 1:c0 + 2], -1e30), reads=["score"], writes=["score"])
            p.dve(lambda e: e.memset(score[0:64, c0 - 1:c0 + 1], 1e9), reads=["score"], writes=["score"])
            p.dve(lambda e: e.memset(score[64:128, c0:c0 + 2], 1e9), reads=["score"], writes=["score"])
            p.dve(lambda e: e.max(m8a[:], score[:]), reads=["score"], writes=["m8a"])
            p.dve(lambda e: e.match_replace(work[:], m8a[:], score[:], -3e38), reads=["score", "m8a"], writes=["work"])
            p.dve(lambda e: e.max(m8b[:], work[:]), reads=["work"], writes=["m8b"])
            p.dve(lambda e: e.tensor_scalar(work[:], score[:], m8b[:, 7:8], None, op0=ALU.is_ge), reads=["score", "m8b"],
                  writes=["work"])
            p.dve(lambda e: e.tensor_tensor(sel[:], work[:], validblk[:], op=ALU.mult), reads=["work", "validblk"],
                  writes=["sel"])
            if c0 + 2 < 128:
                p.dve(lambda e: e.memset(sel[:, c0 + 2:128], 0.0), reads=["sel"], writes=["sel"])
            p.dve(lambda e: e.memset(sel[0:64, c0 + 1:c0 + 2], 0.0), reads=["sel"], writes=["sel"])
            p.pe(lambda e: e.transpose(pb[5][:].rearrange("p a b -> p (a b)").bitcast(BF16)[:, 0:128], sel[:], IDb[:]),
                 reads=["sel", "IDb"], writes=["pb5"])
            p.act(lambda e: e.copy(selT[:], pb[5][:].rearrange("p a b -> p (a b)").bitcast(BF16)[:, 0:128]),
                  reads=["pb5"], writes=["selT"])
            if dbg < 5:
                return
            pend = []
            for cb in range(ibb + 1):
                sbk = next_sb()
                ks = slice(128 * cb, 128 * cb + 128)
                for j in range(2):
                    p.pe(lambda e, j=j, ks=ks, sbk=sbk: e.matmul(
                        pb[sbk][:, 2 * j:2 * j + 2, :].rearrange("p a b -> p (a b)"), bks[:, ks], QBD[("QR", par, j)][:],
                        start=True, stop=True), reads=["bks", f"QR{par}{j}"], writes=[f"pb{sbk}"])
                eb = 2 + cb % 4
                p.pe(lambda e, cb=cb, eb=eb: e.matmul(pb[eb][:, 0, :], E[:, cb, :], selT[:], start=True, stop=True),
                     reads=["E", "selT"], writes=[f"pb{eb}"])
                while len(pend) > 1:
                    pend.pop(0)()
                k_ = next_pt()
                p.act(lambda e, k_=k_, sbk=sbk: e.activation(PT[k_][:], pb[sbk][:], AF.Exp, scale=SCALE),
                      reads=[f"pb{sbk}"], writes=[f"PT{k_}"])
                p.dve(lambda e, k_=k_, eb=eb: e.tensor_tensor(PT[k_][:], PT[k_][:],
                                                             pb[eb][:, 0, :].unsqueeze(1).to_broadcast([128, 4, 128]),
                                                             op=ALU.mult), reads=[f"PT{k_}", f"pb{eb}"],
                      writes=[f"PT{k_}"])
                if cb == ibb:
                    p.dve(lambda e, k_=k_: e.tensor_tensor(PT[k_][:], PT[k_][:],
                                                          MUq[:].unsqueeze(1).to_broadcast([128, 4, 128]), op=ALU.mult),
                          reads=[f"PT{k_}", "MUq"], writes=[f"PT{k_}"])

                def pv(cb=cb, k_=k_):
                    p.pe(lambda e: e.matmul(pb[7][:].rearrange("p a b -> p (a b)"), bvsg[:, cb, :],
                                            PT[k_][:].rearrange("p a b -> p (a b)"), start=(cb == 0),
                                            stop=(cb == ibb)), reads=["bvsg", f"PT{k_}"], writes=["pb7"])
                pend.append(pv)
            while pend:
                pend.pop(0)()
            gate_finish(1, i, False)
            for g_ in range(2):
                p.act(lambda e, g_=g_: e.copy(ocB[64 * g_:64 * g_ + 64, :, qs],
                                              accB[:].rearrange("p (j g) q -> p j g q", g=2)[:, :, g_, :]),
                      reads=["accB"], writes=["ocB", "KKk"])
        for i_ in qblocks:
            do_qblock(i_)
        for j in range(2):
            p.dma("sp", "ocA", lambda e, j=j: e.dma_start(out=oc_d[j, :, :], in_=ocA[:, j, :]), reads=["ocA"],
                  writes=[("oc", j)])
            p.dma("sp", "ocB", lambda e, j=j: e.dma_start(out=oc_d[2 + j, :, :], in_=ocB[:, j, :]), reads=["ocB"],
                  writes=[("oc", 2 + j)])
        p.emit(final_sems=["ocA", "ocB"])
    return nc


def assemble_B(Ab, qtr, wl):
    bf = ml_dtypes.bfloat16
    t0 = qtr * T
    off = 3 * T - t0
    own = Ab[qtr]

    def hist_cols(idx, src, n):
        if qtr == 0:
            return np.zeros((128, n), bf)
        return Ab[qtr - 1][src][idx][:, T - n:]

    def hist_rows(c0, c1, n):
        if qtr == 0:
            return np.zeros((n, c1 - c0), bf)
        return Ab[qtr - 1]["o_v"][T - n:, c0:c1]

    def tokmaj(rows, nblk):
        return np.ascontiguousarray(rows.reshape(nblk, 128, rows.shape[1]).transpose(1, 0, 2))

    d = {}
    d["aq"] = np.ascontiguousarray(own["o_rope"][0:2])
    d["ak"] = np.ascontiguousarray(np.stack([np.concatenate([hist_cols(2 + j, "o_rope", 128), own["o_rope"][2 + j]], 1)
                                             for j in range(2)]))
    d["av"] = tokmaj(np.concatenate([hist_rows(0, 128, 128), own["o_v"][:, 0:128]], 0), 17)
    d["sinks"] = np.ascontiguousarray(np.tile(wl["attn_sinks"][None, :], (128, 1)).astype(np.float32))
    d["offs"] = np.tile(np.array([[off, off // 16, off // 64, 0.0 if qtr == 0 else 1.0]], np.float32), (128, 1))
    d["bqr"] = np.ascontiguousarray(own["o_rope"][4:6])
    d["bq"] = np.ascontiguousarray(own["o_plain"][0:2])
    d["bkw"] = np.ascontiguousarray(np.concatenate([hist_cols(7, "o_rope", 512), own["o_rope"][7]], 1))
    d["bvw"] = tokmaj(np.concatenate([hist_rows(192, 256, 512), own["o_v"][:, 192:256]], 0), 20)
    pad = np.zeros((128, off), bf)
    d["bks"] = np.ascontiguousarray(np.concatenate([pad] + [Ab[k]["o_rope"][6] for k in range(qtr + 1)], 1))
    d["cmpT"] = np.ascontiguousarray(np.concatenate([pad] + [Ab[k]["o_plain"][2] for k in range(qtr + 1)], 1))
    d["bvs"] = tokmaj(np.concatenate([np.zeros((off, 64), bf)] + [Ab[k]["o_v"][:, 128:192] for k in range(qtr + 1)], 0), 64)
    d["bg"] = np.ascontiguousarray(own["o_g"])
    d["w1k"], d["w1v"], d["w2k"], d["w2v"] = wl["cmp_k_w1"], wl["cmp_v_w1"], wl["cmp_k_w2"], wl["cmp_v_w2"]
    for nm, key in (("pek", "cmp_pe_k"), ("pev", "cmp_pe_v")):
        d[nm] = np.ascontiguousarray(wl[key].reshape(16, 2, 64).transpose(1, 2, 0).reshape(128, 16))
    return d


def _fm(v):
    return np.ascontiguousarray(np.asarray(v, np.float32).reshape(8, 128).T)


def _toT(x):
    return np.ascontiguousarray(x.reshape(T, 8, 128).transpose(2, 1, 0))


def assemble_G(Ab, h, l, inp):
    NBK = SEQ // 128
    qkv = np.stack([np.concatenate([Ab[q]["o_c"][w * 4 + h] for q in range(4)], 1) for w in range(3)])
    cwl = inp["gdn_conv_w"][l]
    cw = np.stack([cwl[:, w * 512 + h * 128:w * 512 + (h + 1) * 128].T for w in range(3)], 1).reshape(128, 12)
    oz = np.concatenate([Ab[q]["o_z"] for q in range(4)], 0)
    gab = np.stack([oz[:, 512 + h].reshape(NBK, 128).T, oz[:, 516 + h].reshape(NBK, 128).T], -1)
    hp = np.tile(np.array([[inp["gdn_A_log"][l, h], inp["gdn_dt_bias"][l, h]]], np.float32), (128, 1))
    z = oz[:, h * 128:(h + 1) * 128].reshape(NBK, 128, 128).transpose(1, 0, 2)
    nwb = np.tile(inp["gdn_norm"][l][None, :], (128, 1))
    f = lambda a: np.ascontiguousarray(a, dtype=np.float32)
    return dict(qkv=f(qkv), cw=f(cw), gab=f(gab), hp=f(hp), z=f(z), nwb=f(nwb))


_PROGS = {}


def _prog(name):
    if name not in _PROGS:
        _PROGS[name] = {"M": build_M, "A": build_A, "B": build_B, "G": build_G, "C": lambda: build_C(False),
                        "CF": lambda: build_C(True)}[name]()
    return _PROGS[name]


def _run(name, in_maps):
    res = run_bass_kernel_spmd(_prog(name), in_maps, core_ids=list(range(8)))
    return res.results


def kernel(**inp):
    inp = {k: np.asarray(v) for k, v in inp.items()}
    cores = list(range(8))
    cT = np.ascontiguousarray(inp["c"].astype(np.float32).reshape(2, 8, 128).transpose(2, 1, 0))
    ims = []
    for core in cores:
        l, hf = core // 2, core % 2
        ims.append(dict(cT=cT, aw=np.ascontiguousarray(inp["ada_w"][l][:, hf * 3072:(hf + 1) * 3072]),
                        ab=np.ascontiguousarray(inp["ada_b"][l][hf * 3072:(hf + 1) * 3072].reshape(24, 128).T)))
    rm = _run("M", ims)
    mods = np.zeros((DEPTH, BATCH, 6 * D), np.float32)
    for core in cores:
        l, hf = core // 2, core % 2
        mods[l, :, hf * 3072:(hf + 1) * 3072] = rm[core]["mo"].transpose(2, 1, 0).reshape(2, 3072)
    xT = [_toT(inp["x"][core // 4, (core % 4) * T:(core % 4 + 1) * T].astype(np.float32)) for core in cores]
    for l in range(DEPTH):
        wl = {k: inp[k][l] for k in ("attn_sinks", "cmp_k_w1", "cmp_v_w1", "cmp_k_w2", "cmp_v_w2", "cmp_pe_k", "cmp_pe_v")}
        ims = []
        for core in cores:
            b, q = core // 4, core % 4
            m = mods[l, b]
            tabs = np.concatenate([_fm(inp["norm_mix"][l]), _fm(m[1024:2048]), _fm(m[0:1024])], 1)
            ims.append(dict(xT=xT[core], tabs=tabs, pos0=np.full((128, 1), q * T, np.float32), w=inp["w_in"][l]))
        ra = _run("A", ims)
        rb = _run("B", [assemble_B([ra[(core // 4) * 4 + q] for q in range(4)], core % 4, wl) for core in cores])
        rg = _run("G", [assemble_G([ra[(core // 4) * 4 + q] for q in range(4)], core % 4, l, inp) for core in cores])
        ims = []
        for core in cores:
            b, q = core // 4, core % 4
            m = mods[l, b]
            gd = [np.ascontiguousarray(rg[b * 4 + h]["yo"].transpose(1, 0, 2).reshape(SEQ, 128)[q * T:(q + 1) * T].T)
                  for h in range(4)]
            ocat = np.ascontiguousarray(np.concatenate([rb[core]["oc"], np.stack(gd)], 0))
            tabs = np.concatenate([_fm(m[2048:3072]), _fm(inp["norm_ffn"][l]), _fm(m[4096:5120]), _fm(m[3072:4096]),
                                   _fm(m[5120:6144]), _fm(inp["final_norm"])], 1)
            ims.append(dict(xT=xT[core], ocat=ocat, tabs=tabs, w_out=inp["w_out"][l], w_gu=inp["w_gate_up"][l],
                            w_down=inp["w_down"][l]))
        rc = _run("CF" if l == DEPTH - 1 else "C", ims)
        xT = [rc[core]["xo"] for core in cores]
    out = np.zeros((BATCH, SEQ, D), np.float32)
    for core in cores:
        b, q = core // 4, core % 4
        out[b, q * T:(q + 1) * T] = xT[core].transpose(2, 1, 0).reshape(T, D)
    return out
```

```python
import bisect
import math
from contextlib import ExitStack

import numpy as np
import ml_dtypes
import concourse.bass as bass
import concourse.mybir as mybir
from concourse.bass_utils import run_bass_kernel_spmd

F32 = mybir.dt.float32
BF16 = mybir.dt.bfloat16
I32 = mybir.dt.int32
AF = mybir.ActivationFunctionType
ALU = mybir.AluOpType
AX = mybir.AxisListType

D = 1024
SEQ = 8192
BATCH = 2
DEPTH = 4
T = 2048
NT = 512
DFF = 2816
EPS = 1e-6
ENGS = ("pe", "act", "dve", "pool", "sp")


class _Op:
    __slots__ = ("eng", "fn", "reads", "writes", "dma", "deps", "signal", "ordinal", "gidx")


def _is_psum(r):
    if isinstance(r, tuple):
        r = r[0]
    return isinstance(r, str) and (r.startswith("ps") or r.startswith("pb"))


class Prog:
    def __init__(self, nc):
        self.nc = nc
        self.ops = []
        self.per_eng = {e: [] for e in ENGS}
        self.last_w = {}
        self.readers = {}
        self.dma_groups = {}

    def add(self, eng, fn, reads=(), writes=(), dma=None):
        op = _Op()
        op.eng, op.fn, op.reads, op.writes, op.dma = eng, fn, tuple(reads), tuple(writes), dma
        op.deps = set()
        op.signal = False
        op.ordinal = None
        op.gidx = len(self.ops)
        for r in op.reads:
            w = self.last_w.get(r)
            if w is not None:
                op.deps.add(w)
            if _is_psum(r):
                for rd in self.readers.get(r, ()):
                    if rd.eng != eng:
                        op.deps.add(rd)
            self.readers.setdefault(r, []).append(op)
        for r in op.writes:
            w = self.last_w.get(r)
            if w is not None:
                if dma is not None and w.dma == dma:
                    op.deps |= w.deps
                else:
                    op.deps.add(w)
            for rd in self.readers.get(r, ()):
                if rd is not op:
                    op.deps.add(rd)
            self.readers[r] = []
            self.last_w[r] = op
        op.deps.discard(op)
        self.ops.append(op)
        self.per_eng[eng].append(op)
        if dma is not None:
            self.dma_groups.setdefault(dma, []).append(op)
        return op

    def pe(self, fn, reads=(), writes=()):
        return self.add("pe", fn, reads, writes)

    def act(self, fn, reads=(), writes=()):
        return self.add("act", fn, reads, writes)

    def dve(self, fn, reads=(), writes=()):
        return self.add("dve", fn, reads, writes)

    def pool(self, fn, reads=(), writes=()):
        return self.add("pool", fn, reads, writes)

    def dma(self, q, sem, fn, reads=(), writes=()):
        return self.add(q, fn, reads, writes, dma=sem)

    def emit(self, final_sems=()):
        nc = self.nc
        for op in self.ops:
            for d in op.deps:
                if d.dma is None:
                    if d.eng == "pe" and op.eng == "pe":
                        continue
                    d.signal = True
        for e in ENGS:
            n = 0
            for op in self.per_eng[e]:
                if op.dma is None and op.signal:
                    n += 1
                    op.ordinal = n
        gidx_of = {k: [o.gidx for o in ops] for k, ops in self.dma_groups.items()}
        with ExitStack() as es:
            sems = {e: es.enter_context(nc.semaphore("s_" + e)) for e in ENGS}
            dsems = {k: es.enter_context(nc.semaphore("d_" + str(k))) for k in self.dma_groups}
            block = es.enter_context(nc.Block())
            deco = {"pe": block.tensor, "act": block.scalar, "dve": block.vector, "pool": block.gpsimd,
                    "sp": block.sync}

            def make(e):
                def body(eng):
                    known = {}
                    for op in self.per_eng[e]:
                        need = {}
                        for d in op.deps:
                            if d.dma is not None:
                                key = ("d", d.dma)
                                v = 16 * bisect.bisect_left(gidx_of[d.dma], op.gidx)
                            else:
                                if d.eng == "pe" and e == "pe":
                                    continue
                                key = ("e", d.eng)
                                v = d.ordinal
                            if v > need.get(key, 0):
                                need[key] = v
                        for key, v in need.items():
                            if known.get(key, 0) >= v:
                                continue
                            known[key] = v
                            s = dsems[key[1]] if key[0] == "d" else sems[key[1]]
                            eng.wait_ge(s, v)
                        ins = op.fn(eng)
                        if op.dma is not None:
                            ins.then_inc(dsems[op.dma], 16)
                        elif op.signal:
                            ins.then_inc(sems[e], 1)
                    if e == "sp":
                        for k in final_sems:
                            if k not in dsems:
                                continue
                            eng.wait_ge(dsems[k], 16 * len(self.dma_groups[k]))
                return body

            for e in ENGS:
                if self.per_eng[e] or e == "sp":
                    deco[e](make(e))


class Ctx:
    def __init__(self, nc, es):
        self.nc, self.es = nc, es
        self.p = Prog(nc)
        self.n = 0

    def sb(self, shape, dt, name=None):
        self.n += 1
        return self.es.enter_context(self.nc.sbuf_tensor("s_" + (name or f"sb{self.n}"), list(shape), dt))

    def ps(self, shape, dt=F32, name=None):
        self.n += 1
        return self.es.enter_context(self.nc.psum_tensor(name or f"ps{self.n}", list(shape), dt))

    def din(self, name, shape, dt=F32):
        return self.nc.dram_tensor(name, list(shape), dt, kind="ExternalInput").ap()

    def dout(self, name, shape, dt=F32):
        return self.nc.dram_tensor(name, list(shape), dt, kind="ExternalOutput").ap()


def _swap(c0):
    return [(c0 + 32, c0 + 64), (c0, c0 + 32)]


def _plain(c0, n=64):
    return [(c0, c0 + n)]


FM_BLOCKS = [
    ("aq01", _plain(0, 128)), ("aq01s", _swap(0) + _swap(64)),
    ("aq23", _plain(128, 128)), ("aq23s", _swap(128) + _swap(192)),
    ("ak0", _plain(256) + _plain(256)), ("ak0s", _swap(256) + _swap(256)),
    ("ak1", _plain(320) + _plain(320)), ("ak1s", _swap(320) + _swap(320)),
    ("bq01", _plain(512, 128)), ("bq01s", _swap(512) + _swap(576)),
    ("bq23", _plain(640, 128)), ("bq23s", _swap(640) + _swap(704)),
    ("bks", _plain(896) + _plain(896)), ("bkss", _swap(896) + _swap(896)),
    ("bkw", _plain(1024) + _plain(1024)), ("bkws", _swap(1024) + _swap(1024)),
    ("cmp", _plain(768, 128)),
] + [("c%d" % i, _plain(1164 + 128 * i, 128)) for i in range(12)]
FM_OFF = {n: 128 * i for i, (n, _) in enumerate(FM_BLOCKS)}
GATE_OFF = 128 * len(FM_BLOCKS)
TM_OFF = GATE_OFF + 12
TM_COLS = [(384, 512), (960, 1024), (1088, 1152), (2700, 3212), (3212, 3220)]
TM_N = 128 + 64 + 64 + 512 + 8
WIN_COLS = TM_OFF + TM_N


def build_A(dbg=9):
    nc = bass.Bass("TRN2", target_bir_lowering=False)
    with ExitStack() as es:
        c = Ctx(nc, es)
        p = c.p
        xT = c.din("xT", [128, 8, T])
        tabs = c.din("tabs", [128, 24])
        pos0 = c.din("pos0", [128, 1])
        w = c.din("w", [D, 3220])
        wr = w.rearrange("(k p) c -> p k c", p=128)
        o_rope = c.dout("o_rope", [8, 128, T], BF16)
        o_plain = c.dout("o_plain", [3, 128, T], BF16)
        o_c = c.dout("o_c", [12, 128, T], F32)
        o_g = c.dout("o_g", [12, T], F32)
        o_v = c.dout("o_v", [T, 256], BF16)
        o_z = c.dout("o_z", [T, 520], F32)

        W = c.sb([128, 8, WIN_COLS], BF16, "W")
        tb = c.sb([128, 24], F32, "tb")
        s1 = c.sb([128, 8], F32, "s1")
        p0 = c.sb([128, 1], F32, "p0")
        S2 = c.sb([128, T], F32, "S2")
        ang = c.sb([128, T], F32, "ang")
        C2 = ang
        invrow = c.sb([1, 128], F32, "invrow")
        one1 = c.sb([1, 1], F32, "one1")
        inv = c.sb([128, 1], F32, "inv")
        ones_bf = c.sb([128, 128], BF16, "ones_bf")
        xt = [c.sb([128, 8, NT], F32, f"xt{i}") for i in range(1)]
        sq = c.sb([128, 8, NT], BF16, "sq")
        rstd = c.sb([128, NT], F32, "rstd")
        tmp = [c.sb([128, NT], F32, f"tmp{i}") for i in range(2)]
        hT = c.sb([128, 8, NT], BF16, "hT")
        outR = [c.sb([128, T], BF16, f"outR{i}") for i in range(8)]
        outP = [c.sb([128, T], BF16, f"outP{i}") for i in range(3)]
        t1 = [c.sb([128, NT], F32, f"t1_{i}") for i in range(2)]
        t2 = [c.sb([128, NT], F32, f"t2_{i}") for i in range(2)]
        stc = [c.sb([128, NT], F32, f"stc{i}") for i in range(3)]
        stg = c.sb([12, T], F32, "stg")
        stv = [c.sb([128, 256], BF16, f"stv{i}") for i in range(2)]
        stz = [c.sb([128, 520], F32, f"stz{i}") for i in range(2)]
        psb = [c.ps([128, 512], F32, f"psb{i}") for i in range(8)]

        p.dma("sp", "tb", lambda e: e.dma_start(out=tb[:], in_=tabs[:, :]), writes=["tb"])
        p.dma("sp", "p0", lambda e: e.dma_start(out=p0[:], in_=pos0[:, :]), writes=["p0"])
        col = 0
        wi = 0
        for name, rngs in FM_BLOCKS:
            for (a, b) in rngs:
                p.dma("pool", "W", lambda e, a=a, b=b, col=col: e.dma_start(
                    out=W[:, :, col:col + b - a], in_=wr[:, :, a:b]), writes=["W"])
                col += b - a
        p.dma("pool", "W", lambda e: e.dma_start(out=W[:, :, GATE_OFF:GATE_OFF + 12], in_=wr[:, :, 1152:1164]),
              writes=["W"])
        col = TM_OFF
        for (a, b) in TM_COLS:
            p.dma("pool", "W", lambda e, a=a, b=b, col=col: e.dma_start(
                out=W[:, :, col:col + b - a], in_=wr[:, :, a:b]), writes=["W"])
            col += b - a

        p.dve(lambda e: e.tensor_scalar(s1[:], tb[:, 8:16], 1.0, 32.0, op0=ALU.add, op1=ALU.mult),
              reads=["tb"], writes=["s1"])
        p.dve(lambda e: e.tensor_tensor(s1[:], s1[:], tb[:, 0:8], op=ALU.mult), reads=["tb", "s1"], writes=["s1"])
        p.dve(lambda e: e.memset(ones_bf[:], 1.0), writes=["ones_bf"])
        p.dve(lambda e: e.memset(one1[:], 1.0), writes=["one1"])
        for i in range(32):
            v = float(np.float32(1.0) / np.float32(np.float32(10000.0) ** np.float32(2 * i / 64.0)))
            p.dve(lambda e, i=i, v=v: e.memset(invrow[0:1, i:128:32], v), writes=["invrow"])
        p.pe(lambda e: e.matmul(psb[0][:, 0:1], invrow[0:1, :], one1[0:1, 0:1], start=True, stop=True),
             reads=["invrow", "one1"], writes=["psb0"])
        p.act(lambda e: e.copy(inv[:], psb[0][:, 0:1]), reads=["psb0"], writes=["inv"])
        p.pool(lambda e: e.iota(ang[:], [[1, T]], base=0, channel_multiplier=0,
                                allow_small_or_imprecise_dtypes=True), writes=["ang"])
        p.dve(lambda e: e.tensor_scalar(ang[:], ang[:], p0[:, 0:1], inv[:, 0:1], op0=ALU.add, op1=ALU.mult),
              reads=["ang", "p0", "inv"], writes=["ang"])
        TWO_PI = 2.0 * math.pi
        CW1 = 6.28125
        CW2 = TWO_PI - CW1
        nI = c.sb([128, NT], I32, "nI")
        yy = c.sb([128, NT], F32, "yy")
        mm_ = c.sb([128, NT], F32, "mm_")

        def sin_tile(dst, ts, shift):
            p.dve(lambda e: e.tensor_scalar(yy[:], ang[:, ts], float(shift), None, op0=ALU.add),
                  reads=["ang"], writes=["yy"])
            p.dve(lambda e: e.tensor_scalar(nI[:], yy[:], 1.0 / TWO_PI, None, op0=ALU.mult),
                  reads=["yy"], writes=["nI"])
            p.dve(lambda e: e.scalar_tensor_tensor(yy[:], nI[:], -CW1, yy[:], op0=ALU.mult, op1=ALU.add),
                  reads=["yy", "nI"], writes=["yy"])
            p.dve(lambda e: e.scalar_tensor_tensor(yy[:], nI[:], -CW2, yy[:], op0=ALU.mult, op1=ALU.add),
                  reads=["yy", "nI"], writes=["yy"])
            p.dve(lambda e: e.tensor_scalar(mm_[:], yy[:], math.pi, -TWO_PI, op0=ALU.is_gt, op1=ALU.mult),
                  reads=["yy"], writes=["mm_"])
            p.dve(lambda e: e.tensor_tensor(yy[:], yy[:], mm_[:], op=ALU.add), reads=["yy", "mm_"], writes=["yy"])
            p.dve(lambda e: e.tensor_scalar(yy[:], yy[:], -math.pi, math.pi, op0=ALU.max, op1=ALU.min),
                  reads=["yy"], writes=["yy"])
            p.act(lambda e: e.activation(dst[:, ts], yy[:], AF.Sin), reads=["yy"], writes=["S2", "ang", "C2"])

        for ti in range(T // NT):
            ts_ = slice(ti * NT, (ti + 1) * NT)
            sin_tile(S2, ts_, 0.0)
            sin_tile(C2, ts_, 0.5 * math.pi)
        for base in (0, 64):
            p.act(lambda e, base=base: e.mul(S2[base:base + 32, :], S2[base:base + 32, :], -1.0),
                  reads=["S2"], writes=["S2"])

        rope_pairs = [("aq01", 0), ("aq23", 1), ("ak0", 2), ("ak1", 3), ("bq01", 4), ("bq23", 5), ("bks", 6),
                      ("bkw", 7)]
        bank = [0]
        deps = c.sb([128, 1], F32, "deps")
        p.dve(lambda e: e.memset(deps[:], float(D * EPS)), writes=["deps"])

        def nextbank():
            b = bank[0]
            bank[0] = (b + 1) % 8
            return b

        def mm_block(coff, ncols_m, bk, rd):
            for k in range(8):
                p.pe(lambda e, k=k: e.matmul(psb[bk][0:ncols_m, :], W[:, k, coff:coff + ncols_m], hT[:, k, :],
                                             start=(k == 0), stop=(k == 7)),
                     reads=["W", ("hT", k)], writes=[f"psb{bk}"])

        for ti in range(T // NT if dbg >= 2 else 0):
            x_ = xt[0]
            xr = "xt0"
            ts = slice(ti * NT, (ti + 1) * NT)
            p.dma("sp", xr, lambda e, x_=x_, ts=ts: e.dma_start(out=x_[:], in_=xT[:, :, ts]), writes=[xr])
            p.act(lambda e, x_=x_: e.activation(sq[:], x_[:], AF.Square), reads=[xr], writes=["sq"])
            b0 = nextbank()
            for k in range(8):
                p.pe(lambda e, k=k, b0=b0: e.matmul(psb[b0][:], ones_bf[:], sq[:, k, :], start=(k == 0), stop=(k == 7)),
                     reads=["ones_bf", "sq"], writes=[f"psb{b0}"])
            p.act(lambda e, b0=b0: e.activation(rstd[:], psb[b0][:], AF.Sqrt, bias=deps[:, 0:1]),
                  reads=[f"psb{b0}", "deps"], writes=["rstd"])
            p.dve(lambda e: e.reciprocal(rstd[:], rstd[:]), reads=["rstd"], writes=["rstd"])
            for k in range(8):
                tm = tmp[k % 2]
                tr = f"tmp{k % 2}"
                p.dve(lambda e, k=k, tm=tm, x_=x_: e.tensor_tensor(tm[:], x_[:, k, :], rstd[:], op=ALU.mult),
                      reads=[xr, "rstd"], writes=[tr])
                p.act(lambda e, k=k, tm=tm: e.activation(hT[:, k, :], tm[:], AF.Identity, bias=tb[:, 16 + k:17 + k],
                                                         scale=s1[:, k:k + 1]),
                      reads=[tr, "s1", "tb"], writes=[("hT", k)])
            for name, oi in (rope_pairs if dbg >= 3 else []):
                b1 = nextbank()
                mm_block(FM_OFF[name], 128, b1, None)
                b2 = nextbank()
                mm_block(FM_OFF[name + "s"], 128, b2, None)
                a1 = t1[oi % 2]
                a2 = t2[oi % 2]
                p.dve(lambda e, b1=b1, a1=a1, ts=ts: e.tensor_tensor(a1[:], psb[b1][:], C2[:, ts], op=ALU.mult),
                      reads=[f"psb{b1}", "C2"], writes=[f"t1_{oi % 2}"])
                if name.startswith("bq"):
                    po = outP[oi - 4]
                    p.act(lambda e, b1=b1, po=po, ts=ts: e.copy(po[:, ts], psb[b1][:]),
                          reads=[f"psb{b1}"], writes=[f"outP{oi - 4}"])
                p.dve(lambda e, b2=b2, a2=a2, ts=ts: e.tensor_tensor(a2[:], psb[b2][:], S2[:, ts], op=ALU.mult),
                      reads=[f"psb{b2}", "S2"], writes=[f"t2_{oi % 2}"])
                ro = outR[oi]
                p.dve(lambda e, a1=a1, a2=a2, ro=ro, ts=ts: e.tensor_tensor(ro[:, ts], a1[:], a2[:], op=ALU.add),
                       reads=[f"t1_{oi % 2}", f"t2_{oi % 2}"], writes=[f"outR{oi}"])
            if dbg < 4:
                continue
            b1 = nextbank()
            mm_block(FM_OFF["cmp"], 128, b1, None)
            p.act(lambda e, b1=b1, ts=ts: e.copy(outP[2][:, ts], psb[b1][:]), reads=[f"psb{b1}"], writes=["outP2"])
            for i in range(12):
                b1 = nextbank()
                mm_block(FM_OFF["c%d" % i], 128, b1, None)
                sidx = i % 3
                st = stc[sidx]
                p.act(lambda e, b1=b1, st=st: e.copy(st[:], psb[b1][:]), reads=[f"psb{b1}"], writes=[f"stc{sidx}"])
                p.dma("sp", f"stc{sidx}", lambda e, st=st, i=i, ts=ts: e.dma_start(out=o_c[i, :, ts], in_=st[:]),
                      reads=[f"stc{sidx}"], writes=[("o_c", i, ti)])
            if dbg < 5:
                continue
            b1 = nextbank()
            mm_block(GATE_OFF, 12, b1, None)
            p.act(lambda e, b1=b1, ts=ts: e.copy(stg[:, ts], psb[b1][0:12, :]), reads=[f"psb{b1}"], writes=["stg"])
            if dbg < 6:
                continue
            for s in range(NT // 128):
                ss = slice(s * 128, (s + 1) * 128)
                tok = slice(ti * NT + s * 128, ti * NT + (s + 1) * 128)
                b1 = nextbank()
                for k in range(8):
                    p.pe(lambda e, k=k, b1=b1, ss=ss: e.matmul(psb[b1][:, 0:256], hT[:, k, ss],
                                                             W[:, k, TM_OFF:TM_OFF + 256], start=(k == 0), stop=(k == 7)),
                         reads=["W", ("hT", k)], writes=[f"psb{b1}"])
                b2 = nextbank()
                for k in range(8):
                    p.pe(lambda e, k=k, b2=b2, ss=ss: e.matmul(psb[b2][:, 0:512], hT[:, k, ss],
                                                             W[:, k, TM_OFF + 256:TM_OFF + 768], start=(k == 0),
                                                             stop=(k == 7)),
                         reads=["W", ("hT", k)], writes=[f"psb{b2}"])
                b3 = nextbank()
                for k in range(8):
                    p.pe(lambda e, k=k, b3=b3, ss=ss: e.matmul(psb[b3][:, 0:8], hT[:, k, ss],
                                                             W[:, k, TM_OFF + 768:TM_OFF + 776], start=(k == 0),
                                                             stop=(k == 7)),
                         reads=["W", ("hT", k)], writes=[f"psb{b3}"])
                sv = stv[s % 2]
                sz = stz[s % 2]
                p.act(lambda e, b1=b1, sv=sv: e.copy(sv[:], psb[b1][:, 0:256]), reads=[f"psb{b1}"], writes=[f"stv{s % 2}"])
                p.dve(lambda e, b2=b2, sz=sz: e.tensor_copy(sz[:, 0:512], psb[b2][:, 0:512]), reads=[f"psb{b2}"],
                      writes=[f"stz{s % 2}"])
                p.dve(lambda e, b3=b3, sz=sz: e.tensor_copy(sz[:, 512:520], psb[b3][:, 0:8]), reads=[f"psb{b3}"],
                      writes=[f"stz{s % 2}"])
                p.dma("sp", f"stv{s % 2}", lambda e, sv=sv, tok=tok: e.dma_start(out=o_v[tok, :], in_=sv[:]),
                      reads=[f"stv{s % 2}"], writes=[("o_v", ti, s)])
                p.dma("sp", f"stz{s % 2}", lambda e, sz=sz, tok=tok: e.dma_start(out=o_z[tok, :], in_=sz[:]),
                      reads=[f"stz{s % 2}"], writes=[("o_z", ti, s)])
        fs = ["stc0", "stc1", "stc2", "stv0", "stv1", "stz0", "stz1"]
        for i in range(8):
            p.dma("sp", f"oR{i}", lambda e, i=i: e.dma_start(out=o_rope[i, :, :], in_=outR[i][:]),
                  reads=[f"outR{i}"], writes=[("o_rope", i)])
            fs.append(f"oR{i}")
        for i in range(3):
            p.dma("sp", f"oP{i}", lambda e, i=i: e.dma_start(out=o_plain[i, :, :], in_=outP[i][:]),
                  reads=[f"outP{i}"], writes=[("o_plain", i)])
            fs.append(f"oP{i}")
        p.dma("sp", "og", lambda e: e.dma_start(out=o_g[:, :], in_=stg[:]), reads=["stg"], writes=["o_g"])
        fs.append("og")
        p.emit(final_sems=fs)
    return nc


NTC = 256


def build_C(final=False):
    nc = bass.Bass("TRN2", target_bir_lowering=False)
    with ExitStack() as es:
        c = Ctx(nc, es)
        p = c.p
        xT = c.din("xT", [128, 8, T])
        oc = c.din("ocat", [8, 128, T], BF16)
        tabs = c.din("tabs", [128, 48])
        wo = c.din("w_out", [D, D])
        wgu = c.din("w_gu", [D, 2 * DFF])
        wd = c.din("w_down", [DFF, D])
        xo = c.dout("xo", [128, 8, T])
        wor = wo.rearrange("(k p) c -> p k c", p=128)
        wgur = wgu.rearrange("(k p) c -> p k c", p=128)
        wdr = wd.rearrange("(k p) c -> p k c", p=128)
        NF = DFF // 128

        Wo = c.sb([128, 8, D], BF16, "Wo")
        Wg = c.sb([128, 8, 2 * DFF], BF16, "Wg")
        Wd = c.sb([128, NF, D], BF16, "Wd")
        tb = c.sb([128, 48], F32, "tb")
        s2 = c.sb([128, 8], F32, "s2")
        sfin = c.sb([128, 8], F32, "sfin")
        ones_bf = c.sb([128, 128], BF16, "ones_bf")
        deps = c.sb([128, 1], F32, "deps")
        xt = c.sb([128, 8, NTC], F32, "xt")
        ot = c.sb([128, 8, NTC], BF16, "ot")
        sq = c.sb([128, 8, NTC], BF16, "sq")
        rstd = c.sb([128, NTC], F32, "rstd")
        tmp = [c.sb([128, NTC], F32, f"tmp{i}") for i in range(2)]
        hT = c.sb([128, 8, NTC], BF16, "hT")
        gs = [c.sb([128, NTC], F32, f"gs{i}") for i in range(2)]
        aT = c.sb([128, NF, NTC], BF16, "aT")
        xout = c.sb([128, 8, NTC], F32, "xout")
        psb = [c.ps([128, 2, NTC], F32, f"psb{i}") for i in range(8)]
        bank = [0]

        def nextbank():
            b = bank[0]
            bank[0] = (b + 1) % 8
            return b

        p.dma("sp", "tb", lambda e: e.dma_start(out=tb[:], in_=tabs[:, :]), writes=["tb"])
        for j in range(2):
            p.dma("pool", "Wo", lambda e, j=j: e.dma_start(out=Wo[:, :, j * 512:(j + 1) * 512],
                                                         in_=wor[:, :, j * 512:(j + 1) * 512]), writes=["Wo"])
        for j in range(11):
            p.dma("pool", "Wg", lambda e, j=j: e.dma_start(out=Wg[:, :, j * 512:(j + 1) * 512],
                                                         in_=wgur[:, :, j * 512:(j + 1) * 512]), writes=["Wg"])
        for j in range(2):
            p.dma("pool", "Wd", lambda e, j=j: e.dma_start(out=Wd[:, :, j * 512:(j + 1) * 512],
                                                         in_=wdr[:, :, j * 512:(j + 1) * 512]), writes=["Wd"])
        p.dve(lambda e: e.memset(ones_bf[:], 1.0), writes=["ones_bf"])
        p.dve(lambda e: e.memset(deps[:], float(D * EPS)), writes=["deps"])
        p.dve(lambda e: e.tensor_scalar(s2[:], tb[:, 16:24], 1.0, 32.0, op0=ALU.add, op1=ALU.mult),
              reads=["tb"], writes=["s2"])
        p.dve(lambda e: e.tensor_tensor(s2[:], s2[:], tb[:, 8:16], op=ALU.mult), reads=["tb", "s2"], writes=["s2"])
        p.dve(lambda e: e.tensor_scalar(sfin[:], tb[:, 40:48], 32.0, None, op0=ALU.mult), reads=["tb"], writes=["sfin"])

        def rms_stats(src, srcres):
            p.act(lambda e: e.activation(sq[:], src[:], AF.Square), reads=srcres, writes=["sq"])
            b0 = nextbank()
            for k in range(8):
                p.pe(lambda e, k=k, b0=b0: e.matmul(psb[b0][:, 0, :], ones_bf[:], sq[:, k, :], start=(k == 0),
                                                  stop=(k == 7)), reads=["ones_bf", "sq"], writes=[f"psb{b0}"])
            p.act(lambda e, b0=b0: e.activation(rstd[:], psb[b0][:, 0, :], AF.Sqrt, bias=deps[:, 0:1]),
                  reads=[f"psb{b0}", "deps"], writes=["rstd"])
            p.dve(lambda e: e.reciprocal(rstd[:], rstd[:]), reads=["rstd"], writes=["rstd"])

        for ti in range(T // NTC):
            ts = slice(ti * NTC, (ti + 1) * NTC)
            p.dma("sp", "xt", lambda e, ts=ts: e.dma_start(out=xt[:], in_=xT[:, :, ts]), writes=[("xt", k) for k in range(8)])
            p.dma("sp", "ot", lambda e, ts=ts: e.dma_start(out=ot[:], in_=oc[:, :, ts].rearrange("k p t -> p k t")),
                  writes=["ot"])
            xres = [("xt", k) for k in range(8)]
            for fo2 in range(4):
                b1 = nextbank()
                for j in range(2):
                    fo = fo2 * 2 + j
                    for kc in range(8):
                        p.pe(lambda e, b1=b1, j=j, fo=fo, kc=kc: e.matmul(
                            psb[b1][:, j, :], Wo[:, kc, fo * 128:(fo + 1) * 128], ot[:, kc, :], start=(kc == 0),
                            stop=(kc == 7)), reads=["Wo", "ot"], writes=[f"psb{b1}"])
                for j in range(2):
                    fo = fo2 * 2 + j
                    p.dve(lambda e, b1=b1, j=j, fo=fo: e.scalar_tensor_tensor(
                        xt[:, fo, :], psb[b1][:, j, :], tb[:, fo:fo + 1], xt[:, fo, :], op0=ALU.mult, op1=ALU.add),
                        reads=[f"psb{b1}", "tb", ("xt", fo)], writes=[("xt", fo)])
            rms_stats(xt, xres)
            for k in range(8):
                tm = tmp[k % 2]
                tr = f"tmp{k % 2}"
                p.dve(lambda e, k=k, tm=tm: e.tensor_tensor(tm[:], xt[:, k, :], rstd[:], op=ALU.mult),
                      reads=[("xt", k), "rstd"], writes=[tr])
                p.act(lambda e, k=k, tm=tm: e.activation(hT[:, k, :], tm[:], AF.Identity, bias=tb[:, 24 + k:25 + k],
                                                         scale=s2[:, k:k + 1]),
                      reads=[tr, "s2", "tb"], writes=[("hT", k)])
            for f in range(NF):
                b1 = nextbank()
                for j in range(2):
                    co = j * DFF + f * 128
                    for k in range(8):
                        p.pe(lambda e, b1=b1, j=j, co=co, k=k: e.matmul(
                            psb[b1][:, j, :], Wg[:, k, co:co + 128], hT[:, k, :], start=(k == 0), stop=(k == 7)),
                            reads=["Wg", ("hT", k)], writes=[f"psb{b1}"])
                g_ = gs[f % 2]
                p.act(lambda e, b1=b1, g_=g_: e.activation(g_[:], psb[b1][:, 0, :], AF.Silu),
                      reads=[f"psb{b1}"], writes=[f"gs{f % 2}"])
                p.dve(lambda e, b1=b1, g_=g_, f=f: e.tensor_tensor(aT[:, f, :], g_[:], psb[b1][:, 1, :], op=ALU.mult),
                      reads=[f"psb{b1}", f"gs{f % 2}"], writes=[("aT", f)])
            for fo2 in range(4):
                b1 = nextbank()
                for j in range(2):
                    fo = fo2 * 2 + j
                    for f in range(NF):
                        p.pe(lambda e, b1=b1, j=j, fo=fo, f=f: e.matmul(
                            psb[b1][:, j, :], Wd[:, f, fo * 128:(fo + 1) * 128], aT[:, f, :], start=(f == 0),
                            stop=(f == NF - 1)), reads=["Wd", ("aT", f)], writes=[f"psb{b1}"])
                for j in range(2):
                    fo = fo2 * 2 + j
                    dst = xt if final else xout
                    dres = ("xt", fo) if final else ("xout", fo)
                    p.dve(lambda e, b1=b1, j=j, fo=fo, dst=dst: e.scalar_tensor_tensor(
                        dst[:, fo, :], psb[b1][:, j, :], tb[:, 32 + fo:33 + fo], xt[:, fo, :], op0=ALU.mult,
                        op1=ALU.add), reads=[f"psb{b1}", "tb", ("xt", fo)], writes=[dres])
            if final:
                rms_stats(xt, xres)
                for k in range(8):
                    tm = tmp[k % 2]
                    tr = f"tmp{k % 2}"
                    p.dve(lambda e, k=k, tm=tm: e.tensor_tensor(tm[:], xt[:, k, :], rstd[:], op=ALU.mult),
                          reads=[("xt", k), "rstd"], writes=[tr])
                    p.act(lambda e, k=k, tm=tm: e.activation(xout[:, k, :], tm[:], AF.Copy, scale=sfin[:, k:k + 1]),
                          reads=[tr, "sfin"], writes=[("xout", k)])
            p.dma("sp", "xout", lambda e, ts=ts: e.dma_start(out=xo[:, :, ts], in_=xout[:]),
                  reads=[("xout", k) for k in range(8)], writes=[("xo", ti)])
        p.emit(final_sems=["xout"])
    return nc


def build_M():
    nc = bass.Bass("TRN2", target_bir_lowering=False)
    HC = 3072
    with ExitStack() as es:
        c = Ctx(nc, es)
        p = c.p
        cT = c.din("cT", [128, 8, 2])
        aw = c.din("aw", [D, HC])
        ab = c.din("ab", [128, HC // 128])
        mo = c.dout("mo", [128, HC // 128, 2])
        awr = aw.rearrange("(k p) c -> p k c", p=128)
        NM = HC // 128
        Wt = [c.sb([128, 8, 768], F32, f"Wt{i}") for i in range(2)]
        ct = c.sb([128, 8, 2], F32, "ct")
        sg = c.sb([128, 8, 2], F32, "sg")
        sc = c.sb([128, 8, 2], F32, "sc")
        abt = c.sb([128, NM], F32, "abt")
        res = c.sb([128, NM, 2], F32, "res")
        ps = c.ps([128, NM, 2], F32, "ps_m")
        p.dma("sp", "ct", lambda e: e.dma_start(out=ct[:], in_=cT[:, :, :]), writes=["ct"])
        p.dma("sp", "abt", lambda e: e.dma_start(out=abt[:], in_=ab[:, :]), writes=["abt"])
        p.act(lambda e: e.activation(sg[:], ct[:], AF.Sigmoid), reads=["ct"], writes=["sg"])
        p.dve(lambda e: e.tensor_tensor(sc[:], sg[:], ct[:], op=ALU.mult), reads=["sg", "ct"], writes=["sc"])
        for j in range(HC // 768):
            wt = Wt[j % 2]
            wr_ = f"Wt{j % 2}"
            p.dma("sp", wr_, lambda e, wt=wt, j=j: e.dma_start(out=wt[:], in_=awr[:, :, j * 768:(j + 1) * 768]),
                  writes=[wr_])
            for mm in range(6):
                m = j * 6 + mm
                for k in range(8):
                    p.pe(lambda e, wt=wt, mm=mm, m=m, k=k: e.matmul(ps[:, m, :], wt[:, k, mm * 128:(mm + 1) * 128],
                                                                   sc[:, k, :], start=(k == 0), stop=(k == 7)),
                         reads=[wr_, "sc"], writes=["ps_m"])
        p.dve(lambda e: e.tensor_tensor(res[:], ps[:], abt[:, :, None].to_broadcast([128, NM, 2]), op=ALU.add),
              reads=["ps_m", "abt"], writes=["res"])
        p.dma("sp", "res", lambda e: e.dma_start(out=mo[:, :, :], in_=res[:]), reads=["res"], writes=["mo"])
        p.emit(final_sems=["res"])
    return nc


GT = 512


def build_G(ntiles=SEQ // GT, dbg=99):
    nc = bass.Bass("TRN2", target_bir_lowering=False)
    S_ = ntiles * GT
    NB = S_ // 128
    with ExitStack() as es:
        c = Ctx(nc, es)
        p = c.p
        qkv = c.din("qkv", [3, 128, S_])
        cw = c.din("cw", [128, 12])
        gab = c.din("gab", [128, NB, 2])
        hp = c.din("hp", [128, 2])
        zin = c.din("z", [128, NB, 128])
        nwb = c.din("nwb", [128, 128])
        yo = c.dout("yo", [128, NB, 128], BF16)

        val = c.sb([128, 128], F32, "val")
        MU = c.sb([128, 128], F32, "MU")
        ML = c.sb([128, 128], F32, "ML")
        ID = c.sb([128, 128], F32, "ID")
        ones_bf = c.sb([128, 128], BF16, "ones_bf")
        ones_f = c.sb([128, 128], F32, "ones_f")
        epsc = c.sb([128, 1], F32, "epsc")
        cwt = c.sb([128, 12], F32, "cwt")
        hpt = c.sb([128, 2], F32, "hpt")
        nw = c.sb([128, 128], F32, "nw")
        p.dma("sp", "cwt", lambda e: e.dma_start(out=cwt[:], in_=cw[:, :]), writes=["cwt"])
        p.dma("sp", "hpt", lambda e: e.dma_start(out=hpt[:], in_=hp[:, :]), writes=["hpt"])
        p.dma("sp", "nw", lambda e: e.dma_start(out=nw[:], in_=nwb[:, :]), writes=["nw"])
        p.pool(lambda e: e.iota(val[:], [[1, 128]], base=0, channel_multiplier=-1,
                                allow_small_or_imprecise_dtypes=True), writes=["val"])
        p.dve(lambda e: e.tensor_scalar(MU[:], val[:], 0.0, None, op0=ALU.is_ge), reads=["val"], writes=["MU"])
        p.dve(lambda e: e.tensor_scalar(ML[:], val[:], 0.0, None, op0=ALU.is_lt), reads=["val"], writes=["ML"])
        p.dve(lambda e: e.tensor_scalar(ID[:], val[:], 0.0, None, op0=ALU.is_equal), reads=["val"], writes=["ID"])
        p.dve(lambda e: e.memset(MU[0:64, 64:128], 0.0), reads=["MU"], writes=["MU"])
        p.dve(lambda e: e.memset(ML[64:128, 0:64], 0.0), reads=["ML"], writes=["ML"])
        p.dve(lambda e: e.memset(ones_bf[:], 1.0), writes=["ones_bf"])
        p.dve(lambda e: e.memset(ones_f[:], 1.0), writes=["ones_f"])
        p.dve(lambda e: e.memset(epsc[:], EPS), writes=["epsc"])

        pb = [c.ps([128, 4, 128], F32, f"pb{i}") for i in range(8)]

        gt_ = c.sb([128, NB, 2], F32, "gt_")
        g = c.sb([128, NB], F32, "g")
        beta = c.sb([128, NB], F32, "beta")
        nbeta = c.sb([128, NB], F32, "nbeta")
        tA = c.sb([128, NB], F32, "tA")
        eA = c.sb([128, 1], F32, "eA")
        gh = [c.sb([128, NB], F32, f"gh{a}") for a in range(2)]
        egc = c.sb([128, NB], F32, "egc")
        erem = c.sb([128, NB], F32, "erem")
        egl = [c.sb([128, NB], F32, f"egl{a}") for a in range(2)]
        sckb = c.sb([128, NB], F32, "sckb")
        p.dma("sp", "gt_", lambda e: e.dma_start(out=gt_[:], in_=gab[:, :, :]), writes=["gt_"])
        p.act(lambda e: e.activation(tA[:], gt_[:, :, 0], AF.Exp, bias=hpt[:, 1:2]), reads=["gt_", "hpt"], writes=["tA"])
        p.act(lambda e: e.activation(tA[:], tA[:], AF.Ln, bias=1.0), reads=["tA"], writes=["tA"])
        p.act(lambda e: e.activation(eA[:], hpt[:, 0:1], AF.Exp), reads=["hpt"], writes=["eA"])
        p.dve(lambda e: e.tensor_scalar(g[:], tA[:], eA[:, 0:1], -1.0, op0=ALU.mult, op1=ALU.mult),
              reads=["tA", "eA"], writes=["g"])
        p.act(lambda e: e.activation(beta[:], gt_[:, :, 1], AF.Exp, scale=-1.0), reads=["gt_"], writes=["beta"])
        p.dve(lambda e: e.tensor_scalar(beta[:], beta[:], 1.0, None, op0=ALU.add), reads=["beta"], writes=["beta"])
        p.dve(lambda e: e.reciprocal(beta[:], beta[:]), reads=["beta"], writes=["beta"])
        p.dve(lambda e: e.tensor_scalar(nbeta[:], beta[:], -1.0, None, op0=ALU.mult), reads=["beta"], writes=["nbeta"])
        for a in range(2):
            p.dve(lambda e, a=a: e.memset(gh[a][:], 0.0), writes=[f"gh{a}"])
            p.dve(lambda e, a=a: e.tensor_copy(gh[a][64 * a:64 * a + 64, :], g[64 * a:64 * a + 64, :]),
                  reads=["g", f"gh{a}"], writes=[f"gh{a}"])
        NBC = min(NB, 64)
        p.pe(lambda e: e.matmul(pb[0][:, 0, 0:NB], MU[:], g[:], start=True, stop=True), reads=["MU", "g"], writes=["pb0"])
        p.act(lambda e: e.activation(egc[:], pb[0][:, 0, 0:NB], AF.Exp), reads=["pb0"], writes=["egc"])
        p.pe(lambda e: e.matmul(pb[1][:, 0, 0:NB], ML[:], g[:], start=True, stop=True), reads=["ML", "g"], writes=["pb1"])
        p.act(lambda e: e.activation(erem[:], pb[1][:, 0, 0:NB], AF.Exp), reads=["pb1"], writes=["erem"])
        for a in range(2):
            p.pe(lambda e, a=a: e.matmul(pb[2 + a][:, 0, 0:NB], ones_f[:], gh[a][:], start=True, stop=True),
                 reads=["ones_f", f"gh{a}"], writes=[f"pb{2 + a}"])
            p.act(lambda e, a=a: e.activation(egl[a][:], pb[2 + a][:, 0, 0:NB], AF.Exp), reads=[f"pb{2 + a}"],
                  writes=[f"egl{a}"])
        p.dve(lambda e: e.tensor_tensor(sckb[:], beta[:], egc[:], op=ALU.mult), reads=["beta", "egc"], writes=["sckb"])

        xin = [c.sb([128, GT + 3], F32, f"xin{i}") for i in range(3)]
        acc = [c.sb([128, GT], F32, f"acc{i}") for i in range(3)]
        sil = [c.sb([128, GT], F32, f"sil{i}") for i in range(3)]
        sqb = [c.sb([128, GT], BF16, f"sqb{i}") for i in range(2)]
        rn = [c.sb([128, GT], F32, f"rn{i}") for i in range(2)]
        qn = c.sb([128, 4, 128], F32, "qn")
        kn = c.sb([128, 4, 128], F32, "kn")
        kbg = c.sb([128, 4, 128], BF16, "kbg")
        kdec = c.sb([128, 4, 128], BF16, "kdec")
        vb = c.sb([128, 4, 128], BF16, "vb")
        gU2 = c.sb([128, 4, 128], F32, "gU2")
        gU1 = c.sb([128, 4, 128], F32, "gU1")
        DecL = c.sb([128, 4, 128], F32, "DecL")
        DecT = c.sb([128, 4, 128], F32, "DecT")
        Xs = [c.sb([128, 4, 128], F32, f"Xs{i}") for i in range(2)]
        Ys = [c.sb([128, 4, 128], F32, f"Ys{i}") for i in range(2)]
        Q = c.sb([128, 4, 128], F32, "Q")
        TTb = c.sb([128, 4, 128], BF16, "TTb")
        qkT = c.sb([128, 4, 128], BF16, "qkT")
        u_sb = c.sb([128, 4, 128], F32, "u_sb")
        wT_sb = c.sb([128, 4, 128], F32, "wT_sb")
        o_sb = c.sb([128, 4, 128], F32, "o_sb")
        otmp = c.sb([128, 128], F32, "otmp")
        vnb = c.sb([128, 128], BF16, "vnb")
        S = c.sb([128, 128], F32, "S")
        zt = c.sb([128, 4, 128], F32, "zt")
        zs = c.sb([128, 4, 128], F32, "zs")
        junk = c.sb([128, 128], F32, "junk")
        ss = c.sb([128, 4], F32, "ss")
        yt = c.sb([128, 4, 128], F32, "yt")
        ytb = c.sb([128, 4, 128], BF16, "ytb")
        p.dve(lambda e: e.memset(S[:], 0.0), writes=["S"])

        def bc4(ap2):
            return ap2.unsqueeze(2).to_broadcast([128, 4, 128])

        def bcm(ap2):
            return ap2.unsqueeze(1).to_broadcast([128, 4, 128])

        for ti in range(ntiles):
            t0 = ti * GT
            b0 = ti * 4
            bs = slice(b0, b0 + 4)
            for i in range(3):
                if ti == 0:
                    p.dve(lambda e, i=i: e.memset(xin[i][:, 0:3], 0.0), writes=[f"xin{i}"])
                    p.dma("sp", f"xin{i}", lambda e, i=i: e.dma_start(out=xin[i][:, 3:GT + 3], in_=qkv[i, :, 0:GT]),
                          writes=[f"xin{i}"])
                else:
                    p.dma("sp", f"xin{i}", lambda e, i=i, t0=t0: e.dma_start(out=xin[i][:], in_=qkv[i, :, t0 - 3:t0 + GT]),
                          writes=[f"xin{i}"])
            p.dma("sp", "zt", lambda e, bs=bs: e.dma_start(out=zt[:], in_=zin[:, bs, :]), writes=["zt"])
            if dbg < 2:
                p.dma("sp", "yt", lambda e, bs=bs: e.dma_start(out=yo[:, bs, :], in_=ytb[:]), reads=["ytb"], writes=[("yo", ti)])
                continue
            for i in range(3):
                p.dve(lambda e, i=i: e.tensor_scalar(acc[i][:], xin[i][:, 3:GT + 3], cwt[:, 4 * i + 3:4 * i + 4], None,
                                                     op0=ALU.mult), reads=[f"xin{i}", "cwt"], writes=[f"acc{i}"])
                for j in range(3):
                    p.dve(lambda e, i=i, j=j: e.scalar_tensor_tensor(
                        acc[i][:], xin[i][:, j:GT + j], cwt[:, 4 * i + j:4 * i + j + 1], acc[i][:], op0=ALU.mult,
                        op1=ALU.add), reads=[f"xin{i}", "cwt", f"acc{i}"], writes=[f"acc{i}"])
                p.act(lambda e, i=i: e.activation(sil[i][:], acc[i][:], AF.Silu), reads=[f"acc{i}"], writes=[f"sil{i}"])
            for i in range(2):
                p.act(lambda e, i=i: e.activation(sqb[i][:], sil[i][:], AF.Square), reads=[f"sil{i}"], writes=[f"sqb{i}"])
                p.pe(lambda e, i=i: e.matmul(pb[i][:].rearrange("p a b -> p (a b)"), ones_bf[:], sqb[i][:], start=True,
                                             stop=True), reads=["ones_bf", f"sqb{i}"], writes=[f"pb{i}"])
                p.act(lambda e, i=i: e.activation(rn[i][:], pb[i][:].rearrange("p a b -> p (a b)"), AF.Sqrt,
                                                  bias=epsc[:, 0:1]), reads=[f"pb{i}", "epsc"], writes=[f"rn{i}"])
                p.dve(lambda e, i=i: e.reciprocal(rn[i][:], rn[i][:]), reads=[f"rn{i}"], writes=[f"rn{i}"])
            p.dve(lambda e: e.scalar_tensor_tensor(qn[:].rearrange("p a b -> p (a b)"), sil[0][:], float(128 ** -0.5),
                                                   rn[0][:], op0=ALU.mult, op1=ALU.mult),
                  reads=["sil0", "rn0"], writes=["qn"])
            p.dve(lambda e: e.tensor_tensor(kn[:].rearrange("p a b -> p (a b)"), sil[1][:], rn[1][:], op=ALU.mult),
                  reads=["sil1", "rn1"], writes=["kn"])
            if dbg < 3:
                p.dma("sp", "yt", lambda e, bs=bs: e.dma_start(out=yo[:, bs, :], in_=ytb[:]), reads=["ytb"], writes=[("yo", ti)])
                continue
            for pr in range(4):
                p.pe(lambda e, pr=pr: e.transpose(pb[2][:, pr, :], kn[:, pr, :], ID[:]), reads=["kn", "ID"], writes=["pb2"])
            for pr in range(4):
                p.pe(lambda e, pr=pr: e.transpose(pb[3][:, pr, :], sil[2][:, pr * 128:(pr + 1) * 128], ID[:]),
                     reads=["sil2", "ID"], writes=["pb3"])
            p.dve(lambda e, bs=bs: e.tensor_tensor(kbg[:], pb[2][:], bc4(sckb[:, bs]), op=ALU.mult),
                  reads=["pb2", "sckb"], writes=["kbg"])
            p.dve(lambda e, bs=bs: e.tensor_tensor(kdec[:], pb[2][:], bc4(erem[:, bs]), op=ALU.mult),
                  reads=["pb2", "erem"], writes=["kdec"])
            p.dve(lambda e, bs=bs: e.tensor_tensor(vb[:], pb[3][:], bc4(beta[:, bs]), op=ALU.mult),
                  reads=["pb3", "beta"], writes=["vb"])
            if dbg < 4:
                p.dma("sp", "yt", lambda e, bs=bs: e.dma_start(out=yo[:, bs, :], in_=ytb[:]), reads=["ytb"], writes=[("yo", ti)])
                continue
            p.dve(lambda e, bs=bs: e.tensor_tensor(gU2[:], bcm(ML[:]), bc4(g[:, bs]), op=ALU.mult),
                  reads=["ML", "g"], writes=["gU2"])
            p.dve(lambda e, bs=bs: e.tensor_tensor(gU1[:], bcm(MU[:]), bc4(g[:, bs]), op=ALU.mult),
                  reads=["MU", "g"], writes=["gU1"])
            p.pe(lambda e: e.matmul(pb[4][:].rearrange("p a b -> p (a b)"), MU[:], gU2[:].rearrange("p a b -> p (a b)"),
                                    start=True, stop=True), reads=["MU", "gU2"], writes=["pb4"])
            p.pe(lambda e: e.matmul(pb[5][:].rearrange("p a b -> p (a b)"), ML[:], gU1[:].rearrange("p a b -> p (a b)"),
                                    start=True, stop=True), reads=["ML", "gU1"], writes=["pb5"])
            p.act(lambda e: e.activation(DecL[:], pb[4][:], AF.Exp), reads=["pb4"], writes=["DecL"])
            p.act(lambda e: e.activation(DecT[:], pb[5][:], AF.Exp), reads=["pb5"], writes=["DecT"])
            p.dve(lambda e: e.tensor_tensor(DecL[:], DecL[:], bcm(ML[:]), op=ALU.mult), reads=["DecL", "ML"], writes=["DecL"])
            p.dve(lambda e: e.tensor_tensor(DecT[:], DecT[:], bcm(MU[:]), op=ALU.mult), reads=["DecT", "MU"], writes=["DecT"])
            if dbg < 5:
                p.dma("sp", "yt", lambda e, bs=bs: e.dma_start(out=yo[:, bs, :], in_=ytb[:]), reads=["ytb"], writes=[("yo", ti)])
                continue
            for pr in range(4):
                p.pe(lambda e, pr=pr: e.matmul(pb[0][:, pr, :], kn[:, pr, :], kn[:, pr, :], start=True, stop=True),
                     reads=["kn"], writes=["pb0"])
            for pr in range(4):
                p.pe(lambda e, pr=pr: e.matmul(pb[1][:, pr, :], kn[:, pr, :], qn[:, pr, :], start=True, stop=True),
                     reads=["kn", "qn"], writes=["pb1"])
            X, Y = Xs[0], Ys[0]
            p.dve(lambda e: e.tensor_tensor(X[:], pb[0][:], DecL[:], op=ALU.mult), reads=["pb0", "DecL"], writes=["Xs0"])
            p.dve(lambda e, bs=bs: e.tensor_tensor(X[:], X[:], bc4(nbeta[:, bs]), op=ALU.mult),
                  reads=["Xs0", "nbeta"], writes=["Xs0"])
            p.dve(lambda e: e.tensor_tensor(qkT[:], pb[1][:], DecT[:], op=ALU.mult), reads=["pb1", "DecT"], writes=["qkT"])
            for pr in range(4):
                p.pe(lambda e, pr=pr: e.transpose(pb[2][:, pr, :], Xs[0][:, pr, :], ID[:]), reads=["Xs0", "ID"],
                     writes=["pb2"])
            p.act(lambda e: e.copy(Ys[0][:], pb[2][:]), reads=["pb2"], writes=["Ys0"])
            p.dve(lambda e: e.tensor_tensor(Q[:], pb[2][:], bcm(ID[:]), op=ALU.add), reads=["pb2", "ID"], writes=["Q"])
            if dbg < 6:
                p.dma("sp", "yt", lambda e, bs=bs: e.dma_start(out=yo[:, bs, :], in_=ytb[:]), reads=["ytb"], writes=[("yo", ti)])
                continue
            for lvl in range(1, 6):
                cur, nxt = (lvl - 1) % 2, lvl % 2
                for pr in range(4):
                    p.pe(lambda e, pr=pr, cur=cur: e.matmul(pb[3][:, pr, :], Ys[cur][:, pr, :], Xs[cur][:, pr, :],
                                                          start=True, stop=True),
                         reads=[f"Ys{cur}", f"Xs{cur}"], writes=["pb3"])
                if lvl < 5:
                    for pr in range(4):
                        p.pe(lambda e, pr=pr, cur=cur: e.matmul(pb[4][:, pr, :], Xs[cur][:, pr, :], Ys[cur][:, pr, :],
                                                              start=True, stop=True),
                             reads=[f"Ys{cur}", f"Xs{cur}"], writes=["pb4"])
                p.act(lambda e, nxt=nxt: e.copy(Xs[nxt][:], pb[3][:]), reads=["pb3"], writes=[f"Xs{nxt}"])
                if lvl < 5:
                    p.dve(lambda e, nxt=nxt: e.tensor_copy(Ys[nxt][:], pb[4][:]), reads=["pb4"], writes=[f"Ys{nxt}"])
                for pr in range(4):
                    p.pe(lambda e, pr=pr, nxt=nxt: e.matmul(pb[5][:, pr, :], Xs[nxt][:, pr, :], Q[:, pr, :], start=True,
                                                          stop=True), reads=[f"Xs{nxt}", "Q"], writes=["pb5"])
                p.dve(lambda e: e.tensor_tensor(Q[:], Q[:], pb[5][:], op=ALU.add), reads=["Q", "pb5"], writes=["Q"])
            p.act(lambda e: e.copy(TTb[:], Q[:]), reads=["Q"], writes=["TTb"])
            if dbg < 7:
                p.dma("sp", "yt", lambda e, bs=bs: e.dma_start(out=yo[:, bs, :], in_=ytb[:]), reads=["ytb"], writes=[("yo", ti)])
                continue
            for pr in range(4):
                p.pe(lambda e, pr=pr: e.matmul(pb[6][:, pr, :], TTb[:, pr, :], vb[:, pr, :], start=True, stop=True),
                     reads=["TTb", "vb"], writes=["pb6"])
            for pr in range(4):
                p.pe(lambda e, pr=pr: e.matmul(pb[7][:, pr, :], kbg[:, pr, :], TTb[:, pr, :], start=True, stop=True),
                     reads=["TTb", "kbg"], writes=["pb7"])
            p.act(lambda e: e.copy(u_sb[:], pb[6][:]), reads=["pb6"], writes=["u_sb"])
            p.dve(lambda e: e.tensor_copy(wT_sb[:], pb[7][:]), reads=["pb7"], writes=["wT_sb"])
            if dbg < 8:
                p.dma("sp", "yt", lambda e, bs=bs: e.dma_start(out=yo[:, bs, :], in_=ytb[:]), reads=["ytb"], writes=[("yo", ti)])
                continue
            for pr in range(4):
                blk = b0 + pr
                for a in range(2):
                    rs = slice(64 * a, 64 * a + 64)
                    cs = slice(64 * a, 64 * a + 64)
                    p.pe(lambda e, pr=pr, rs=rs, cs=cs: e.matmul(pb[0][rs, 0, :], wT_sb[:, pr, cs], S[:], start=True,
                                                               stop=True), reads=["wT_sb", "S"], writes=["pb0"])
                    p.pe(lambda e, pr=pr, rs=rs, cs=cs: e.matmul(pb[1][rs, 0, :], qn[:, pr, cs], S[:], start=True,
                                                               stop=True), reads=["qn", "S"], writes=["pb1"])
                    p.dve(lambda e, pr=pr, rs=rs: e.tensor_tensor(vnb[rs, :], u_sb[rs, pr, :], pb[0][rs, 0, :],
                                                                 op=ALU.subtract), reads=["u_sb", "pb0"], writes=["vnb"])
                    p.act(lambda e, rs=rs, blk=blk: e.activation(otmp[rs, :], pb[1][rs, 0, :], AF.Copy,
                                                                 scale=egc[rs, blk:blk + 1]),
                          reads=["pb1", "egc"], writes=["otmp"])
                    p.pe(lambda e, pr=pr, rs=rs, cs=cs: e.matmul(pb[2][rs, 0, :], qkT[rs, pr, cs], vnb[rs, :], start=True,
                                                               stop=True), reads=["qkT", "vnb"], writes=["pb2"])
                    p.pe(lambda e, pr=pr, rs=rs: e.matmul(pb[3][:, 0, :], kdec[rs, pr, :], vnb[rs, :], start=True,
                                                        stop=True), reads=["kdec", "vnb"], writes=["pb3"])
                    p.dve(lambda e, pr=pr, rs=rs: e.tensor_tensor(o_sb[rs, pr, :], otmp[rs, :], pb[2][rs, 0, :],
                                                                 op=ALU.add), reads=["otmp", "pb2"], writes=["o_sb"])
                    p.dve(lambda e, a=a, blk=blk: e.scalar_tensor_tensor(S[:], S[:], egl[a][:, blk:blk + 1],
                                                                        pb[3][:, 0, :], op0=ALU.mult, op1=ALU.add),
                          reads=["S", f"egl{a}", "pb3"], writes=["S"])
            if dbg < 9:
                p.dma("sp", "yt", lambda e, bs=bs: e.dma_start(out=yo[:, bs, :], in_=ytb[:]), reads=["ytb"], writes=[("yo", ti)])
                continue
            for pr in range(4):
                p.act(lambda e, pr=pr: e.activation(junk[:], o_sb[:, pr, :], AF.Square, accum_out=ss[:, pr:pr + 1]),
                      reads=["o_sb"], writes=["junk", ("ss", pr)])
            p.act(lambda e: e.activation(ss[:], ss[:], AF.Sqrt, bias=epsc[:, 0:1], scale=1.0 / 128.0),
                  reads=[("ss", i) for i in range(4)] + ["epsc"], writes=[("ss", i) for i in range(4)])
            p.dve(lambda e: e.reciprocal(ss[:], ss[:]), reads=[("ss", i) for i in range(4)],
                  writes=[("ss", i) for i in range(4)])
            p.act(lambda e: e.activation(zs[:], zt[:], AF.Silu), reads=["zt"], writes=["zs"])
            p.dve(lambda e: e.tensor_tensor(yt[:], o_sb[:], bc4(ss[:, :]), op=ALU.mult),
                  reads=["o_sb"] + [("ss", i) for i in range(4)], writes=["yt"])
            p.dve(lambda e: e.tensor_tensor(yt[:], yt[:], bcm(nw[:]), op=ALU.mult), reads=["yt", "nw"], writes=["yt"])
            p.dve(lambda e: e.tensor_tensor(ytb[:], yt[:], zs[:], op=ALU.mult), reads=["yt", "zs"], writes=["ytb"])
            p.dma("sp", "yt", lambda e, bs=bs: e.dma_start(out=yo[:, bs, :], in_=ytb[:]), reads=["ytb"], writes=[("yo", ti)])
        p.emit(final_sems=["yt"])
    return nc


SCALE = 0.125
NEG = -30000.0


def build_B(qblocks=tuple(range(16)), dbg=99):
    nc = bass.Bass("TRN2", target_bir_lowering=False)
    with ExitStack() as es:
        c = Ctx(nc, es)
        p = c.p
        aq_d = c.din("aq", [2, 128, T], BF16)
        ak_d = c.din("ak", [2, 128, T + 128], BF16)
        av_d = c.din("av", [128, 17, 128], BF16)
        sk_d = c.din("sinks", [128, 4])
        offs_d = c.din("offs", [128, 4])
        bqr_d = c.din("bqr", [2, 128, T], BF16)
        bq_d = c.din("bq", [2, 128, T], BF16)
        bkw_d = c.din("bkw", [128, T + 512], BF16)
        bvw_d = c.din("bvw", [128, 20, 64], BF16)
        bks_d = c.din("bks", [128, SEQ], BF16)
        bvs_d = c.din("bvs", [128, 64, 64], BF16)
        cmp_d = c.din("cmpT", [128, SEQ], BF16)
        bg_d = c.din("bg", [12, T])
        w1k_d = c.din("w1k", [2048, 256])
        w1v_d = c.din("w1v", [2048, 256])
        w2k_d = c.din("w2k", [256, 64])
        w2v_d = c.din("w2v", [256, 64])
        pek_d = c.din("pek", [128, 16])
        pev_d = c.din("pev", [128, 16])
        oc_d = c.dout("oc", [4, 128, T], BF16)

        pb = [c.ps([128, 4, 128], F32, f"pb{i}") for i in range(8)]
        PB2 = ["pb2"]

        aq = c.sb([128, 2, T], BF16, "aq")
        ak = c.sb([128, 2, T + 128], BF16, "ak")
        avg = c.sb([128, 17, 2, 128], BF16, "avg")
        bqr = c.sb([128, 2, T], BF16, "bqr")
        bq = c.sb([128, 2, T], BF16, "bq")
        bkw = c.sb([128, T + 512], BF16, "bkw")
        bvwg = c.sb([128, 20, 128], BF16, "bvwg")
        bks = c.sb([128, SEQ], BF16, "bks")
        bvsg = c.sb([128, 64, 128], BF16, "bvsg")
        KKk = c.sb([128, SEQ], BF16, "KKk")
        KKv = c.sb([128, SEQ], BF16, "KKv")
        sk = c.sb([128, 4], F32, "sk")
        offs = c.sb([128, 4], F32, "offs")
        bg = c.sb([12, T], F32, "bg")
        w1k = c.sb([128, 16, 256], BF16, "w1k")
        w1v = c.sb([128, 16, 256], BF16, "w1v")
        w2k = c.sb([128, 2, 128], BF16, "w2k")
        w2v = c.sb([128, 2, 64], BF16, "w2v")
        pek = c.sb([128, 16], BF16, "pek")
        pev = c.sb([128, 16], BF16, "pev")
        for j in range(2):
            p.dma("sp", "aq", lambda e, j=j: e.dma_start(out=aq[:, j, :], in_=aq_d[j, :, :]), writes=["aq"])
            p.dma("sp", "ak", lambda e, j=j: e.dma_start(out=ak[:, j, :], in_=ak_d[j, :, :]), writes=["ak"])
            p.dma("sp", "bqr", lambda e, j=j: e.dma_start(out=bqr[:, j, :], in_=bqr_d[j, :, :]), writes=["bqr"])
            p.dma("sp", "bq", lambda e, j=j: e.dma_start(out=bq[:, j, :], in_=bq_d[j, :, :]), writes=["bq"])
        p.dve(lambda e: e.memset(avg[:], 1.0), writes=["avg"])
        p.dve(lambda e: e.memset(bvwg[:], 1.0), writes=["bvwg"])
        p.dve(lambda e: e.memset(bvsg[:], 1.0), writes=["bvsg"])
        for j in range(2):
            p.dma("sp", "avg", lambda e, j=j: e.dma_start(out=avg[:, :, j, 0:64], in_=av_d[:, :, 64 * j:64 * j + 64]),
                  reads=["avg"], writes=["avg"])
        p.dma("sp", "bvwg", lambda e: e.dma_start(out=bvwg[:, :, 0:64], in_=bvw_d[:, :, :]), reads=["bvwg"], writes=["bvwg"])
        p.dma("sp", "bvsg", lambda e: e.dma_start(out=bvsg[:, :, 0:64], in_=bvs_d[:, :, :]), reads=["bvsg"], writes=["bvsg"])
        p.dma("sp", "bkw", lambda e: e.dma_start(out=bkw[:], in_=bkw_d[:, :]), writes=["bkw"])
        p.dma("sp", "bks", lambda e: e.dma_start(out=bks[:], in_=bks_d[:, :]), writes=["bks"])
        p.dve(lambda e: e.memset(KKk[:, SEQ - 8:SEQ], 0.0), writes=["KKk"])
        p.dve(lambda e: e.memset(KKv[:, SEQ - 8:SEQ], 0.0), writes=["KKv"])
        p.dma("sp", "KKk", lambda e: e.dma_start(out=KKk[0:64, :], in_=cmp_d[0:64, :]), reads=["KKk"], writes=["KKk"])
        p.dma("sp", "KKk", lambda e: e.dma_start(out=KKk[64:128, 0:SEQ - 1], in_=cmp_d[0:64, 1:SEQ]), reads=["KKk"],
              writes=["KKk"])
        p.dma("sp", "KKv", lambda e: e.dma_start(out=KKv[0:64, :], in_=cmp_d[64:128, :]), reads=["KKv"], writes=["KKv"])
        p.dma("sp", "KKv", lambda e: e.dma_start(out=KKv[64:128, 0:SEQ - 1], in_=cmp_d[64:128, 1:SEQ]), reads=["KKv"],
              writes=["KKv"])
        p.dma("sp", "sk", lambda e: e.dma_start(out=sk[:], in_=sk_d[:, :]), writes=["sk"])
        p.dma("sp", "offs", lambda e: e.dma_start(out=offs[:], in_=offs_d[:, :]), writes=["offs"])
        p.dma("sp", "bg", lambda e: e.dma_start(out=bg[:], in_=bg_d[:, :]), writes=["bg"])
        p.dma("pool", "w1k", lambda e: e.dma_start(out=w1k[:], in_=w1k_d.rearrange("(m p) c -> p m c", p=128)), writes=["w1k"])
        p.dma("pool", "w1v", lambda e: e.dma_start(out=w1v[:], in_=w1v_d.rearrange("(m p) c -> p m c", p=128)), writes=["w1v"])
        for dup in range(2):
            p.dma("pool", "w2k", lambda e, dup=dup: e.dma_start(out=w2k[:, :, 64 * dup:64 * dup + 64],
                                                              in_=w2k_d.rearrange("(m p) c -> p m c", p=128)), writes=["w2k"])
        p.dma("pool", "w2v", lambda e: e.dma_start(out=w2v[:], in_=w2v_d.rearrange("(m p) c -> p m c", p=128)), writes=["w2v"])
        p.dma("pool", "pek", lambda e: e.dma_start(out=pek[:], in_=pek_d[:, :]), writes=["pek"])
        p.dma("pool", "pev", lambda e: e.dma_start(out=pev[:], in_=pev_d[:, :]), writes=["pev"])

        val = c.sb([128, 128], F32, "val")
        MUq = c.sb([128, 128], BF16, "MUq")
        MLq = c.sb([128, 128], BF16, "MLq")
        M2 = c.sb([128, 2, 128], BF16, "M2")
        IDb = c.sb([128, 128], BF16, "IDb")
        ones_b = c.sb([128, 128], BF16, "ones_b")
        p.pool(lambda e: e.iota(val[:], [[1, 128]], base=0, channel_multiplier=-1, allow_small_or_imprecise_dtypes=True),
               writes=["val"])
        p.dve(lambda e: e.tensor_scalar(MUq[:], val[:], 0.0, None, op0=ALU.is_ge), reads=["val"], writes=["MUq"])
        p.dve(lambda e: e.tensor_scalar(MLq[:], val[:], 0.0, None, op0=ALU.is_lt), reads=["val"], writes=["MLq"])
        p.dve(lambda e: e.tensor_scalar(IDb[:], val[:], 0.0, None, op0=ALU.is_equal), reads=["val"], writes=["IDb"])
        p.dve(lambda e: e.tensor_copy(M2[:, 0, :], MLq[:]), reads=["MLq"], writes=["M2"])
        p.dve(lambda e: e.tensor_copy(M2[:, 1, :], MUq[:]), reads=["MUq", "M2"], writes=["M2"])
        p.dve(lambda e: e.memset(ones_b[:], 1.0), writes=["ones_b"])
        Et = c.sb([128, 16, 2, 64], F32, "Et")
        E = c.sb([128, 64, 128], BF16, "E")
        for qtr in range(4):
            p.pool(lambda e, qtr=qtr: e.iota(Et[:], [[-2, 16], [-1, 2], [0, 64]], base=-32 * qtr, channel_multiplier=1,
                                             allow_small_or_imprecise_dtypes=True), reads=["Et"], writes=["Et"])
            p.dve(lambda e, qtr=qtr: e.tensor_scalar(E[:, 16 * qtr:16 * qtr + 16, :].rearrange("p a (b c) -> p a b c", b=2),
                                                     Et[:], 0.0, None, op0=ALU.is_equal), reads=["Et"], writes=["E", "Et"])
        Gt = c.sb([128, 4, 128], F32, "Gt")
        Gm = c.sb([128, 4, 128], F32, "Gm")
        Gm2 = Gt
        p.pool(lambda e: e.iota(Gt[:], [[128, 4], [-4, 128]], base=0, channel_multiplier=1,
                                allow_small_or_imprecise_dtypes=True), writes=["Gt"])
        p.dve(lambda e: e.tensor_scalar(Gm[:], Gt[:], -1.0, None, op0=ALU.is_ge), reads=["Gt"], writes=["Gm"])
        p.dve(lambda e: e.tensor_scalar(Gm2[:], Gt[:], 3.0, None, op0=ALU.is_le), reads=["Gt", "Gm"], writes=["Gm2", "Gt"])
        p.dve(lambda e: e.tensor_tensor(Gm[:], Gm[:], Gm2[:], op=ALU.mult), reads=["Gm", "Gm2"], writes=["Gm"])
        valc = c.sb([128, 2, 128], F32, "valc")
        p.pool(lambda e: e.iota(valc[:], [[2048, 2], [-1, 128]], base=4096, channel_multiplier=16,
                                allow_small_or_imprecise_dtypes=True), writes=["valc"])
        nidx = c.sb([128, 4], F32, "nidx")
        biasn = c.sb([128, 4], F32, "biasn")
        bt2 = c.sb([128, 4], F32, "bt2")
        p.pool(lambda e: e.iota(nidx[:], [[128, 4]], base=0, channel_multiplier=1, allow_small_or_imprecise_dtypes=True),
               writes=["nidx"])
        p.dve(lambda e: e.tensor_scalar(biasn[:], nidx[:], offs[:, 1:2], NEG, op0=ALU.is_lt, op1=ALU.mult),
              reads=["nidx", "offs"], writes=["biasn"])
        p.dve(lambda e: e.tensor_scalar(bt2[:], nidx[:], 510.5, NEG, op0=ALU.is_gt, op1=ALU.mult),
              reads=["nidx"], writes=["bt2"])
        p.dve(lambda e: e.tensor_tensor(biasn[:], biasn[:], bt2[:], op=ALU.add), reads=["biasn", "bt2"], writes=["biasn"])
        bbi = c.sb([128, 128], F32, "bbi")
        validblk = c.sb([128, 128], F32, "validblk")
        f0m = c.sb([128, 128], F32, "f0m")
        p.pool(lambda e: e.iota(bbi[:], [[1, 128]], base=0, channel_multiplier=0, allow_small_or_imprecise_dtypes=True),
               writes=["bbi"])
        p.dve(lambda e: e.tensor_scalar(validblk[:], bbi[:], offs[:, 2:3], None, op0=ALU.is_ge), reads=["bbi", "offs"],
              writes=["validblk"])
        p.dve(lambda e: e.tensor_scalar(f0m[:], bbi[:], offs[:, 2:3], 2e9, op0=ALU.is_equal, op1=ALU.mult),
              reads=["bbi", "offs"], writes=["f0m"])
        p.dve(lambda e: e.tensor_scalar(f0m[:], f0m[:], -1e9, None, op0=ALU.add), reads=["f0m"], writes=["f0m"])
        esk = c.sb([128, 4], F32, "esk")
        p.act(lambda e: e.activation(esk[:], sk[:], AF.Exp), reads=["sk"], writes=["esk"])
        sg = bg
        p.act(lambda e: e.activation(sg[:], bg[:], AF.Exp, scale=-1.0), reads=["bg"], writes=["sg", "bg"])
        p.dve(lambda e: e.tensor_scalar(sg[:], sg[:], 1.0, None, op0=ALU.add), reads=["sg"], writes=["sg"])
        p.dve(lambda e: e.reciprocal(sg[:], sg[:]), reads=["sg"], writes=["sg"])
        selg = c.sb([12, 12, 64], F32, "selg")
        p.pool(lambda e: e.iota(selg[:], [[-1, 12], [0, 64]], base=0, channel_multiplier=1,
                                allow_small_or_imprecise_dtypes=True), writes=["selg"])
        p.dve(lambda e: e.tensor_scalar(selg[:], selg[:], 0.0, None, op0=ALU.is_equal), reads=["selg"], writes=["selg"])

        hid = [c.sb([128, 2, 512], BF16, f"hid{i}") for i in range(2)]
        bh = c.sb([128, 4], F32, "bh")
        kcT = c.sb([128, 512], BF16, "kcT")
        vcg = c.sb([128, 4, 128], BF16, "vcg")
        p.dve(lambda e: e.memset(vcg[:], 1.0), writes=["vcg"])
        for wi, (w1, KK, pe_) in enumerate([(w1k, KKk, pek), (w1v, KKv, pev)]):
            w1n, kkn, pen = ["w1k", "w1v"][wi], ["KKk", "KKv"][wi], ["pek", "pev"][wi]
            p.dve(lambda e, wi=wi: e.memset(hid[wi][:], 0.0), writes=[f"hid{wi}"])
            for hh in range(2):
                bk = 2 * wi + hh
                for m in range(16):
                    p.pe(lambda e, bk=bk, m=m, hh=hh, w1=w1, pe_=pe_: e.matmul(
                        pb[4][:, bk, 0:1], w1[:, m, hh * 128:(hh + 1) * 128], pe_[:, m:m + 1], start=(m == 0),
                        stop=(m == 15)), reads=[w1n, pen], writes=["pb4"])
            p.dve(lambda e, wi=wi: e.tensor_copy(bh[:, 2 * wi:2 * wi + 2], pb[4][:, 2 * wi:2 * wi + 2, 0]),
                  reads=["pb4"], writes=["bh"])
            for hh in range(2):
                bk = hh
                for m in range(16):
                    p.pe(lambda e, bk=bk, m=m, hh=hh, w1=w1, KK=KK: e.matmul(
                        pb[bk][:].rearrange("p a b -> p (a b)")[:, 0:511], w1[:, m, hh * 128:(hh + 1) * 128],
                        KK[:, 2 * m:2 * m + 16 * 510 + 1:16], start=(m == 0), stop=(m == 15)),
                        reads=[w1n, kkn], writes=[f"pb{bk}"])
                p.act(lambda e, bk=bk, hh=hh, wi=wi: e.activation(
                    hid[wi][:, hh, 0:511], pb[bk][:].rearrange("p a b -> p (a b)")[:, 0:511], AF.Silu,
                    bias=bh[:, 2 * wi + hh:2 * wi + hh + 1]), reads=[f"pb{bk}", "bh"], writes=[f"hid{wi}"])
        for hh in range(2):
            p.pe(lambda e, hh=hh: e.matmul(pb[2][:].rearrange("p a b -> p (a b)"), w2k[:, hh, :], hid[0][:, hh, :],
                                           start=(hh == 0), stop=(hh == 1)), reads=["w2k", "hid0"], writes=PB2)
        p.act(lambda e: e.copy(kcT[:], pb[2][:].rearrange("p a b -> p (a b)")), reads=PB2, writes=["kcT"])
        for cc in range(4):
            for hh in range(2):
                p.pe(lambda e, cc=cc, hh=hh: e.matmul(pb[3][:, cc, 0:64], hid[1][:, hh, cc * 128:(cc + 1) * 128],
                                                      w2v[:, hh, :], start=(hh == 0), stop=(hh == 1)),
                     reads=["w2v", "hid1"], writes=["pb3"])
        p.act(lambda e: e.copy(vcg[:, :, 0:64], pb[3][:, :, 0:64]), reads=["pb3", "vcg"], writes=["vcg"])

        PT = [c.sb([128, 4, 128], BF16, f"PT{i}") for i in range(5)]
        QBD = {}
        for nm in ("QA", "QR", "QU"):
            for par in range(2):
                for j in range(2):
                    t_ = c.sb([128, 256], BF16, f"{nm}{par}{j}")
                    QBD[(nm, par, j)] = t_
                    p.pool(lambda e, t_=t_: e.memset(t_[:], 0.0), writes=[f"{nm}{par}{j}"])
        PTc = c.sb([128, 4, 4, 128], BF16, "PTc")
        mk = c.sb([128, 2, 128], BF16, "mk")
        ocA = KKk[:, 0:2 * T].rearrange("p (j t) -> p j t", j=2)
        ocB = KKk[:, 2 * T:4 * T].rearrange("p (j t) -> p j t", j=2)
        Etf = Et[:].rearrange("p a b c -> p (a b c)")
        rd = Etf[:, 0:512].rearrange("p (a b) -> p a b", a=4)
        rdA = c.sb([64, 4, 128], F32, "rdA")
        accBs = [c.sb([64, 4, 128], F32, f"accB{k}") for k in range(2)]
        w1kf = w1k[:].rearrange("p a b -> p (a b)").bitcast(F32)
        tmpBp = w1kf[0:64, 0:512].rearrange("p (a b) -> p a b", a=4)
        fBp = w1kf[0:64, 512:1024].rearrange("p (a b) -> p a b", a=4)
        tmpBs = w1kf[0:64, 1024:1536].rearrange("p (a b) -> p a b", a=4)
        fBs = w1kf[0:64, 1536:2048].rearrange("p (a b) -> p a b", a=4)
        pn = Etf[:, 512:1024].rearrange("p (a b) -> p a b", a=4)
        psh = Etf[:, 1024:1536].rearrange("p (a b) -> p a b", a=4)
        score = c.sb([128, 128], F32, "score")
        work = c.sb([128, 128], F32, "work")
        m8a = c.sb([128, 8], F32, "m8a")
        m8b = c.sb([128, 8], F32, "m8b")
        sel = c.sb([128, 128], BF16, "sel")
        selTs = [c.sb([128, 128], BF16, f"selT{k}") for k in range(2)]
        w1vf = w1v[:].rearrange("p a b -> p (a b)")
        selb = [w1vf[:, 512 * k:512 * k + 512].rearrange("p (a b) -> p a b", a=4) for k in range(2)]
        pbT = c.ps([128, 128], BF16, "pbT") if False else None
        ptn = [0]

        def next_pt():
            k = ptn[0]
            ptn[0] = (k + 1) % 5
            return k
        PTp = [w1vf[:, 1024 + 512 * i:1536 + 512 * i].rearrange("p (a b) -> p a b", a=4) for i in range(3)]
        ptnp = [0]

        def next_ptp():
            k = ptnp[0]
            ptnp[0] = (k + 1) % 3
            return k
        sbank = [0]

        def next_sb():
            k = sbank[0]
            sbank[0] = 1 - k
            return k

        def gate_finish(br, i, acc_first, gb, src, fB, tmpB, fn_, tn_):
            qs = slice(128 * i, 128 * i + 128)
            accB = accBs[i % 2]
            an_ = f"accB{i % 2}"
            for h in range(4):
                p.pe(lambda e, h=h: e.matmul(pb[gb][0:64, h, :], selg[:, br * 4 + h, :], sg[:, qs], start=True, stop=True),
                     reads=["selg", "sg"], writes=[f"pb{gb}"])
            p.dve(lambda e: e.tensor_scalar(fB[:], pb[src][64:128, :, :], 1e-30, None, op0=ALU.add),
                  reads=[f"pb{src}"], writes=[fn_])
            p.dve(lambda e: e.reciprocal(fB[:], fB[:]), reads=[fn_], writes=[fn_])
            p.dve(lambda e: e.tensor_tensor(fB[:], fB[:], pb[gb][0:64, :, :], op=ALU.mult), reads=[fn_, f"pb{gb}"], writes=[fn_])
            if acc_first:
                p.dve(lambda e: e.tensor_tensor(accB[:], pb[src][0:64, :, :], fB[:], op=ALU.mult),
                      reads=[f"pb{src}", fn_], writes=[an_])
            else:
                p.dve(lambda e: e.tensor_tensor(tmpB[:], pb[src][0:64, :, :], fB[:], op=ALU.mult),
                      reads=[f"pb{src}", fn_], writes=[tn_])
                p.dve(lambda e: e.tensor_tensor(accB[:], accB[:], tmpB[:], op=ALU.add), reads=[an_, tn_],
                      writes=[an_])

        def pre_qblock(i):
            ibb = 48 + i
            qs = slice(128 * i, 128 * i + 128)
            par = i % 2
            selT = selTs[par]
            sn_ = f"selT{par}"
            next_sb = lambda: 4
            next_pt = next_ptp
            PT = PTp
            for nm, src, srcn in (("QA", aq, "aq"), ("QR", bqr, "bqr"), ("QU", bq, "bq")):
                for j in range(2):
                    t_ = QBD[(nm, par, j)]
                    for g_ in range(2):
                        p.pool(lambda e, t_=t_, src=src, j=j, g_=g_: e.tensor_copy(
                            t_[64 * g_:64 * g_ + 64, 128 * g_:128 * g_ + 128], src[64 * g_:64 * g_ + 64, j, qs]),
                            reads=[srcn, f"{nm}{par}{j}"], writes=[f"{nm}{par}{j}"])
            if dbg < 2:
                return
            v = 9
            for j in range(2):
                sbk = next_sb()
                for r in range(2):
                    ks = slice(128 * (i + r), 128 * (i + r) + 128)
                    p.pe(lambda e, j=j, r=r, ks=ks, sbk=sbk: e.matmul(
                        pb[sbk][:, 2 * r:2 * r + 2, :].rearrange("p a b -> p (a b)"), ak[:, j, ks], QBD[("QA", par, j)][:],
                        start=True, stop=True), reads=["ak", f"QA{par}{j}"], writes=[f"pb{sbk}"])
                k_ = next_pt()
                p.act(lambda e, k_=k_, sbk=sbk: e.activation(PT[k_][:], pb[sbk][:], AF.Exp, scale=SCALE),
                      reads=[f"pb{sbk}"], writes=[f"PT{k_}"])
                if v < 2:
                    continue
                p.dve(lambda e, k_=k_: e.tensor_tensor(
                    PT[k_][:].rearrange("p (r g) q -> p r g q", r=2), PT[k_][:].rearrange("p (r g) q -> p r g q", r=2),
                    M2[:].unsqueeze(2).to_broadcast([128, 2, 2, 128]), op=ALU.mult),
                    reads=[f"PT{k_}", "M2"], writes=[f"PT{k_}"])
                if v < 3:
                    continue
                if i == 0:
                    p.dve(lambda e, k_=k_: e.tensor_scalar(PT[k_][:, 0:2, :], PT[k_][:, 0:2, :], offs[:, 3:4], None,
                                                          op0=ALU.mult), reads=[f"PT{k_}", "offs"], writes=[f"PT{k_}"])
                if v < 4:
                    continue
                for r in range(2):
                    p.pe(lambda e, j=j, r=r, k_=k_: e.matmul(
                        pb[5][:, 2 * j:2 * j + 2, :].rearrange("p a b -> p (a b)"), avg[:, i + r, j, :],
                        PT[k_][:, 2 * r:2 * r + 2, :].rearrange("p a b -> p (a b)"), start=(r == 0), stop=(r == 1)),
                        reads=["avg", f"PT{k_}"], writes=["pb5"])
            if v < 5:
                return
            p.dve(lambda e: e.tensor_tensor(rdA[:], pb[5][64:128, :, :], esk[64:128, :].unsqueeze(2).to_broadcast([64, 4, 128]),
                                            op=ALU.add), reads=["pb5"] + ["esk"], writes=["rdA"])
            p.dve(lambda e: e.reciprocal(rdA[:], rdA[:]), reads=["rdA"], writes=["rdA"])
            if v < 6:
                return
            for g_ in range(2):
                p.dve(lambda e, g_=g_: e.tensor_tensor(
                    ocA[64 * g_:64 * g_ + 64, :, qs], pb[5][0:64, :, :].rearrange("p (j g) q -> p j g q", g=2)[:, :, g_, :],
                    rdA[:].rearrange("p (j g) q -> p j g q", g=2)[:, :, g_, :], op=ALU.mult),
                    reads=["pb5"] + ["rdA"], writes=["ocA", "KKk"])
            if dbg < 3:
                return
            for r in range(5):
                sbk = next_sb()
                ks = slice(128 * (i + r), 128 * (i + r) + 128)
                for j in range(2):
                    p.pe(lambda e, j=j, ks=ks, sbk=sbk: e.matmul(
                        pb[sbk][:, 2 * j:2 * j + 2, :].rearrange("p a b -> p (a b)"), bkw[:, ks], QBD[("QR", par, j)][:],
                        start=True, stop=True), reads=["bkw", f"QR{par}{j}"], writes=[f"pb{sbk}"])
                k_ = next_pt()
                p.act(lambda e, k_=k_, sbk=sbk: e.activation(PT[k_][:], pb[sbk][:], AF.Exp, scale=SCALE),
                      reads=[f"pb{sbk}"], writes=[f"PT{k_}"])
                if r == 0 or r == 4:
                    msk = MLq if r == 0 else MUq
                    mn = "MLq" if r == 0 else "MUq"
                    p.dve(lambda e, k_=k_, msk=msk: e.tensor_tensor(PT[k_][:], PT[k_][:],
                                                                   msk[:].unsqueeze(1).to_broadcast([128, 4, 128]),
                                                                   op=ALU.mult), reads=[f"PT{k_}", mn], writes=[f"PT{k_}"])
                if i + r < 4:
                    p.dve(lambda e, k_=k_: e.tensor_scalar(PT[k_][:], PT[k_][:], offs[:, 3:4], None, op0=ALU.mult),
                          reads=[f"PT{k_}", "offs"], writes=[f"PT{k_}"])
                p.pe(lambda e, r=r, k_=k_: e.matmul(pb[5][:].rearrange("p a b -> p (a b)"), bvwg[:, i + r, :],
                                                    PT[k_][:].rearrange("p a b -> p (a b)"), start=(r == 0), stop=(r == 4)),
                     reads=["bvwg", f"PT{k_}"], writes=["pb5"])
            gate_finish(2, i, True, 4, 5, fBp, tmpBp, "fBp", "tmpBp")
            if dbg < 4:
                return
            p.dve(lambda e: e.tensor_scalar(mk[:], valc[:], float(128 * ibb - 31), None, op0=ALU.is_le),
                  reads=["valc"], writes=["mk"])
            for cc in range(4):
                sbk = next_sb()
                for j in range(2):
                    p.pe(lambda e, j=j, cc=cc, sbk=sbk: e.matmul(
                        pb[sbk][:, 2 * j:2 * j + 2, :].rearrange("p a b -> p (a b)"), kcT[:, cc * 128:(cc + 1) * 128],
                        QBD[("QU", par, j)][:], start=True, stop=True), reads=["kcT", f"QU{par}{j}"], writes=[f"pb{sbk}"])
                p.act(lambda e, cc=cc, sbk=sbk: e.activation(PTc[:, cc, :, :], pb[sbk][:], AF.Exp, scale=SCALE,
                                                             bias=biasn[:, cc:cc + 1]),
                      reads=[f"pb{sbk}", "biasn"], writes=[("PTc", cc)])
                if cc >= 2:
                    p.dve(lambda e, cc=cc: e.tensor_tensor(PTc[:, cc, :, :], PTc[:, cc, :, :],
                                                          mk[:, cc - 2, :].unsqueeze(1).to_broadcast([128, 4, 128]),
                                                          op=ALU.mult), reads=[("PTc", cc), "mk"], writes=[("PTc", cc)])
                p.pe(lambda e, cc=cc: e.matmul(pb[5][:].rearrange("p a b -> p (a b)"), vcg[:, cc, :],
                                               PTc[:, cc, :, :].rearrange("p a b -> p (a b)"), start=(cc == 0),
                                               stop=(cc == 3)), reads=["vcg", ("PTc", cc)], writes=["pb5"])
                p.pe(lambda e, cc=cc: e.matmul(pb[6][:].rearrange("p a b -> p (a b)"), ones_b[:],
                                               PTc[:, cc, :, :].rearrange("p a b -> p (a b)"), start=(cc == 0),
                                               stop=(cc == 3)), reads=["ones_b", ("PTc", cc)], writes=["pb6"])
            p.dve(lambda e: e.tensor_scalar(rd[:], pb[6][:], 1e-30, None, op0=ALU.add), reads=["pb6", "Et"], writes=["rd"])
            p.dve(lambda e: e.reciprocal(rd[:], rd[:]), reads=["rd"], writes=["rd"])
            for cc in range(4):
                p.dve(lambda e, cc=cc: e.tensor_tensor(pn[:], PTc[:, cc, :, :], rd[:], op=ALU.mult),
                      reads=[("PTc", cc), "rd"], writes=["pn"])
                p.dve(lambda e, cc=cc: e.tensor_reduce(psh[:, cc, :], pn[:].rearrange("p h q -> p q h"), axis=AX.X,
                                                       op=ALU.add), reads=["pn"], writes=[("psh", cc)])
                p.pe(lambda e, cc=cc: e.matmul(pb[6][:, 0, :], psh[:, cc, :], Gm[:, cc, :], start=(cc == 0), stop=(cc == 3)),
                     reads=[("psh", cc), "Gm"], writes=["pb6"])
            gate_finish(0, i, False, 4, 5, fBp, tmpBp, "fBp", "tmpBp")
            p.dve(lambda e: e.tensor_tensor(score[:], pb[6][:, 0, :], f0m[:], op=ALU.max), reads=["pb6", "f0m"],
                  writes=["score"])
            c0 = 2 * ibb
            if c0 + 2 < 128:
                p.dve(lambda e: e.memset(score[:, c0 + 2:128], -1e30), reads=["score"], writes=["score"])
            p.dve(lambda e: e.memset(score[0:64, c0 + 1:c0 + 2], -1e30), reads=["score"], writes=["score"])
            p.dve(lambda e: e.memset(score[0:64, c0 - 1:c0 + 1], 1e9), reads=["score"], writes=["score"])
            p.dve(lambda e: e.memset(score[64:128, c0:c0 + 2], 1e9), reads=["score"], writes=["score"])
            p.dve(lambda e: e.max(m8a[:], score[:]), reads=["score"], writes=["m8a"])
            p.dve(lambda e: e.match_replace(work[:], m8a[:], score[:], -3e38), reads=["score", "m8a"], writes=["work"])
            p.dve(lambda e: e.max(m8b[:], work[:]), reads=["work"], writes=["m8b"])
            p.dve(lambda e: e.tensor_scalar(work[:], score[:], m8b[:, 7:8], None, op0=ALU.is_ge), reads=["score", "m8b"],
                  writes=["work"])
            p.dve(lambda e: e.tensor_tensor(sel[:], work[:], validblk[:], op=ALU.mult), reads=["work", "validblk"],
                  writes=["sel"])
            if c0 + 2 < 128:
                p.dve(lambda e: e.memset(sel[:, c0 + 2:128], 0.0), reads=["sel"], writes=["sel"])
            p.dve(lambda e: e.memset(sel[0:64, c0 + 1:c0 + 2], 0.0), reads=["sel"], writes=["sel"])
            p.pe(lambda e: e.transpose(pb[6][:].rearrange("p a b -> p (a b)").bitcast(BF16)[:, 0:128], sel[:], IDb[:]),
                 reads=["sel", "IDb"], writes=["pb6"])
            p.act(lambda e: e.copy(selT[:], pb[6][:].rearrange("p a b -> p (a b)").bitcast(BF16)[:, 0:128]),
                  reads=["pb6"], writes=[sn_])

        def slc_qblock(i):
            ibb = 48 + i
            qs = slice(128 * i, 128 * i + 128)
            par = i % 2
            selT = selTs[par]
            sn_ = f"selT{par}"
            accB = accBs[par]
            p.dve(lambda e: e.tensor_scalar(selb[par][:], selT[:].unsqueeze(1).to_broadcast([128, 4, 128]), -1.0, 30000.0,
                                            op0=ALU.add, op1=ALU.mult), reads=[sn_], writes=[f"selb{par}"])
            pend = []
            for cb in range(ibb + 1):
                sbk = next_sb()
                ks = slice(128 * cb, 128 * cb + 128)
                p.pe(lambda e, cb=cb, sbk=sbk: e.matmul(pb[sbk][:].rearrange("p a b -> p (a b)"), E[:, cb, :],
                                                        selb[par][:].rearrange("p a b -> p (a b)"), start=True, stop=False),
                     reads=["E", f"selb{par}"], writes=[f"pb{sbk}"])
                for j in range(2):
                    p.pe(lambda e, j=j, ks=ks, sbk=sbk: e.matmul(
                        pb[sbk][:, 2 * j:2 * j + 2, :].rearrange("p a b -> p (a b)"), bks[:, ks], QBD[("QR", par, j)][:],
                        start=False, stop=(j == 1)), reads=["bks", f"QR{par}{j}"], writes=[f"pb{sbk}"])
                while len(pend) > 1:
                    pend.pop(0)()
                k_ = next_pt()
                p.act(lambda e, k_=k_, sbk=sbk: e.activation(PT[k_][:], pb[sbk][:], AF.Exp, scale=SCALE),
                      reads=[f"pb{sbk}"], writes=[f"PT{k_}"])
                if cb == ibb:
                    p.dve(lambda e, k_=k_: e.tensor_tensor(PT[k_][:], PT[k_][:],
                                                          MUq[:].unsqueeze(1).to_broadcast([128, 4, 128]), op=ALU.mult),
                          reads=[f"PT{k_}", "MUq"], writes=[f"PT{k_}"])

                def pv(cb=cb, k_=k_):
                    p.pe(lambda e: e.matmul(pb[7][:].rearrange("p a b -> p (a b)"), bvsg[:, cb, :],
                                            PT[k_][:].rearrange("p a b -> p (a b)"), start=(cb == 0),
                                            stop=(cb == ibb)), reads=["bvsg", f"PT{k_}"], writes=["pb7"])
                pend.append(pv)
            while pend:
                pend.pop(0)()
            gate_finish(1, i, False, 2, 7, fBs, tmpBs, "fBs", "tmpBs")
            for g_ in range(2):
                p.act(lambda e, g_=g_: e.copy(ocB[64 * g_:64 * g_ + 64, :, qs],
                                              accB[:].rearrange("p (j g) q -> p j g q", g=2)[:, :, g_, :]),
                      reads=[f"accB{par}"], writes=["ocB", "KKk"])
        real_add = p.add

        def record(fn, i):
            lst = []
            p.add = lambda eng, f, reads=(), writes=(), dma=None: lst.append((eng, f, reads, writes, dma))
            try:
                fn(i)
            finally:
                p.add = real_add
            return lst

        qbl = list(qblocks)
        for op in record(pre_qblock, qbl[0]):
            real_add(*op)
        for n_, i_ in enumerate(qbl):
            lists = [record(slc_qblock, i_)]
            if n_ + 1 < len(qbl):
                lists.append(record(pre_qblock, qbl[n_ + 1]))
            lists = [l_ for l_ in lists if l_]
            pos = [0] * len(lists)
            for _ in range(sum(len(l_) for l_ in lists)):
                best, bi = None, -1
                for li, l_ in enumerate(lists):
                    if pos[li] < len(l_):
                        frac = pos[li] / len(l_)
                        if best is None or frac < best:
                            best, bi = frac, li
                real_add(*lists[bi][pos[bi]])
                pos[bi] += 1
        for j in range(2):
            p.dma("sp", "ocA", lambda e, j=j: e.dma_start(out=oc_d[j, :, :], in_=ocA[:, j, :]), reads=["ocA"],
                  writes=[("oc", j)])
            p.dma("sp", "ocB", lambda e, j=j: e.dma_start(out=oc_d[2 + j, :, :], in_=ocB[:, j, :]), reads=["ocB"],
                  writes=[("oc", 2 + j)])
        p.emit(final_sems=["ocA", "ocB"])
    return nc


def assemble_B(Ab, qtr, wl):
    bf = ml_dtypes.bfloat16
    t0 = qtr * T
    off = 3 * T - t0
    own = Ab[qtr]

    def hist_cols(idx, src, n):
        if qtr == 0:
            return np.zeros((128, n), bf)
        return Ab[qtr - 1][src][idx][:, T - n:]

    def hist_rows(c0, c1, n):
        if qtr == 0:
            return np.zeros((n, c1 - c0), bf)
        return Ab[qtr - 1]["o_v"][T - n:, c0:c1]

    def tokmaj(rows, nblk):
        return np.ascontiguousarray(rows.reshape(nblk, 128, rows.shape[1]).transpose(1, 0, 2))

    d = {}
    d["aq"] = np.ascontiguousarray(own["o_rope"][0:2])
    d["ak"] = np.ascontiguousarray(np.stack([np.concatenate([hist_cols(2 + j, "o_rope", 128), own["o_rope"][2 + j]], 1)
                                             for j in range(2)]))
    d["av"] = tokmaj(np.concatenate([hist_rows(0, 128, 128), own["o_v"][:, 0:128]], 0), 17)
    d["sinks"] = np.ascontiguousarray(np.tile(wl["attn_sinks"][None, :], (128, 1)).astype(np.float32))
    d["offs"] = np.tile(np.array([[off, off // 16, off // 64, 0.0 if qtr == 0 else 1.0]], np.float32), (128, 1))
    d["bqr"] = np.ascontiguousarray(own["o_rope"][4:6])
    d["bq"] = np.ascontiguousarray(own["o_plain"][0:2])
    d["bkw"] = np.ascontiguousarray(np.concatenate([hist_cols(7, "o_rope", 512), own["o_rope"][7]], 1))
    d["bvw"] = tokmaj(np.concatenate([hist_rows(192, 256, 512), own["o_v"][:, 192:256]], 0), 20)
    pad = np.zeros((128, off), bf)
    d["bks"] = np.ascontiguousarray(np.concatenate([pad] + [Ab[k]["o_rope"][6] for k in range(qtr + 1)], 1))
    d["cmpT"] = np.ascontiguousarray(np.concatenate([pad] + [Ab[k]["o_plain"][2] for k in range(qtr + 1)], 1))
    d["bvs"] = tokmaj(np.concatenate([np.zeros((off, 64), bf)] + [Ab[k]["o_v"][:, 128:192] for k in range(qtr + 1)], 0), 64)
    d["bg"] = np.ascontiguousarray(own["o_g"])
    d["w1k"], d["w1v"], d["w2k"], d["w2v"] = wl["cmp_k_w1"], wl["cmp_v_w1"], wl["cmp_k_w2"], wl["cmp_v_w2"]
    for nm, key in (("pek", "cmp_pe_k"), ("pev", "cmp_pe_v")):
        d[nm] = np.ascontiguousarray(wl[key].reshape(16, 2, 64).transpose(1, 2, 0).reshape(128, 16))
    return d


def build_G2(ntiles=SEQ // GT):
    nc = bass.Bass("TRN2", target_bir_lowering=False)
    S_ = ntiles * GT
    NB = S_ // 128
    with ExitStack() as es:
        c = Ctx(nc, es)
        p = c.p
        qkv = c.din("qkv", [3, 128, S_])
        cw = c.din("cw", [128, 12])
        gab = c.din("gab", [128, NB, 2])
        hp = c.din("hp", [128, 2])
        zin = c.din("z", [128, NB, 128])
        nwb = c.din("nwb", [128, 128])
        yo = c.dout("yo", [128, NB, 128], BF16)

        val = c.sb([128, 128], F32, "val")
        MU = c.sb([128, 128], F32, "MU")
        ML = c.sb([128, 128], F32, "ML")
        ID = c.sb([128, 128], F32, "ID")
        ones_bf = c.sb([128, 128], BF16, "ones_bf")
        ones_f = c.sb([128, 128], F32, "ones_f")
        epsc = c.sb([128, 1], F32, "epsc")
        cwt = c.sb([128, 12], F32, "cwt")
        hpt = c.sb([128, 2], F32, "hpt")
        nw = c.sb([128, 128], F32, "nw")
        p.dma("sp", "cwt", lambda e: e.dma_start(out=cwt[:], in_=cw[:, :]), writes=["cwt"])
        p.dma("sp", "hpt", lambda e: e.dma_start(out=hpt[:], in_=hp[:, :]), writes=["hpt"])
        p.dma("sp", "nw", lambda e: e.dma_start(out=nw[:], in_=nwb[:, :]), writes=["nw"])
        p.pool(lambda e: e.iota(val[:], [[1, 128]], base=0, channel_multiplier=-1,
                                allow_small_or_imprecise_dtypes=True), writes=["val"])
        p.dve(lambda e: e.tensor_scalar(MU[:], val[:], 0.0, None, op0=ALU.is_ge), reads=["val"], writes=["MU"])
        p.dve(lambda e: e.tensor_scalar(ML[:], val[:], 0.0, None, op0=ALU.is_lt), reads=["val"], writes=["ML"])
        p.dve(lambda e: e.tensor_scalar(ID[:], val[:], 0.0, None, op0=ALU.is_equal), reads=["val"], writes=["ID"])
        p.dve(lambda e: e.memset(MU[0:64, 64:128], 0.0), reads=["MU"], writes=["MU"])
        p.dve(lambda e: e.memset(ML[64:128, 0:64], 0.0), reads=["ML"], writes=["ML"])
        p.dve(lambda e: e.memset(ones_bf[:], 1.0), writes=["ones_bf"])
        p.dve(lambda e: e.memset(ones_f[:], 1.0), writes=["ones_f"])
        p.dve(lambda e: e.memset(epsc[:], EPS), writes=["epsc"])

        pb = [c.ps([128, 4, 128], F32, f"pb{i}") for i in range(8)]

        gt_ = c.sb([128, NB, 2], F32, "gt_")
        g = c.sb([128, NB], F32, "g")
        beta = c.sb([128, NB], F32, "beta")
        nbeta = c.sb([128, NB], F32, "nbeta")
        tA = c.sb([128, NB], F32, "tA")
        eA = c.sb([128, 1], F32, "eA")
        gh = [c.sb([128, NB], F32, f"gh{a}") for a in range(2)]
        egc = c.sb([128, NB], F32, "egc")
        erem = c.sb([128, NB], F32, "erem")
        egl = [c.sb([128, NB], F32, f"egl{a}") for a in range(2)]
        sckb = c.sb([128, NB], F32, "sckb")
        p.dma("sp", "gt_", lambda e: e.dma_start(out=gt_[:], in_=gab[:, :, :]), writes=["gt_"])
        p.act(lambda e: e.activation(tA[:], gt_[:, :, 0], AF.Exp, bias=hpt[:, 1:2]), reads=["gt_", "hpt"], writes=["tA"])
        p.act(lambda e: e.activation(tA[:], tA[:], AF.Ln, bias=1.0), reads=["tA"], writes=["tA"])
        p.act(lambda e: e.activation(eA[:], hpt[:, 0:1], AF.Exp), reads=["hpt"], writes=["eA"])
        p.dve(lambda e: e.tensor_scalar(g[:], tA[:], eA[:, 0:1], -1.0, op0=ALU.mult, op1=ALU.mult),
              reads=["tA", "eA"], writes=["g"])
        p.act(lambda e: e.activation(beta[:], gt_[:, :, 1], AF.Exp, scale=-1.0), reads=["gt_"], writes=["beta"])
        p.dve(lambda e: e.tensor_scalar(beta[:], beta[:], 1.0, None, op0=ALU.add), reads=["beta"], writes=["beta"])
        p.dve(lambda e: e.reciprocal(beta[:], beta[:]), reads=["beta"], writes=["beta"])
        p.dve(lambda e: e.tensor_scalar(nbeta[:], beta[:], -1.0, None, op0=ALU.mult), reads=["beta"], writes=["nbeta"])
        for a in range(2):
            p.dve(lambda e, a=a: e.memset(gh[a][:], 0.0), writes=[f"gh{a}"])
            p.dve(lambda e, a=a: e.tensor_copy(gh[a][64 * a:64 * a + 64, :], g[64 * a:64 * a + 64, :]),
                  reads=["g", f"gh{a}"], writes=[f"gh{a}"])
        NBC = min(NB, 64)
        p.pe(lambda e: e.matmul(pb[0][:, 0, 0:NB], MU[:], g[:], start=True, stop=True), reads=["MU", "g"], writes=["pb0"])
        p.act(lambda e: e.activation(egc[:], pb[0][:, 0, 0:NB], AF.Exp), reads=["pb0"], writes=["egc"])
        p.pe(lambda e: e.matmul(pb[1][:, 0, 0:NB], ML[:], g[:], start=True, stop=True), reads=["ML", "g"], writes=["pb1"])
        p.act(lambda e: e.activation(erem[:], pb[1][:, 0, 0:NB], AF.Exp), reads=["pb1"], writes=["erem"])
        for a in range(2):
            p.pe(lambda e, a=a: e.matmul(pb[2 + a][:, 0, 0:NB], ones_f[:], gh[a][:], start=True, stop=True),
                 reads=["ones_f", f"gh{a}"], writes=[f"pb{2 + a}"])
            p.act(lambda e, a=a: e.activation(egl[a][:], pb[2 + a][:, 0, 0:NB], AF.Exp), reads=[f"pb{2 + a}"],
                  writes=[f"egl{a}"])
        p.dve(lambda e: e.tensor_tensor(sckb[:], beta[:], egc[:], op=ALU.mult), reads=["beta", "egc"], writes=["sckb"])

        NC3 = 3

        def B3(name, shape, dt):
            return [c.sb(shape, dt, f"{name}_{k}") for k in range(NC3)]
        xin = [B3(f"xin{i}", [128, GT + 3], F32) for i in range(3)]
        acc = [B3(f"acc{i}", [128, GT], F32) for i in range(3)]
        sil = [B3(f"sil{i}", [128, GT], F32) for i in range(3)]
        sqb = [B3(f"sqb{i}", [128, GT], BF16) for i in range(2)]
        rn = [B3(f"rn{i}", [128, GT], F32) for i in range(2)]
        qn = B3("qn", [128, 4, 128], F32)
        kn = B3("kn", [128, 4, 128], F32)
        kbg = B3("kbg", [128, 4, 128], BF16)
        kdec = B3("kdec", [128, 4, 128], BF16)
        vb = B3("vb", [128, 4, 128], BF16)
        gU2 = B3("gU2", [128, 4, 128], F32)
        gU1 = B3("gU1", [128, 4, 128], F32)
        DecL = B3("DecL", [128, 4, 128], F32)
        DecT = B3("DecT", [128, 4, 128], F32)
        Xs = [B3(f"Xs{i}", [128, 4, 128], F32) for i in range(2)]
        Ys = [B3(f"Ys{i}", [128, 4, 128], F32) for i in range(2)]
        Q = B3("Q", [128, 4, 128], F32)
        TTb = B3("TTb", [128, 4, 128], BF16)
        qkT = B3("qkT", [128, 4, 128], BF16)
        u_sb = B3("u_sb", [128, 4, 128], F32)
        wT_sb = B3("wT_sb", [128, 4, 128], F32)
        o_sb = B3("o_sb", [128, 4, 128], F32)
        zt = B3("zt", [128, 4, 128], F32)
        zs = B3("zs", [128, 4, 128], F32)
        yt = B3("yt", [128, 4, 128], F32)
        ytb = B3("ytb", [128, 4, 128], BF16)
        ss = B3("ss", [128, 4], F32)
        otmp = c.sb([128, 128], F32, "otmp")
        vnb = c.sb([128, 128], BF16, "vnb")
        S = c.sb([128, 128], F32, "S")
        junk = c.sb([128, 128], F32, "junk")
        p.dve(lambda e: e.memset(S[:], 0.0), writes=["S"])

        def bc4(ap2):
            return ap2.unsqueeze(2).to_broadcast([128, 4, 128])

        def bcm(ap2):
            return ap2.unsqueeze(1).to_broadcast([128, 4, 128])

        def fl(t):
            return t[:].rearrange("p a b -> p (a b)")

        def P0(ti):
            k = ti % NC3
            t0 = ti * GT
            bs = slice(ti * 4, ti * 4 + 4)
            for i in range(3):
                xn = f"xin{i}_{k}"
                if ti == 0:
                    p.dve(lambda e, i=i: e.memset(xin[i][k][:, 0:3], 0.0), writes=[xn])
                    p.dma("sp", xn, lambda e, i=i: e.dma_start(out=xin[i][k][:, 3:GT + 3], in_=qkv[i, :, 0:GT]), writes=[xn])
                else:
                    p.dma("sp", xn, lambda e, i=i: e.dma_start(out=xin[i][k][:], in_=qkv[i, :, t0 - 3:t0 + GT]), writes=[xn])
            p.dma("sp", f"zt_{k}", lambda e: e.dma_start(out=zt[k][:], in_=zin[:, bs, :]), writes=[f"zt_{k}"])
            for i in range(3):
                xn, an, sn = f"xin{i}_{k}", f"acc{i}_{k}", f"sil{i}_{k}"
                p.dve(lambda e, i=i: e.tensor_scalar(acc[i][k][:], xin[i][k][:, 3:GT + 3], cwt[:, 4 * i + 3:4 * i + 4], None,
                                                     op0=ALU.mult), reads=[xn, "cwt"], writes=[an])
                for j in range(3):
                    p.dve(lambda e, i=i, j=j: e.scalar_tensor_tensor(
                        acc[i][k][:], xin[i][k][:, j:GT + j], cwt[:, 4 * i + j:4 * i + j + 1], acc[i][k][:], op0=ALU.mult,
                        op1=ALU.add), reads=[xn, "cwt", an], writes=[an])
                p.act(lambda e, i=i: e.activation(sil[i][k][:], acc[i][k][:], AF.Silu), reads=[an], writes=[sn])
            for i in range(2):
                sn, qn_, rn_ = f"sil{i}_{k}", f"sqb{i}_{k}", f"rn{i}_{k}"
                p.act(lambda e, i=i: e.activation(sqb[i][k][:], sil[i][k][:], AF.Square), reads=[sn], writes=[qn_])
                p.pe(lambda e, i=i: e.matmul(fl(pb[0]), ones_bf[:], sqb[i][k][:], start=True, stop=True),
                     reads=["ones_bf", qn_], writes=["pb0"])
                p.act(lambda e, i=i: e.activation(rn[i][k][:], fl(pb[0]), AF.Sqrt, bias=epsc[:, 0:1]),
                      reads=["pb0", "epsc"], writes=[rn_])
                p.dve(lambda e, i=i: e.reciprocal(rn[i][k][:], rn[i][k][:]), reads=[rn_], writes=[rn_])
            p.dve(lambda e: e.scalar_tensor_tensor(fl(qn[k]), sil[0][k][:], float(128 ** -0.5), rn[0][k][:], op0=ALU.mult,
                                                   op1=ALU.mult), reads=[f"sil0_{k}", f"rn0_{k}"], writes=[f"qn_{k}"])
            p.dve(lambda e: e.tensor_tensor(fl(kn[k]), sil[1][k][:], rn[1][k][:], op=ALU.mult),
                  reads=[f"sil1_{k}", f"rn1_{k}"], writes=[f"kn_{k}"])
            p.act(lambda e: e.activation(zs[k][:], zt[k][:], AF.Silu), reads=[f"zt_{k}"], writes=[f"zs_{k}"])

        def P1(ti):
            k = ti % NC3
            bs = slice(ti * 4, ti * 4 + 4)
            rot = [5]

            def nb():
                b_ = rot[0]
                rot[0] = 5 + (b_ - 4) % 3
                return b_
            R = lambda nm: f"{nm}_{k}"
            bk_, bv_ = nb(), nb()
            for pr in range(4):
                p.pe(lambda e, pr=pr: e.transpose(pb[bk_][:, pr, :], kn[k][:, pr, :], ID[:]), reads=[R("kn"), "ID"],
                     writes=[f"pb{bk_}"])
            for pr in range(4):
                p.pe(lambda e, pr=pr: e.transpose(pb[bv_][:, pr, :], sil[2][k][:, pr * 128:(pr + 1) * 128], ID[:]),
                     reads=[R("sil2"), "ID"], writes=[f"pb{bv_}"])
            p.dve(lambda e: e.tensor_tensor(kbg[k][:], pb[bk_][:], bc4(sckb[:, bs]), op=ALU.mult),
                  reads=[f"pb{bk_}", "sckb"], writes=[R("kbg")])
            p.dve(lambda e: e.tensor_tensor(kdec[k][:], pb[bk_][:], bc4(erem[:, bs]), op=ALU.mult),
                  reads=[f"pb{bk_}", "erem"], writes=[R("kdec")])
            p.dve(lambda e: e.tensor_tensor(vb[k][:], pb[bv_][:], bc4(beta[:, bs]), op=ALU.mult),
                  reads=[f"pb{bv_}", "beta"], writes=[R("vb")])
            p.dve(lambda e: e.tensor_tensor(gU2[k][:], bcm(ML[:]), bc4(g[:, bs]), op=ALU.mult), reads=["ML", "g"],
                  writes=[R("gU2")])
            p.dve(lambda e: e.tensor_tensor(gU1[k][:], bcm(MU[:]), bc4(g[:, bs]), op=ALU.mult), reads=["MU", "g"],
                  writes=[R("gU1")])
            bd_, bt_ = nb(), nb()
            p.pe(lambda e: e.matmul(fl(pb[bd_]), MU[:], fl(gU2[k]), start=True, stop=True), reads=["MU", R("gU2")],
                 writes=[f"pb{bd_}"])
            p.pe(lambda e: e.matmul(fl(pb[bt_]), ML[:], fl(gU1[k]), start=True, stop=True), reads=["ML", R("gU1")],
                 writes=[f"pb{bt_}"])
            p.act(lambda e: e.activation(DecL[k][:], pb[bd_][:], AF.Exp), reads=[f"pb{bd_}"], writes=[R("DecL")])
            p.act(lambda e: e.activation(DecT[k][:], pb[bt_][:], AF.Exp), reads=[f"pb{bt_}"], writes=[R("DecT")])
            p.dve(lambda e: e.tensor_tensor(DecL[k][:], DecL[k][:], bcm(ML[:]), op=ALU.mult), reads=[R("DecL"), "ML"],
                  writes=[R("DecL")])
            p.dve(lambda e: e.tensor_tensor(DecT[k][:], DecT[k][:], bcm(MU[:]), op=ALU.mult), reads=[R("DecT"), "MU"],
                  writes=[R("DecT")])
            bg_, bq_ = nb(), nb()
            for pr in range(4):
                p.pe(lambda e, pr=pr: e.matmul(pb[bg_][:, pr, :], kn[k][:, pr, :], kn[k][:, pr, :], start=True, stop=True),
                     reads=[R("kn")], writes=[f"pb{bg_}"])
            for pr in range(4):
                p.pe(lambda e, pr=pr: e.matmul(pb[bq_][:, pr, :], kn[k][:, pr, :], qn[k][:, pr, :], start=True, stop=True),
                     reads=[R("kn"), R("qn")], writes=[f"pb{bq_}"])
            p.dve(lambda e: e.tensor_tensor(Xs[0][k][:], pb[bg_][:], DecL[k][:], op=ALU.mult),
                  reads=[f"pb{bg_}", R("DecL")], writes=[R("Xs0")])
            p.dve(lambda e: e.tensor_tensor(Xs[0][k][:], Xs[0][k][:], bc4(nbeta[:, bs]), op=ALU.mult),
                  reads=[R("Xs0"), "nbeta"], writes=[R("Xs0")])
            p.dve(lambda e: e.tensor_tensor(qkT[k][:], pb[bq_][:], DecT[k][:], op=ALU.mult),
                  reads=[f"pb{bq_}", R("DecT")], writes=[R("qkT")])
            by_ = nb()
            for pr in range(4):
                p.pe(lambda e, pr=pr: e.transpose(pb[by_][:, pr, :], Xs[0][k][:, pr, :], ID[:]), reads=[R("Xs0"), "ID"],
                     writes=[f"pb{by_}"])
            p.act(lambda e: e.copy(Ys[0][k][:], pb[by_][:]), reads=[f"pb{by_}"], writes=[R("Ys0")])
            p.dve(lambda e: e.tensor_tensor(Q[k][:], pb[by_][:], bcm(ID[:]), op=ALU.add), reads=[f"pb{by_}", "ID"],
                  writes=[R("Q")])
            for lvl in range(1, 6):
                cur, nxt = (lvl - 1) % 2, lvl % 2
                bx_ = nb()
                for pr in range(4):
                    p.pe(lambda e, pr=pr, cur=cur, bx_=bx_: e.matmul(pb[bx_][:, pr, :], Ys[cur][k][:, pr, :],
                                                                   Xs[cur][k][:, pr, :], start=True, stop=True),
                         reads=[R(f"Ys{cur}"), R(f"Xs{cur}")], writes=[f"pb{bx_}"])
                if lvl < 5:
                    byy = nb()
                    for pr in range(4):
                        p.pe(lambda e, pr=pr, cur=cur, byy=byy: e.matmul(pb[byy][:, pr, :], Xs[cur][k][:, pr, :],
                                                                       Ys[cur][k][:, pr, :], start=True, stop=True),
                             reads=[R(f"Ys{cur}"), R(f"Xs{cur}")], writes=[f"pb{byy}"])
                p.act(lambda e, nxt=nxt, bx_=bx_: e.copy(Xs[nxt][k][:], pb[bx_][:]), reads=[f"pb{bx_}"],
                      writes=[R(f"Xs{nxt}")])
                if lvl < 5:
                    p.dve(lambda e, nxt=nxt, byy=byy: e.tensor_copy(Ys[nxt][k][:], pb[byy][:]), reads=[f"pb{byy}"],
                          writes=[R(f"Ys{nxt}")])
                bqq = nb()
                for pr in range(4):
                    p.pe(lambda e, pr=pr, nxt=nxt, bqq=bqq: e.matmul(pb[bqq][:, pr, :], Xs[nxt][k][:, pr, :], Q[k][:, pr, :],
                                                                   start=True, stop=True),
                         reads=[R(f"Xs{nxt}"), R("Q")], writes=[f"pb{bqq}"])
                p.dve(lambda e, bqq=bqq: e.tensor_tensor(Q[k][:], Q[k][:], pb[bqq][:], op=ALU.add),
                      reads=[R("Q"), f"pb{bqq}"], writes=[R("Q")])
            p.act(lambda e: e.copy(TTb[k][:], Q[k][:]), reads=[R("Q")], writes=[R("TTb")])
            bu_, bw_ = nb(), nb()
            for pr in range(4):
                p.pe(lambda e, pr=pr: e.matmul(pb[bu_][:, pr, :], TTb[k][:, pr, :], vb[k][:, pr, :], start=True, stop=True),
                     reads=[R("TTb"), R("vb")], writes=[f"pb{bu_}"])
            for pr in range(4):
                p.pe(lambda e, pr=pr: e.matmul(pb[bw_][:, pr, :], kbg[k][:, pr, :], TTb[k][:, pr, :], start=True, stop=True),
                     reads=[R("TTb"), R("kbg")], writes=[f"pb{bw_}"])
            p.act(lambda e: e.copy(u_sb[k][:], pb[bu_][:]), reads=[f"pb{bu_}"], writes=[R("u_sb")])
            p.dve(lambda e: e.tensor_copy(wT_sb[k][:], pb[bw_][:]), reads=[f"pb{bw_}"], writes=[R("wT_sb")])

        def P2(ti):
            k = ti % NC3
            b0 = ti * 4
            bs = slice(b0, b0 + 4)
            R = lambda nm: f"{nm}_{k}"
            for pr in range(4):
                blk = b0 + pr
                for a in range(2):
                    rs = slice(64 * a, 64 * a + 64)
                    cs = slice(64 * a, 64 * a + 64)
                    p.pe(lambda e, pr=pr, rs=rs, cs=cs: e.matmul(pb[1][rs, 0, :], wT_sb[k][:, pr, cs], S[:], start=True,
                                                               stop=True), reads=[R("wT_sb"), "S"], writes=["pb1"])
                    p.pe(lambda e, pr=pr, rs=rs, cs=cs: e.matmul(pb[2][rs, 0, :], qn[k][:, pr, cs], S[:], start=True,
                                                               stop=True), reads=[R("qn"), "S"], writes=["pb2"])
                    p.dve(lambda e, pr=pr, rs=rs: e.tensor_tensor(vnb[rs, :], u_sb[k][rs, pr, :], pb[1][rs, 0, :],
                                                                 op=ALU.subtract), reads=[R("u_sb"), "pb1"], writes=["vnb"])
                    p.act(lambda e, rs=rs, blk=blk: e.activation(otmp[rs, :], pb[2][rs, 0, :], AF.Copy,
                                                                 scale=egc[rs, blk:blk + 1]),
                          reads=["pb2", "egc"], writes=["otmp"])
                    p.pe(lambda e, pr=pr, rs=rs, cs=cs: e.matmul(pb[3][rs, 0, :], qkT[k][rs, pr, cs], vnb[rs, :], start=True,
                                                               stop=True), reads=[R("qkT"), "vnb"], writes=["pb3"])
                    p.pe(lambda e, pr=pr, rs=rs: e.matmul(pb[4][:, 0, :], kdec[k][rs, pr, :], vnb[rs, :], start=True,
                                                        stop=True), reads=[R("kdec"), "vnb"], writes=["pb4"])
                    p.dve(lambda e, pr=pr, rs=rs: e.tensor_tensor(o_sb[k][rs, pr, :], otmp[rs, :], pb[3][rs, 0, :],
                                                                 op=ALU.add), reads=["otmp", "pb3"], writes=[R("o_sb")])
                    p.dve(lambda e, a=a, blk=blk: e.scalar_tensor_tensor(S[:], S[:], egl[a][:, blk:blk + 1],
                                                                        pb[4][:, 0, :], op0=ALU.mult, op1=ALU.add),
                          reads=["S", f"egl{a}", "pb4"], writes=["S"])
            for pr in range(4):
                p.act(lambda e, pr=pr: e.activation(junk[:], o_sb[k][:, pr, :], AF.Square, accum_out=ss[k][:, pr:pr + 1]),
                      reads=[R("o_sb")], writes=["junk", R("ss")])
            p.act(lambda e: e.activation(ss[k][:], ss[k][:], AF.Sqrt, bias=epsc[:, 0:1], scale=1.0 / 128.0),
                  reads=[R("ss"), "epsc"], writes=[R("ss")])
            p.dve(lambda e: e.reciprocal(ss[k][:], ss[k][:]), reads=[R("ss")], writes=[R("ss")])
            p.dve(lambda e: e.tensor_tensor(yt[k][:], o_sb[k][:], bc4(ss[k][:, :]), op=ALU.mult),
                  reads=[R("o_sb"), R("ss")], writes=[R("yt")])
            p.dve(lambda e: e.tensor_tensor(yt[k][:], yt[k][:], bcm(nw[:]), op=ALU.mult), reads=[R("yt"), "nw"],
                  writes=[R("yt")])
            p.dve(lambda e: e.tensor_tensor(ytb[k][:], yt[k][:], zs[k][:], op=ALU.mult), reads=[R("yt"), R("zs")],
                  writes=[R("ytb")])
            p.dma("sp", f"ytb_{k}", lambda e: e.dma_start(out=yo[:, bs, :], in_=ytb[k][:]), reads=[R("ytb")],
                  writes=[("yo", ti)])

        real_add = p.add

        def record(fn, ti):
            lst = []
            p.add = lambda eng, f, reads=(), writes=(), dma=None: lst.append((eng, f, reads, writes, dma))
            try:
                fn(ti)
            finally:
                p.add = real_add
            return lst

        for s_ in range(ntiles + 2):
            lists = []
            if s_ - 2 >= 0:
                lists.append(record(P2, s_ - 2))
            if 0 <= s_ - 1 < ntiles:
                lists.append(record(P1, s_ - 1))
            if s_ < ntiles:
                lists.append(record(P0, s_))
            lists = [l_ for l_ in lists if l_]
            pos = [0] * len(lists)
            total = sum(len(l_) for l_ in lists)
            for _ in range(total):
                best, bi = None, -1
                for li, l_ in enumerate(lists):
                    if pos[li] < len(l_):
                        frac = pos[li] / len(l_)
                        if best is None or frac < best:
                            best, bi = frac, li
                op = lists[bi][pos[bi]]
                pos[bi] += 1
                real_add(*op)
        p.emit(final_sems=[f"ytb_{k}" for k in range(NC3)])
    return nc


def _fm(v):
    return np.ascontiguousarray(np.asarray(v, np.float32).reshape(8, 128).T)


def _toT(x):
    return np.ascontiguousarray(x.reshape(T, 8, 128).transpose(2, 1, 0))


def assemble_G(Ab, h, l, inp):
    NBK = SEQ // 128
    qkv = np.stack([np.concatenate([Ab[q]["o_c"][w * 4 + h] for q in range(4)], 1) for w in range(3)])
    cwl = inp["gdn_conv_w"][l]
    cw = np.stack([cwl[:, w * 512 + h * 128:w * 512 + (h + 1) * 128].T for w in range(3)], 1).reshape(128, 12)
    oz = np.concatenate([Ab[q]["o_z"] for q in range(4)], 0)
    gab = np.stack([oz[:, 512 + h].reshape(NBK, 128).T, oz[:, 516 + h].reshape(NBK, 128).T], -1)
    hp = np.tile(np.array([[inp["gdn_A_log"][l, h], inp["gdn_dt_bias"][l, h]]], np.float32), (128, 1))
    z = oz[:, h * 128:(h + 1) * 128].reshape(NBK, 128, 128).transpose(1, 0, 2)
    nwb = np.tile(inp["gdn_norm"][l][None, :], (128, 1))
    f = lambda a: np.ascontiguousarray(a, dtype=np.float32)
    return dict(qkv=f(qkv), cw=f(cw), gab=f(gab), hp=f(hp), z=f(z), nwb=f(nwb))


_PROGS = {}


def _prog(name):
    if name not in _PROGS:
        _PROGS[name] = {"M": build_M, "A": build_A, "B": build_B, "G": build_G2, "C": lambda: build_C(False),
                        "CF": lambda: build_C(True)}[name]()
    return _PROGS[name]


def _run(name, in_maps):
    res = run_bass_kernel_spmd(_prog(name), in_maps, core_ids=list(range(8)))
    return res.results


def kernel(**inp):
    inp = {k: np.asarray(v) for k, v in inp.items()}
    cores = list(range(8))
    cT = np.ascontiguousarray(inp["c"].astype(np.float32).reshape(2, 8, 128).transpose(2, 1, 0))
    ims = []
    for core in cores:
        l, hf = core // 2, core % 2
        ims.append(dict(cT=cT, aw=np.ascontiguousarray(inp["ada_w"][l][:, hf * 3072:(hf + 1) * 3072]),
                        ab=np.ascontiguousarray(inp["ada_b"][l][hf * 3072:(hf + 1) * 3072].reshape(24, 128).T)))
    rm = _run("M", ims)
    mods = np.zeros((DEPTH, BATCH, 6 * D), np.float32)
    for core in cores:
        l, hf = core // 2, core % 2
        mods[l, :, hf * 3072:(hf + 1) * 3072] = rm[core]["mo"].transpose(2, 1, 0).reshape(2, 3072)
    xT = [_toT(inp["x"][core // 4, (core % 4) * T:(core % 4 + 1) * T].astype(np.float32)) for core in cores]
    for l in range(DEPTH):
        wl = {k: inp[k][l] for k in ("attn_sinks", "cmp_k_w1", "cmp_v_w1", "cmp_k_w2", "cmp_v_w2", "cmp_pe_k", "cmp_pe_v")}
        ims = []
        for core in cores:
            b, q = core // 4, core % 4
            m = mods[l, b]
            tabs = np.concatenate([_fm(inp["norm_mix"][l]), _fm(m[1024:2048]), _fm(m[0:1024])], 1)
            ims.append(dict(xT=xT[core], tabs=tabs, pos0=np.full((128, 1), q * T, np.float32), w=inp["w_in"][l]))
        ra = _run("A", ims)
        rb = _run("B", [assemble_B([ra[(core // 4) * 4 + q] for q in range(4)], core % 4, wl) for core in cores])
        rg = _run("G", [assemble_G([ra[(core // 4) * 4 + q] for q in range(4)], core % 4, l, inp) for core in cores])
        ims = []
        for core in cores:
            b, q = core // 4, core % 4
            m = mods[l, b]
            gd = [np.ascontiguousarray(rg[b * 4 + h]["yo"].transpose(1, 0, 2).reshape(SEQ, 128)[q * T:(q + 1) * T].T)
                  for h in range(4)]
            ocat = np.ascontiguousarray(np.concatenate([rb[core]["oc"], np.stack(gd)], 0))
            tabs = np.concatenate([_fm(m[2048:3072]), _fm(inp["norm_ffn"][l]), _fm(m[4096:5120]), _fm(m[3072:4096]),
                                   _fm(m[5120:6144]), _fm(inp["final_norm"])], 1)
            ims.append(dict(xT=xT[core], ocat=ocat, tabs=tabs, w_out=inp["w_out"][l], w_gu=inp["w_gate_up"][l],
                            w_down=inp["w_down"][l]))
        rc = _run("CF" if l == DEPTH - 1 else "C", ims)
        xT = [rc[core]["xo"] for core in cores]
    out = np.zeros((BATCH, SEQ, D), np.float32)
    for core in cores:
        b, q = core // 4, core % 4
        out[b, q * T:(q + 1) * T] = xT[core].transpose(2, 1, 0).reshape(T, D)
    return out
```

```python
import bisect
import math
from contextlib import ExitStack

import numpy as np
import ml_dtypes
import concourse.bass as bass
import concourse.mybir as mybir
from concourse.bass_utils import run_bass_kernel_spmd

F32 = mybir.dt.float32
BF16 = mybir.dt.bfloat16
I32 = mybir.dt.int32
AF = mybir.ActivationFunctionType
ALU = mybir.AluOpType
AX = mybir.AxisListType

D = 1024
SEQ = 8192
BATCH = 2
DEPTH = 4
T = 2048
NT = 512
DFF = 2816
EPS = 1e-6
ENGS = ("pe", "act", "dve", "pool", "sp")


class _Op:
    __slots__ = ("eng", "fn", "reads", "writes", "dma", "deps", "signal", "ordinal", "gidx")


def _is_psum(r):
    if isinstance(r, tuple):
        r = r[0]
    return isinstance(r, str) and (r.startswith("ps") or r.startswith("pb"))


class Prog:
    def __init__(self, nc):
        self.nc = nc
        self.ops = []
        self.per_eng = {e: [] for e in ENGS}
        self.last_w = {}
        self.readers = {}
        self.dma_groups = {}

    def add(self, eng, fn, reads=(), writes=(), dma=None):
        op = _Op()
        op.eng, op.fn, op.reads, op.writes, op.dma = eng, fn, tuple(reads), tuple(writes), dma
        op.deps = set()
        op.signal = False
        op.ordinal = None
        op.gidx = len(self.ops)
        for r in op.reads:
            w = self.last_w.get(r)
            if w is not None:
                op.deps.add(w)
            if _is_psum(r):
                for rd in self.readers.get(r, ()):
                    if rd.eng != eng:
                        op.deps.add(rd)
            self.readers.setdefault(r, []).append(op)
        for r in op.writes:
            w = self.last_w.get(r)
            if w is not None:
                if dma is not None and w.dma == dma:
                    op.deps |= w.deps
                else:
                    op.deps.add(w)
            for rd in self.readers.get(r, ()):
                if rd is not op:
                    op.deps.add(rd)
            self.readers[r] = []
            self.last_w[r] = op
        op.deps.discard(op)
        self.ops.append(op)
        self.per_eng[eng].append(op)
        if dma is not None:
            self.dma_groups.setdefault(dma, []).append(op)
        return op

    def pe(self, fn, reads=(), writes=()):
        return self.add("pe", fn, reads, writes)

    def act(self, fn, reads=(), writes=()):
        return self.add("act", fn, reads, writes)

    def dve(self, fn, reads=(), writes=()):
        return self.add("dve", fn, reads, writes)

    def pool(self, fn, reads=(), writes=()):
        return self.add("pool", fn, reads, writes)

    def dma(self, q, sem, fn, reads=(), writes=()):
        return self.add(q, fn, reads, writes, dma=sem)

    def emit(self, final_sems=()):
        nc = self.nc
        for op in self.ops:
            for d in op.deps:
                if d.dma is None:
                    if d.eng == "pe" and op.eng == "pe":
                        continue
                    d.signal = True
        for e in ENGS:
            n = 0
            for op in self.per_eng[e]:
                if op.dma is None and op.signal:
                    n += 1
                    op.ordinal = n
        gidx_of = {k: [o.gidx for o in ops] for k, ops in self.dma_groups.items()}
        with ExitStack() as es:
            sems = {e: es.enter_context(nc.semaphore("s_" + e)) for e in ENGS}
            dsems = {k: es.enter_context(nc.semaphore("d_" + str(k))) for k in self.dma_groups}
            block = es.enter_context(nc.Block())
            deco = {"pe": block.tensor, "act": block.scalar, "dve": block.vector, "pool": block.gpsimd,
                    "sp": block.sync}

            def make(e):
                def body(eng):
                    known = {}
                    for op in self.per_eng[e]:
                        need = {}
                        for d in op.deps:
                            if d.dma is not None:
                                key = ("d", d.dma)
                                v = 16 * bisect.bisect_left(gidx_of[d.dma], op.gidx)
                            else:
                                if d.eng == "pe" and e == "pe":
                                    continue
                                key = ("e", d.eng)
                                v = d.ordinal
                            if v > need.get(key, 0):
                                need[key] = v
                        for key, v in need.items():
                            if known.get(key, 0) >= v:
                                continue
                            known[key] = v
                            s = dsems[key[1]] if key[0] == "d" else sems[key[1]]
                            eng.wait_ge(s, v)
                        ins = op.fn(eng)
                        if op.dma is not None:
                            ins.then_inc(dsems[op.dma], 16)
                        elif op.signal:
                            ins.then_inc(sems[e], 1)
                    if e == "sp":
                        for k in final_sems:
                            if k not in dsems:
                                continue
                            eng.wait_ge(dsems[k], 16 * len(self.dma_groups[k]))
                return body

            for e in ENGS:
                if self.per_eng[e] or e == "sp":
                    deco[e](make(e))


class Ctx:
    def __init__(self, nc, es):
        self.nc, self.es = nc, es
        self.p = Prog(nc)
        self.n = 0

    def sb(self, shape, dt, name=None):
        self.n += 1
        return self.es.enter_context(self.nc.sbuf_tensor("s_" + (name or f"sb{self.n}"), list(shape), dt))

    def ps(self, shape, dt=F32, name=None):
        self.n += 1
        return self.es.enter_context(self.nc.psum_tensor(name or f"ps{self.n}", list(shape), dt))

    def din(self, name, shape, dt=F32):
        return self.nc.dram_tensor(name, list(shape), dt, kind="ExternalInput").ap()

    def dout(self, name, shape, dt=F32):
        return self.nc.dram_tensor(name, list(shape), dt, kind="ExternalOutput").ap()


def _swap(c0):
    return [(c0 + 32, c0 + 64), (c0, c0 + 32)]


def _plain(c0, n=64):
    return [(c0, c0 + n)]


FM_BLOCKS = [
    ("aq01", _plain(0, 128)), ("aq01s", _swap(0) + _swap(64)),
    ("aq23", _plain(128, 128)), ("aq23s", _swap(128) + _swap(192)),
    ("ak0", _plain(256) + _plain(256)), ("ak0s", _swap(256) + _swap(256)),
    ("ak1", _plain(320) + _plain(320)), ("ak1s", _swap(320) + _swap(320)),
    ("bq01", _plain(512, 128)), ("bq01s", _swap(512) + _swap(576)),
    ("bq23", _plain(640, 128)), ("bq23s", _swap(640) + _swap(704)),
    ("bks", _plain(896) + _plain(896)), ("bkss", _swap(896) + _swap(896)),
    ("bkw", _plain(1024) + _plain(1024)), ("bkws", _swap(1024) + _swap(1024)),
    ("cmp", _plain(768, 128)),
] + [("c%d" % i, _plain(1164 + 128 * i, 128)) for i in range(12)]
FM_OFF = {n: 128 * i for i, (n, _) in enumerate(FM_BLOCKS)}
GATE_OFF = 128 * len(FM_BLOCKS)
TM_OFF = GATE_OFF + 12
TM_COLS = [(384, 512), (960, 1024), (1088, 1152), (2700, 3212), (3212, 3220)]
TM_N = 128 + 64 + 64 + 512 + 8
WIN_COLS = TM_OFF + TM_N


def build_A(dbg=9):
    nc = bass.Bass("TRN2", target_bir_lowering=False)
    with ExitStack() as es:
        c = Ctx(nc, es)
        p = c.p
        xT = c.din("xT", [128, 8, T])
        tabs = c.din("tabs", [128, 24])
        pos0 = c.din("pos0", [128, 1])
        w = c.din("w", [D, 3220])
        wr = w.rearrange("(k p) c -> p k c", p=128)
        o_rope = c.dout("o_rope", [8, 128, T], BF16)
        o_plain = c.dout("o_plain", [3, 128, T], BF16)
        o_c = c.dout("o_c", [12, 128, T], F32)
        o_g = c.dout("o_g", [12, T], F32)
        o_v = c.dout("o_v", [T, 256], BF16)
        o_z = c.dout("o_z", [T, 520], F32)

        W = c.sb([128, 8, WIN_COLS], BF16, "W")
        tb = c.sb([128, 24], F32, "tb")
        s1 = c.sb([128, 8], F32, "s1")
        p0 = c.sb([128, 1], F32, "p0")
        S2 = c.sb([128, T], F32, "S2")
        ang = c.sb([128, T], F32, "ang")
        C2 = ang
        invrow = c.sb([1, 128], F32, "invrow")
        one1 = c.sb([1, 1], F32, "one1")
        inv = c.sb([128, 1], F32, "inv")
        ones_bf = c.sb([128, 128], BF16, "ones_bf")
        xt = [c.sb([128, 8, NT], F32, f"xt{i}") for i in range(1)]
        sq = c.sb([128, 8, NT], BF16, "sq")
        rstd = c.sb([128, NT], F32, "rstd")
        tmp = [c.sb([128, NT], F32, f"tmp{i}") for i in range(2)]
        hT = c.sb([128, 8, NT], BF16, "hT")
        outR = [c.sb([128, T], BF16, f"outR{i}") for i in range(8)]
        outP = [c.sb([128, T], BF16, f"outP{i}") for i in range(3)]
        t1 = [c.sb([128, NT], F32, f"t1_{i}") for i in range(2)]
        t2 = [c.sb([128, NT], F32, f"t2_{i}") for i in range(2)]
        stc = [c.sb([128, NT], F32, f"stc{i}") for i in range(3)]
        stg = c.sb([12, T], F32, "stg")
        stv = [c.sb([128, 256], BF16, f"stv{i}") for i in range(2)]
        stz = [c.sb([128, 520], F32, f"stz{i}") for i in range(2)]
        psb = [c.ps([128, 512], F32, f"psb{i}") for i in range(8)]

        p.dma("sp", "tb", lambda e: e.dma_start(out=tb[:], in_=tabs[:, :]), writes=["tb"])
        p.dma("sp", "p0", lambda e: e.dma_start(out=p0[:], in_=pos0[:, :]), writes=["p0"])
        col = 0
        for name, rngs in FM_BLOCKS:
            for (a, b) in rngs:
                p.dma("pool", "W_" + name, lambda e, a=a, b=b, col=col: e.dma_start(
                    out=W[:, :, col:col + b - a], in_=wr[:, :, a:b]), writes=[("W", name)])
                col += b - a
        p.dma("pool", "W_gate", lambda e: e.dma_start(out=W[:, :, GATE_OFF:GATE_OFF + 12], in_=wr[:, :, 1152:1164]),
              writes=[("W", "gate")])
        col = TM_OFF
        for (a, b) in TM_COLS:
            p.dma("pool", "W_tm", lambda e, a=a, b=b, col=col: e.dma_start(
                out=W[:, :, col:col + b - a], in_=wr[:, :, a:b]), writes=[("W", "tm")])
            col += b - a

        p.dve(lambda e: e.tensor_scalar(s1[:], tb[:, 8:16], 1.0, 32.0, op0=ALU.add, op1=ALU.mult),
              reads=["tb"], writes=["s1"])
        p.dve(lambda e: e.tensor_tensor(s1[:], s1[:], tb[:, 0:8], op=ALU.mult), reads=["tb", "s1"], writes=["s1"])
        p.dve(lambda e: e.memset(ones_bf[:], 1.0), writes=["ones_bf"])
        p.dve(lambda e: e.memset(one1[:], 1.0), writes=["one1"])
        for i in range(32):
            v = float(np.float32(1.0) / np.float32(np.float32(10000.0) ** np.float32(2 * i / 64.0)))
            p.dve(lambda e, i=i, v=v: e.memset(invrow[0:1, i:128:32], v), writes=["invrow"])
        p.pe(lambda e: e.matmul(psb[0][:, 0:1], invrow[0:1, :], one1[0:1, 0:1], start=True, stop=True),
             reads=["invrow", "one1"], writes=["psb0"])
        p.act(lambda e: e.copy(inv[:], psb[0][:, 0:1]), reads=["psb0"], writes=["inv"])
        p.pool(lambda e: e.iota(ang[:], [[1, T]], base=0, channel_multiplier=0,
                                allow_small_or_imprecise_dtypes=True), writes=["ang"])
        p.dve(lambda e: e.tensor_scalar(ang[:], ang[:], p0[:, 0:1], inv[:, 0:1], op0=ALU.add, op1=ALU.mult),
              reads=["ang", "p0", "inv"], writes=["ang"])
        TWO_PI = 2.0 * math.pi
        CW1 = 6.28125
        CW2 = TWO_PI - CW1
        nI = c.sb([128, NT], I32, "nI")
        yy = c.sb([128, NT], F32, "yy")
        mm_ = c.sb([128, NT], F32, "mm_")

        def sin_tile(dst, ts, shift):
            p.dve(lambda e: e.tensor_scalar(yy[:], ang[:, ts], float(shift), None, op0=ALU.add),
                  reads=["ang"], writes=["yy"])
            p.dve(lambda e: e.tensor_scalar(nI[:], yy[:], 1.0 / TWO_PI, None, op0=ALU.mult),
                  reads=["yy"], writes=["nI"])
            p.dve(lambda e: e.scalar_tensor_tensor(yy[:], nI[:], -CW1, yy[:], op0=ALU.mult, op1=ALU.add),
                  reads=["yy", "nI"], writes=["yy"])
            p.dve(lambda e: e.scalar_tensor_tensor(yy[:], nI[:], -CW2, yy[:], op0=ALU.mult, op1=ALU.add),
                  reads=["yy", "nI"], writes=["yy"])
            p.dve(lambda e: e.tensor_scalar(mm_[:], yy[:], math.pi, -TWO_PI, op0=ALU.is_gt, op1=ALU.mult),
                  reads=["yy"], writes=["mm_"])
            p.dve(lambda e: e.tensor_tensor(yy[:], yy[:], mm_[:], op=ALU.add), reads=["yy", "mm_"], writes=["yy"])
            p.dve(lambda e: e.tensor_scalar(yy[:], yy[:], -math.pi, math.pi, op0=ALU.max, op1=ALU.min),
                  reads=["yy"], writes=["yy"])
            p.act(lambda e: e.activation(dst[:, ts], yy[:], AF.Sin), reads=["yy"], writes=["S2", "ang", "C2"])

        for ti in range(T // NT):
            ts_ = slice(ti * NT, (ti + 1) * NT)
            sin_tile(S2, ts_, 0.0)
            sin_tile(C2, ts_, 0.5 * math.pi)
        for base in (0, 64):
            p.act(lambda e, base=base: e.mul(S2[base:base + 32, :], S2[base:base + 32, :], -1.0),
                  reads=["S2"], writes=["S2"])

        rope_pairs = [("aq01", 0), ("aq23", 1), ("ak0", 2), ("ak1", 3), ("bq01", 4), ("bq23", 5), ("bks", 6),
                      ("bkw", 7)]
        bank = [0]
        deps = c.sb([128, 1], F32, "deps")
        p.dve(lambda e: e.memset(deps[:], float(D * EPS)), writes=["deps"])

        def nextbank():
            b = bank[0]
            bank[0] = (b + 1) % 8
            return b

        def mm_block(coff, ncols_m, bk, rd):
            for k in range(8):
                p.pe(lambda e, k=k: e.matmul(psb[bk][0:ncols_m, :], W[:, k, coff:coff + ncols_m], hT[:, k, :],
                                             start=(k == 0), stop=(k == 7)),
                     reads=[("W", rd), ("hT", k)], writes=[f"psb{bk}"])

        for ti in range(T // NT if dbg >= 2 else 0):
            x_ = xt[0]
            xr = "xt0"
            ts = slice(ti * NT, (ti + 1) * NT)
            p.dma("sp", xr, lambda e, x_=x_, ts=ts: e.dma_start(out=x_[:], in_=xT[:, :, ts]), writes=[xr])
            p.act(lambda e, x_=x_: e.activation(sq[:], x_[:], AF.Square), reads=[xr], writes=["sq"])
            b0 = nextbank()
            for k in range(8):
                p.pe(lambda e, k=k, b0=b0: e.matmul(psb[b0][:], ones_bf[:], sq[:, k, :], start=(k == 0), stop=(k == 7)),
                     reads=["ones_bf", "sq"], writes=[f"psb{b0}"])
            p.act(lambda e, b0=b0: e.activation(rstd[:], psb[b0][:], AF.Sqrt, bias=deps[:, 0:1]),
                  reads=[f"psb{b0}", "deps"], writes=["rstd"])
            p.dve(lambda e: e.reciprocal(rstd[:], rstd[:]), reads=["rstd"], writes=["rstd"])
            for k in range(8):
                tm = tmp[k % 2]
                tr = f"tmp{k % 2}"
                p.dve(lambda e, k=k, tm=tm, x_=x_: e.tensor_tensor(tm[:], x_[:, k, :], rstd[:], op=ALU.mult),
                      reads=[xr, "rstd"], writes=[tr])
                p.act(lambda e, k=k, tm=tm: e.activation(hT[:, k, :], tm[:], AF.Identity, bias=tb[:, 16 + k:17 + k],
                                                         scale=s1[:, k:k + 1]),
                      reads=[tr, "s1", "tb"], writes=[("hT", k)])
            for name, oi in (rope_pairs if dbg >= 3 else []):
                b1 = nextbank()
                mm_block(FM_OFF[name], 128, b1, name)
                b2 = nextbank()
                mm_block(FM_OFF[name + "s"], 128, b2, name + "s")
                a1 = t1[oi % 2]
                a2 = t2[oi % 2]
                p.dve(lambda e, b1=b1, a1=a1, ts=ts: e.tensor_tensor(a1[:], psb[b1][:], C2[:, ts], op=ALU.mult),
                      reads=[f"psb{b1}", "C2"], writes=[f"t1_{oi % 2}"])
                if name.startswith("bq"):
                    po = outP[oi - 4]
                    p.act(lambda e, b1=b1, po=po, ts=ts: e.copy(po[:, ts], psb[b1][:]),
                          reads=[f"psb{b1}"], writes=[f"outP{oi - 4}"])
                p.dve(lambda e, b2=b2, a2=a2, ts=ts: e.tensor_tensor(a2[:], psb[b2][:], S2[:, ts], op=ALU.mult),
                      reads=[f"psb{b2}", "S2"], writes=[f"t2_{oi % 2}"])
                ro = outR[oi]
                p.dve(lambda e, a1=a1, a2=a2, ro=ro, ts=ts: e.tensor_tensor(ro[:, ts], a1[:], a2[:], op=ALU.add),
                       reads=[f"t1_{oi % 2}", f"t2_{oi % 2}"], writes=[f"outR{oi}"])
            if dbg < 4:
                continue
            b1 = nextbank()
            mm_block(FM_OFF["cmp"], 128, b1, "cmp")
            p.act(lambda e, b1=b1, ts=ts: e.copy(outP[2][:, ts], psb[b1][:]), reads=[f"psb{b1}"], writes=["outP2"])
            for i in range(12):
                b1 = nextbank()
                mm_block(FM_OFF["c%d" % i], 128, b1, "c%d" % i)
                sidx = i % 3
                st = stc[sidx]
                p.act(lambda e, b1=b1, st=st: e.copy(st[:], psb[b1][:]), reads=[f"psb{b1}"], writes=[f"stc{sidx}"])
                p.dma("sp", f"stc{sidx}", lambda e, st=st, i=i, ts=ts: e.dma_start(out=o_c[i, :, ts], in_=st[:]),
                      reads=[f"stc{sidx}"], writes=[("o_c", i, ti)])
            if dbg < 5:
                continue
            b1 = nextbank()
            mm_block(GATE_OFF, 12, b1, "gate")
            p.act(lambda e, b1=b1, ts=ts: e.copy(stg[:, ts], psb[b1][0:12, :]), reads=[f"psb{b1}"], writes=["stg"])
            if dbg < 6:
                continue
            for s in range(NT // 128):
                ss = slice(s * 128, (s + 1) * 128)
                tok = slice(ti * NT + s * 128, ti * NT + (s + 1) * 128)
                b1 = nextbank()
                for k in range(8):
                    p.pe(lambda e, k=k, b1=b1, ss=ss: e.matmul(psb[b1][:, 0:256], hT[:, k, ss],
                                                             W[:, k, TM_OFF:TM_OFF + 256], start=(k == 0), stop=(k == 7)),
                         reads=[("W", "tm"), ("hT", k)], writes=[f"psb{b1}"])
                b2 = nextbank()
                for k in range(8):
                    p.pe(lambda e, k=k, b2=b2, ss=ss: e.matmul(psb[b2][:, 0:512], hT[:, k, ss],
                                                             W[:, k, TM_OFF + 256:TM_OFF + 768], start=(k == 0),
                                                             stop=(k == 7)),
                         reads=[("W", "tm"), ("hT", k)], writes=[f"psb{b2}"])
                b3 = nextbank()
                for k in range(8):
                    p.pe(lambda e, k=k, b3=b3, ss=ss: e.matmul(psb[b3][:, 0:8], hT[:, k, ss],
                                                             W[:, k, TM_OFF + 768:TM_OFF + 776], start=(k == 0),
                                                             stop=(k == 7)),
                         reads=[("W", "tm"), ("hT", k)], writes=[f"psb{b3}"])
                sv = stv[s % 2]
                sz = stz[s % 2]
                p.act(lambda e, b1=b1, sv=sv: e.copy(sv[:], psb[b1][:, 0:256]), reads=[f"psb{b1}"], writes=[f"stv{s % 2}"])
                p.dve(lambda e, b2=b2, sz=sz: e.tensor_copy(sz[:, 0:512], psb[b2][:, 0:512]), reads=[f"psb{b2}"],
                      writes=[f"stz{s % 2}"])
                p.dve(lambda e, b3=b3, sz=sz: e.tensor_copy(sz[:, 512:520], psb[b3][:, 0:8]), reads=[f"psb{b3}"],
                      writes=[f"stz{s % 2}"])
                p.dma("sp", f"stv{s % 2}", lambda e, sv=sv, tok=tok: e.dma_start(out=o_v[tok, :], in_=sv[:]),
                      reads=[f"stv{s % 2}"], writes=[("o_v", ti, s)])
                p.dma("sp", f"stz{s % 2}", lambda e, sz=sz, tok=tok: e.dma_start(out=o_z[tok, :], in_=sz[:]),
                      reads=[f"stz{s % 2}"], writes=[("o_z", ti, s)])
        fs = ["stc0", "stc1", "stc2", "stv0", "stv1", "stz0", "stz1"]
        for i in range(8):
            p.dma("sp", f"oR{i}", lambda e, i=i: e.dma_start(out=o_rope[i, :, :], in_=outR[i][:]),
                  reads=[f"outR{i}"], writes=[("o_rope", i)])
            fs.append(f"oR{i}")
        for i in range(3):
            p.dma("sp", f"oP{i}", lambda e, i=i: e.dma_start(out=o_plain[i, :, :], in_=outP[i][:]),
                  reads=[f"outP{i}"], writes=[("o_plain", i)])
            fs.append(f"oP{i}")
        p.dma("sp", "og", lambda e: e.dma_start(out=o_g[:, :], in_=stg[:]), reads=["stg"], writes=["o_g"])
        fs.append("og")
        p.emit(final_sems=fs)
    return nc


NTC = 256


def build_C(final=False):
    nc = bass.Bass("TRN2", target_bir_lowering=False)
    with ExitStack() as es:
        c = Ctx(nc, es)
        p = c.p
        xT = c.din("xT", [128, 8, T])
        oc = c.din("ocat", [8, 128, T], BF16)
        tabs = c.din("tabs", [128, 48])
        wo = c.din("w_out", [D, D])
        wgu = c.din("w_gu", [D, 2 * DFF])
        wd = c.din("w_down", [DFF, D])
        xo = c.dout("xo", [128, 8, T])
        wor = wo.rearrange("(k p) c -> p k c", p=128)
        wgur = wgu.rearrange("(k p) c -> p k c", p=128)
        wdr = wd.rearrange("(k p) c -> p k c", p=128)
        NF = DFF // 128

        Wo = c.sb([128, 8, D], BF16, "Wo")
        Wg = c.sb([128, 8, 2 * DFF], BF16, "Wg")
        Wd = c.sb([128, NF, D], BF16, "Wd")
        tb = c.sb([128, 48], F32, "tb")
        s2 = c.sb([128, 8], F32, "s2")
        sfin = c.sb([128, 8], F32, "sfin")
        ones_bf = c.sb([128, 128], BF16, "ones_bf")
        deps = c.sb([128, 1], F32, "deps")
        xt = c.sb([128, 8, NTC], F32, "xt")
        ot = c.sb([128, 8, NTC], BF16, "ot")
        sq = c.sb([128, 8, NTC], BF16, "sq")
        rstd = c.sb([128, NTC], F32, "rstd")
        tmp = [c.sb([128, NTC], F32, f"tmp{i}") for i in range(2)]
        hT = c.sb([128, 8, NTC], BF16, "hT")
        gs = [c.sb([128, NTC], F32, f"gs{i}") for i in range(2)]
        aT = c.sb([128, NF, NTC], BF16, "aT")
        xout = c.sb([128, 8, NTC], F32, "xout")
        psb = [c.ps([128, 2, NTC], F32, f"psb{i}") for i in range(8)]
        bank = [0]

        def nextbank():
            b = bank[0]
            bank[0] = (b + 1) % 8
            return b

        p.dma("sp", "tb", lambda e: e.dma_start(out=tb[:], in_=tabs[:, :]), writes=["tb"])
        for j in range(2):
            p.dma("pool", f"Wo{j}", lambda e, j=j: e.dma_start(out=Wo[:, :, j * 512:(j + 1) * 512],
                                                             in_=wor[:, :, j * 512:(j + 1) * 512]), writes=[("Wo", j)])
        for fg in range(6):
            w_ = 512 if fg < 5 else 256
            for half in range(2):
                c0_ = half * DFF + fg * 512
                p.dma("pool", f"Wg{fg}", lambda e, c0_=c0_, w_=w_: e.dma_start(out=Wg[:, :, c0_:c0_ + w_],
                                                                              in_=wgur[:, :, c0_:c0_ + w_]),
                      writes=[("Wg", fg)])
        for j in range(2):
            p.dma("pool", f"Wd{j}", lambda e, j=j: e.dma_start(out=Wd[:, :, j * 512:(j + 1) * 512],
                                                             in_=wdr[:, :, j * 512:(j + 1) * 512]), writes=[("Wd", j)])
        p.dve(lambda e: e.memset(ones_bf[:], 1.0), writes=["ones_bf"])
        p.dve(lambda e: e.memset(deps[:], float(D * EPS)), writes=["deps"])
        p.dve(lambda e: e.tensor_scalar(s2[:], tb[:, 16:24], 1.0, 32.0, op0=ALU.add, op1=ALU.mult),
              reads=["tb"], writes=["s2"])
        p.dve(lambda e: e.tensor_tensor(s2[:], s2[:], tb[:, 8:16], op=ALU.mult), reads=["tb", "s2"], writes=["s2"])
        p.dve(lambda e: e.tensor_scalar(sfin[:], tb[:, 40:48], 32.0, None, op0=ALU.mult), reads=["tb"], writes=["sfin"])

        def rms_stats(src, srcres):
            p.act(lambda e: e.activation(sq[:], src[:], AF.Square), reads=srcres, writes=["sq"])
            b0 = nextbank()
            for k in range(8):
                p.pe(lambda e, k=k, b0=b0: e.matmul(psb[b0][:, 0, :], ones_bf[:], sq[:, k, :], start=(k == 0),
                                                  stop=(k == 7)), reads=["ones_bf", "sq"], writes=[f"psb{b0}"])
            p.act(lambda e, b0=b0: e.activation(rstd[:], psb[b0][:, 0, :], AF.Sqrt, bias=deps[:, 0:1]),
                  reads=[f"psb{b0}", "deps"], writes=["rstd"])
            p.dve(lambda e: e.reciprocal(rstd[:], rstd[:]), reads=["rstd"], writes=["rstd"])

        for ti in range(T // NTC):
            ts = slice(ti * NTC, (ti + 1) * NTC)
            p.dma("sp", "xt", lambda e, ts=ts: e.dma_start(out=xt[:], in_=xT[:, :, ts]), writes=[("xt", k) for k in range(8)])
            p.dma("sp", "ot", lambda e, ts=ts: e.dma_start(out=ot[:], in_=oc[:, :, ts].rearrange("k p t -> p k t")),
                  writes=["ot"])
            xres = [("xt", k) for k in range(8)]
            for fo2 in range(4):
                b1 = nextbank()
                for j in range(2):
                    fo = fo2 * 2 + j
                    for kc in range(8):
                        p.pe(lambda e, b1=b1, j=j, fo=fo, kc=kc: e.matmul(
                            psb[b1][:, j, :], Wo[:, kc, fo * 128:(fo + 1) * 128], ot[:, kc, :], start=(kc == 0),
                            stop=(kc == 7)), reads=[("Wo", fo // 4), "ot"], writes=[f"psb{b1}"])
                for j in range(2):
                    fo = fo2 * 2 + j
                    p.dve(lambda e, b1=b1, j=j, fo=fo: e.scalar_tensor_tensor(
                        xt[:, fo, :], psb[b1][:, j, :], tb[:, fo:fo + 1], xt[:, fo, :], op0=ALU.mult, op1=ALU.add),
                        reads=[f"psb{b1}", "tb", ("xt", fo)], writes=[("xt", fo)])
            rms_stats(xt, xres)
            for k in range(8):
                tm = tmp[k % 2]
                tr = f"tmp{k % 2}"
                p.dve(lambda e, k=k, tm=tm: e.tensor_tensor(tm[:], xt[:, k, :], rstd[:], op=ALU.mult),
                      reads=[("xt", k), "rstd"], writes=[tr])
                p.act(lambda e, k=k, tm=tm: e.activation(hT[:, k, :], tm[:], AF.Identity, bias=tb[:, 24 + k:25 + k],
                                                         scale=s2[:, k:k + 1]),
                      reads=[tr, "s2", "tb"], writes=[("hT", k)])
            for f in range(NF):
                b1 = nextbank()
                for j in range(2):
                    co = j * DFF + f * 128
                    for k in range(8):
                        p.pe(lambda e, b1=b1, j=j, co=co, k=k: e.matmul(
                            psb[b1][:, j, :], Wg[:, k, co:co + 128], hT[:, k, :], start=(k == 0), stop=(k == 7)),
                            reads=[("Wg", f // 4), ("hT", k)], writes=[f"psb{b1}"])
                g_ = gs[f % 2]
                p.act(lambda e, b1=b1, g_=g_: e.activation(g_[:], psb[b1][:, 0, :], AF.Silu),
                      reads=[f"psb{b1}"], writes=[f"gs{f % 2}"])
                p.dve(lambda e, b1=b1, g_=g_, f=f: e.tensor_tensor(aT[:, f, :], g_[:], psb[b1][:, 1, :], op=ALU.mult),
                      reads=[f"psb{b1}", f"gs{f % 2}"], writes=[("aT", f)])
            for fo2 in range(4):
                b1 = nextbank()
                for j in range(2):
                    fo = fo2 * 2 + j
                    for f in range(NF):
                        p.pe(lambda e, b1=b1, j=j, fo=fo, f=f: e.matmul(
                            psb[b1][:, j, :], Wd[:, f, fo * 128:(fo + 1) * 128], aT[:, f, :], start=(f == 0),
                            stop=(f == NF - 1)), reads=[("Wd", fo // 4), ("aT", f)], writes=[f"psb{b1}"])
                for j in range(2):
                    fo = fo2 * 2 + j
                    dst = xt if final else xout
                    dres = ("xt", fo) if final else ("xout", fo)
                    p.dve(lambda e, b1=b1, j=j, fo=fo, dst=dst: e.scalar_tensor_tensor(
                        dst[:, fo, :], psb[b1][:, j, :], tb[:, 32 + fo:33 + fo], xt[:, fo, :], op0=ALU.mult,
                        op1=ALU.add), reads=[f"psb{b1}", "tb", ("xt", fo)], writes=[dres])
            if final:
                rms_stats(xt, xres)
                for k in range(8):
                    tm = tmp[k % 2]
                    tr = f"tmp{k % 2}"
                    p.dve(lambda e, k=k, tm=tm: e.tensor_tensor(tm[:], xt[:, k, :], rstd[:], op=ALU.mult),
                          reads=[("xt", k), "rstd"], writes=[tr])
                    p.act(lambda e, k=k, tm=tm: e.activation(xout[:, k, :], tm[:], AF.Copy, scale=sfin[:, k:k + 1]),
                          reads=[tr, "sfin"], writes=[("xout", k)])
            p.dma("sp", "xout", lambda e, ts=ts: e.dma_start(out=xo[:, :, ts], in_=xout[:]),
                  reads=[("xout", k) for k in range(8)], writes=[("xo", ti)])
        p.emit(final_sems=["xout"])
    return nc


def build_M():
    nc = bass.Bass("TRN2", target_bir_lowering=False)
    HC = 3072
    with ExitStack() as es:
        c = Ctx(nc, es)
        p = c.p
        cT = c.din("cT", [128, 8, 2])
        aw = c.din("aw", [D, HC])
        ab = c.din("ab", [128, HC // 128])
        mo = c.dout("mo", [128, HC // 128, 2])
        awr = aw.rearrange("(k p) c -> p k c", p=128)
        NM = HC // 128
        Wt = [c.sb([128, 8, 768], F32, f"Wt{i}") for i in range(2)]
        ct = c.sb([128, 8, 2], F32, "ct")
        sg = c.sb([128, 8, 2], F32, "sg")
        sc = c.sb([128, 8, 2], F32, "sc")
        abt = c.sb([128, NM], F32, "abt")
        res = c.sb([128, NM, 2], F32, "res")
        ps = c.ps([128, NM, 2], F32, "ps_m")
        p.dma("sp", "ct", lambda e: e.dma_start(out=ct[:], in_=cT[:, :, :]), writes=["ct"])
        p.dma("sp", "abt", lambda e: e.dma_start(out=abt[:], in_=ab[:, :]), writes=["abt"])
        p.act(lambda e: e.activation(sg[:], ct[:], AF.Sigmoid), reads=["ct"], writes=["sg"])
        p.dve(lambda e: e.tensor_tensor(sc[:], sg[:], ct[:], op=ALU.mult), reads=["sg", "ct"], writes=["sc"])
        for j in range(HC // 768):
            wt = Wt[j % 2]
            wr_ = f"Wt{j % 2}"
            p.dma("sp", wr_, lambda e, wt=wt, j=j: e.dma_start(out=wt[:], in_=awr[:, :, j * 768:(j + 1) * 768]),
                  writes=[wr_])
            for mm in range(6):
                m = j * 6 + mm
                for k in range(8):
                    p.pe(lambda e, wt=wt, mm=mm, m=m, k=k: e.matmul(ps[:, m, :], wt[:, k, mm * 128:(mm + 1) * 128],
                                                                   sc[:, k, :], start=(k == 0), stop=(k == 7)),
                         reads=[wr_, "sc"], writes=["ps_m"])
        p.dve(lambda e: e.tensor_tensor(res[:], ps[:], abt[:, :, None].to_broadcast([128, NM, 2]), op=ALU.add),
              reads=["ps_m", "abt"], writes=["res"])
        p.dma("sp", "res", lambda e: e.dma_start(out=mo[:, :, :], in_=res[:]), reads=["res"], writes=["mo"])
        p.emit(final_sems=["res"])
    return nc


GT = 512


def build_G(ntiles=SEQ // GT, dbg=99):
    nc = bass.Bass("TRN2", target_bir_lowering=False)
    S_ = ntiles * GT
    NB = S_ // 128
    with ExitStack() as es:
        c = Ctx(nc, es)
        p = c.p
        qkv = c.din("qkv", [3, 128, S_])
        cw = c.din("cw", [128, 12])
        gab = c.din("gab", [128, NB, 2])
        hp = c.din("hp", [128, 2])
        zin = c.din("z", [128, NB, 128])
        nwb = c.din("nwb", [128, 128])
        yo = c.dout("yo", [128, NB, 128], BF16)

        val = c.sb([128, 128], F32, "val")
        MU = c.sb([128, 128], F32, "MU")
        ML = c.sb([128, 128], F32, "ML")
        ID = c.sb([128, 128], F32, "ID")
        ones_bf = c.sb([128, 128], BF16, "ones_bf")
        ones_f = c.sb([128, 128], F32, "ones_f")
        epsc = c.sb([128, 1], F32, "epsc")
        cwt = c.sb([128, 12], F32, "cwt")
        hpt = c.sb([128, 2], F32, "hpt")
        nw = c.sb([128, 128], F32, "nw")
        p.dma("sp", "cwt", lambda e: e.dma_start(out=cwt[:], in_=cw[:, :]), writes=["cwt"])
        p.dma("sp", "hpt", lambda e: e.dma_start(out=hpt[:], in_=hp[:, :]), writes=["hpt"])
        p.dma("sp", "nw", lambda e: e.dma_start(out=nw[:], in_=nwb[:, :]), writes=["nw"])
        p.pool(lambda e: e.iota(val[:], [[1, 128]], base=0, channel_multiplier=-1,
                                allow_small_or_imprecise_dtypes=True), writes=["val"])
        p.dve(lambda e: e.tensor_scalar(MU[:], val[:], 0.0, None, op0=ALU.is_ge), reads=["val"], writes=["MU"])
        p.dve(lambda e: e.tensor_scalar(ML[:], val[:], 0.0, None, op0=ALU.is_lt), reads=["val"], writes=["ML"])
        p.dve(lambda e: e.tensor_scalar(ID[:], val[:], 0.0, None, op0=ALU.is_equal), reads=["val"], writes=["ID"])
        p.dve(lambda e: e.memset(MU[0:64, 64:128], 0.0), reads=["MU"], writes=["MU"])
        p.dve(lambda e: e.memset(ML[64:128, 0:64], 0.0), reads=["ML"], writes=["ML"])
        p.dve(lambda e: e.memset(ones_bf[:], 1.0), writes=["ones_bf"])
        p.dve(lambda e: e.memset(ones_f[:], 1.0), writes=["ones_f"])
        p.dve(lambda e: e.memset(epsc[:], EPS), writes=["epsc"])

        pb = [c.ps([128, 4, 128], F32, f"pb{i}") for i in range(8)]

        gt_ = c.sb([128, NB, 2], F32, "gt_")
        g = c.sb([128, NB], F32, "g")
        beta = c.sb([128, NB], F32, "beta")
        nbeta = c.sb([128, NB], F32, "nbeta")
        tA = c.sb([128, NB], F32, "tA")
        eA = c.sb([128, 1], F32, "eA")
        gh = [c.sb([128, NB], F32, f"gh{a}") for a in range(2)]
        egc = c.sb([128, NB], F32, "egc")
        erem = c.sb([128, NB], F32, "erem")
        egl = [c.sb([128, NB], F32, f"egl{a}") for a in range(2)]
        sckb = c.sb([128, NB], F32, "sckb")
        p.dma("sp", "gt_", lambda e: e.dma_start(out=gt_[:], in_=gab[:, :, :]), writes=["gt_"])
        p.act(lambda e: e.activation(tA[:], gt_[:, :, 0], AF.Exp, bias=hpt[:, 1:2]), reads=["gt_", "hpt"], writes=["tA"])
        p.act(lambda e: e.activation(tA[:], tA[:], AF.Ln, bias=1.0), reads=["tA"], writes=["tA"])
        p.act(lambda e: e.activation(eA[:], hpt[:, 0:1], AF.Exp), reads=["hpt"], writes=["eA"])
        p.dve(lambda e: e.tensor_scalar(g[:], tA[:], eA[:, 0:1], -1.0, op0=ALU.mult, op1=ALU.mult),
              reads=["tA", "eA"], writes=["g"])
        p.act(lambda e: e.activation(beta[:], gt_[:, :, 1], AF.Exp, scale=-1.0), reads=["gt_"], writes=["beta"])
        p.dve(lambda e: e.tensor_scalar(beta[:], beta[:], 1.0, None, op0=ALU.add), reads=["beta"], writes=["beta"])
        p.dve(lambda e: e.reciprocal(beta[:], beta[:]), reads=["beta"], writes=["beta"])
        p.dve(lambda e: e.tensor_scalar(nbeta[:], beta[:], -1.0, None, op0=ALU.mult), reads=["beta"], writes=["nbeta"])
        for a in range(2):
            p.dve(lambda e, a=a: e.memset(gh[a][:], 0.0), writes=[f"gh{a}"])
            p.dve(lambda e, a=a: e.tensor_copy(gh[a][64 * a:64 * a + 64, :], g[64 * a:64 * a + 64, :]),
                  reads=["g", f"gh{a}"], writes=[f"gh{a}"])
        NBC = min(NB, 64)
        p.pe(lambda e: e.matmul(pb[0][:, 0, 0:NB], MU[:], g[:], start=True, stop=True), reads=["MU", "g"], writes=["pb0"])
        p.act(lambda e: e.activation(egc[:], pb[0][:, 0, 0:NB], AF.Exp), reads=["pb0"], writes=["egc"])
        p.pe(lambda e: e.matmul(pb[1][:, 0, 0:NB], ML[:], g[:], start=True, stop=True), reads=["ML", "g"], writes=["pb1"])
        p.act(lambda e: e.activation(erem[:], pb[1][:, 0, 0:NB], AF.Exp), reads=["pb1"], writes=["erem"])
        for a in range(2):
            p.pe(lambda e, a=a: e.matmul(pb[2 + a][:, 0, 0:NB], ones_f[:], gh[a][:], start=True, stop=True),
                 reads=["ones_f", f"gh{a}"], writes=[f"pb{2 + a}"])
            p.act(lambda e, a=a: e.activation(egl[a][:], pb[2 + a][:, 0, 0:NB], AF.Exp), reads=[f"pb{2 + a}"],
                  writes=[f"egl{a}"])
        p.dve(lambda e: e.tensor_tensor(sckb[:], beta[:], egc[:], op=ALU.mult), reads=["beta", "egc"], writes=["sckb"])

        xin = [c.sb([128, GT + 3], F32, f"xin{i}") for i in range(3)]
        acc = [c.sb([128, GT], F32, f"acc{i}") for i in range(3)]
        sil = [c.sb([128, GT], F32, f"sil{i}") for i in range(3)]
        sqb = [c.sb([128, GT], BF16, f"sqb{i}") for i in range(2)]
        rn = [c.sb([128, GT], F32, f"rn{i}") for i in range(2)]
        qn = c.sb([128, 4, 128], F32, "qn")
        kn = c.sb([128, 4, 128], F32, "kn")
        kbg = c.sb([128, 4, 128], BF16, "kbg")
        kdec = c.sb([128, 4, 128], BF16, "kdec")
        vb = c.sb([128, 4, 128], BF16, "vb")
        gU2 = c.sb([128, 4, 128], F32, "gU2")
        gU1 = c.sb([128, 4, 128], F32, "gU1")
        DecL = c.sb([128, 4, 128], F32, "DecL")
        DecT = c.sb([128, 4, 128], F32, "DecT")
        Xs = [c.sb([128, 4, 128], F32, f"Xs{i}") for i in range(2)]
        Ys = [c.sb([128, 4, 128], F32, f"Ys{i}") for i in range(2)]
        Q = c.sb([128, 4, 128], F32, "Q")
        TTb = c.sb([128, 4, 128], BF16, "TTb")
        qkT = c.sb([128, 4, 128], BF16, "qkT")
        u_sb = c.sb([128, 4, 128], F32, "u_sb")
        wT_sb = c.sb([128, 4, 128], F32, "wT_sb")
        o_sb = c.sb([128, 4, 128], F32, "o_sb")
        otmp = c.sb([128, 128], F32, "otmp")
        vnb = c.sb([128, 128], BF16, "vnb")
        S = c.sb([128, 128], F32, "S")
        zt = c.sb([128, 4, 128], F32, "zt")
        zs = c.sb([128, 4, 128], F32, "zs")
        junk = c.sb([128, 128], F32, "junk")
        ss = c.sb([128, 4], F32, "ss")
        yt = c.sb([128, 4, 128], F32, "yt")
        ytb = c.sb([128, 4, 128], BF16, "ytb")
        p.dve(lambda e: e.memset(S[:], 0.0), writes=["S"])

        def bc4(ap2):
            return ap2.unsqueeze(2).to_broadcast([128, 4, 128])

        def bcm(ap2):
            return ap2.unsqueeze(1).to_broadcast([128, 4, 128])

        for ti in range(ntiles):
            t0 = ti * GT
            b0 = ti * 4
            bs = slice(b0, b0 + 4)
            for i in range(3):
                if ti == 0:
                    p.dve(lambda e, i=i: e.memset(xin[i][:, 0:3], 0.0), writes=[f"xin{i}"])
                    p.dma("sp", f"xin{i}", lambda e, i=i: e.dma_start(out=xin[i][:, 3:GT + 3], in_=qkv[i, :, 0:GT]),
                          writes=[f"xin{i}"])
                else:
                    p.dma("sp", f"xin{i}", lambda e, i=i, t0=t0: e.dma_start(out=xin[i][:], in_=qkv[i, :, t0 - 3:t0 + GT]),
                          writes=[f"xin{i}"])
            p.dma("sp", "zt", lambda e, bs=bs: e.dma_start(out=zt[:], in_=zin[:, bs, :]), writes=["zt"])
            if dbg < 2:
                p.dma("sp", "yt", lambda e, bs=bs: e.dma_start(out=yo[:, bs, :], in_=ytb[:]), reads=["ytb"], writes=[("yo", ti)])
                continue
            for i in range(3):
                p.dve(lambda e, i=i: e.tensor_scalar(acc[i][:], xin[i][:, 3:GT + 3], cwt[:, 4 * i + 3:4 * i + 4], None,
                                                     op0=ALU.mult), reads=[f"xin{i}", "cwt"], writes=[f"acc{i}"])
                for j in range(3):
                    p.dve(lambda e, i=i, j=j: e.scalar_tensor_tensor(
                        acc[i][:], xin[i][:, j:GT + j], cwt[:, 4 * i + j:4 * i + j + 1], acc[i][:], op0=ALU.mult,
                        op1=ALU.add), reads=[f"xin{i}", "cwt", f"acc{i}"], writes=[f"acc{i}"])
                p.act(lambda e, i=i: e.activation(sil[i][:], acc[i][:], AF.Silu), reads=[f"acc{i}"], writes=[f"sil{i}"])
            for i in range(2):
                p.act(lambda e, i=i: e.activation(sqb[i][:], sil[i][:], AF.Square), reads=[f"sil{i}"], writes=[f"sqb{i}"])
                p.pe(lambda e, i=i: e.matmul(pb[i][:].rearrange("p a b -> p (a b)"), ones_bf[:], sqb[i][:], start=True,
                                             stop=True), reads=["ones_bf", f"sqb{i}"], writes=[f"pb{i}"])
                p.act(lambda e, i=i: e.activation(rn[i][:], pb[i][:].rearrange("p a b -> p (a b)"), AF.Sqrt,
                                                  bias=epsc[:, 0:1]), reads=[f"pb{i}", "epsc"], writes=[f"rn{i}"])
                p.dve(lambda e, i=i: e.reciprocal(rn[i][:], rn[i][:]), reads=[f"rn{i}"], writes=[f"rn{i}"])
            p.dve(lambda e: e.scalar_tensor_tensor(qn[:].rearrange("p a b -> p (a b)"), sil[0][:], float(128 ** -0.5),
                                                   rn[0][:], op0=ALU.mult, op1=ALU.mult),
                  reads=["sil0", "rn0"], writes=["qn"])
            p.dve(lambda e: e.tensor_tensor(kn[:].rearrange("p a b -> p (a b)"), sil[1][:], rn[1][:], op=ALU.mult),
                  reads=["sil1", "rn1"], writes=["kn"])
            if dbg < 3:
                p.dma("sp", "yt", lambda e, bs=bs: e.dma_start(out=yo[:, bs, :], in_=ytb[:]), reads=["ytb"], writes=[("yo", ti)])
                continue
            for pr in range(4):
                p.pe(lambda e, pr=pr: e.transpose(pb[2][:, pr, :], kn[:, pr, :], ID[:]), reads=["kn", "ID"], writes=["pb2"])
            for pr in range(4):
                p.pe(lambda e, pr=pr: e.transpose(pb[3][:, pr, :], sil[2][:, pr * 128:(pr + 1) * 128], ID[:]),
                     reads=["sil2", "ID"], writes=["pb3"])
            p.dve(lambda e, bs=bs: e.tensor_tensor(kbg[:], pb[2][:], bc4(sckb[:, bs]), op=ALU.mult),
                  reads=["pb2", "sckb"], writes=["kbg"])
            p.dve(lambda e, bs=bs: e.tensor_tensor(kdec[:], pb[2][:], bc4(erem[:, bs]), op=ALU.mult),
                  reads=["pb2", "erem"], writes=["kdec"])
            p.dve(lambda e, bs=bs: e.tensor_tensor(vb[:], pb[3][:], bc4(beta[:, bs]), op=ALU.mult),
                  reads=["pb3", "beta"], writes=["vb"])
            if dbg < 4:
                p.dma("sp", "yt", lambda e, bs=bs: e.dma_start(out=yo[:, bs, :], in_=ytb[:]), reads=["ytb"], writes=[("yo", ti)])
                continue
            p.dve(lambda e, bs=bs: e.tensor_tensor(gU2[:], bcm(ML[:]), bc4(g[:, bs]), op=ALU.mult),
                  reads=["ML", "g"], writes=["gU2"])
            p.dve(lambda e, bs=bs: e.tensor_tensor(gU1[:], bcm(MU[:]), bc4(g[:, bs]), op=ALU.mult),
                  reads=["MU", "g"], writes=["gU1"])
            p.pe(lambda e: e.matmul(pb[4][:].rearrange("p a b -> p (a b)"), MU[:], gU2[:].rearrange("p a b -> p (a b)"),
                                    start=True, stop=True), reads=["MU", "gU2"], writes=["pb4"])
            p.pe(lambda e: e.matmul(pb[5][:].rearrange("p a b -> p (a b)"), ML[:], gU1[:].rearrange("p a b -> p (a b)"),
                                    start=True, stop=True), reads=["ML", "gU1"], writes=["pb5"])
            p.act(lambda e: e.activation(DecL[:], pb[4][:], AF.Exp), reads=["pb4"], writes=["DecL"])
            p.act(lambda e: e.activation(DecT[:], pb[5][:], AF.Exp), reads=["pb5"], writes=["DecT"])
            p.dve(lambda e: e.tensor_tensor(DecL[:], DecL[:], bcm(ML[:]), op=ALU.mult), reads=["DecL", "ML"], writes=["DecL"])
            p.dve(lambda e: e.tensor_tensor(DecT[:], DecT[:], bcm(MU[:]), op=ALU.mult), reads=["DecT", "MU"], writes=["DecT"])
            if dbg < 5:
                p.dma("sp", "yt", lambda e, bs=bs: e.dma_start(out=yo[:, bs, :], in_=ytb[:]), reads=["ytb"], writes=[("yo", ti)])
                continue
            for pr in range(4):
                p.pe(lambda e, pr=pr: e.matmul(pb[0][:, pr, :], kn[:, pr, :], kn[:, pr, :], start=True, stop=True),
                     reads=["kn"], writes=["pb0"])
            for pr in range(4):
                p.pe(lambda e, pr=pr: e.matmul(pb[1][:, pr, :], kn[:, pr, :], qn[:, pr, :], start=True, stop=True),
                     reads=["kn", "qn"], writes=["pb1"])
            X, Y = Xs[0], Ys[0]
            p.dve(lambda e: e.tensor_tensor(X[:], pb[0][:], DecL[:], op=ALU.mult), reads=["pb0", "DecL"], writes=["Xs0"])
            p.dve(lambda e, bs=bs: e.tensor_tensor(X[:], X[:], bc4(nbeta[:, bs]), op=ALU.mult),
                  reads=["Xs0", "nbeta"], writes=["Xs0"])
            p.dve(lambda e: e.tensor_tensor(qkT[:], pb[1][:], DecT[:], op=ALU.mult), reads=["pb1", "DecT"], writes=["qkT"])
            for pr in range(4):
                p.pe(lambda e, pr=pr: e.transpose(pb[2][:, pr, :], Xs[0][:, pr, :], ID[:]), reads=["Xs0", "ID"],
                     writes=["pb2"])
            p.act(lambda e: e.copy(Ys[0][:], pb[2][:]), reads=["pb2"], writes=["Ys0"])
            p.dve(lambda e: e.tensor_tensor(Q[:], pb[2][:], bcm(ID[:]), op=ALU.add), reads=["pb2", "ID"], writes=["Q"])
            if dbg < 6:
                p.dma("sp", "yt", lambda e, bs=bs: e.dma_start(out=yo[:, bs, :], in_=ytb[:]), reads=["ytb"], writes=[("yo", ti)])
                continue
            for lvl in range(1, 6):
                cur, nxt = (lvl - 1) % 2, lvl % 2
                for pr in range(4):
                    p.pe(lambda e, pr=pr, cur=cur: e.matmul(pb[3][:, pr, :], Ys[cur][:, pr, :], Xs[cur][:, pr, :],
                                                          start=True, stop=True),
                         reads=[f"Ys{cur}", f"Xs{cur}"], writes=["pb3"])
                if lvl < 5:
                    for pr in range(4):
                        p.pe(lambda e, pr=pr, cur=cur: e.matmul(pb[4][:, pr, :], Xs[cur][:, pr, :], Ys[cur][:, pr, :],
                                                              start=True, stop=True),
                             reads=[f"Ys{cur}", f"Xs{cur}"], writes=["pb4"])
                p.act(lambda e, nxt=nxt: e.copy(Xs[nxt][:], pb[3][:]), reads=["pb3"], writes=[f"Xs{nxt}"])
                if lvl < 5:
                    p.dve(lambda e, nxt=nxt: e.tensor_copy(Ys[nxt][:], pb[4][:]), reads=["pb4"], writes=[f"Ys{nxt}"])
                for pr in range(4):
                    p.pe(lambda e, pr=pr, nxt=nxt: e.matmul(pb[5][:, pr, :], Xs[nxt][:, pr, :], Q[:, pr, :], start=True,
                                                          stop=True), reads=[f"Xs{nxt}", "Q"], writes=["pb5"])
                p.dve(lambda e: e.tensor_tensor(Q[:], Q[:], pb[5][:], op=ALU.add), reads=["Q", "pb5"], writes=["Q"])
            p.act(lambda e: e.copy(TTb[:], Q[:]), reads=["Q"], writes=["TTb"])
            if dbg < 7:
                p.dma("sp", "yt", lambda e, bs=bs: e.dma_start(out=yo[:, bs, :], in_=ytb[:]), reads=["ytb"], writes=[("yo", ti)])
                continue
            for pr in range(4):
                p.pe(lambda e, pr=pr: e.matmul(pb[6][:, pr, :], TTb[:, pr, :], vb[:, pr, :], start=True, stop=True),
                     reads=["TTb", "vb"], writes=["pb6"])
            for pr in range(4):
                p.pe(lambda e, pr=pr: e.matmul(pb[7][:, pr, :], kbg[:, pr, :], TTb[:, pr, :], start=True, stop=True),
                     reads=["TTb", "kbg"], writes=["pb7"])
            p.act(lambda e: e.copy(u_sb[:], pb[6][:]), reads=["pb6"], writes=["u_sb"])
            p.dve(lambda e: e.tensor_copy(wT_sb[:], pb[7][:]), reads=["pb7"], writes=["wT_sb"])
            if dbg < 8:
                p.dma("sp", "yt", lambda e, bs=bs: e.dma_start(out=yo[:, bs, :], in_=ytb[:]), reads=["ytb"], writes=[("yo", ti)])
                continue
            for pr in range(4):
                blk = b0 + pr
                for a in range(2):
                    rs = slice(64 * a, 64 * a + 64)
                    cs = slice(64 * a, 64 * a + 64)
                    p.pe(lambda e, pr=pr, rs=rs, cs=cs: e.matmul(pb[0][rs, 0, :], wT_sb[:, pr, cs], S[:], start=True,
                                                               stop=True), reads=["wT_sb", "S"], writes=["pb0"])
                    p.pe(lambda e, pr=pr, rs=rs, cs=cs: e.matmul(pb[1][rs, 0, :], qn[:, pr, cs], S[:], start=True,
                                                               stop=True), reads=["qn", "S"], writes=["pb1"])
                    p.dve(lambda e, pr=pr, rs=rs: e.tensor_tensor(vnb[rs, :], u_sb[rs, pr, :], pb[0][rs, 0, :],
                                                                 op=ALU.subtract), reads=["u_sb", "pb0"], writes=["vnb"])
                    p.act(lambda e, rs=rs, blk=blk: e.activation(otmp[rs, :], pb[1][rs, 0, :], AF.Copy,
                                                                 scale=egc[rs, blk:blk + 1]),
                          reads=["pb1", "egc"], writes=["otmp"])
                    p.pe(lambda e, pr=pr, rs=rs, cs=cs: e.matmul(pb[2][rs, 0, :], qkT[rs, pr, cs], vnb[rs, :], start=True,
                                                               stop=True), reads=["qkT", "vnb"], writes=["pb2"])
                    p.pe(lambda e, pr=pr, rs=rs: e.matmul(pb[3][:, 0, :], kdec[rs, pr, :], vnb[rs, :], start=True,
                                                        stop=True), reads=["kdec", "vnb"], writes=["pb3"])
                    p.dve(lambda e, pr=pr, rs=rs: e.tensor_tensor(o_sb[rs, pr, :], otmp[rs, :], pb[2][rs, 0, :],
                                                                 op=ALU.add), reads=["otmp", "pb2"], writes=["o_sb"])
                    p.dve(lambda e, a=a, blk=blk: e.scalar_tensor_tensor(S[:], S[:], egl[a][:, blk:blk + 1],
                                                                        pb[3][:, 0, :], op0=ALU.mult, op1=ALU.add),
                          reads=["S", f"egl{a}", "pb3"], writes=["S"])
            if dbg < 9:
                p.dma("sp", "yt", lambda e, bs=bs: e.dma_start(out=yo[:, bs, :], in_=ytb[:]), reads=["ytb"], writes=[("yo", ti)])
                continue
            for pr in range(4):
                p.act(lambda e, pr=pr: e.activation(junk[:], o_sb[:, pr, :], AF.Square, accum_out=ss[:, pr:pr + 1]),
                      reads=["o_sb"], writes=["junk", ("ss", pr)])
            p.act(lambda e: e.activation(ss[:], ss[:], AF.Sqrt, bias=epsc[:, 0:1], scale=1.0 / 128.0),
                  reads=[("ss", i) for i in range(4)] + ["epsc"], writes=[("ss", i) for i in range(4)])
            p.dve(lambda e: e.reciprocal(ss[:], ss[:]), reads=[("ss", i) for i in range(4)],
                  writes=[("ss", i) for i in range(4)])
            p.act(lambda e: e.activation(zs[:], zt[:], AF.Silu), reads=["zt"], writes=["zs"])
            p.dve(lambda e: e.tensor_tensor(yt[:], o_sb[:], bc4(ss[:, :]), op=ALU.mult),
                  reads=["o_sb"] + [("ss", i) for i in range(4)], writes=["yt"])
            p.dve(lambda e: e.tensor_tensor(yt[:], yt[:], bcm(nw[:]), op=ALU.mult), reads=["yt", "nw"], writes=["yt"])
            p.dve(lambda e: e.tensor_tensor(ytb[:], yt[:], zs[:], op=ALU.mult), reads=["yt", "zs"], writes=["ytb"])
            p.dma("sp", "yt", lambda e, bs=bs: e.dma_start(out=yo[:, bs, :], in_=ytb[:]), reads=["ytb"], writes=[("yo", ti)])
        p.emit(final_sems=["yt"])
    return nc


SCALE = 0.125
NEG = -30000.0


def build_B(qblocks=tuple(range(16)), dbg=99):
    nc = bass.Bass("TRN2", target_bir_lowering=False)
    with ExitStack() as es:
        c = Ctx(nc, es)
        p = c.p
        aq_d = c.din("aq", [2, 128, T], BF16)
        ak_d = c.din("ak", [2, 128, T + 128], BF16)
        av_d = c.din("av", [128, 17, 128], BF16)
        sk_d = c.din("sinks", [128, 4])
        offs_d = c.din("offs", [128, 4])
        bqr_d = c.din("bqr", [2, 128, T], BF16)
        bq_d = c.din("bq", [2, 128, T], BF16)
        bkw_d = c.din("bkw", [128, T + 512], BF16)
        bvw_d = c.din("bvw", [128, 20, 64], BF16)
        bks_d = c.din("bks", [128, SEQ], BF16)
        bvs_d = c.din("bvs", [128, 64, 64], BF16)
        cmp_d = c.din("cmpT", [128, SEQ], BF16)
        bg_d = c.din("bg", [12, T])
        w1k_d = c.din("w1k", [2048, 256])
        w1v_d = c.din("w1v", [2048, 256])
        w2k_d = c.din("w2k", [256, 64])
        w2v_d = c.din("w2v", [256, 64])
        pek_d = c.din("pek", [128, 16])
        pev_d = c.din("pev", [128, 16])
        oc_d = c.dout("oc", [4, 128, T], BF16)

        pb = [c.ps([128, 4, 128], F32, f"pb{i}") for i in range(8)]
        PB2 = ["pb2"]

        aq = c.sb([128, 2, T], BF16, "aq")
        ak = c.sb([128, 2, T + 128], BF16, "ak")
        avg = c.sb([128, 17, 2, 128], BF16, "avg")
        bqr = c.sb([128, 2, T], BF16, "bqr")
        bq = c.sb([128, 2, T], BF16, "bq")
        bkw = c.sb([128, T + 512], BF16, "bkw")
        bvwg = c.sb([128, 20, 128], BF16, "bvwg")
        bks = c.sb([128, SEQ], BF16, "bks")
        bvsg = c.sb([128, 64, 128], BF16, "bvsg")
        KKk = c.sb([128, SEQ], BF16, "KKk")
        KKv = c.sb([128, SEQ], BF16, "KKv")
        sk = c.sb([128, 4], F32, "sk")
        offs = c.sb([128, 4], F32, "offs")
        bg = c.sb([12, T], F32, "bg")
        w1k = c.sb([128, 16, 256], BF16, "w1k")
        w1v = c.sb([128, 16, 256], BF16, "w1v")
        w2k = c.sb([128, 2, 128], BF16, "w2k")
        w2v = c.sb([128, 2, 64], BF16, "w2v")
        pek = c.sb([128, 16], BF16, "pek")
        pev = c.sb([128, 16], BF16, "pev")
        for j in range(2):
            p.dma("sp", "aq", lambda e, j=j: e.dma_start(out=aq[:, j, :], in_=aq_d[j, :, :]), writes=["aq"])
            p.dma("sp", "ak", lambda e, j=j: e.dma_start(out=ak[:, j, :], in_=ak_d[j, :, :]), writes=["ak"])
            p.dma("sp", "bqr", lambda e, j=j: e.dma_start(out=bqr[:, j, :], in_=bqr_d[j, :, :]), writes=["bqr"])
            p.dma("sp", "bq", lambda e, j=j: e.dma_start(out=bq[:, j, :], in_=bq_d[j, :, :]), writes=["bq"])
        p.dve(lambda e: e.memset(avg[:], 1.0), writes=["avg"])
        p.dve(lambda e: e.memset(bvwg[:], 1.0), writes=["bvwg"])
        p.dve(lambda e: e.memset(bvsg[:], 1.0), writes=["bvsg"])
        for j in range(2):
            p.dma("sp", "avg", lambda e, j=j: e.dma_start(out=avg[:, :, j, 0:64], in_=av_d[:, :, 64 * j:64 * j + 64]),
                  reads=["avg"], writes=["avg"])
        p.dma("sp", "bvwg", lambda e: e.dma_start(out=bvwg[:, :, 0:64], in_=bvw_d[:, :, :]), reads=["bvwg"], writes=["bvwg"])
        p.dma("sp", "bvsg", lambda e: e.dma_start(out=bvsg[:, :, 0:64], in_=bvs_d[:, :, :]), reads=["bvsg"], writes=["bvsg"])
        p.dma("sp", "bkw", lambda e: e.dma_start(out=bkw[:], in_=bkw_d[:, :]), writes=["bkw"])
        p.dma("sp", "bks", lambda e: e.dma_start(out=bks[:], in_=bks_d[:, :]), writes=["bks"])
        p.dve(lambda e: e.memset(KKk[:, SEQ - 8:SEQ], 0.0), writes=["KKk"])
        p.dve(lambda e: e.memset(KKv[:, SEQ - 8:SEQ], 0.0), writes=["KKv"])
        p.dma("sp", "KKk", lambda e: e.dma_start(out=KKk[0:64, :], in_=cmp_d[0:64, :]), reads=["KKk"], writes=["KKk"])
        p.dma("sp", "KKk", lambda e: e.dma_start(out=KKk[64:128, 0:SEQ - 1], in_=cmp_d[0:64, 1:SEQ]), reads=["KKk"],
              writes=["KKk"])
        p.dma("sp", "KKv", lambda e: e.dma_start(out=KKv[0:64, :], in_=cmp_d[64:128, :]), reads=["KKv"], writes=["KKv"])
        p.dma("sp", "KKv", lambda e: e.dma_start(out=KKv[64:128, 0:SEQ - 1], in_=cmp_d[64:128, 1:SEQ]), reads=["KKv"],
              writes=["KKv"])
        p.dma("sp", "sk", lambda e: e.dma_start(out=sk[:], in_=sk_d[:, :]), writes=["sk"])
        p.dma("sp", "offs", lambda e: e.dma_start(out=offs[:], in_=offs_d[:, :]), writes=["offs"])
        p.dma("sp", "bg", lambda e: e.dma_start(out=bg[:], in_=bg_d[:, :]), writes=["bg"])
        p.dma("pool", "w1k", lambda e: e.dma_start(out=w1k[:], in_=w1k_d.rearrange("(m p) c -> p m c", p=128)), writes=["w1k"])
        p.dma("pool", "w1v", lambda e: e.dma_start(out=w1v[:], in_=w1v_d.rearrange("(m p) c -> p m c", p=128)), writes=["w1v"])
        for dup in range(2):
            p.dma("pool", "w2k", lambda e, dup=dup: e.dma_start(out=w2k[:, :, 64 * dup:64 * dup + 64],
                                                              in_=w2k_d.rearrange("(m p) c -> p m c", p=128)), writes=["w2k"])
        p.dma("pool", "w2v", lambda e: e.dma_start(out=w2v[:], in_=w2v_d.rearrange("(m p) c -> p m c", p=128)), writes=["w2v"])
        p.dma("pool", "pek", lambda e: e.dma_start(out=pek[:], in_=pek_d[:, :]), writes=["pek"])
        p.dma("pool", "pev", lambda e: e.dma_start(out=pev[:], in_=pev_d[:, :]), writes=["pev"])

        val = c.sb([128, 128], F32, "val")
        MUq = c.sb([128, 128], BF16, "MUq")
        MLq = c.sb([128, 128], BF16, "MLq")
        M2 = c.sb([128, 2, 128], BF16, "M2")
        IDb = c.sb([128, 128], BF16, "IDb")
        ones_b = c.sb([128, 128], BF16, "ones_b")
        p.pool(lambda e: e.iota(val[:], [[1, 128]], base=0, channel_multiplier=-1, allow_small_or_imprecise_dtypes=True),
               writes=["val"])
        p.dve(lambda e: e.tensor_scalar(MUq[:], val[:], 0.0, None, op0=ALU.is_ge), reads=["val"], writes=["MUq"])
        p.dve(lambda e: e.tensor_scalar(MLq[:], val[:], 0.0, None, op0=ALU.is_lt), reads=["val"], writes=["MLq"])
        p.dve(lambda e: e.tensor_scalar(IDb[:], val[:], 0.0, None, op0=ALU.is_equal), reads=["val"], writes=["IDb"])
        p.dve(lambda e: e.tensor_copy(M2[:, 0, :], MLq[:]), reads=["MLq"], writes=["M2"])
        p.dve(lambda e: e.tensor_copy(M2[:, 1, :], MUq[:]), reads=["MUq", "M2"], writes=["M2"])
        p.dve(lambda e: e.memset(ones_b[:], 1.0), writes=["ones_b"])
        Et = c.sb([128, 16, 2, 64], F32, "Et")
        E = c.sb([128, 64, 128], BF16, "E")
        for qtr in range(4):
            p.pool(lambda e, qtr=qtr: e.iota(Et[:], [[-2, 16], [-1, 2], [0, 64]], base=-32 * qtr, channel_multiplier=1,
                                             allow_small_or_imprecise_dtypes=True), reads=["Et"], writes=["Et"])
            p.dve(lambda e, qtr=qtr: e.tensor_scalar(E[:, 16 * qtr:16 * qtr + 16, :].rearrange("p a (b c) -> p a b c", b=2),
                                                     Et[:], 0.0, None, op0=ALU.is_equal), reads=["Et"], writes=["E", "Et"])
        Gt = c.sb([128, 4, 128], F32, "Gt")
        Gm = c.sb([128, 4, 128], F32, "Gm")
        Gm2 = Gt
        p.pool(lambda e: e.iota(Gt[:], [[128, 4], [-4, 128]], base=0, channel_multiplier=1,
                                allow_small_or_imprecise_dtypes=True), writes=["Gt"])
        p.dve(lambda e: e.tensor_scalar(Gm[:], Gt[:], -1.0, None, op0=ALU.is_ge), reads=["Gt"], writes=["Gm"])
        p.dve(lambda e: e.tensor_scalar(Gm2[:], Gt[:], 3.0, None, op0=ALU.is_le), reads=["Gt", "Gm"], writes=["Gm2", "Gt"])
        p.dve(lambda e: e.tensor_tensor(Gm[:], Gm[:], Gm2[:], op=ALU.mult), reads=["Gm", "Gm2"], writes=["Gm"])
        valc = c.sb([128, 2, 128], F32, "valc")
        p.pool(lambda e: e.iota(valc[:], [[2048, 2], [-1, 128]], base=4096, channel_multiplier=16,
                                allow_small_or_imprecise_dtypes=True), writes=["valc"])
        nidx = c.sb([128, 4], F32, "nidx")
        biasn = c.sb([128, 4], F32, "biasn")
        bt2 = c.sb([128, 4], F32, "bt2")
        p.pool(lambda e: e.iota(nidx[:], [[128, 4]], base=0, channel_multiplier=1, allow_small_or_imprecise_dtypes=True),
               writes=["nidx"])
        p.dve(lambda e: e.tensor_scalar(biasn[:], nidx[:], offs[:, 1:2], NEG, op0=ALU.is_lt, op1=ALU.mult),
              reads=["nidx", "offs"], writes=["biasn"])
        p.dve(lambda e: e.tensor_scalar(bt2[:], nidx[:], 510.5, NEG, op0=ALU.is_gt, op1=ALU.mult),
              reads=["nidx"], writes=["bt2"])
        p.dve(lambda e: e.tensor_tensor(biasn[:], biasn[:], bt2[:], op=ALU.add), reads=["biasn", "bt2"], writes=["biasn"])
        bbi = c.sb([128, 128], F32, "bbi")
        validblk = c.sb([128, 128], F32, "validblk")
        f0m = c.sb([128, 128], F32, "f0m")
        p.pool(lambda e: e.iota(bbi[:], [[1, 128]], base=0, channel_multiplier=0, allow_small_or_imprecise_dtypes=True),
               writes=["bbi"])
        p.dve(lambda e: e.tensor_scalar(validblk[:], bbi[:], offs[:, 2:3], None, op0=ALU.is_ge), reads=["bbi", "offs"],
              writes=["validblk"])
        p.dve(lambda e: e.tensor_scalar(f0m[:], bbi[:], offs[:, 2:3], 2e9, op0=ALU.is_equal, op1=ALU.mult),
              reads=["bbi", "offs"], writes=["f0m"])
        p.dve(lambda e: e.tensor_scalar(f0m[:], f0m[:], -1e9, None, op0=ALU.add), reads=["f0m"], writes=["f0m"])
        esk = c.sb([128, 4], F32, "esk")
        p.act(lambda e: e.activation(esk[:], sk[:], AF.Exp), reads=["sk"], writes=["esk"])
        sg = bg
        p.act(lambda e: e.activation(sg[:], bg[:], AF.Exp, scale=-1.0), reads=["bg"], writes=["sg", "bg"])
        p.dve(lambda e: e.tensor_scalar(sg[:], sg[:], 1.0, None, op0=ALU.add), reads=["sg"], writes=["sg"])
        p.dve(lambda e: e.reciprocal(sg[:], sg[:]), reads=["sg"], writes=["sg"])
        selg = c.sb([12, 12, 64], F32, "selg")
        p.pool(lambda e: e.iota(selg[:], [[-1, 12], [0, 64]], base=0, channel_multiplier=1,
                                allow_small_or_imprecise_dtypes=True), writes=["selg"])
        p.dve(lambda e: e.tensor_scalar(selg[:], selg[:], 0.0, None, op0=ALU.is_equal), reads=["selg"], writes=["selg"])

        hid = [c.sb([128, 2, 512], BF16, f"hid{i}") for i in range(2)]
        bh = c.sb([128, 4], F32, "bh")
        kcT = c.sb([128, 512], BF16, "kcT")
        vcg = c.sb([128, 4, 128], BF16, "vcg")
        p.dve(lambda e: e.memset(vcg[:], 1.0), writes=["vcg"])
        for wi, (w1, KK, pe_) in enumerate([(w1k, KKk, pek), (w1v, KKv, pev)]):
            w1n, kkn, pen = ["w1k", "w1v"][wi], ["KKk", "KKv"][wi], ["pek", "pev"][wi]
            p.dve(lambda e, wi=wi: e.memset(hid[wi][:], 0.0), writes=[f"hid{wi}"])
            for hh in range(2):
                bk = 2 * wi + hh
                for m in range(16):
                    p.pe(lambda e, bk=bk, m=m, hh=hh, w1=w1, pe_=pe_: e.matmul(
                        pb[4][:, bk, 0:1], w1[:, m, hh * 128:(hh + 1) * 128], pe_[:, m:m + 1], start=(m == 0),
                        stop=(m == 15)), reads=[w1n, pen], writes=["pb4"])
            p.dve(lambda e, wi=wi: e.tensor_copy(bh[:, 2 * wi:2 * wi + 2], pb[4][:, 2 * wi:2 * wi + 2, 0]),
                  reads=["pb4"], writes=["bh"])
            for hh in range(2):
                bk = hh
                for m in range(16):
                    p.pe(lambda e, bk=bk, m=m, hh=hh, w1=w1, KK=KK: e.matmul(
                        pb[bk][:].rearrange("p a b -> p (a b)")[:, 0:511], w1[:, m, hh * 128:(hh + 1) * 128],
                        KK[:, 2 * m:2 * m + 16 * 510 + 1:16], start=(m == 0), stop=(m == 15)),
                        reads=[w1n, kkn], writes=[f"pb{bk}"])
                p.act(lambda e, bk=bk, hh=hh, wi=wi: e.activation(
                    hid[wi][:, hh, 0:511], pb[bk][:].rearrange("p a b -> p (a b)")[:, 0:511], AF.Silu,
                    bias=bh[:, 2 * wi + hh:2 * wi + hh + 1]), reads=[f"pb{bk}", "bh"], writes=[f"hid{wi}"])
        for hh in range(2):
            p.pe(lambda e, hh=hh: e.matmul(pb[2][:].rearrange("p a b -> p (a b)"), w2k[:, hh, :], hid[0][:, hh, :],
                                           start=(hh == 0), stop=(hh == 1)), reads=["w2k", "hid0"], writes=PB2)
        p.act(lambda e: e.copy(kcT[:], pb[2][:].rearrange("p a b -> p (a b)")), reads=PB2, writes=["kcT"])
        for cc in range(4):
            for hh in range(2):
                p.pe(lambda e, cc=cc, hh=hh: e.matmul(pb[3][:, cc, 0:64], hid[1][:, hh, cc * 128:(cc + 1) * 128],
                                                      w2v[:, hh, :], start=(hh == 0), stop=(hh == 1)),
                     reads=["w2v", "hid1"], writes=["pb3"])
        p.act(lambda e: e.copy(vcg[:, :, 0:64], pb[3][:, :, 0:64]), reads=["pb3", "vcg"], writes=["vcg"])

        PT = [c.sb([128, 4, 128], BF16, f"PT{i}") for i in range(5)]
        QBD = {}
        for nm in ("QA", "QR", "QU"):
            for par in range(2):
                for j in range(2):
                    t_ = c.sb([128, 256], BF16, f"{nm}{par}{j}")
                    QBD[(nm, par, j)] = t_
                    p.pool(lambda e, t_=t_: e.memset(t_[:], 0.0), writes=[f"{nm}{par}{j}"])
        PTc = c.sb([128, 4, 4, 128], BF16, "PTc")
        mk = c.sb([128, 2, 128], BF16, "mk")
        ocA = KKk[:, 0:2 * T].rearrange("p (j t) -> p j t", j=2)
        ocB = KKk[:, 2 * T:4 * T].rearrange("p (j t) -> p j t", j=2)
        Etf = Et[:].rearrange("p a b c -> p (a b c)")
        rd = Etf[:, 0:512].rearrange("p (a b) -> p a b", a=4)
        rdA = c.sb([64, 4, 128], F32, "rdA")
        accBs = [c.sb([64, 4, 128], F32, f"accB{k}") for k in range(2)]
        w1kf = w1k[:].rearrange("p a b -> p (a b)").bitcast(F32)
        tmpBp = w1kf[0:64, 0:512].rearrange("p (a b) -> p a b", a=4)
        fBp = w1kf[0:64, 512:1024].rearrange("p (a b) -> p a b", a=4)
        tmpBs = w1kf[0:64, 1024:1536].rearrange("p (a b) -> p a b", a=4)
        fBs = w1kf[0:64, 1536:2048].rearrange("p (a b) -> p a b", a=4)
        pn = Etf[:, 512:1024].rearrange("p (a b) -> p a b", a=4)
        psh = Etf[:, 1024:1536].rearrange("p (a b) -> p a b", a=4)
        score = c.sb([128, 128], F32, "score")
        work = c.sb([128, 128], F32, "work")
        m8a = c.sb([128, 8], F32, "m8a")
        m8b = c.sb([128, 8], F32, "m8b")
        sel = c.sb([128, 128], BF16, "sel")
        selTs = [c.sb([128, 128], BF16, f"selT{k}") for k in range(2)]
        w1vf = w1v[:].rearrange("p a b -> p (a b)")
        selb = [w1vf[:, 512 * k:512 * k + 512].rearrange("p (a b) -> p a b", a=4) for k in range(2)]
        pbT = c.ps([128, 128], BF16, "pbT") if False else None
        ptn = [0]

        def next_pt():
            k = ptn[0]
            ptn[0] = (k + 1) % 5
            return k
        PTp = [w1vf[:, 1024 + 512 * i:1536 + 512 * i].rearrange("p (a b) -> p a b", a=4) for i in range(3)]
        ptnp = [0]

        def next_ptp():
            k = ptnp[0]
            ptnp[0] = (k + 1) % 3
            return k
        sbank = [0]

        def next_sb():
            k = sbank[0]
            sbank[0] = 1 - k
            return k

        def gate_finish(br, i, acc_first, gb, src, fB, tmpB, fn_, tn_):
            qs = slice(128 * i, 128 * i + 128)
            accB = accBs[i % 2]
            an_ = f"accB{i % 2}"
            for h in range(4):
                p.pe(lambda e, h=h: e.matmul(pb[gb][0:64, h, :], selg[:, br * 4 + h, :], sg[:, qs], start=True, stop=True),
                     reads=["selg", "sg"], writes=[f"pb{gb}"])
            p.dve(lambda e: e.tensor_scalar(fB[:], pb[src][64:128, :, :], 1e-30, None, op0=ALU.add),
                  reads=[f"pb{src}"], writes=[fn_])
            p.dve(lambda e: e.reciprocal(fB[:], fB[:]), reads=[fn_], writes=[fn_])
            p.dve(lambda e: e.tensor_tensor(fB[:], fB[:], pb[gb][0:64, :, :], op=ALU.mult), reads=[fn_, f"pb{gb}"], writes=[fn_])
            if acc_first:
                p.dve(lambda e: e.tensor_tensor(accB[:], pb[src][0:64, :, :], fB[:], op=ALU.mult),
                      reads=[f"pb{src}", fn_], writes=[an_])
            else:
                p.dve(lambda e: e.tensor_tensor(tmpB[:], pb[src][0:64, :, :], fB[:], op=ALU.mult),
                      reads=[f"pb{src}", fn_], writes=[tn_])
                p.dve(lambda e: e.tensor_tensor(accB[:], accB[:], tmpB[:], op=ALU.add), reads=[an_, tn_],
                      writes=[an_])

        def pre_qblock(i):
            ibb = 48 + i
            qs = slice(128 * i, 128 * i + 128)
            par = i % 2
            selT = selTs[par]
            sn_ = f"selT{par}"
            next_sb = lambda: 4
            next_pt = next_ptp
            PT = PTp
            for nm, src, srcn in (("QA", aq, "aq"), ("QR", bqr, "bqr"), ("QU", bq, "bq")):
                for j in range(2):
                    t_ = QBD[(nm, par, j)]
                    for g_ in range(2):
                        p.pool(lambda e, t_=t_, src=src, j=j, g_=g_: e.tensor_copy(
                            t_[64 * g_:64 * g_ + 64, 128 * g_:128 * g_ + 128], src[64 * g_:64 * g_ + 64, j, qs]),
                            reads=[srcn, f"{nm}{par}{j}"], writes=[f"{nm}{par}{j}"])
            if dbg < 2:
                return
            v = 9
            for j in range(2):
                sbk = next_sb()
                for r in range(2):
                    ks = slice(128 * (i + r), 128 * (i + r) + 128)
                    p.pe(lambda e, j=j, r=r, ks=ks, sbk=sbk: e.matmul(
                        pb[sbk][:, 2 * r:2 * r + 2, :].rearrange("p a b -> p (a b)"), ak[:, j, ks], QBD[("QA", par, j)][:],
                        start=True, stop=True), reads=["ak", f"QA{par}{j}"], writes=[f"pb{sbk}"])
                k_ = next_pt()
                p.act(lambda e, k_=k_, sbk=sbk: e.activation(PT[k_][:], pb[sbk][:], AF.Exp, scale=SCALE),
                      reads=[f"pb{sbk}"], writes=[f"PT{k_}"])
                if v < 2:
                    continue
                p.dve(lambda e, k_=k_: e.tensor_tensor(
                    PT[k_][:].rearrange("p (r g) q -> p r g q", r=2), PT[k_][:].rearrange("p (r g) q -> p r g q", r=2),
                    M2[:].unsqueeze(2).to_broadcast([128, 2, 2, 128]), op=ALU.mult),
                    reads=[f"PT{k_}", "M2"], writes=[f"PT{k_}"])
                if v < 3:
                    continue
                if i == 0:
                    p.dve(lambda e, k_=k_: e.tensor_scalar(PT[k_][:, 0:2, :], PT[k_][:, 0:2, :], offs[:, 3:4], None,
                                                          op0=ALU.mult), reads=[f"PT{k_}", "offs"], writes=[f"PT{k_}"])
                if v < 4:
                    continue
                for r in range(2):
                    p.pe(lambda e, j=j, r=r, k_=k_: e.matmul(
                        pb[5][:, 2 * j:2 * j + 2, :].rearrange("p a b -> p (a b)"), avg[:, i + r, j, :],
                        PT[k_][:, 2 * r:2 * r + 2, :].rearrange("p a b -> p (a b)"), start=(r == 0), stop=(r == 1)),
                        reads=["avg", f"PT{k_}"], writes=["pb5"])
            if v < 5:
                return
            p.dve(lambda e: e.tensor_tensor(rdA[:], pb[5][64:128, :, :], esk[64:128, :].unsqueeze(2).to_broadcast([64, 4, 128]),
                                            op=ALU.add), reads=["pb5"] + ["esk"], writes=["rdA"])
            p.dve(lambda e: e.reciprocal(rdA[:], rdA[:]), reads=["rdA"], writes=["rdA"])
            if v < 6:
                return
            for g_ in range(2):
                p.dve(lambda e, g_=g_: e.tensor_tensor(
                    ocA[64 * g_:64 * g_ + 64, :, qs], pb[5][0:64, :, :].rearrange("p (j g) q -> p j g q", g=2)[:, :, g_, :],
                    rdA[:].rearrange("p (j g) q -> p j g q", g=2)[:, :, g_, :], op=ALU.mult),
                    reads=["pb5"] + ["rdA"], writes=["ocA", "KKk"])
            if dbg < 3:
                return
            for r in range(5):
                sbk = next_sb()
                ks = slice(128 * (i + r), 128 * (i + r) + 128)
                for j in range(2):
                    p.pe(lambda e, j=j, ks=ks, sbk=sbk: e.matmul(
                        pb[sbk][:, 2 * j:2 * j + 2, :].rearrange("p a b -> p (a b)"), bkw[:, ks], QBD[("QR", par, j)][:],
                        start=True, stop=True), reads=["bkw", f"QR{par}{j}"], writes=[f"pb{sbk}"])
                k_ = next_pt()
                p.act(lambda e, k_=k_, sbk=sbk: e.activation(PT[k_][:], pb[sbk][:], AF.Exp, scale=SCALE),
                      reads=[f"pb{sbk}"], writes=[f"PT{k_}"])
                if r == 0 or r == 4:
                    msk = MLq if r == 0 else MUq
                    mn = "MLq" if r == 0 else "MUq"
                    p.dve(lambda e, k_=k_, msk=msk: e.tensor_tensor(PT[k_][:], PT[k_][:],
                                                                   msk[:].unsqueeze(1).to_broadcast([128, 4, 128]),
                                                                   op=ALU.mult), reads=[f"PT{k_}", mn], writes=[f"PT{k_}"])
                if i + r < 4:
                    p.dve(lambda e, k_=k_: e.tensor_scalar(PT[k_][:], PT[k_][:], offs[:, 3:4], None, op0=ALU.mult),
                          reads=[f"PT{k_}", "offs"], writes=[f"PT{k_}"])
                p.pe(lambda e, r=r, k_=k_: e.matmul(pb[5][:].rearrange("p a b -> p (a b)"), bvwg[:, i + r, :],
                                                    PT[k_][:].rearrange("p a b -> p (a b)"), start=(r == 0), stop=(r == 4)),
                     reads=["bvwg", f"PT{k_}"], writes=["pb5"])
            gate_finish(2, i, True, 4, 5, fBp, tmpBp, "fBp", "tmpBp")
            if dbg < 4:
                return
            p.dve(lambda e: e.tensor_scalar(mk[:], valc[:], float(128 * ibb - 31), None, op0=ALU.is_le),
                  reads=["valc"], writes=["mk"])
            for cc in range(4):
                sbk = next_sb()
                for j in range(2):
                    p.pe(lambda e, j=j, cc=cc, sbk=sbk: e.matmul(
                        pb[sbk][:, 2 * j:2 * j + 2, :].rearrange("p a b -> p (a b)"), kcT[:, cc * 128:(cc + 1) * 128],
                        QBD[("QU", par, j)][:], start=True, stop=True), reads=["kcT", f"QU{par}{j}"], writes=[f"pb{sbk}"])
                p.act(lambda e, cc=cc, sbk=sbk: e.activation(PTc[:, cc, :, :], pb[sbk][:], AF.Exp, scale=SCALE,
                                                             bias=biasn[:, cc:cc + 1]),
                      reads=[f"pb{sbk}", "biasn"], writes=[("PTc", cc)])
                if cc >= 2:
                    p.dve(lambda e, cc=cc: e.tensor_tensor(PTc[:, cc, :, :], PTc[:, cc, :, :],
                                                          mk[:, cc - 2, :].unsqueeze(1).to_broadcast([128, 4, 128]),
                                                          op=ALU.mult), reads=[("PTc", cc), "mk"], writes=[("PTc", cc)])
                p.pe(lambda e, cc=cc: e.matmul(pb[5][:].rearrange("p a b -> p (a b)"), vcg[:, cc, :],
                                               PTc[:, cc, :, :].rearrange("p a b -> p (a b)"), start=(cc == 0),
                                               stop=(cc == 3)), reads=["vcg", ("PTc", cc)], writes=["pb5"])
                p.pe(lambda e, cc=cc: e.matmul(pb[6][:].rearrange("p a b -> p (a b)"), ones_b[:],
                                               PTc[:, cc, :, :].rearrange("p a b -> p (a b)"), start=(cc == 0),
                                               stop=(cc == 3)), reads=["ones_b", ("PTc", cc)], writes=["pb6"])
            p.dve(lambda e: e.tensor_scalar(rd[:], pb[6][:], 1e-30, None, op0=ALU.add), reads=["pb6", "Et"], writes=["rd"])
            p.dve(lambda e: e.reciprocal(rd[:], rd[:]), reads=["rd"], writes=["rd"])
            for cc in range(4):
                p.dve(lambda e, cc=cc: e.tensor_tensor(pn[:], PTc[:, cc, :, :], rd[:], op=ALU.mult),
                      reads=[("PTc", cc), "rd"], writes=["pn"])
                p.dve(lambda e, cc=cc: e.tensor_reduce(psh[:, cc, :], pn[:].rearrange("p h q -> p q h"), axis=AX.X,
                                                       op=ALU.add), reads=["pn"], writes=[("psh", cc)])
                p.pe(lambda e, cc=cc: e.matmul(pb[6][:, 0, :], psh[:, cc, :], Gm[:, cc, :], start=(cc == 0), stop=(cc == 3)),
                     reads=[("psh", cc), "Gm"], writes=["pb6"])
            gate_finish(0, i, False, 4, 5, fBp, tmpBp, "fBp", "tmpBp")
            p.dve(lambda e: e.tensor_tensor(score[:], pb[6][:, 0, :], f0m[:], op=ALU.max), reads=["pb6", "f0m"],
                  writes=["score"])
            c0 = 2 * ibb
            if c0 + 2 < 128:
                p.dve(lambda e: e.memset(score[:, c0 + 2:128], -1e30), reads=["score"], writes=["score"])
            p.dve(lambda e: e.memset(score[0:64, c0 + 1:c0 + 2], -1e30), reads=["score"], writes=["score"])
            p.dve(lambda e: e.memset(score[0:64, c0 - 1:c0 + 1], 1e9), reads=["score"], writes=["score"])
            p.dve(lambda e: e.memset(score[64:128, c0:c0 + 2], 1e9), reads=["score"], writes=["score"])
            p.dve(lambda e: e.max(m8a[:], score[:]), reads=["score"], writes=["m8a"])
            p.dve(lambda e: e.match_replace(work[:], m8a[:], score[:], -3e38), reads=["score", "m8a"], writes=["work"])
            p.dve(lambda e: e.max(m8b[:], work[:]), reads=["work"], writes=["m8b"])
            p.dve(lambda e: e.tensor_scalar(work[:], score[:], m8b[:, 7:8], None, op0=ALU.is_ge), reads=["score", "m8b"],
                  writes=["work"])
            p.dve(lambda e: e.tensor_tensor(sel[:], work[:], validblk[:], op=ALU.mult), reads=["work", "validblk"],
                  writes=["sel"])
            if c0 + 2 < 128:
                p.dve(lambda e: e.memset(sel[:, c0 + 2:128], 0.0), reads=["sel"], writes=["sel"])
            p.dve(lambda e: e.memset(sel[0:64, c0 + 1:c0 + 2], 0.0), reads=["sel"], writes=["sel"])
            p.pe(lambda e: e.transpose(pb[6][:].rearrange("p a b -> p (a b)").bitcast(BF16)[:, 0:128], sel[:], IDb[:]),
                 reads=["sel", "IDb"], writes=["pb6"])
            p.act(lambda e: e.copy(selT[:], pb[6][:].rearrange("p a b -> p (a b)").bitcast(BF16)[:, 0:128]),
                  reads=["pb6"], writes=[sn_])

        def slc_qblock(i):
            ibb = 48 + i
            qs = slice(128 * i, 128 * i + 128)
            par = i % 2
            selT = selTs[par]
            sn_ = f"selT{par}"
            accB = accBs[par]
            p.dve(lambda e: e.tensor_scalar(selb[par][:], selT[:].unsqueeze(1).to_broadcast([128, 4, 128]), -1.0, 30000.0,
                                            op0=ALU.add, op1=ALU.mult), reads=[sn_], writes=[f"selb{par}"])
            pend = []
            for cb in range(ibb + 1):
                sbk = next_sb()
                ks = slice(128 * cb, 128 * cb + 128)
                p.pe(lambda e, cb=cb, sbk=sbk: e.matmul(pb[sbk][:].rearrange("p a b -> p (a b)"), E[:, cb, :],
                                                        selb[par][:].rearrange("p a b -> p (a b)"), start=True, stop=False),
                     reads=["E", f"selb{par}"], writes=[f"pb{sbk}"])
                for j in range(2):
                    p.pe(lambda e, j=j, ks=ks, sbk=sbk: e.matmul(
                        pb[sbk][:, 2 * j:2 * j + 2, :].rearrange("p a b -> p (a b)"), bks[:, ks], QBD[("QR", par, j)][:],
                        start=False, stop=(j == 1)), reads=["bks", f"QR{par}{j}"], writes=[f"pb{sbk}"])
                while len(pend) > 1:
                    pend.pop(0)()
                k_ = next_pt()
                p.act(lambda e, k_=k_, sbk=sbk: e.activation(PT[k_][:], pb[sbk][:], AF.Exp, scale=SCALE),
                      reads=[f"pb{sbk}"], writes=[f"PT{k_}"])
                if cb == ibb:
                    p.dve(lambda e, k_=k_: e.tensor_tensor(PT[k_][:], PT[k_][:],
                                                          MUq[:].unsqueeze(1).to_broadcast([128, 4, 128]), op=ALU.mult),
                          reads=[f"PT{k_}", "MUq"], writes=[f"PT{k_}"])

                def pv(cb=cb, k_=k_):
                    p.pe(lambda e: e.matmul(pb[7][:].rearrange("p a b -> p (a b)"), bvsg[:, cb, :],
                                            PT[k_][:].rearrange("p a b -> p (a b)"), start=(cb == 0),
                                            stop=(cb == ibb)), reads=["bvsg", f"PT{k_}"], writes=["pb7"])
                pend.append(pv)
            while pend:
                pend.pop(0)()
            gate_finish(1, i, False, 2, 7, fBs, tmpBs, "fBs", "tmpBs")
            for g_ in range(2):
                p.act(lambda e, g_=g_: e.copy(ocB[64 * g_:64 * g_ + 64, :, qs],
                                              accB[:].rearrange("p (j g) q -> p j g q", g=2)[:, :, g_, :]),
                      reads=[f"accB{par}"], writes=["ocB", "KKk"])
        real_add = p.add

        def record(fn, i):
            lst = []
            p.add = lambda eng, f, reads=(), writes=(), dma=None: lst.append((eng, f, reads, writes, dma))
            try:
                fn(i)
            finally:
                p.add = real_add
            return lst

        qbl = list(qblocks)
        for op in record(pre_qblock, qbl[0]):
            real_add(*op)
        for n_, i_ in enumerate(qbl):
            lists = [record(slc_qblock, i_)]
            if n_ + 1 < len(qbl):
                lists.append(record(pre_qblock, qbl[n_ + 1]))
            lists = [l_ for l_ in lists if l_]
            pos = [0] * len(lists)
            for _ in range(sum(len(l_) for l_ in lists)):
                best, bi = None, -1
                for li, l_ in enumerate(lists):
                    if pos[li] < len(l_):
                        frac = pos[li] / len(l_)
                        if best is None or frac < best:
                            best, bi = frac, li
                real_add(*lists[bi][pos[bi]])
                pos[bi] += 1
        for j in range(2):
            p.dma("sp", "ocA", lambda e, j=j: e.dma_start(out=oc_d[j, :, :], in_=ocA[:, j, :]), reads=["ocA"],
                  writes=[("oc", j)])
            p.dma("sp", "ocB", lambda e, j=j: e.dma_start(out=oc_d[2 + j, :, :], in_=ocB[:, j, :]), reads=["ocB"],
                  writes=[("oc", 2 + j)])
        p.emit(final_sems=["ocA", "ocB"])
    return nc


def assemble_B(Ab, qtr, wl):
    bf = ml_dtypes.bfloat16
    t0 = qtr * T
    off = 3 * T - t0
    own = Ab[qtr]

    def hist_cols(idx, src, n):
        if qtr == 0:
            return np.zeros((128, n), bf)
        return Ab[qtr - 1][src][idx][:, T - n:]

    def hist_rows(c0, c1, n):
        if qtr == 0:
            return np.zeros((n, c1 - c0), bf)
        return Ab[qtr - 1]["o_v"][T - n:, c0:c1]

    def tokmaj(rows, nblk):
        return np.ascontiguousarray(rows.reshape(nblk, 128, rows.shape[1]).transpose(1, 0, 2))

    d = {}
    d["aq"] = np.ascontiguousarray(own["o_rope"][0:2])
    d["ak"] = np.ascontiguousarray(np.stack([np.concatenate([hist_cols(2 + j, "o_rope", 128), own["o_rope"][2 + j]], 1)
                                             for j in range(2)]))
    d["av"] = tokmaj(np.concatenate([hist_rows(0, 128, 128), own["o_v"][:, 0:128]], 0), 17)
    d["sinks"] = np.ascontiguousarray(np.tile(wl["attn_sinks"][None, :], (128, 1)).astype(np.float32))
    d["offs"] = np.tile(np.array([[off, off // 16, off // 64, 0.0 if qtr == 0 else 1.0]], np.float32), (128, 1))
    d["bqr"] = np.ascontiguousarray(own["o_rope"][4:6])
    d["bq"] = np.ascontiguousarray(own["o_plain"][0:2])
    d["bkw"] = np.ascontiguousarray(np.concatenate([hist_cols(7, "o_rope", 512), own["o_rope"][7]], 1))
    d["bvw"] = tokmaj(np.concatenate([hist_rows(192, 256, 512), own["o_v"][:, 192:256]], 0), 20)
    pad = np.zeros((128, off), bf)
    d["bks"] = np.ascontiguousarray(np.concatenate([pad] + [Ab[k]["o_rope"][6] for k in range(qtr + 1)], 1))
    d["cmpT"] = np.ascontiguousarray(np.concatenate([pad] + [Ab[k]["o_plain"][2] for k in range(qtr + 1)], 1))
    d["bvs"] = tokmaj(np.concatenate([np.zeros((off, 64), bf)] + [Ab[k]["o_v"][:, 128:192] for k in range(qtr + 1)], 0), 64)
    d["bg"] = np.ascontiguousarray(own["o_g"])
    d["w1k"], d["w1v"], d["w2k"], d["w2v"] = wl["cmp_k_w1"], wl["cmp_v_w1"], wl["cmp_k_w2"], wl["cmp_v_w2"]
    for nm, key in (("pek", "cmp_pe_k"), ("pev", "cmp_pe_v")):
        d[nm] = np.ascontiguousarray(wl[key].reshape(16, 2, 64).transpose(1, 2, 0).reshape(128, 16))
    return d


def build_G2(ntiles=SEQ // GT):
    nc = bass.Bass("TRN2", target_bir_lowering=False)
    S_ = ntiles * GT
    NB = S_ // 128
    with ExitStack() as es:
        c = Ctx(nc, es)
        p = c.p
        qkv = c.din("qkv", [3, 128, S_])
        cw = c.din("cw", [128, 12])
        gab = c.din("gab", [128, NB, 2])
        hp = c.din("hp", [128, 2])
        zin = c.din("z", [128, NB, 128])
        nwb = c.din("nwb", [128, 128])
        yo = c.dout("yo", [128, NB, 128], BF16)

        val = c.sb([128, 128], F32, "val")
        MU = c.sb([128, 128], F32, "MU")
        ML = c.sb([128, 128], F32, "ML")
        ID = c.sb([128, 128], F32, "ID")
        ones_bf = c.sb([128, 128], BF16, "ones_bf")
        ones_f = c.sb([128, 128], F32, "ones_f")
        epsc = c.sb([128, 1], F32, "epsc")
        cwt = c.sb([128, 12], F32, "cwt")
        hpt = c.sb([128, 2], F32, "hpt")
        nw = c.sb([128, 128], F32, "nw")
        p.dma("sp", "cwt", lambda e: e.dma_start(out=cwt[:], in_=cw[:, :]), writes=["cwt"])
        p.dma("sp", "hpt", lambda e: e.dma_start(out=hpt[:], in_=hp[:, :]), writes=["hpt"])
        p.dma("sp", "nw", lambda e: e.dma_start(out=nw[:], in_=nwb[:, :]), writes=["nw"])
        p.pool(lambda e: e.iota(val[:], [[1, 128]], base=0, channel_multiplier=-1,
                                allow_small_or_imprecise_dtypes=True), writes=["val"])
        p.dve(lambda e: e.tensor_scalar(MU[:], val[:], 0.0, None, op0=ALU.is_ge), reads=["val"], writes=["MU"])
        p.dve(lambda e: e.tensor_scalar(ML[:], val[:], 0.0, None, op0=ALU.is_lt), reads=["val"], writes=["ML"])
        p.dve(lambda e: e.tensor_scalar(ID[:], val[:], 0.0, None, op0=ALU.is_equal), reads=["val"], writes=["ID"])
        p.dve(lambda e: e.memset(MU[0:64, 64:128], 0.0), reads=["MU"], writes=["MU"])
        p.dve(lambda e: e.memset(ML[64:128, 0:64], 0.0), reads=["ML"], writes=["ML"])
        p.dve(lambda e: e.memset(ones_bf[:], 1.0), writes=["ones_bf"])
        p.dve(lambda e: e.memset(ones_f[:], 1.0), writes=["ones_f"])
        p.dve(lambda e: e.memset(epsc[:], EPS), writes=["epsc"])

        pb = [c.ps([128, 4, 128], F32, f"pb{i}") for i in range(8)]

        gt_ = c.sb([128, NB, 2], F32, "gt_")
        g = c.sb([128, NB], F32, "g")
        beta = c.sb([128, NB], F32, "beta")
        nbeta = c.sb([128, NB], F32, "nbeta")
        tA = c.sb([128, NB], F32, "tA")
        eA = c.sb([128, 1], F32, "eA")
        gh = [c.sb([128, NB], F32, f"gh{a}") for a in range(2)]
        egc = c.sb([128, NB], F32, "egc")
        erem = c.sb([128, NB], F32, "erem")
        egl = [c.sb([128, NB], F32, f"egl{a}") for a in range(2)]
        sckb = c.sb([128, NB], F32, "sckb")
        p.dma("sp", "gt_", lambda e: e.dma_start(out=gt_[:], in_=gab[:, :, :]), writes=["gt_"])
        p.act(lambda e: e.activation(tA[:], gt_[:, :, 0], AF.Exp, bias=hpt[:, 1:2]), reads=["gt_", "hpt"], writes=["tA"])
        p.act(lambda e: e.activation(tA[:], tA[:], AF.Ln, bias=1.0), reads=["tA"], writes=["tA"])
        p.act(lambda e: e.activation(eA[:], hpt[:, 0:1], AF.Exp), reads=["hpt"], writes=["eA"])
        p.dve(lambda e: e.tensor_scalar(g[:], tA[:], eA[:, 0:1], -1.0, op0=ALU.mult, op1=ALU.mult),
              reads=["tA", "eA"], writes=["g"])
        p.act(lambda e: e.activation(beta[:], gt_[:, :, 1], AF.Exp, scale=-1.0), reads=["gt_"], writes=["beta"])
        p.dve(lambda e: e.tensor_scalar(beta[:], beta[:], 1.0, None, op0=ALU.add), reads=["beta"], writes=["beta"])
        p.dve(lambda e: e.reciprocal(beta[:], beta[:]), reads=["beta"], writes=["beta"])
        p.dve(lambda e: e.tensor_scalar(nbeta[:], beta[:], -1.0, None, op0=ALU.mult), reads=["beta"], writes=["nbeta"])
        for a in range(2):
            p.dve(lambda e, a=a: e.memset(gh[a][:], 0.0), writes=[f"gh{a}"])
            p.dve(lambda e, a=a: e.tensor_copy(gh[a][64 * a:64 * a + 64, :], g[64 * a:64 * a + 64, :]),
                  reads=["g", f"gh{a}"], writes=[f"gh{a}"])
        NBC = min(NB, 64)
        p.pe(lambda e: e.matmul(pb[0][:, 0, 0:NB], MU[:], g[:], start=True, stop=True), reads=["MU", "g"], writes=["pb0"])
        p.act(lambda e: e.activation(egc[:], pb[0][:, 0, 0:NB], AF.Exp), reads=["pb0"], writes=["egc"])
        p.pe(lambda e: e.matmul(pb[1][:, 0, 0:NB], ML[:], g[:], start=True, stop=True), reads=["ML", "g"], writes=["pb1"])
        p.act(lambda e: e.activation(erem[:], pb[1][:, 0, 0:NB], AF.Exp), reads=["pb1"], writes=["erem"])
        for a in range(2):
            p.pe(lambda e, a=a: e.matmul(pb[2 + a][:, 0, 0:NB], ones_f[:], gh[a][:], start=True, stop=True),
                 reads=["ones_f", f"gh{a}"], writes=[f"pb{2 + a}"])
            p.act(lambda e, a=a: e.activation(egl[a][:], pb[2 + a][:, 0, 0:NB], AF.Exp), reads=[f"pb{2 + a}"],
                  writes=[f"egl{a}"])
        p.dve(lambda e: e.tensor_tensor(sckb[:], beta[:], egc[:], op=ALU.mult), reads=["beta", "egc"], writes=["sckb"])

        NC3 = 3

        def B3(name, shape, dt):
            return [c.sb(shape, dt, f"{name}_{k}") for k in range(NC3)]
        xin = [B3(f"xin{i}", [128, GT + 3], F32) for i in range(3)]
        acc = [B3(f"acc{i}", [128, GT], F32) for i in range(3)]
        sil = [B3(f"sil{i}", [128, GT], F32) for i in range(3)]
        sqb = [B3(f"sqb{i}", [128, GT], BF16) for i in range(2)]
        rn = [B3(f"rn{i}", [128, GT], F32) for i in range(2)]
        qn = B3("qn", [128, 4, 128], F32)
        kn = B3("kn", [128, 4, 128], F32)
        kbg = B3("kbg", [128, 4, 128], BF16)
        kdec = B3("kdec", [128, 4, 128], BF16)
        vb = B3("vb", [128, 4, 128], BF16)
        gU2 = B3("gU2", [128, 4, 128], F32)
        gU1 = B3("gU1", [128, 4, 128], F32)
        DecL = B3("DecL", [128, 4, 128], F32)
        DecT = B3("DecT", [128, 4, 128], F32)
        Xs = [B3(f"Xs{i}", [128, 4, 128], F32) for i in range(2)]
        Ys = [B3(f"Ys{i}", [128, 4, 128], F32) for i in range(2)]
        Q = B3("Q", [128, 4, 128], F32)
        TTb = B3("TTb", [128, 4, 128], BF16)
        qkT = B3("qkT", [128, 4, 128], BF16)
        u_sb = B3("u_sb", [128, 4, 128], F32)
        wT_sb = B3("wT_sb", [128, 4, 128], F32)
        o_sb = B3("o_sb", [128, 4, 128], F32)
        zt = B3("zt", [128, 4, 128], F32)
        zs = B3("zs", [128, 4, 128], F32)
        yt = B3("yt", [128, 4, 128], F32)
        ytb = B3("ytb", [128, 4, 128], BF16)
        ss = B3("ss", [128, 4], F32)
        otmp = c.sb([128, 128], F32, "otmp")
        vnb = c.sb([128, 128], BF16, "vnb")
        S = c.sb([128, 128], F32, "S")
        junk = c.sb([128, 128], F32, "junk")
        p.dve(lambda e: e.memset(S[:], 0.0), writes=["S"])

        def bc4(ap2):
            return ap2.unsqueeze(2).to_broadcast([128, 4, 128])

        def bcm(ap2):
            return ap2.unsqueeze(1).to_broadcast([128, 4, 128])

        def fl(t):
            return t[:].rearrange("p a b -> p (a b)")

        def P0(ti):
            k = ti % NC3
            t0 = ti * GT
            bs = slice(ti * 4, ti * 4 + 4)
            for i in range(3):
                xn = f"xin{i}_{k}"
                if ti == 0:
                    p.dve(lambda e, i=i: e.memset(xin[i][k][:, 0:3], 0.0), writes=[xn])
                    p.dma("sp", xn, lambda e, i=i: e.dma_start(out=xin[i][k][:, 3:GT + 3], in_=qkv[i, :, 0:GT]), writes=[xn])
                else:
                    p.dma("sp", xn, lambda e, i=i: e.dma_start(out=xin[i][k][:], in_=qkv[i, :, t0 - 3:t0 + GT]), writes=[xn])
            p.dma("sp", f"zt_{k}", lambda e: e.dma_start(out=zt[k][:], in_=zin[:, bs, :]), writes=[f"zt_{k}"])
            for i in range(3):
                xn, an, sn = f"xin{i}_{k}", f"acc{i}_{k}", f"sil{i}_{k}"
                p.dve(lambda e, i=i: e.tensor_scalar(acc[i][k][:], xin[i][k][:, 3:GT + 3], cwt[:, 4 * i + 3:4 * i + 4], None,
                                                     op0=ALU.mult), reads=[xn, "cwt"], writes=[an])
                for j in range(3):
                    p.dve(lambda e, i=i, j=j: e.scalar_tensor_tensor(
                        acc[i][k][:], xin[i][k][:, j:GT + j], cwt[:, 4 * i + j:4 * i + j + 1], acc[i][k][:], op0=ALU.mult,
                        op1=ALU.add), reads=[xn, "cwt", an], writes=[an])
                p.act(lambda e, i=i: e.activation(sil[i][k][:], acc[i][k][:], AF.Silu), reads=[an], writes=[sn])
            for i in range(2):
                sn, qn_, rn_ = f"sil{i}_{k}", f"sqb{i}_{k}", f"rn{i}_{k}"
                p.act(lambda e, i=i: e.activation(sqb[i][k][:], sil[i][k][:], AF.Square), reads=[sn], writes=[qn_])
                p.pe(lambda e, i=i: e.matmul(fl(pb[0]), ones_bf[:], sqb[i][k][:], start=True, stop=True),
                     reads=["ones_bf", qn_], writes=["pb0"])
                p.act(lambda e, i=i: e.activation(rn[i][k][:], fl(pb[0]), AF.Sqrt, bias=epsc[:, 0:1]),
                      reads=["pb0", "epsc"], writes=[rn_])
                p.dve(lambda e, i=i: e.reciprocal(rn[i][k][:], rn[i][k][:]), reads=[rn_], writes=[rn_])
            p.dve(lambda e: e.scalar_tensor_tensor(fl(qn[k]), sil[0][k][:], float(128 ** -0.5), rn[0][k][:], op0=ALU.mult,
                                                   op1=ALU.mult), reads=[f"sil0_{k}", f"rn0_{k}"], writes=[f"qn_{k}"])
            p.dve(lambda e: e.tensor_tensor(fl(kn[k]), sil[1][k][:], rn[1][k][:], op=ALU.mult),
                  reads=[f"sil1_{k}", f"rn1_{k}"], writes=[f"kn_{k}"])
            p.act(lambda e: e.activation(zs[k][:], zt[k][:], AF.Silu), reads=[f"zt_{k}"], writes=[f"zs_{k}"])

        def P1(ti):
            k = ti % NC3
            bs = slice(ti * 4, ti * 4 + 4)
            rot = [5]

            def nb():
                b_ = rot[0]
                rot[0] = 5 + (b_ - 4) % 3
                return b_
            R = lambda nm: f"{nm}_{k}"
            bk_, bv_ = nb(), nb()
            for pr in range(4):
                p.pe(lambda e, pr=pr: e.transpose(pb[bk_][:, pr, :], kn[k][:, pr, :], ID[:]), reads=[R("kn"), "ID"],
                     writes=[f"pb{bk_}"])
            for pr in range(4):
                p.pe(lambda e, pr=pr: e.transpose(pb[bv_][:, pr, :], sil[2][k][:, pr * 128:(pr + 1) * 128], ID[:]),
                     reads=[R("sil2"), "ID"], writes=[f"pb{bv_}"])
            p.dve(lambda e: e.tensor_tensor(kbg[k][:], pb[bk_][:], bc4(sckb[:, bs]), op=ALU.mult),
                  reads=[f"pb{bk_}", "sckb"], writes=[R("kbg")])
            p.dve(lambda e: e.tensor_tensor(kdec[k][:], pb[bk_][:], bc4(erem[:, bs]), op=ALU.mult),
                  reads=[f"pb{bk_}", "erem"], writes=[R("kdec")])
            p.dve(lambda e: e.tensor_tensor(vb[k][:], pb[bv_][:], bc4(beta[:, bs]), op=ALU.mult),
                  reads=[f"pb{bv_}", "beta"], writes=[R("vb")])
            p.dve(lambda e: e.tensor_tensor(gU2[k][:], bcm(ML[:]), bc4(g[:, bs]), op=ALU.mult), reads=["ML", "g"],
                  writes=[R("gU2")])
            p.dve(lambda e: e.tensor_tensor(gU1[k][:], bcm(MU[:]), bc4(g[:, bs]), op=ALU.mult), reads=["MU", "g"],
                  writes=[R("gU1")])
            bd_, bt_ = nb(), nb()
            p.pe(lambda e: e.matmul(fl(pb[bd_]), MU[:], fl(gU2[k]), start=True, stop=True), reads=["MU", R("gU2")],
                 writes=[f"pb{bd_}"])
            p.pe(lambda e: e.matmul(fl(pb[bt_]), ML[:], fl(gU1[k]), start=True, stop=True), reads=["ML", R("gU1")],
                 writes=[f"pb{bt_}"])
            p.act(lambda e: e.activation(DecL[k][:], pb[bd_][:], AF.Exp), reads=[f"pb{bd_}"], writes=[R("DecL")])
            p.act(lambda e: e.activation(DecT[k][:], pb[bt_][:], AF.Exp), reads=[f"pb{bt_}"], writes=[R("DecT")])
            p.dve(lambda e: e.tensor_tensor(DecL[k][:], DecL[k][:], bcm(ML[:]), op=ALU.mult), reads=[R("DecL"), "ML"],
                  writes=[R("DecL")])
            p.dve(lambda e: e.tensor_tensor(DecT[k][:], DecT[k][:], bcm(MU[:]), op=ALU.mult), reads=[R("DecT"), "MU"],
                  writes=[R("DecT")])
            bg_, bq_ = nb(), nb()
            for pr in range(4):
                p.pe(lambda e, pr=pr: e.matmul(pb[bg_][:, pr, :], kn[k][:, pr, :], kn[k][:, pr, :], start=True, stop=True),
                     reads=[R("kn")], writes=[f"pb{bg_}"])
            for pr in range(4):
                p.pe(lambda e, pr=pr: e.matmul(pb[bq_][:, pr, :], kn[k][:, pr, :], qn[k][:, pr, :], start=True, stop=True),
                     reads=[R("kn"), R("qn")], writes=[f"pb{bq_}"])
            p.dve(lambda e: e.tensor_tensor(Xs[0][k][:], pb[bg_][:], DecL[k][:], op=ALU.mult),
                  reads=[f"pb{bg_}", R("DecL")], writes=[R("Xs0")])
            p.dve(lambda e: e.tensor_tensor(Xs[0][k][:], Xs[0][k][:], bc4(nbeta[:, bs]), op=ALU.mult),
                  reads=[R("Xs0"), "nbeta"], writes=[R("Xs0")])
            p.dve(lambda e: e.tensor_tensor(qkT[k][:], pb[bq_][:], DecT[k][:], op=ALU.mult),
                  reads=[f"pb{bq_}", R("DecT")], writes=[R("qkT")])
            by_ = nb()
            for pr in range(4):
                p.pe(lambda e, pr=pr: e.transpose(pb[by_][:, pr, :], Xs[0][k][:, pr, :], ID[:]), reads=[R("Xs0"), "ID"],
                     writes=[f"pb{by_}"])
            p.act(lambda e: e.copy(Ys[0][k][:], pb[by_][:]), reads=[f"pb{by_}"], writes=[R("Ys0")])
            p.dve(lambda e: e.tensor_tensor(Q[k][:], pb[by_][:], bcm(ID[:]), op=ALU.add), reads=[f"pb{by_}", "ID"],
                  writes=[R("Q")])
            for lvl in range(1, 6):
                cur, nxt = (lvl - 1) % 2, lvl % 2
                bx_ = nb()
                for pr in range(4):
                    p.pe(lambda e, pr=pr, cur=cur, bx_=bx_: e.matmul(pb[bx_][:, pr, :], Ys[cur][k][:, pr, :],
                                                                   Xs[cur][k][:, pr, :], start=True, stop=True),
                         reads=[R(f"Ys{cur}"), R(f"Xs{cur}")], writes=[f"pb{bx_}"])
                if lvl < 5:
                    byy = nb()
                    for pr in range(4):
                        p.pe(lambda e, pr=pr, cur=cur, byy=byy: e.matmul(pb[byy][:, pr, :], Xs[cur][k][:, pr, :],
                                                                       Ys[cur][k][:, pr, :], start=True, stop=True),
                             reads=[R(f"Ys{cur}"), R(f"Xs{cur}")], writes=[f"pb{byy}"])
                p.act(lambda e, nxt=nxt, bx_=bx_: e.copy(Xs[nxt][k][:], pb[bx_][:]), reads=[f"pb{bx_}"],
                      writes=[R(f"Xs{nxt}")])
                if lvl < 5:
                    p.dve(lambda e, nxt=nxt, byy=byy: e.tensor_copy(Ys[nxt][k][:], pb[byy][:]), reads=[f"pb{byy}"],
                          writes=[R(f"Ys{nxt}")])
                bqq = nb()
                for pr in range(4):
                    p.pe(lambda e, pr=pr, nxt=nxt, bqq=bqq: e.matmul(pb[bqq][:, pr, :], Xs[nxt][k][:, pr, :], Q[k][:, pr, :],
                                                                   start=True, stop=True),
                         reads=[R(f"Xs{nxt}"), R("Q")], writes=[f"pb{bqq}"])
                p.dve(lambda e, bqq=bqq: e.tensor_tensor(Q[k][:], Q[k][:], pb[bqq][:], op=ALU.add),
                      reads=[R("Q"), f"pb{bqq}"], writes=[R("Q")])
            p.act(lambda e: e.copy(TTb[k][:], Q[k][:]), reads=[R("Q")], writes=[R("TTb")])
            bu_, bw_ = nb(), nb()
            for pr in range(4):
                p.pe(lambda e, pr=pr: e.matmul(pb[bu_][:, pr, :], TTb[k][:, pr, :], vb[k][:, pr, :], start=True, stop=True),
                     reads=[R("TTb"), R("vb")], writes=[f"pb{bu_}"])
            for pr in range(4):
                p.pe(lambda e, pr=pr: e.matmul(pb[bw_][:, pr, :], kbg[k][:, pr, :], TTb[k][:, pr, :], start=True, stop=True),
                     reads=[R("TTb"), R("kbg")], writes=[f"pb{bw_}"])
            p.act(lambda e: e.copy(u_sb[k][:], pb[bu_][:]), reads=[f"pb{bu_}"], writes=[R("u_sb")])
            p.dve(lambda e: e.tensor_copy(wT_sb[k][:], pb[bw_][:]), reads=[f"pb{bw_}"], writes=[R("wT_sb")])

        def P2(ti):
            k = ti % NC3
            b0 = ti * 4
            bs = slice(b0, b0 + 4)
            R = lambda nm: f"{nm}_{k}"
            for pr in range(4):
                blk = b0 + pr
                for a in range(2):
                    rs = slice(64 * a, 64 * a + 64)
                    cs = slice(64 * a, 64 * a + 64)
                    p.pe(lambda e, pr=pr, rs=rs, cs=cs: e.matmul(pb[1][rs, 0, :], wT_sb[k][:, pr, cs], S[:], start=True,
                                                               stop=True), reads=[R("wT_sb"), "S"], writes=["pb1"])
                    p.pe(lambda e, pr=pr, rs=rs, cs=cs: e.matmul(pb[2][rs, 0, :], qn[k][:, pr, cs], S[:], start=True,
                                                               stop=True), reads=[R("qn"), "S"], writes=["pb2"])
                    p.dve(lambda e, pr=pr, rs=rs: e.tensor_tensor(vnb[rs, :], u_sb[k][rs, pr, :], pb[1][rs, 0, :],
                                                                 op=ALU.subtract), reads=[R("u_sb"), "pb1"], writes=["vnb"])
                    p.act(lambda e, rs=rs, blk=blk: e.activation(otmp[rs, :], pb[2][rs, 0, :], AF.Copy,
                                                                 scale=egc[rs, blk:blk + 1]),
                          reads=["pb2", "egc"], writes=["otmp"])
                    p.pe(lambda e, pr=pr, rs=rs, cs=cs: e.matmul(pb[3][rs, 0, :], qkT[k][rs, pr, cs], vnb[rs, :], start=True,
                                                               stop=True), reads=[R("qkT"), "vnb"], writes=["pb3"])
                    p.pe(lambda e, pr=pr, rs=rs: e.matmul(pb[4][:, 0, :], kdec[k][rs, pr, :], vnb[rs, :], start=True,
                                                        stop=True), reads=[R("kdec"), "vnb"], writes=["pb4"])
                    p.dve(lambda e, pr=pr, rs=rs: e.tensor_tensor(o_sb[k][rs, pr, :], otmp[rs, :], pb[3][rs, 0, :],
                                                                 op=ALU.add), reads=["otmp", "pb3"], writes=[R("o_sb")])
                    p.dve(lambda e, a=a, blk=blk: e.scalar_tensor_tensor(S[:], S[:], egl[a][:, blk:blk + 1],
                                                                        pb[4][:, 0, :], op0=ALU.mult, op1=ALU.add),
                          reads=["S", f"egl{a}", "pb4"], writes=["S"])
            for pr in range(4):
                p.act(lambda e, pr=pr: e.activation(junk[:], o_sb[k][:, pr, :], AF.Square, accum_out=ss[k][:, pr:pr + 1]),
                      reads=[R("o_sb")], writes=["junk", R("ss")])
            p.act(lambda e: e.activation(ss[k][:], ss[k][:], AF.Sqrt, bias=epsc[:, 0:1], scale=1.0 / 128.0),
                  reads=[R("ss"), "epsc"], writes=[R("ss")])
            p.dve(lambda e: e.reciprocal(ss[k][:], ss[k][:]), reads=[R("ss")], writes=[R("ss")])
            p.dve(lambda e: e.tensor_tensor(yt[k][:], o_sb[k][:], bc4(ss[k][:, :]), op=ALU.mult),
                  reads=[R("o_sb"), R("ss")], writes=[R("yt")])
            p.dve(lambda e: e.tensor_tensor(yt[k][:], yt[k][:], bcm(nw[:]), op=ALU.mult), reads=[R("yt"), "nw"],
                  writes=[R("yt")])
            p.dve(lambda e: e.tensor_tensor(ytb[k][:], yt[k][:], zs[k][:], op=ALU.mult), reads=[R("yt"), R("zs")],
                  writes=[R("ytb")])
            p.dma("sp", f"ytb_{k}", lambda e: e.dma_start(out=yo[:, bs, :], in_=ytb[k][:]), reads=[R("ytb")],
                  writes=[("yo", ti)])

        real_add = p.add

        def record(fn, ti):
            lst = []
            p.add = lambda eng, f, reads=(), writes=(), dma=None: lst.append((eng, f, reads, writes, dma))
            try:
                fn(ti)
            finally:
                p.add = real_add
            return lst

        for s_ in range(ntiles + 2):
            lists = []
            if s_ - 2 >= 0:
                lists.append(record(P2, s_ - 2))
            if 0 <= s_ - 1 < ntiles:
                lists.append(record(P1, s_ - 1))
            if s_ < ntiles:
                lists.append(record(P0, s_))
            lists = [l_ for l_ in lists if l_]
            pos = [0] * len(lists)
            total = sum(len(l_) for l_ in lists)
            for _ in range(total):
                best, bi = None, -1
                for li, l_ in enumerate(lists):
                    if pos[li] < len(l_):
                        frac = pos[li] / len(l_)
                        if best is None or frac < best:
                            best, bi = frac, li
                op = lists[bi][pos[bi]]
                pos[bi] += 1
                real_add(*op)
        p.emit(final_sems=[f"ytb_{k}" for k in range(NC3)])
    return nc


def _fm(v):
    return np.ascontiguousarray(np.asarray(v, np.float32).reshape(8, 128).T)


def _toT(x):
    return np.ascontiguousarray(x.reshape(T, 8, 128).transpose(2, 1, 0))


def assemble_G(Ab, h, l, inp):
    NBK = SEQ // 128
    qkv = np.stack([np.concatenate([Ab[q]["o_c"][w * 4 + h] for q in range(4)], 1) for w in range(3)])
    cwl = inp["gdn_conv_w"][l]
    cw = np.stack([cwl[:, w * 512 + h * 128:w * 512 + (h + 1) * 128].T for w in range(3)], 1).reshape(128, 12)
    oz = np.concatenate([Ab[q]["o_z"] for q in range(4)], 0)
    gab = np.stack([oz[:, 512 + h].reshape(NBK, 128).T, oz[:, 516 + h].reshape(NBK, 128).T], -1)
    hp = np.tile(np.array([[inp["gdn_A_log"][l, h], inp["gdn_dt_bias"][l, h]]], np.float32), (128, 1))
    z = oz[:, h * 128:(h + 1) * 128].reshape(NBK, 128, 128).transpose(1, 0, 2)
    nwb = np.tile(inp["gdn_norm"][l][None, :], (128, 1))
    f = lambda a: np.ascontiguousarray(a, dtype=np.float32)
    return dict(qkv=f(qkv), cw=f(cw), gab=f(gab), hp=f(hp), z=f(z), nwb=f(nwb))


_PROGS = {}


def _prog(name):
    if name not in _PROGS:
        _PROGS[name] = {"M": build_M, "A": build_A, "B": build_B, "G": build_G2, "C": lambda: build_C(False),
                        "CF": lambda: build_C(True)}[name]()
    return _PROGS[name]


def _run(name, in_maps):
    res = run_bass_kernel_spmd(_prog(name), in_maps, core_ids=list(range(8)))
    return res.results


def kernel(**inp):
    inp = {k: np.asarray(v) for k, v in inp.items()}
    cores = list(range(8))
    cT = np.ascontiguousarray(inp["c"].astype(np.float32).reshape(2, 8, 128).transpose(2, 1, 0))
    ims = []
    for core in cores:
        l, hf = core // 2, core % 2
        ims.append(dict(cT=cT, aw=np.ascontiguousarray(inp["ada_w"][l][:, hf * 3072:(hf + 1) * 3072]),
                        ab=np.ascontiguousarray(inp["ada_b"][l][hf * 3072:(hf + 1) * 3072].reshape(24, 128).T)))
    rm = _run("M", ims)
    mods = np.zeros((DEPTH, BATCH, 6 * D), np.float32)
    for core in cores:
        l, hf = core // 2, core % 2
        mods[l, :, hf * 3072:(hf + 1) * 3072] = rm[core]["mo"].transpose(2, 1, 0).reshape(2, 3072)
    xT = [_toT(inp["x"][core // 4, (core % 4) * T:(core % 4 + 1) * T].astype(np.float32)) for core in cores]
    for l in range(DEPTH):
        wl = {k: inp[k][l] for k in ("attn_sinks", "cmp_k_w1", "cmp_v_w1", "cmp_k_w2", "cmp_v_w2", "cmp_pe_k", "cmp_pe_v")}
        ims = []
        for core in cores:
            b, q = core // 4, core % 4
            m = mods[l, b]
            tabs = np.concatenate([_fm(inp["norm_mix"][l]), _fm(m[1024:2048]), _fm(m[0:1024])], 1)
            ims.append(dict(xT=xT[core], tabs=tabs, pos0=np.full((128, 1), q * T, np.float32), w=inp["w_in"][l]))
        ra = _run("A", ims)
        rb = _run("B", [assemble_B([ra[(core // 4) * 4 + q] for q in range(4)], core % 4, wl) for core in cores])
        rg = _run("G", [assemble_G([ra[(core // 4) * 4 + q] for q in range(4)], core % 4, l, inp) for core in cores])
        ims = []
        for core in cores:
            b, q = core // 4, core % 4
            m = mods[l, b]
            gd = [np.ascontiguousarray(rg[b * 4 + h]["yo"].transpose(1, 0, 2).reshape(SEQ, 128)[q * T:(q + 1) * T].T)
                  for h in range(4)]
            ocat = np.ascontiguousarray(np.concatenate([rb[core]["oc"], np.stack(gd)], 0))
            tabs = np.concatenate([_fm(m[2048:3072]), _fm(inp["norm_ffn"][l]), _fm(m[4096:5120]), _fm(m[3072:4096]),
                                   _fm(m[5120:6144]), _fm(inp["final_norm"])], 1)
            ims.append(dict(xT=xT[core], ocat=ocat, tabs=tabs, w_out=inp["w_out"][l], w_gu=inp["w_gate_up"][l],
                            w_down=inp["w_down"][l]))
        rc = _run("CF" if l == DEPTH - 1 else "C", ims)
        xT = [rc[core]["xo"] for core in cores]
    out = np.zeros((BATCH, SEQ, D), np.float32)
    for core in cores:
        b, q = core // 4, core % 4
        out[b, q * T:(q + 1) * T] = xT[core].transpose(2, 1, 0).reshape(T, D)
    return out
```
